# Optimizing a Trainium2 kernel written in Bass

```python
import jax
import jax.numpy as jnp
from jax import lax
import numpy as np

D_MODEL = 1024
BATCH = 2
SEQ = 8192
DEPTH = 1

CONV_CH = D_MODEL
CONV_WIDTH = 31
GLA_HEADS = 4
GLA_DK = D_MODEL // 2 // GLA_HEADS
GLA_DV = D_MODEL // GLA_HEADS
GLA_KDIM = GLA_HEADS * GLA_DK
GLA_VDIM = GLA_HEADS * GLA_DV
GATE_RANK = 16
GATE_NORMALIZER = 16.0
GLA_CHUNK = 64
FFN_DIM = 2816
FFN_CONV_WIDTH = 3
EPS = 1e-6

IN_SIZES = (2 * CONV_CH, GLA_KDIM, GLA_KDIM, GLA_VDIM, GLA_VDIM, GATE_RANK, D_MODEL, D_MODEL)
IN_DIM = sum(IN_SIZES)

kernel_name = "hybrid_conformer_gla_gated_merge"


def _rmsnorm(x, g):
    xf = x.astype(jnp.float32)
    y = xf * lax.rsqrt(jnp.mean(xf * xf, axis=-1, keepdims=True) + EPS)
    return (y * g.astype(jnp.float32)).astype(x.dtype)


def _layernorm(x, g, b):
    xf = x.astype(jnp.float32)
    mu = jnp.mean(xf, axis=-1, keepdims=True)
    xc = xf - mu
    var = jnp.mean(xc * xc, axis=-1, keepdims=True)
    y = xc * lax.rsqrt(var + EPS)
    return (y * g.astype(jnp.float32) + b.astype(jnp.float32)).astype(x.dtype)


def _causal_dwconv(x, w, b):
    width = w.shape[0]
    y = lax.conv_general_dilated(
        x, w[:, None, :].astype(x.dtype), window_strides=(1,),
        padding=[(width - 1, 0)], dimension_numbers=("NWC", "WIO", "NWC"),
        feature_group_count=x.shape[-1])
    return y + b.astype(x.dtype)


def _split_cols(p):
    offs = np.cumsum(np.array(IN_SIZES))[:-1].tolist()
    return jnp.split(p, offs, axis=-1)


def _gla_chunked(q, k, v, gk):
    bsz, seq, nh, dk = q.shape
    dv = v.shape[-1]
    n = seq // GLA_CHUNK

    def blk(t):
        return t.reshape(bsz, n, GLA_CHUNK, nh, t.shape[-1]).transpose(0, 3, 1, 2, 4)

    q, k, v, gk = blk(q), blk(k), blk(v), blk(gk)
    b = jnp.cumsum(gk, axis=3)
    b_last = b[:, :, :, -1:, :]
    q_dec = q * jnp.exp(b)
    k_inv = k * jnp.exp(-b)
    k_end = k * jnp.exp(b_last - b)
    causal = jnp.tril(jnp.ones((GLA_CHUNK, GLA_CHUNK), dtype=bool))
    att = jnp.einsum("bhnid,bhnjd->bhnij", q_dec, k_inv)
    att = jnp.where(causal, att, 0.0)
    o_intra = jnp.einsum("bhnij,bhnjv->bhniv", att, v)
    kv_chunk = jnp.einsum("bhnjd,bhnjv->bhndv", k_end, v)
    decay = jnp.exp(b_last[:, :, :, 0, :])

    def step(state, inp):
        dec, kv = inp
        return dec[..., None] * state + kv, state

    init = jnp.zeros((bsz, nh, dk, dv), q.dtype)
    _, s_prev = lax.scan(step, init, (jnp.moveaxis(decay, 2, 0), jnp.moveaxis(kv_chunk, 2, 0)))
    s_prev = jnp.moveaxis(s_prev, 0, 2)
    o_inter = jnp.einsum("bhnid,bhndv->bhniv", q_dec, s_prev)
    o = o_intra + o_inter
    return o.transpose(0, 2, 3, 1, 4).reshape(bsz, seq, nh, dv)


def setup_inputs(seed: int = 0) -> dict:
    key = jax.random.key(seed)
    ks = jax.random.split(key, 20)
    f32 = jnp.float32
    L, D = DEPTH, D_MODEL

    def nrm(k, shape, scale):
        return jax.random.normal(k, shape, f32) * scale

    return {
        "x": jax.random.normal(ks[0], (BATCH, SEQ, D), f32),
        "norm1_g": 1.0 + nrm(ks[1], (L, D), 0.02),
        "w_in": nrm(ks[2], (L, D, IN_DIM), D ** -0.5),
        "conv_dw_w": nrm(ks[3], (L, CONV_WIDTH, CONV_CH), CONV_WIDTH ** -0.5),
        "conv_dw_b": nrm(ks[4], (L, CONV_CH), 0.02),
        "conv_ln_g": 1.0 + nrm(ks[5], (L, CONV_CH), 0.02),
        "conv_ln_b": nrm(ks[6], (L, CONV_CH), 0.02),
        "w_conv_out": nrm(ks[7], (L, CONV_CH, D), CONV_CH ** -0.5),
        "w_gate_up": nrm(ks[8], (L, GATE_RANK, GLA_KDIM), GATE_RANK ** -0.5),
        "b_gate": nrm(ks[9], (L, GLA_KDIM), 0.1),
        "gla_norm_g": 1.0 + nrm(ks[10], (L, GLA_DV), 0.02),
        "w_gla_out": nrm(ks[11], (L, GLA_VDIM, D), GLA_VDIM ** -0.5),
        "w_o": nrm(ks[12], (L, D, D), D ** -0.5),
        "norm2_g": 1.0 + nrm(ks[13], (L, D), 0.02),
        "w_ffn_up": nrm(ks[14], (L, D, 2 * FFN_DIM), D ** -0.5),
        "ffn_dw_w": nrm(ks[15], (L, FFN_CONV_WIDTH, 2 * FFN_DIM), FFN_CONV_WIDTH ** -0.5),
        "ffn_dw_b": nrm(ks[16], (L, 2 * FFN_DIM), 0.02),
        "w_ffn_down": nrm(ks[17], (L, FFN_DIM, D), FFN_DIM ** -0.5),
        "final_g": 1.0 + nrm(ks[18], (D,), 0.02),
    }


def reference(x, norm1_g, w_in, conv_dw_w, conv_dw_b, conv_ln_g, conv_ln_b, w_conv_out,
              w_gate_up, b_gate, gla_norm_g, w_gla_out, w_o, norm2_g, w_ffn_up,
              ffn_dw_w, ffn_dw_b, w_ffn_down, final_g):
    bsz, seq, _ = x.shape
    h = x
    for l in range(DEPTH):
        u = _rmsnorm(h, norm1_g[l])
        proj = u @ w_in[l]
        a_glu, q, k, v, r, g_low, gate_a, gate_b = _split_cols(proj)

        c = a_glu[..., :CONV_CH] * jax.nn.sigmoid(a_glu[..., CONV_CH:])
        c = _causal_dwconv(c, conv_dw_w[l], conv_dw_b[l])
        c = jax.nn.silu(_layernorm(c, conv_ln_g[l], conv_ln_b[l]))
        y_conv = c @ w_conv_out[l]

        gk = jax.nn.log_sigmoid((g_low @ w_gate_up[l] + b_gate[l]).astype(jnp.float32)) / GATE_NORMALIZER
        qh = q.astype(jnp.float32).reshape(bsz, seq, GLA_HEADS, GLA_DK) * (GLA_DK ** -0.5)
        kh = k.astype(jnp.float32).reshape(bsz, seq, GLA_HEADS, GLA_DK)
        vh = v.astype(jnp.float32).reshape(bsz, seq, GLA_HEADS, GLA_DV)
        gkh = gk.reshape(bsz, seq, GLA_HEADS, GLA_DK)
        o = _gla_chunked(qh, kh, vh, gkh)
        o = o * lax.rsqrt(jnp.mean(o * o, axis=-1, keepdims=True) + EPS) * gla_norm_g[l].astype(jnp.float32)
        o = o.reshape(bsz, seq, GLA_VDIM).astype(x.dtype) * jax.nn.silu(r)
        y_gla = o @ w_gla_out[l]

        merged = jax.nn.sigmoid(gate_a) * y_conv + jax.nn.sigmoid(gate_b) * y_gla
        h = h + merged @ w_o[l]

        u2 = _rmsnorm(h, norm2_g[l])
        z = _causal_dwconv(u2 @ w_ffn_up[l], ffn_dw_w[l], ffn_dw_b[l])
        h = h + (jax.nn.silu(z[..., :FFN_DIM]) * z[..., FFN_DIM:]) @ w_ffn_down[l]
    return _rmsnorm(h, final_g)
```

```python
import sys
import numpy as np
import concourse.bass as bass
import concourse.mybir as mybir
from concourse.bass_utils import run_bass_kernel_spmd
from contextlib import ExitStack

F32 = mybir.dt.float32
BF16 = mybir.dt.bfloat16
AF = mybir.ActivationFunctionType
ALU = mybir.AluOpType

D = 1024
IN_DIM = 7184
FFN = 2816
NCORES = 8
SEG = 2048
HIST = 6144
HALO = 128
ROWS = HIST + HALO + SEG
EPS = 1e-6
EPOCH = 30000

O_A, O_BG, O_Q, O_K, O_V, O_R, O_GL, O_GA, O_GB = 0, 1024, 2048, 2560, 3072, 4096, 5120, 5136, 6160


class Buf:
    __slots__ = ("name", "lastw", "readers", "sem", "dcount")

    def __init__(self, name):
        self.name = name
        self.lastw = None
        self.readers = {}
        self.sem = None
        self.dcount = 0


class Sched:
    ENGS = ("sync", "scalar", "vector", "gpsimd", "tensor")

    def __init__(self, nc, stack):
        self.nc = nc
        self.stack = stack
        self.ops = {e: [] for e in self.ENGS}
        self.count = {e: 0 for e in self.ENGS}
        self.sems = {e: [] for e in self.ENGS}
        self.waited = {e: {} for e in self.ENGS}
        self.same_engine_sync = {"scalar": True, "vector": True, "gpsimd": True, "tensor": False, "sync": False}

    def new_sem(self, name):
        return self.stack.enter_context(self.nc.semaphore(name))

    def eng_sem(self, e, epoch):
        while len(self.sems[e]) <= epoch:
            self.sems[e].append(self.new_sem(f"p_{e}_{len(self.sems[e])}"))
        return self.sems[e][epoch]

    def _wait(self, eng, tok):
        if tok[0] == "e":
            _, src, idx = tok
            if src == eng and not self.same_engine_sync[eng]:
                return
            epoch, val = divmod(idx - 1, EPOCH)
            val += 1
            key = ("e", src, epoch)
            sem = self.eng_sem(src, epoch)
        else:
            _, buf, val = tok
            key = ("d", id(buf))
            sem = buf.sem
        if self.waited[eng].get(key, 0) >= val:
            return
        self.waited[eng][key] = val
        self.ops[eng].append(lambda e, sem=sem, val=val: e.wait_ge(sem, val))

    def _deps(self, eng, reads, writes):
        for b in reads:
            if b.lastw is not None:
                self._wait(eng, b.lastw)
        for b in writes:
            if b.lastw is not None and not (b.lastw[0] == "e" and b.lastw[1] == eng):
                self._wait(eng, b.lastw)
            for t in b.readers.values():
                if not (t[0] == "e" and t[1] == eng):
                    self._wait(eng, t)

    def emit(self, eng, fn, reads=(), writes=()):
        self._deps(eng, reads, writes)
        self.count[eng] += 1
        idx = self.count[eng]
        sem = self.eng_sem(eng, (idx - 1) // EPOCH)
        self.ops[eng].append(lambda e, fn=fn, sem=sem: fn(e).then_inc(sem, 1))
        tok = ("e", eng, idx)
        for b in writes:
            b.lastw = tok
            b.readers = {}
        for b in reads:
            b.readers[eng] = tok
        return tok

    def dma(self, eng, out, in_, reads=(), writes=(), track=None, **kw):
        self._deps(eng, reads, writes)
        if track.sem is None:
            track.sem = self.new_sem("d_" + track.name)
        track.dcount += 1
        val = track.dcount * 16
        sem = track.sem
        self.ops[eng].append(
            lambda e, sem=sem, out=out, in_=in_, kw=kw: e.dma_start(out=out, in_=in_, **kw).then_inc(sem, 16))
        tok = ("d", track, val)
        for b in writes:
            b.lastw = tok
            b.readers = {}
        for b in reads:
            b.readers["dma_" + track.name] = tok
        return tok

    def replay(self, block):
        for e in self.ENGS:
            ops = self.ops[e]

            def body(engine, ops=ops):
                for f in ops:
                    f(engine)
            getattr(block, e)(body)


def build_program(n_hist_tiles=HIST // 512, n_main_tiles=SEG // 512, dbg=None):
    nc = bass.Bass("TRN2", target_bir_lowering=False)
    dt_in = lambda name, shape: nc.dram_tensor(name, shape, F32, kind="ExternalInput").ap()
    x = dt_in("x", [ROWS, D])
    hasprev = dt_in("hasprev", [128, 1])
    norm1_g = dt_in("norm1_g", [D]); w_in = dt_in("w_in", [D, IN_DIM])
    conv_dw_w = dt_in("conv_dw_w", [31, D]); conv_dw_b = dt_in("conv_dw_b", [D])
    conv_ln_g = dt_in("conv_ln_g", [D]); conv_ln_b = dt_in("conv_ln_b", [D])
    w_conv_out = dt_in("w_conv_out", [D, D]); w_gate_up = dt_in("w_gate_up", [16, 512])
    b_gate = dt_in("b_gate", [512]); gla_norm_g = dt_in("gla_norm_g", [256])
    w_gla_out = dt_in("w_gla_out", [D, D]); w_o = dt_in("w_o", [D, D])
    norm2_g = dt_in("norm2_g", [D]); w_ffn_up = dt_in("w_ffn_up", [D, 2 * FFN])
    ffn_dw_w = dt_in("ffn_dw_w", [3, 2 * FFN]); ffn_dw_b = dt_in("ffn_dw_b", [2 * FFN])
    w_ffn_down = dt_in("w_ffn_down", [FFN, D]); final_g = dt_in("final_g", [D])
    out = nc.dram_tensor("out", [SEG, D], F32, kind="ExternalOutput").ap()
    dgd = nc.dram_tensor("dgd", [24, 128, 11 * 128], BF16).ap()
    wb = {"in": nc.dram_tensor("wb_in", [D, IN_DIM], BF16).ap(), "co": nc.dram_tensor("wb_co", [D, D], BF16).ap(),
          "go": nc.dram_tensor("wb_go", [D, D], BF16).ap(), "wo": nc.dram_tensor("wb_wo", [D, D], BF16).ap(),
          "up": nc.dram_tensor("wb_up", [D, 2 * FFN], BF16).ap(), "dn": nc.dram_tensor("wb_dn", [FFN, D], BF16).ap()}
    dbg_out = {}
    if dbg:
        for name, shape in dbg.items():
            dbg_out[name] = nc.dram_tensor("dbg_" + name, list(shape), F32, kind="ExternalOutput").ap()

    with ExitStack() as stack:
        S = Sched(nc, stack)
        _n = [0]

        def sb(shape, dt, name=None):
            _n[0] += 1
            return stack.enter_context(nc.sbuf_tensor(name or f"t{_n[0]}", list(shape), dt))

        identf = sb([128, 128], F32); identfB = Buf("identf")
        identb = sb([128, 128], BF16); identbB = Buf("identb")
        tri_inc = sb([128, 128], F32); tri_end = sb([128, 128], F32); cmask = sb([128, 128], F32)
        chsel = sb([128, 2], F32); onesf = sb([128, 128], F32); ones_row = sb([1, 128], BF16)
        constB = Buf("consts")
        stage = sb([128, 128], F32); stageB = Buf("stage")
        g1T = sb([128, 8], F32); g2T = sb([128, 8], F32); cbT = sb([128, 8], F32)
        lngT = sb([128, 8], F32); lnbT = sb([128, 8], F32); fbT = sb([128, 44], F32)
        gnT = sb([128, 2], F32); cwT = sb([128, 248], F32); fwT = sb([128, 132], F32)
        fgB_t = sb([128, D], F32)
        bgrow = sb([1, 512], BF16); wgu = sb([16, 512], BF16)
        hp = sb([128, 1], F32)
        vecB = Buf("vecs")
        banks = [stack.enter_context(nc.psum_tensor(f"ps{i}", [128, 512], F32)) for i in range(6)]
        bankB = [Buf(f"ps{i}") for i in range(6)]
        pTs = [stack.enter_context(nc.psum_tensor(f"pT{i}", [128, 1024], BF16)) for i in range(2)]; pTBs = [Buf(f"pT{i}") for i in range(2)]
        _pt = [0]

        def npT():
            i = _pt[0] % 2
            _pt[0] += 1
            return pTs[i], pTBs[i]
        _bk = [0]

        def nb():
            i = _bk[0] % 6
            _bk[0] += 1
            return banks[i], bankB[i]

        NSLOT = 4
        slots = [sb([128, 8, 512], BF16, f"slot{i}") for i in range(NSLOT)]
        slotB = [Buf(f"slot{i}") for i in range(NSLOT)]
        xs = [sb([128, D], F32, f"xs{i}") for i in range(2)]; xsB = [Buf(f"xs{i}") for i in range(2)]
        utms = [sb([128, D], BF16, f"utm{i}") for i in range(2)]; utmBs = [Buf(f"utm{i}") for i in range(2)]
        uT = sb([128, 8, 512], BF16); uTB = Buf("uT")
        glowT = sb([128, 512], BF16); glowTB = Buf("glowT")
        lbuf = sb([128, 4, 512], F32); lB = [Buf(f"l{i}") for i in range(4)]
        NTMP = 6
        tmps = [sb([128, 516], F32, f"tmp{i}") for i in range(NTMP)]; tmpB = [Buf(f"tmp{i}") for i in range(NTMP)]
        _tk = [0]

        def ntmp():
            i = _tk[0] % NTMP
            _tk[0] += 1
            return tmps[i], tmpB[i]
        Epl = sb([128, 4, 512], BF16); Emi = sb([128, 4, 512], BF16); EB = [Buf(f"E{i}") for i in range(4)]
        decs = [sb([128, 4, 8], F32, f"dec{p}") for p in range(2)]; decBs = [[Buf(f"dec{p}_{i}") for i in range(4)] for p in range(2)]
        dec = decs[0]; decB = decBs[0]
        kend = sb([128, 4, 512], BF16); kendB = [Buf(f"kend{i}") for i in range(4)]
        vtm = sb([128, 4, D], BF16); vtmB = [Buf(f"v{i}") for i in range(4)]
        kT = sb([128, 4, 512], BF16); kTB = [Buf(f"kT{i}") for i in range(4)]
        qT = sb([128, 4, 512], BF16); qTB = [Buf(f"qT{i}") for i in range(4)]
        silur = sb([128, 8, 512], BF16); silurB = [Buf(f"sr{i}") for i in range(8)]
        attm = sb([128, 4, 128], BF16); attmB = [Buf(f"att{i}") for i in range(4)]
        Sst = sb([128, D], F32); SstB = Buf("S")
        Sbf = [sb([128, D], BF16, f"Sbf{i}") for i in range(2)]; SbfB = [Buf(f"Sbf{i}") for i in range(2)]
        on = sb([128, D], BF16); onB = Buf("on")
        oss = sb([128, 8], F32); ossB = Buf("oss")
        actT = sb([128, 8, 512], BF16); actTB = [Buf(f"actT{i}") for i in range(8)]
        cin = sb([128, 8, 542], BF16); cinB = [Buf(f"cin{i}") for i in range(8)]
        arena = sb([128, 22 * 256], F32, "arena")
        cacc = arena[:, 0:4096].rearrange("p (k t) -> p k t", k=8)
        gT = arena.bitcast(BF16).rearrange("p (k t) -> p k t", k=22)
        arB = [Buf(f"ar{i}") for i in range(8)]
        m1 = sb([128, 8, 512], BF16); m1B = [Buf(f"m1{i}") for i in range(8)]
        hbuf = sb([128, 4, D], F32); hB = [Buf(f"h{i}") for i in range(4)]
        zhalo = sb([128, 44, 2], F32); zhB = Buf("zhalo")
        ss = sb([128, 8], F32); ssB = Buf("ss")
        dg = [sb([128, 11, 128], BF16, f"dg{i}") for i in range(3)]; dgB = [Buf(f"dg{i}") for i in range(3)]
        dgdB = Buf("dgd")
        PARTS = ((0, 11), (11, 10), (21, 10))
        lnst = [sb([128, 512], F32, f"lnst{i}") for i in range(2)]; lnstB = [Buf(f"lnst{i}") for i in range(2)]

        block = stack.enter_context(nc.Block())
        V, A, P, T_, SY = "vector", "scalar", "gpsimd", "tensor", "sync"

        def iota_sel(t, pattern, cmp, fill, base, cm, src=None):
            S.emit(P, lambda e: e.affine_select(out=t, in_=(src if src is not None else t), pattern=pattern,
                                                compare_op=cmp, fill=fill, base=base, channel_multiplier=cm),
                   reads=[constB], writes=[constB])
        S.emit(P, lambda e: e.memset(identf[:], 0.0), writes=[constB])
        iota_sel(identf[:], [[-1, 128]], ALU.not_equal, 1.0, 0, 1)
        S.emit(V, lambda e: e.tensor_copy(out=identb[:], in_=identf[:]), reads=[constB], writes=[identbB])
        S.emit(P, lambda e: e.memset(onesf[:], 1.0), writes=[constB])
        S.emit(P, lambda e: e.memset(ones_row[:], 1.0), writes=[constB])
        for t, val in ((tri_inc, -1.0 / 16), (tri_end, -1.0 / 16), (cmask, 1.0)):
            S.emit(P, lambda e, t=t, val=val: e.memset(t[:], val), writes=[constB])
        iota_sel(tri_inc[:], [[1, 128]], ALU.is_ge, 0.0, 0, -1)
        iota_sel(cmask[:], [[1, 128]], ALU.is_ge, 0.0, 0, -1)
        iota_sel(tri_end[:], [[-1, 128]], ALU.is_gt, 0.0, 0, 1)
        S.emit(P, lambda e: e.memset(tri_inc[0:64, 64:128], 0.0), writes=[constB])
        S.emit(P, lambda e: e.memset(cmask[0:64, 64:128], 0.0), writes=[constB])
        S.emit(P, lambda e: e.memset(tri_end[64:128, 0:64], 0.0), writes=[constB])
        S.emit(P, lambda e: e.memset(chsel[:], 0.0), writes=[constB])
        S.emit(P, lambda e: e.memset(chsel[0:64, 0:1], -1.0 / 16), writes=[constB])
        S.emit(P, lambda e: e.memset(chsel[64:128, 1:2], -1.0 / 16), writes=[constB])
        S.emit(P, lambda e: e.memset(zhalo[:], 0.0), writes=[zhB])
        S.emit(P, lambda e: e.memset(Sst[:], 0.0), writes=[SstB])
        S.emit(P, lambda e: e.memset(cin[:], 0.0), writes=cinB)

        def load_cols(dst, rows_ap, nrows):
            r0 = 0
            while r0 < nrows:
                n = min(128, nrows - r0)
                S.dma(SY, stage[0:n, :], rows_ap[r0:r0 + n, :], writes=[stageB], track=stageB)
                ps, psB = nb()
                S.emit(T_, lambda e, ps=ps, n=n: e.matmul(ps[:, 0:n], lhsT=stage[0:n, :], rhs=identf[0:n, 0:n], start=True, stop=True),
                       reads=[stageB, constB], writes=[psB])
                S.emit(V, lambda e, ps=ps, n=n, r0=r0: e.tensor_copy(out=dst[:, r0:r0 + n], in_=ps[:, 0:n]), reads=[psB], writes=[vecB])
                r0 += n
        load_cols(g1T, norm1_g.rearrange("(k p) -> k p", p=128), 8)
        load_cols(g2T, norm2_g.rearrange("(k p) -> k p", p=128), 8)
        load_cols(cbT, conv_dw_b.rearrange("(k p) -> k p", p=128), 8)
        load_cols(lngT, conv_ln_g.rearrange("(k p) -> k p", p=128), 8)
        load_cols(lnbT, conv_ln_b.rearrange("(k p) -> k p", p=128), 8)
        load_cols(fbT, ffn_dw_b.rearrange("(k p) -> k p", p=128), 44)
        load_cols(gnT, gla_norm_g.rearrange("(k p) -> k p", p=128), 2)
        load_cols(cwT, conv_dw_w.rearrange("j (k p) -> (j k) p", p=128), 248)
        load_cols(fwT, ffn_dw_w.rearrange("j (k p) -> (j k) p", p=128), 132)
        rowst = xs[0]
        S.dma(SY, rowst[0:1, :], final_g.rearrange("(o n) -> o n", o=1), writes=[xsB[0]], track=xsB[0])
        for hf in range(2):
            ps, psB = nb()
            S.emit(T_, lambda e, ps=ps, hf=hf: e.matmul(ps[:], lhsT=onesf[0:1, :], rhs=rowst[0:1, hf * 512:(hf + 1) * 512], start=True, stop=True),
                   reads=[xsB[0], constB], writes=[psB])
            S.emit(V, lambda e, ps=ps, hf=hf: e.tensor_copy(out=fgB_t[:, hf * 512:(hf + 1) * 512], in_=ps[:]), reads=[psB], writes=[vecB])
        setup_toks = [vecB.lastw, constB.lastw, identbB.lastw]
        setup_toks.append(S.dma(P, bgrow[:], b_gate.rearrange("(o n) -> o n", o=1), writes=[vecB], track=Buf("bgrow")))
        setup_toks.append(S.dma(P, wgu[:], w_gate_up, writes=[vecB], track=Buf("wgu")))
        setup_toks.append(S.dma(SY, hp[:], hasprev, writes=[vecB], track=Buf("hp")))
        for eng in (A, V, T_, P):
            for tk in setup_toks:
                S._wait(eng, tk)
        vecB.lastw = None; vecB.readers = {}
        constB.lastw = None; constB.readers = {}

        tapon = [False]

        def dbg_tap(name, ap, bufs, force=False):
            if dbg and name in dbg and (tapon[0] or force):
                tb = Buf("dbg_" + name)
                S.dma(P, dbg_out[name], ap, reads=bufs, track=tb)
                S._wait(P, ("d", tb, tb.dcount * 16))

        for g in range(24):
            kc, part = divmod(g, 3)
            j0, nj = PARTS[part]
            bi = g % 3
            for jj in range(nj):
                col = (j0 + jj) * 8 + kc
                S.emit(V, lambda e, bi=bi, jj=jj, col=col: e.tensor_scalar(out=dg[bi][:, jj, :], in0=identb[:], scalar1=cwT[:, col:col + 1], scalar2=None, op0=ALU.mult),
                       reads=[identbB], writes=[dgB[bi]])
            tok = S.dma(SY, dgd[g][:, 0:nj * 128], dg[bi][:, 0:nj, :].rearrange("p j i -> p (j i)"), reads=[dgB[bi]], track=dgdB)
            dgdB.lastw = tok

        def load_w(slot_i, src, c0, ncols, r0=0, nk=8):
            S.dma(P, slots[slot_i][:, 0:nk, 0:ncols],
                  src[r0 * 128:(r0 + nk) * 128, c0:c0 + ncols].rearrange("(k p) n -> p k n", p=128),
                  writes=[slotB[slot_i]], track=slotB[slot_i])
        wbB = Buf("wb")

        def load_wb(slot_i, src, c0, ncols, r0=0, nk=8):
            S.dma(P, slots[slot_i][:, 0:nk, 0:ncols],
                  src[r0 * 128:(r0 + nk) * 128, c0:c0 + ncols].rearrange("(k p) n -> p k n", p=128),
                  reads=[wbB], writes=[slotB[slot_i]], track=slotB[slot_i])

        def convert_weights():
            bounce = [(actT, actTB), (silur, silurB), (m1, m1B)]
            cvB = [Buf(f"cv{i}") for i in range(3)]
            blocks = []
            for c0 in range(0, IN_DIM - 16, 512):
                blocks.append(("in", 0, 8, c0, 512))
            blocks.append(("in", 0, 8, IN_DIM - 16, 16))
            for nm in ("co", "go", "wo"):
                for hf in range(2):
                    blocks.append((nm, 0, 8, hf * 512, 512))
            for c0 in range(0, 2 * FFN, 512):
                blocks.append(("up", 0, 8, c0, 512))
            for r0, nk in ((0, 8), (8, 8), (16, 6)):
                for hf in range(2):
                    blocks.append(("dn", r0, nk, hf * 512, 512))
            for bi, (nm, r0, nk, c0, ncols) in enumerate(blocks):
                bt, bB = bounce[bi % 3]
                srcap = WSRC[nm][r0 * 128:(r0 + nk) * 128, c0:c0 + ncols].rearrange("(k p) n -> p k n", p=128)
                dstap = wb[nm][r0 * 128:(r0 + nk) * 128, c0:c0 + ncols].rearrange("(k p) n -> p k n", p=128)
                S.dma(P, bt[:, 0:nk, 0:ncols], srcap, writes=bB, track=cvB[bi % 3])
                tok = S.dma(P, dstap, bt[:, 0:nk, 0:ncols], reads=bB, track=wbB)
                wbB.lastw = tok

        WSRC = {"in": w_in, "co": w_conv_out, "go": w_gla_out, "wo": w_o, "up": w_ffn_up, "dn": w_ffn_down}

        def tile_wseq(is_halo):
            q = [("in", O_GL, 512, 0, 8), ("in", O_K, 512, 0, 8), ("in", O_V, 512, 0, 8), ("in", O_V + 512, 512, 0, 8), ("in", O_Q, 512, 0, 8),
                 ("in", O_R, 512, 0, 8), ("in", O_R + 512, 512, 0, 8)]
            for hf in range(2):
                q += [("in", O_A + hf * 512, 512, 0, 8), ("in", O_BG + hf * 512, 512, 0, 8)]
            for hf in range(2):
                q += [("go", hf * 512, 512, 0, 8), ("in", O_GB + hf * 512, 512, 0, 8)]
            for hf in range(2):
                q += [("in", O_GA + hf * 512, 512, 0, 8)]
            for hf in range(2):
                q += [("co", hf * 512, 512, 0, 8)]
            for hf in range(2):
                q += [("wo", hf * 512, 512, 0, 8)]
            for g in range(6):
                nblk = 4 if g < 5 else 2
                q += [("up", g * 512, nblk * 128, 0, 8), ("up", FFN + g * 512, nblk * 128, 0, 8)]
            if not is_halo:
                for hf in range(2):
                    for g3 in range(3):
                        q += [("dn", hf * 512, 512, g3 * 8, 8 if g3 < 2 else 6)]
            return q
        WSEQ = tile_wseq(True)
        for _ in range(n_main_tiles):
            WSEQ += tile_wseq(False)
        _wi = [0, 0]

        def next_w(*key):
            i = _wi[0]
            assert WSEQ[i] == key, (i, WSEQ[i], key)
            while _wi[1] < len(WSEQ) and _wi[1] <= i + NSLOT - 2:
                k = WSEQ[_wi[1]]
                load_wb(_wi[1] % NSLOT, wb[k[0]], k[1], k[2], r0=k[3], nk=k[4])
                _wi[1] += 1
            _wi[0] += 1
            return i % NSLOT

        def norm_a(srcs, ums):
            n = len(srcs)
            for i, (ap, bf) in enumerate(srcs):
                um, umB = ums[i]
                S.emit(A, lambda e, ap=ap, i=i, um=um: e.activation(out=um, in_=ap, func=AF.Square, accum_out=ss[:, i:i + 1]),
                       reads=[bf], writes=list(umB) + [ssB])
            S.emit(A, lambda e: e.activation(out=ss[:, 4:4 + n], in_=ss[:, 0:n], func=AF.Ln, scale=1.0 / D, bias=EPS), reads=[ssB], writes=[ssB])
            S.emit(A, lambda e: e.activation(out=ss[:, 4:4 + n], in_=ss[:, 4:4 + n], func=AF.Exp, scale=-0.5), reads=[ssB], writes=[ssB])
            for i, (ap, bf) in enumerate(srcs):
                um, umB = ums[i]
                if i % 2 == 0:
                    S.emit(V, lambda e, ap=ap, i=i, um=um: e.tensor_scalar(out=um, in0=ap, scalar1=ss[:, 4 + i:5 + i], scalar2=None, op0=ALU.mult),
                           reads=[bf, ssB], writes=list(umB))
                else:
                    S.emit(A, lambda e, ap=ap, i=i, um=um: e.activation(out=um, in_=ap, func=AF.Identity, scale=ss[:, 4 + i:5 + i]),
                           reads=[bf, ssB], writes=list(umB))

        def norm_b(ums, gT_, s0):
            for i, (um, umB) in enumerate(ums):
                pT, pTB = npT()
                for kc in range(8):
                    S.emit(T_, lambda e, kc=kc, um=um, pT=pT: e.transpose(out=pT[:, kc * 128:(kc + 1) * 128], in_=um[:, kc * 128:(kc + 1) * 128], identity=identb[:]),
                           reads=list(umB) + [identbB], writes=[pTB])
                s = s0 + i
                for kc in range(8):
                    S.emit(V, lambda e, kc=kc, s=s, pT=pT: e.tensor_scalar(out=uT[:, kc, s * 128:(s + 1) * 128], in0=pT[:, kc * 128:(kc + 1) * 128],
                                                                         scalar1=gT_[:, kc:kc + 1], scalar2=None, op0=ALU.mult),
                           reads=[pTB, vecB], writes=[uTB])

        def um_std(i):
            return (utms[i][:], [utmBs[i]])

        def norm_stage(srcs, gT_, s0):
            for c in range(0, len(srcs), 2):
                ums = [um_std(i) for i in range(min(2, len(srcs) - c))]
                norm_a(srcs[c:c + 2], ums)
                norm_b(ums, gT_, s0 + c)

        def load_x(row0, i):
            S.dma(SY, xs[i][:], x[row0:row0 + 128, :], writes=[xsB[i]], track=xsB[i])

        def proj_T(slot_i, s, ncols=512):
            ps, psB = nb()
            for kc in range(8):
                S.emit(T_, lambda e, kc=kc, ps=ps: e.matmul(ps[:, 0:ncols], lhsT=uT[:, kc, s * 128:(s + 1) * 128], rhs=slots[slot_i][:, kc, 0:ncols],
                                                           start=(kc == 0), stop=(kc == 7)),
                       reads=[uTB, slotB[slot_i]], writes=[psB])
            return ps, psB

        def proj_F(w_ap_fn, wB, rhs, rhsB, T, nk=8, M=128):
            ps, psB = nb()
            for kc in range(nk):
                S.emit(T_, lambda e, kc=kc, ps=ps: e.matmul(ps[0:M, 0:T], lhsT=w_ap_fn(kc), rhs=rhs[:, kc, 0:T],
                                                           start=(kc == 0), stop=(kc == nk - 1)),
                       reads=list(wB) + list(rhsB), writes=[psB])
            return ps, psB

        def gate_stage(T, NS, main, gsl, p=0):
            gate_g1(T, NS, gsl)
            gate_g2(NS, main, p)

        def gate_g1(T, NS, gsl):
            ps, psB = proj_F(lambda kc: slots[gsl][:, kc, 0:128], [slotB[gsl]], uT, [uTB], T)
            S.emit(A, lambda e, ps=ps: e.activation(out=glowT[:, 0:T], in_=ps[:, 0:T], func=AF.Copy), reads=[psB], writes=[glowTB])
            dbg_tap("glowT", glowT[0:16, :], [glowTB]); dbg_tap("wgu", wgu[:], []); dbg_tap("bgrow", bgrow[:], [])
            pre = []
            for s in range(NS):
                ps, psB = nb()
                S.emit(T_, lambda e, ps=ps, s=s: e.matmul(ps[:], lhsT=glowT[0:16, s * 128:(s + 1) * 128], rhs=wgu[:], start=True, stop=False),
                       reads=[glowTB, vecB], writes=[psB])
                S.emit(T_, lambda e, ps=ps: e.matmul(ps[:], lhsT=ones_row[:], rhs=bgrow[:], start=False, stop=True),
                       reads=[constB, vecB], writes=[psB])
                pre.append((ps, psB))
            for s in range(NS):
                ps, psB = pre[s]
                tm, tmB = ntmp()
                S.emit(A, lambda e, ps=ps, tm=tm: e.activation(out=tm[:, 0:512], in_=ps[:], func=AF.Exp, scale=-1.0), reads=[psB], writes=[tmB])
                if s == 0:
                    dbg_tap("expn", tm[:, 0:512], [tmB])
                S.emit(A, lambda e, tm=tm, s=s: e.activation(out=lbuf[:, s, :], in_=tm[:, 0:512], func=AF.Ln, bias=1.0), reads=[tmB], writes=[lB[s]])

        def gate_g2(NS, main, p=0):
            dec, decB = decs[p], decBs[p]
            for s in range(NS):
                ps, psB = nb()
                for h in range(4):
                    S.emit(T_, lambda e, ps=ps, s=s, h=h: e.matmul(ps[:, h * 2:h * 2 + 2], lhsT=lbuf[:, s, h * 128:(h + 1) * 128], rhs=chsel[:], start=True, stop=True),
                           reads=[lB[s], constB], writes=[psB])
                S.emit(A, lambda e, ps=ps, s=s: e.activation(out=dec[:, s, :], in_=ps[:, 0:8], func=AF.Exp), reads=[psB], writes=[decB[s]])
                if main:
                    ps, psB = nb()
                    for h in range(4):
                        S.emit(T_, lambda e, ps=ps, s=s, h=h: e.matmul(ps[:, h * 128:(h + 1) * 128], lhsT=lbuf[:, s, h * 128:(h + 1) * 128], rhs=tri_inc[:], start=True, stop=True),
                               reads=[lB[s], constB], writes=[psB])
                    psv = ps[:].rearrange("p (h t) -> p h t", h=4)
                    S.emit(A, lambda e, psv=psv, s=s: e.activation(out=Epl[:, :, s * 128:(s + 1) * 128], in_=psv, func=AF.Exp), reads=[psB], writes=[EB[s]])
                    S.emit(A, lambda e, psv=psv, s=s: e.activation(out=Emi[:, :, s * 128:(s + 1) * 128], in_=psv, func=AF.Exp, scale=-1.0), reads=[psB], writes=[EB[s]])

        def kv_stage(NS, main, kslot_fn, vslot_fns, after_k=None):
            kslot = kslot_fn()
            for s in range(NS):
                ps, psB = nb()
                S.emit(T_, lambda e, ps=ps, s=s: e.matmul(ps[:], lhsT=tri_end[:], rhs=lbuf[:, s, :], start=True, stop=True),
                       reads=[lB[s], constB], writes=[psB])
                tm, tmB = ntmp()
                S.emit(A, lambda e, ps=ps, tm=tm: e.activation(out=tm[:, 0:512], in_=ps[:], func=AF.Exp), reads=[psB], writes=[tmB])
                ps, psB = proj_T(kslot, s)
                S.emit(V, lambda e, ps=ps, tm=tm, s=s: e.tensor_tensor(out=kend[:, s, :], in0=ps[:], in1=tm[:, 0:512], op=ALU.mult),
                       reads=[psB, tmB], writes=[kendB[s]])
            if after_k is not None:
                after_k(kslot)
            for hf in range(2):
                vs = vslot_fns[hf]()
                for s in range(NS):
                    ps, psB = proj_T(vs, s)
                    S.emit(A, lambda e, ps=ps, s=s, hf=hf: e.activation(out=vtm[:, s, hf * 512:(hf + 1) * 512], in_=ps[:], func=AF.Copy),
                           reads=[psB], writes=[vtmB[s]])

        def state_chunk(s, c, want_bf=None, p=0):
            dec, decB = decs[p], decBs[p]
            pk = [nb(), nb()]
            for h in range(4):
                ps, psB = pk[h // 2]
                S.emit(T_, lambda e, ps=ps, h=h: e.matmul(ps[:, (h % 2) * 256:(h % 2) * 256 + 256], lhsT=kend[c * 64:(c + 1) * 64, s, h * 128:(h + 1) * 128],
                                                         rhs=vtm[c * 64:(c + 1) * 64, s, h * 256:(h + 1) * 256], start=True, stop=True),
                       reads=[kendB[s], vtmB[s]], writes=[psB])
            for h in range(4):
                ps, psB = pk[h // 2]
                S.emit(V, lambda e, ps=ps, h=h: e.scalar_tensor_tensor(out=Sst[:, h * 256:(h + 1) * 256], in0=Sst[:, h * 256:(h + 1) * 256],
                                                                      scalar=dec[:, s, h * 2 + c:h * 2 + c + 1], in1=ps[:, (h % 2) * 256:(h % 2) * 256 + 256],
                                                                      op0=ALU.mult, op1=ALU.add),
                       reads=[SstB, decB[s], psB], writes=[SstB])
            if want_bf is not None:
                S.emit(A, lambda e: e.activation(out=Sbf[want_bf][:], in_=Sst[:], func=AF.Copy), reads=[SstB], writes=[SbfB[want_bf]])

        hist_slots = None
        if n_hist_tiles > 0 or True:
            load_w(0, w_in, O_K, 512); load_w(1, w_in, O_V, 512); load_w(2, w_in, O_V + 512, 512); load_w(3, w_in, O_GL, 512)
            convert_weights()
        hums = [um_std(0), um_std(1), (on[:], [onB]), (Sbf[1][:], [SbfB[1]])]

        def hist_na(t):
            for pr in range(2):
                for i in range(2):
                    load_x(t * 512 + (pr * 2 + i) * 128, i)
                norm_a([(xs[i][:], xsB[i]) for i in range(2)], hums[pr * 2:pr * 2 + 2])

        def hist_state(t):
            for s4 in range(4):
                state_chunk(s4, 0, p=t % 2)
                state_chunk(s4, 1, p=t % 2)

        if n_hist_tiles > 0:
            hist_na(0)
            norm_b(hums, g1T, 0)
            gate_stage(512, 4, False, 3, p=0)
        for t in range(n_hist_tiles):
            if t + 1 < n_hist_tiles:
                hist_na(t + 1)
            kv_stage(4, False, lambda: 0, (lambda: 1, lambda: 2))
            if t + 1 < n_hist_tiles:
                norm_b(hums, g1T, 0)
                gate_g1(512, 4, 3)
                hist_state(t)
                gate_g2(4, False, (t + 1) % 2)
            else:
                hist_state(t)
        S.emit(A, lambda e: e.activation(out=Sbf[0][:], in_=Sst[:], func=AF.Copy), reads=[SstB], writes=[SbfB[0]])
        sbf_cur = [0]

        dbg_tap("S_hist", Sst[:], [SstB], force=True)

        def main_tile(row0, T, is_halo, out_row0, prefetched=False, next_row0=None):
            NS = T // 128
            pf_ums = [um_std(0), um_std(1),
                      (actT[:, 0:2, :].rearrange("p k t -> p (k t)"), actTB[0:2]), (actT[:, 2:4, :].rearrange("p k t -> p (k t)"), actTB[2:4])]
            for pr in range((NS + 1) // 2):
                nn = min(2, NS - pr * 2)
                if prefetched:
                    ums = pf_ums[pr * 2:pr * 2 + nn]
                else:
                    ums = [um_std(i) for i in range(nn)]
                    for i in range(nn):
                        load_x(row0 + (pr * 2 + i) * 128, i)
                    norm_a([(xs[i][:], xsB[i]) for i in range(nn)], ums)
                norm_b(ums, g1T, pr * 2)

            def prefetch_next():
                if next_row0 is not None:
                    for pr in range(2):
                        for i in range(2):
                            load_x(next_row0 + (pr * 2 + i) * 128, i)
                        norm_a([(xs[i][:], xsB[i]) for i in range(2)], pf_ums[pr * 2:pr * 2 + 2])
            dbg_tap("uT", uT[:], [uTB])
            gate_stage(T, NS, True, next_w("in", O_GL, 512, 0, 8))
            def k_feature_major(ks):
                for h in range(4):
                    ps, psB = proj_F(lambda kc, h=h: slots[ks][:, kc, h * 128:(h + 1) * 128], [slotB[ks]], uT, [uTB], T)
                    S.emit(V, lambda e, ps=ps, h=h: e.tensor_tensor(out=kT[:, h, 0:T], in0=ps[:, 0:T], in1=Emi[:, h, 0:T], op=ALU.mult),
                           reads=[psB] + EB[0:NS], writes=[kTB[h]])
            kv_stage(NS, True, lambda: next_w("in", O_K, 512, 0, 8),
                     (lambda: next_w("in", O_V, 512, 0, 8), lambda: next_w("in", O_V + 512, 512, 0, 8)), after_k=k_feature_major)
            qs = next_w("in", O_Q, 512, 0, 8)
            for h in range(4):
                ps, psB = proj_F(lambda kc, h=h: slots[qs][:, kc, h * 128:(h + 1) * 128], [slotB[qs]], uT, [uTB], T)
                S.emit(V, lambda e, ps=ps, h=h: e.scalar_tensor_tensor(out=qT[:, h, 0:T], in0=ps[:, 0:T], scalar=128.0 ** -0.5, in1=Epl[:, h, 0:T],
                                                                      op0=ALU.mult, op1=ALU.mult),
                       reads=[psB] + EB[0:NS], writes=[qTB[h]])
            for hf in range(2):
                rs = next_w("in", O_R + hf * 512, 512, 0, 8)
                for j in range(4):
                    kc = hf * 4 + j
                    ps, psB = proj_F(lambda kk, j=j, rs=rs: slots[rs][:, kk, j * 128:(j + 1) * 128], [slotB[rs]], uT, [uTB], T)
                    S.emit(A, lambda e, ps=ps, kc=kc: e.activation(out=silur[:, kc, 0:T], in_=ps[:, 0:T], func=AF.Silu), reads=[psB], writes=[silurB[kc]])
            dbg_tap("l", lbuf[:], lB); dbg_tap("kT", kT[:], kTB); dbg_tap("qT", qT[:], qTB); dbg_tap("vtm", vtm[:], vtmB)
            dbg_tap("kend", kend[:], kendB); dbg_tap("silur", silur[:], silurB); dbg_tap("dec", decs[0][:], decBs[0])
            for hf in range(2):
                sa = next_w("in", O_A + hf * 512, 512, 0, 8)
                sg = next_w("in", O_BG + hf * 512, 512, 0, 8)
                for j in range(4):
                    kc = hf * 4 + j
                    psa, psaB = proj_F(lambda kk, j=j, sa=sa: slots[sa][:, kk, j * 128:(j + 1) * 128], [slotB[sa]], uT, [uTB], T)
                    psg, psgB = proj_F(lambda kk, j=j, sg=sg: slots[sg][:, kk, j * 128:(j + 1) * 128], [slotB[sg]], uT, [uTB], T)
                    tm, tmB = ntmp()
                    S.emit(A, lambda e, psg=psg, tm=tm: e.activation(out=tm[:, 0:T], in_=psg[:, 0:T], func=AF.Sigmoid), reads=[psgB], writes=[tmB])
                    S.emit(V, lambda e, psa=psa, tm=tm, kc=kc: e.tensor_tensor(out=cin[:, kc, 30:30 + T], in0=psa[:, 0:T], in1=tm[:, 0:T], op=ALU.mult),
                           reads=[psaB, tmB], writes=[cinB[kc]])
            dgtrk = [Buf(f"dgl{i}") for i in range(3)] if not hasattr(main_tile, "_dgtrk") else main_tile._dgtrk
            main_tile._dgtrk = dgtrk

            def load_part(g):
                if g >= 24:
                    return
                j0, nj = PARTS[g % 3]
                bi = g % 3
                S.dma(SY, dg[bi][:, 0:nj, :], dgd[g][:, 0:nj * 128].rearrange("p (j i) -> p j i", i=128), reads=[dgdB], writes=[dgB[bi]], track=dgtrk[bi])

            def conv_block(kc):
                psc, pscB = nb()
                for part in range(3):
                    g = kc * 3 + part
                    j0, nj = PARTS[part]
                    bi = g % 3
                    load_part(g + 2)
                    for jj in range(nj):
                        j = j0 + jj
                        S.emit(T_, lambda e, bi=bi, jj=jj, j=j: e.matmul(psc[:, 0:T], lhsT=dg[bi][:, jj, :], rhs=cin[:, kc, j:j + T], start=(j == 0), stop=(j == 30)),
                               reads=[dgB[bi], cinB[kc]], writes=[pscB])
                S.emit(A, lambda e: e.activation(out=cacc[:, kc, 0:T], in_=psc[:, 0:T], func=AF.Identity, bias=cbT[:, kc:kc + 1]),
                       reads=[pscB], writes=[arB[kc]])
                S.emit(A, lambda e: e.activation(out=cin[:, kc, 0:30], in_=cin[:, kc, T:T + 30], func=AF.Copy), reads=[cinB[kc]], writes=[cinB[kc]])

            gla_po = {}

            def gla_a(s):
                tok = slice(s * 128, (s + 1) * 128)
                for h in range(4):
                    ps, psB = nb()
                    S.emit(T_, lambda e, ps=ps, h=h: e.matmul(ps[:, 0:128], lhsT=kT[:, h, tok], rhs=qT[:, h, tok], start=True, stop=True),
                           reads=[kTB[h], qTB[h]], writes=[psB])
                    S.emit(V, lambda e, ps=ps, h=h: e.tensor_tensor(out=attm[:, h, :], in0=ps[:, 0:128], in1=cmask[:], op=ALU.mult),
                           reads=[psB, constB], writes=[attmB[h]])
                c0 = sbf_cur[0]; c1 = 1 - c0
                state_chunk(s, 0, want_bf=c1)

            def gla_b(s):
                c0 = sbf_cur[0]; c1 = 1 - c0
                po = [nb(), nb()]
                gla_po[s] = po
                for h in range(4):
                    ps, psB = po[h // 2]
                    cols = slice((h % 2) * 256, (h % 2) * 256 + 256)
                    hc = slice(h * 256, (h + 1) * 256)
                    S.emit(T_, lambda e, ps=ps, h=h, cols=cols, hc=hc: e.matmul(ps[:, cols], lhsT=attm[:, h, :], rhs=vtm[:, s, hc], start=True, stop=False),
                           reads=[attmB[h], vtmB[s]], writes=[psB])
                    S.emit(T_, lambda e, ps=ps, h=h, cols=cols, hc=hc: e.matmul(ps[0:64, cols], lhsT=qT[:, h, s * 128:s * 128 + 64], rhs=Sbf[c0][:, hc], start=False, stop=False),
                           reads=[qTB[h], SbfB[c0]], writes=[psB])
                    S.emit(T_, lambda e, ps=ps, h=h, cols=cols, hc=hc: e.matmul(ps[64:128, cols], lhsT=qT[:, h, s * 128 + 64:s * 128 + 128], rhs=Sbf[c1][:, hc], start=False, stop=True,
                                                                             tile_position=(0, 64)),
                           reads=[qTB[h], SbfB[c1]], writes=[psB])
                state_chunk(s, 1, want_bf=c0)

            def gla_c(s):
                tok = slice(s * 128, (s + 1) * 128)
                po = gla_po[s]
                for h in range(4):
                    ps, psB = po[h // 2]
                    cols = slice((h % 2) * 256, (h % 2) * 256 + 256)
                    S.emit(A, lambda e, ps=ps, h=h, cols=cols: e.activation(out=on[:, h * 256:(h + 1) * 256], in_=ps[:, cols], func=AF.Square, accum_out=oss[:, h:h + 1]),
                           reads=[psB], writes=[onB, ossB])
                S.emit(A, lambda e: e.activation(out=oss[:, 4:8], in_=oss[:, 0:4], func=AF.Ln, scale=1.0 / 256, bias=EPS), reads=[ossB], writes=[ossB])
                S.emit(A, lambda e: e.activation(out=oss[:, 4:8], in_=oss[:, 4:8], func=AF.Exp, scale=-0.5), reads=[ossB], writes=[ossB])
                for h in range(4):
                    ps, psB = po[h // 2]
                    cols = slice((h % 2) * 256, (h % 2) * 256 + 256)
                    S.emit(A, lambda e, ps=ps, h=h, cols=cols: e.activation(out=on[:, h * 256:(h + 1) * 256], in_=ps[:, cols], func=AF.Identity, scale=oss[:, 4 + h:5 + h]),
                           reads=[psB, ossB], writes=[onB])
                pT, pTB = npT()
                for kc in range(8):
                    S.emit(T_, lambda e, kc=kc, pT=pT: e.transpose(out=pT[:, kc * 128:(kc + 1) * 128], in_=on[:, kc * 128:(kc + 1) * 128], identity=identb[:]),
                           reads=[onB, identbB], writes=[pTB])
                for kc in range(8):
                    S.emit(V, lambda e, kc=kc, pT=pT: e.scalar_tensor_tensor(out=actT[:, kc, tok], in0=pT[:, kc * 128:(kc + 1) * 128], scalar=gnT[:, kc % 2:kc % 2 + 1],
                                                                      in1=silur[:, kc, tok], op0=ALU.mult, op1=ALU.mult),
                           reads=[pTB, vecB, silurB[kc]], writes=[actTB[kc]])
            load_part(0)
            load_part(1)
            kc_next = 0
            for s in range(NS):
                gla_a(s)
                conv_block(kc_next); kc_next += 1
                gla_b(s)
                conv_block(kc_next); kc_next += 1
                gla_c(s)
            while kc_next < 8:
                conv_block(kc_next); kc_next += 1
            dbg_tap("cin", cin[:, :, 30:542], cinB); dbg_tap("cacc", cacc, arB); dbg_tap("oT", actT[:], actTB)
            psm = pTs[0].bitcast(F32); psmB = pTBs[0]
            psq = pTs[1].bitcast(F32); psqB = pTBs[1]
            for hf in range(2):
                sw = next_w("go", hf * 512, 512, 0, 8)
                sg = next_w("in", O_GB + hf * 512, 512, 0, 8)
                for j in range(4):
                    ob = hf * 4 + j
                    psy, psyB = proj_F(lambda kk, j=j, sw=sw: slots[sw][:, kk, j * 128:(j + 1) * 128], [slotB[sw]], actT, actTB, T)
                    psg, psgB = proj_F(lambda kk, j=j, sg=sg: slots[sg][:, kk, j * 128:(j + 1) * 128], [slotB[sg]], uT, [uTB], T)
                    tq, tqB = ntmp()
                    S.emit(A, lambda e, ob=ob, tq=tq: e.activation(out=tq[:, 0:T], in_=cacc[:, ob, 0:T], func=AF.Square), reads=[arB[ob]], writes=[tqB])
                    tm, tmB = ntmp()
                    S.emit(A, lambda e, psg=psg, tm=tm: e.activation(out=tm[:, 0:T], in_=psg[:, 0:T], func=AF.Sigmoid), reads=[psgB], writes=[tmB])
                    S.emit(V, lambda e, psy=psy, tm=tm, ob=ob: e.tensor_tensor(out=m1[:, ob, 0:T], in0=psy[:, 0:T], in1=tm[:, 0:T], op=ALU.mult),
                           reads=[psyB, tmB], writes=[m1B[ob]])
                    S.emit(T_, lambda e, ob=ob: e.matmul(psm[:, 0:T], lhsT=onesf[:], rhs=cacc[:, ob, 0:T], start=(ob == 0), stop=(ob == 7)),
                           reads=[constB, arB[ob]], writes=[psmB])
                    S.emit(T_, lambda e, ob=ob, tq=tq: e.matmul(psq[:, 0:T], lhsT=onesf[:], rhs=tq[:, 0:T], start=(ob == 0), stop=(ob == 7)),
                           reads=[constB, tqB], writes=[psqB])
            mean, rstd = lnst[0], lnst[1]
            meanB, rstdB = lnstB
            S.emit(A, lambda e: e.activation(out=mean[:, 0:T], in_=psm[:, 0:T], func=AF.Identity, scale=1.0 / D), reads=[psmB], writes=[meanB])
            S.emit(V, lambda e: e.tensor_tensor(out=rstd[:, 0:T], in0=mean[:, 0:T], in1=mean[:, 0:T], op=ALU.mult), reads=[meanB], writes=[rstdB])
            S.emit(V, lambda e: e.scalar_tensor_tensor(out=rstd[:, 0:T], in0=psq[:, 0:T], scalar=1.0 / D, in1=rstd[:, 0:T], op0=ALU.mult, op1=ALU.subtract),
                   reads=[psqB, rstdB], writes=[rstdB])
            S.emit(A, lambda e: e.activation(out=rstd[:, 0:T], in_=rstd[:, 0:T], func=AF.Ln, bias=EPS), reads=[rstdB], writes=[rstdB])
            S.emit(A, lambda e: e.activation(out=rstd[:, 0:T], in_=rstd[:, 0:T], func=AF.Exp, scale=-0.5), reads=[rstdB], writes=[rstdB])
            nmr, nmrB = mean, meanB
            S.emit(V, lambda e: e.scalar_tensor_tensor(out=mean[:, 0:T], in0=mean[:, 0:T], scalar=-1.0, in1=rstd[:, 0:T], op0=ALU.mult, op1=ALU.mult),
                   reads=[meanB, rstdB], writes=[meanB])
            for hf in range(2):
                sg = next_w("in", O_GA + hf * 512, 512, 0, 8)
                for j in range(4):
                    kc = hf * 4 + j
                    psg, psgB = proj_F(lambda kk, j=j, sg=sg: slots[sg][:, kk, j * 128:(j + 1) * 128], [slotB[sg]], uT, [uTB], T)
                    tm, tmB = ntmp()
                    S.emit(V, lambda e, kc=kc, tm=tm: e.tensor_tensor(out=tm[:, 0:T], in0=cacc[:, kc, 0:T], in1=rstd[:, 0:T], op=ALU.mult),
                           reads=[arB[kc], rstdB], writes=[tmB])
                    S.emit(V, lambda e, tm=tm: e.tensor_tensor(out=tm[:, 0:T], in0=tm[:, 0:T], in1=nmr[:, 0:T], op=ALU.add), reads=[tmB, nmrB], writes=[tmB])
                    S.emit(A, lambda e, kc=kc, tm=tm: e.activation(out=actT[:, kc, 0:T], in_=tm[:, 0:T], func=AF.Silu, scale=lngT[:, kc:kc + 1], bias=lnbT[:, kc:kc + 1]),
                           reads=[tmB, vecB], writes=[actTB[kc]])
                    S.emit(A, lambda e, psg=psg, kc=kc: e.activation(out=cin[:, kc, 30:30 + T], in_=psg[:, 0:T], func=AF.Sigmoid), reads=[psgB], writes=[cinB[kc]])
            dbg_tap("cact", actT[:], actTB)
            for hf in range(2):
                sw = next_w("co", hf * 512, 512, 0, 8)
                for j in range(4):
                    ob = hf * 4 + j
                    psy, psyB = proj_F(lambda kk, j=j, sw=sw: slots[sw][:, kk, j * 128:(j + 1) * 128], [slotB[sw]], actT, actTB, T)
                    tm, tmB = ntmp()
                    S.emit(V, lambda e, psy=psy, tm=tm, ob=ob: e.tensor_tensor(out=tm[:, 0:T], in0=psy[:, 0:T], in1=cin[:, ob, 30:30 + T], op=ALU.mult),
                           reads=[psyB, cinB[ob]], writes=[tmB])
                    S.emit(V, lambda e, tm=tm, ob=ob: e.tensor_tensor(out=silur[:, ob, 0:T], in0=tm[:, 0:T], in1=m1[:, ob, 0:T], op=ALU.add),
                           reads=[tmB, m1B[ob]], writes=[silurB[ob]])
            dbg_tap("merged", silur[:], silurB)
            def wo_half(hf):
                sw = next_w("wo", hf * 512, 512, 0, 8)
                for s in range(NS):
                    ps, psB = nb()
                    for kc in range(8):
                        S.emit(T_, lambda e, kc=kc, ps=ps, s=s: e.matmul(ps[:], lhsT=silur[:, kc, s * 128:(s + 1) * 128], rhs=slots[sw][:, kc, :], start=(kc == 0), stop=(kc == 7)),
                               reads=[silurB[kc], slotB[sw]], writes=[psB])
                    xi = (hf * NS + s) % 2
                    S.dma(SY, xs[xi][:, 0:512], x[row0 + s * 128:row0 + (s + 1) * 128, hf * 512:(hf + 1) * 512], writes=[xsB[xi]], track=xsB[xi])
                    S.emit(V, lambda e, ps=ps, s=s, hf=hf, xi=xi: e.tensor_tensor(out=hbuf[:, s, hf * 512:(hf + 1) * 512], in0=ps[:], in1=xs[xi][:, 0:512], op=ALU.add),
                           reads=[psB, xsB[xi]], writes=[hB[s]])
            for hf in range(2):
                wo_half(hf)
            dbg_tap("h1", hbuf[:], hB)
            norm_stage([(hbuf[:, s, :], hB[s]) for s in range(NS)], g2T, 0)
            ngrp = 6
            for g in range(ngrp):
                nblk = 4 if g < 5 else 2
                sa = next_w("up", g * 512, nblk * 128, 0, 8)
                sbb = next_w("up", FFN + g * 512, nblk * 128, 0, 8)
                for j in range(nblk):
                    i = g * 4 + j
                    accs = []
                    for which, sl in ((0, sa), (1, sbb)):
                        blk = which * 22 + i
                        ps, psB = proj_F(lambda kk, j=j, sl=sl: slots[sl][:, kk, j * 128:(j + 1) * 128], [slotB[sl]], uT, [uTB], T)
                        zb, zbB = ntmp()
                        S.emit(A, lambda e, zb=zb, blk=blk: e.activation(out=zb[:, 0:2], in_=zhalo[:, blk, :], func=AF.Copy), reads=[zhB], writes=[zbB])
                        S.emit(A, lambda e, zb=zb, ps=ps: e.activation(out=zb[:, 2:2 + T], in_=ps[:, 0:T], func=AF.Copy), reads=[psB], writes=[zbB])
                        S.emit(A, lambda e, zb=zb, blk=blk: e.activation(out=zhalo[:, blk, :], in_=zb[:, T:T + 2], func=AF.Copy), reads=[zbB], writes=[zhB])
                        ac, acB = ntmp()
                        S.emit(A, lambda e, zb=zb, ac=ac, blk=blk: e.activation(out=ac[:, 0:T], in_=zb[:, 0:T], func=AF.Identity,
                                                                               scale=fwT[:, blk:blk + 1], bias=fbT[:, blk:blk + 1]),
                               reads=[zbB, vecB], writes=[acB])
                        for jj in (1, 2):
                            S.emit(V, lambda e, zb=zb, ac=ac, blk=blk, jj=jj: e.scalar_tensor_tensor(out=ac[:, 0:T], in0=zb[:, jj:jj + T], scalar=fwT[:, jj * 44 + blk:jj * 44 + blk + 1],
                                                                                                in1=ac[:, 0:T], op0=ALU.mult, op1=ALU.add),
                                   reads=[zbB, vecB, acB], writes=[acB])
                        accs.append((ac, acB))
                    if not is_halo:
                        (aa, aaB), (ab, abB) = accs
                        S.emit(A, lambda e, aa=aa: e.activation(out=aa[:, 0:T], in_=aa[:, 0:T], func=AF.Silu), reads=[aaB], writes=[aaB])
                        S.emit(V, lambda e, aa=aa, ab=ab, i=i: e.tensor_tensor(out=gT[:, i, 0:T], in0=aa[:, 0:T], in1=ab[:, 0:T], op=ALU.mult),
                               reads=[aaB, abB], writes=[arB[i % 8]])
            prefetch_next()
            if is_halo:
                S.emit(V, lambda e: e.tensor_scalar(out=zhalo[:], in0=zhalo[:], scalar1=hp[:, 0:1], scalar2=None, op0=ALU.mult), reads=[zhB, vecB], writes=[zhB])
                return
            dbg_tap("gT", gT, arB)
            for hf in range(2):
                pss = [nb() for _ in range(NS)]
                for g3 in range(3):
                    nk = 8 if g3 < 2 else 6
                    sw = next_w("dn", hf * 512, 512, g3 * 8, nk)
                    for s in range(NS):
                        ps, psB = pss[s]
                        for kk in range(nk):
                            i = g3 * 8 + kk
                            S.emit(T_, lambda e, ps=ps, s=s, kk=kk, i=i, sw=sw: e.matmul(ps[:], lhsT=gT[:, i, s * 128:(s + 1) * 128], rhs=slots[sw][:, kk, :],
                                                                                   start=(i == 0), stop=(i == 21)),
                                   reads=[arB[i % 8], slotB[sw]], writes=[psB])
                for s in range(NS):
                    ps, psB = pss[s]
                    S.emit(V, lambda e, ps=ps, s=s, hf=hf: e.tensor_tensor(out=hbuf[:, s, hf * 512:(hf + 1) * 512], in0=ps[:], in1=hbuf[:, s, hf * 512:(hf + 1) * 512], op=ALU.add),
                           reads=[psB, hB[s]], writes=[hB[s]])
            for s in range(NS):
                S.emit(A, lambda e, s=s: e.activation(out=on[:], in_=hbuf[:, s, :], func=AF.Square, accum_out=ss[:, s:s + 1]), reads=[hB[s]], writes=[onB, ssB])
            S.emit(A, lambda e: e.activation(out=ss[:, 4:8], in_=ss[:, 0:4], func=AF.Ln, scale=1.0 / D, bias=EPS), reads=[ssB], writes=[ssB])
            S.emit(A, lambda e: e.activation(out=ss[:, 4:8], in_=ss[:, 4:8], func=AF.Exp, scale=-0.5), reads=[ssB], writes=[ssB])
            for s in range(NS):
                xi = s % 2
                S.emit(V, lambda e, s=s, xi=xi: e.scalar_tensor_tensor(out=xs[xi][:], in0=hbuf[:, s, :], scalar=ss[:, 4 + s:5 + s], in1=fgB_t[:], op0=ALU.mult, op1=ALU.mult),
                       reads=[hB[s], ssB, vecB], writes=[xsB[xi]])
                S.dma(SY, out[out_row0 + s * 128:out_row0 + (s + 1) * 128, :], xs[xi][:], reads=[xsB[xi]], track=xsB[xi])

        main_tile(HIST, HALO, True, None, prefetched=False, next_row0=(HIST + HALO if n_main_tiles > 0 else None))
        for t in range(n_main_tiles):
            tapon[0] = (t == 0)
            main_tile(HIST + HALO + t * 512, 512, False, t * 512, prefetched=True,
                      next_row0=(HIST + HALO + (t + 1) * 512 if t + 1 < n_main_tiles else None))
        for i in range(2):
            S._wait(SY, ("d", xsB[i], xsB[i].dcount * 16))
        S.replay(block)
    return nc


def make_in_maps(inputs):
    x = np.asarray(inputs["x"], dtype=np.float32)
    sq = lambda k: np.ascontiguousarray(np.asarray(inputs[k], dtype=np.float32)[0])
    shared = {k: sq(k) for k in ("norm1_g", "w_in", "conv_dw_w", "conv_dw_b", "conv_ln_g", "conv_ln_b", "w_conv_out",
                                 "w_gate_up", "b_gate", "gla_norm_g", "w_gla_out", "w_o", "norm2_g", "w_ffn_up",
                                 "ffn_dw_w", "ffn_dw_b", "w_ffn_down")}
    shared["final_g"] = np.ascontiguousarray(np.asarray(inputs["final_g"], dtype=np.float32))
    in_maps = []
    for c in range(NCORES):
        b, r = divmod(c, 4)
        start = r * SEG
        xc = np.zeros((ROWS, D), np.float32)
        lo = start - (HIST + HALO)
        src_lo = max(lo, 0)
        xc[src_lo - lo:] = x[b, src_lo:start + SEG]
        m = dict(shared)
        m["x"] = xc
        m["hasprev"] = np.full((128, 1), 1.0 if r > 0 else 0.0, np.float32)
        in_maps.append(m)
    return in_maps


_NC_CACHE = {}


def kernel(**inputs):
    if "nc" not in _NC_CACHE:
        _NC_CACHE["nc"] = build_program()
    nc = _NC_CACHE["nc"]
    in_maps = make_in_maps(inputs)
    res = run_bass_kernel_spmd(nc, in_maps, core_ids=list(range(NCORES)))
    outp = np.zeros((2, 4 * SEG, D), np.float32)
    for c in range(NCORES):
        b, r = divmod(c, 4)
        outp[b, r * SEG:(r + 1) * SEG] = res.results[c]["out"]
    return outp
```

```python
import sys
import numpy as np
import concourse.bass as bass
import concourse.mybir as mybir
from concourse.bass_utils import run_bass_kernel_spmd
from contextlib import ExitStack

F32 = mybir.dt.float32
BF16 = mybir.dt.bfloat16
AF = mybir.ActivationFunctionType
ALU = mybir.AluOpType

D = 1024
IN_DIM = 7184
FFN = 2816
NCORES = 8
SEG = 2048
HIST = 6144
HALO = 128
ROWS = HIST + HALO + SEG
EPS = 1e-6
EPOCH = 30000

O_A, O_BG, O_Q, O_K, O_V, O_R, O_GL, O_GA, O_GB = 0, 1024, 2048, 2560, 3072, 4096, 5120, 5136, 6160


class Buf:
    __slots__ = ("name", "lastw", "readers", "sem", "dcount")

    def __init__(self, name):
        self.name = name
        self.lastw = None
        self.readers = {}
        self.sem = None
        self.dcount = 0


class Sched:
    ENGS = ("sync", "scalar", "vector", "gpsimd", "tensor")

    def __init__(self, nc, stack):
        self.nc = nc
        self.stack = stack
        self.ops = {e: [] for e in self.ENGS}
        self.count = {e: 0 for e in self.ENGS}
        self.sems = {e: [] for e in self.ENGS}
        self.waited = {e: {} for e in self.ENGS}
        self.same_engine_sync = {"scalar": True, "vector": True, "gpsimd": True, "tensor": False, "sync": False}

    def new_sem(self, name):
        return self.stack.enter_context(self.nc.semaphore(name))

    def eng_sem(self, e, epoch):
        while len(self.sems[e]) <= epoch:
            self.sems[e].append(self.new_sem(f"p_{e}_{len(self.sems[e])}"))
        return self.sems[e][epoch]

    def _wait(self, eng, tok):
        if tok[0] == "e":
            _, src, idx = tok
            if src == eng and not self.same_engine_sync[eng]:
                return
            epoch, val = divmod(idx - 1, EPOCH)
            val += 1
            key = ("e", src, epoch)
            sem = self.eng_sem(src, epoch)
        else:
            _, buf, val = tok
            key = ("d", id(buf))
            sem = buf.sem
        if self.waited[eng].get(key, 0) >= val:
            return
        self.waited[eng][key] = val
        self.ops[eng].append(lambda e, sem=sem, val=val: e.wait_ge(sem, val))

    def _deps(self, eng, reads, writes):
        for b in reads:
            if b.lastw is not None:
                self._wait(eng, b.lastw)
        for b in writes:
            if b.lastw is not None and not (b.lastw[0] == "e" and b.lastw[1] == eng):
                self._wait(eng, b.lastw)
            for t in b.readers.values():
                if not (t[0] == "e" and t[1] == eng):
                    self._wait(eng, t)

    def emit(self, eng, fn, reads=(), writes=()):
        self._deps(eng, reads, writes)
        self.count[eng] += 1
        idx = self.count[eng]
        sem = self.eng_sem(eng, (idx - 1) // EPOCH)
        self.ops[eng].append(lambda e, fn=fn, sem=sem: fn(e).then_inc(sem, 1))
        tok = ("e", eng, idx)
        for b in writes:
            b.lastw = tok
            b.readers = {}
        for b in reads:
            b.readers[eng] = tok
        return tok

    def dma(self, eng, out, in_, reads=(), writes=(), track=None, **kw):
        self._deps(eng, reads, writes)
        if track.sem is None:
            track.sem = self.new_sem("d_" + track.name)
        track.dcount += 1
        val = track.dcount * 16
        sem = track.sem
        self.ops[eng].append(
            lambda e, sem=sem, out=out, in_=in_, kw=kw: e.dma_start(out=out, in_=in_, **kw).then_inc(sem, 16))
        tok = ("d", track, val)
        for b in writes:
            b.lastw = tok
            b.readers = {}
        for b in reads:
            b.readers["dma_" + track.name] = tok
        return tok

    def replay(self, block):
        for e in self.ENGS:
            ops = self.ops[e]

            def body(engine, ops=ops):
                for f in ops:
                    f(engine)
            getattr(block, e)(body)


def build_program(n_hist_tiles=HIST // 512, n_main_tiles=SEG // 512, dbg=None):
    nc = bass.Bass("TRN2", target_bir_lowering=False)
    dt_in = lambda name, shape: nc.dram_tensor(name, shape, F32, kind="ExternalInput").ap()
    x = dt_in("x", [ROWS, D])
    hasprev = dt_in("hasprev", [128, 1])
    norm1_g = dt_in("norm1_g", [D]); w_in = dt_in("w_in", [D, IN_DIM])
    conv_dw_w = dt_in("conv_dw_w", [31, D]); conv_dw_b = dt_in("conv_dw_b", [D])
    conv_ln_g = dt_in("conv_ln_g", [D]); conv_ln_b = dt_in("conv_ln_b", [D])
    w_conv_out = dt_in("w_conv_out", [D, D]); w_gate_up = dt_in("w_gate_up", [16, 512])
    b_gate = dt_in("b_gate", [512]); gla_norm_g = dt_in("gla_norm_g", [256])
    w_gla_out = dt_in("w_gla_out", [D, D]); w_o = dt_in("w_o", [D, D])
    norm2_g = dt_in("norm2_g", [D]); w_ffn_up = dt_in("w_ffn_up", [D, 2 * FFN])
    ffn_dw_w = dt_in("ffn_dw_w", [3, 2 * FFN]); ffn_dw_b = dt_in("ffn_dw_b", [2 * FFN])
    w_ffn_down = dt_in("w_ffn_down", [FFN, D]); final_g = dt_in("final_g", [D])
    out = nc.dram_tensor("out", [SEG, D], F32, kind="ExternalOutput").ap()
    dgd = nc.dram_tensor("dgd", [24, 128, 11 * 128], BF16).ap()
    wb = {"in": nc.dram_tensor("wb_in", [D, IN_DIM], BF16).ap(), "co": nc.dram_tensor("wb_co", [D, D], BF16).ap(),
          "go": nc.dram_tensor("wb_go", [D, D], BF16).ap(), "wo": nc.dram_tensor("wb_wo", [D, D], BF16).ap(),
          "up": nc.dram_tensor("wb_up", [D, 2 * FFN], BF16).ap(), "dn": nc.dram_tensor("wb_dn", [FFN, D], BF16).ap()}
    dbg_out = {}
    if dbg:
        for name, shape in dbg.items():
            dbg_out[name] = nc.dram_tensor("dbg_" + name, list(shape), F32, kind="ExternalOutput").ap()

    with ExitStack() as stack:
        S = Sched(nc, stack)
        _n = [0]

        def sb(shape, dt, name=None):
            _n[0] += 1
            return stack.enter_context(nc.sbuf_tensor(name or f"t{_n[0]}", list(shape), dt))

        identf = sb([128, 128], F32); identfB = Buf("identf")
        identb = sb([128, 128], BF16); identbB = Buf("identb")
        tri_inc = sb([128, 128], F32); tri_end = sb([128, 128], F32); cmask = sb([128, 128], F32)
        chsel = sb([128, 2], F32); onesf = sb([128, 128], F32); ones_row = sb([1, 128], BF16)
        constB = Buf("consts")
        stage = sb([128, 128], F32); stageB = Buf("stage")
        g1T = sb([128, 8], F32); g2T = sb([128, 8], F32); cbT = sb([128, 8], F32)
        lngT = sb([128, 8], F32); lnbT = sb([128, 8], F32); fbT = sb([128, 44], F32)
        gnT = sb([128, 2], F32); cwT = sb([128, 248], F32); fwT = sb([128, 132], F32)
        fgB_t = sb([128, D], F32)
        bgrow = sb([1, 512], BF16); wgu = sb([16, 512], BF16)
        hp = sb([128, 1], F32)
        vecB = Buf("vecs")
        banks = [stack.enter_context(nc.psum_tensor(f"ps{i}", [128, 512], F32)) for i in range(6)]
        bankB = [Buf(f"ps{i}") for i in range(6)]
        pTs = [stack.enter_context(nc.psum_tensor(f"pT{i}", [128, 1024], BF16)) for i in range(2)]; pTBs = [Buf(f"pT{i}") for i in range(2)]
        _pt = [0]

        def npT():
            i = _pt[0] % 2
            _pt[0] += 1
            return pTs[i], pTBs[i]
        _bk = [0]

        def nb():
            i = _bk[0] % 6
            _bk[0] += 1
            return banks[i], bankB[i]

        NSLOT = 4
        slots = [sb([128, 8, 512], BF16, f"slot{i}") for i in range(NSLOT)]
        slotB = [Buf(f"slot{i}") for i in range(NSLOT)]
        xs = [sb([128, D], F32, f"xs{i}") for i in range(2)]; xsB = [Buf(f"xs{i}") for i in range(2)]
        utms = [sb([128, D], BF16, f"utm{i}") for i in range(2)]; utmBs = [Buf(f"utm{i}") for i in range(2)]
        uT = sb([128, 8, 512], BF16); uTB = Buf("uT")
        glowT = sb([128, 512], BF16); glowTB = Buf("glowT")
        lbuf = sb([128, 4, 512], F32); lB = [Buf(f"l{i}") for i in range(4)]
        NTMP = 6
        tmps = [sb([128, 516], F32, f"tmp{i}") for i in range(NTMP)]; tmpB = [Buf(f"tmp{i}") for i in range(NTMP)]
        _tk = [0]

        def ntmp():
            i = _tk[0] % NTMP
            _tk[0] += 1
            return tmps[i], tmpB[i]
        Epl = sb([128, 4, 512], BF16); Emi = sb([128, 4, 512], BF16); EB = [Buf(f"E{i}") for i in range(4)]
        decs = [sb([128, 4, 8], F32, f"dec{p}") for p in range(2)]; decBs = [[Buf(f"dec{p}_{i}") for i in range(4)] for p in range(2)]
        dec = decs[0]; decB = decBs[0]
        kend = sb([128, 4, 512], BF16); kendB = [Buf(f"kend{i}") for i in range(4)]
        vtm = sb([128, 4, D], BF16); vtmB = [Buf(f"v{i}") for i in range(4)]
        kT = sb([128, 4, 512], BF16); kTB = [Buf(f"kT{i}") for i in range(4)]
        qT = sb([128, 4, 512], BF16); qTB = [Buf(f"qT{i}") for i in range(4)]
        silur = sb([128, 8, 512], BF16); silurB = [Buf(f"sr{i}") for i in range(8)]
        attm = sb([128, 4, 128], BF16); attmB = [Buf(f"att{i}") for i in range(4)]
        Sst = sb([128, D], F32); SstB = Buf("S")
        Sbf = [sb([128, D], BF16, f"Sbf{i}") for i in range(2)]; SbfB = [Buf(f"Sbf{i}") for i in range(2)]
        on = sb([128, D], BF16); onB = Buf("on")
        oss = sb([128, 8], F32); ossB = Buf("oss")
        actT = sb([128, 8, 512], BF16); actTB = [Buf(f"actT{i}") for i in range(8)]
        cin = sb([128, 8, 542], BF16); cinB = [Buf(f"cin{i}") for i in range(8)]
        arena = sb([128, 22 * 256], F32, "arena")
        cacc = arena[:, 0:4096].rearrange("p (k t) -> p k t", k=8)
        gT = arena.bitcast(BF16).rearrange("p (k t) -> p k t", k=22)
        arB = [Buf(f"ar{i}") for i in range(8)]
        m1 = sb([128, 8, 512], BF16); m1B = [Buf(f"m1{i}") for i in range(8)]
        hbuf = sb([128, 4, D], F32); hB = [Buf(f"h{i}") for i in range(4)]
        zhalo = sb([128, 44, 2], F32); zhB = Buf("zhalo")
        ss = sb([128, 8], F32); ssB = Buf("ss")
        dg = [sb([128, 11, 128], BF16, f"dg{i}") for i in range(3)]; dgB = [Buf(f"dg{i}") for i in range(3)]
        dgdB = Buf("dgd")
        PARTS = ((0, 11), (11, 10), (21, 10))
        lnst = [sb([128, 512], F32, f"lnst{i}") for i in range(2)]; lnstB = [Buf(f"lnst{i}") for i in range(2)]

        block = stack.enter_context(nc.Block())
        V, A, P, T_, SY = "vector", "scalar", "gpsimd", "tensor", "sync"

        def iota_sel(t, pattern, cmp, fill, base, cm, src=None):
            S.emit(P, lambda e: e.affine_select(out=t, in_=(src if src is not None else t), pattern=pattern,
                                                compare_op=cmp, fill=fill, base=base, channel_multiplier=cm),
                   reads=[constB], writes=[constB])
        S.emit(P, lambda e: e.memset(identf[:], 0.0), writes=[constB])
        iota_sel(identf[:], [[-1, 128]], ALU.not_equal, 1.0, 0, 1)
        S.emit(V, lambda e: e.tensor_copy(out=identb[:], in_=identf[:]), reads=[constB], writes=[identbB])
        S.emit(P, lambda e: e.memset(onesf[:], 1.0), writes=[constB])
        S.emit(P, lambda e: e.memset(ones_row[:], 1.0), writes=[constB])
        for t, val in ((tri_inc, -1.0 / 16), (tri_end, -1.0 / 16), (cmask, 1.0)):
            S.emit(P, lambda e, t=t, val=val: e.memset(t[:], val), writes=[constB])
        iota_sel(tri_inc[:], [[1, 128]], ALU.is_ge, 0.0, 0, -1)
        iota_sel(cmask[:], [[1, 128]], ALU.is_ge, 0.0, 0, -1)
        iota_sel(tri_end[:], [[-1, 128]], ALU.is_gt, 0.0, 0, 1)
        S.emit(P, lambda e: e.memset(tri_inc[0:64, 64:128], 0.0), writes=[constB])
        S.emit(P, lambda e: e.memset(cmask[0:64, 64:128], 0.0), writes=[constB])
        S.emit(P, lambda e: e.memset(tri_end[64:128, 0:64], 0.0), writes=[constB])
        S.emit(P, lambda e: e.memset(chsel[:], 0.0), writes=[constB])
        S.emit(P, lambda e: e.memset(chsel[0:64, 0:1], -1.0 / 16), writes=[constB])
        S.emit(P, lambda e: e.memset(chsel[64:128, 1:2], -1.0 / 16), writes=[constB])
        S.emit(P, lambda e: e.memset(zhalo[:], 0.0), writes=[zhB])
        S.emit(P, lambda e: e.memset(Sst[:], 0.0), writes=[SstB])
        S.emit(P, lambda e: e.memset(cin[:], 0.0), writes=cinB)

        def load_cols(dst, rows_ap, nrows):
            r0 = 0
            while r0 < nrows:
                n = min(128, nrows - r0)
                S.dma(SY, stage[0:n, :], rows_ap[r0:r0 + n, :], writes=[stageB], track=stageB)
                ps, psB = nb()
                S.emit(T_, lambda e, ps=ps, n=n: e.matmul(ps[:, 0:n], lhsT=stage[0:n, :], rhs=identf[0:n, 0:n], start=True, stop=True),
                       reads=[stageB, constB], writes=[psB])
                S.emit(V, lambda e, ps=ps, n=n, r0=r0: e.tensor_copy(out=dst[:, r0:r0 + n], in_=ps[:, 0:n]), reads=[psB], writes=[vecB])
                r0 += n
        load_cols(g1T, norm1_g.rearrange("(k p) -> k p", p=128), 8)
        load_cols(g2T, norm2_g.rearrange("(k p) -> k p", p=128), 8)
        load_cols(cbT, conv_dw_b.rearrange("(k p) -> k p", p=128), 8)
        load_cols(lngT, conv_ln_g.rearrange("(k p) -> k p", p=128), 8)
        load_cols(lnbT, conv_ln_b.rearrange("(k p) -> k p", p=128), 8)
        load_cols(fbT, ffn_dw_b.rearrange("(k p) -> k p", p=128), 44)
        load_cols(gnT, gla_norm_g.rearrange("(k p) -> k p", p=128), 2)
        load_cols(cwT, conv_dw_w.rearrange("j (k p) -> (j k) p", p=128), 248)
        load_cols(fwT, ffn_dw_w.rearrange("j (k p) -> (j k) p", p=128), 132)
        rowst = xs[0]
        S.dma(SY, rowst[0:1, :], final_g.rearrange("(o n) -> o n", o=1), writes=[xsB[0]], track=xsB[0])
        for hf in range(2):
            ps, psB = nb()
            S.emit(T_, lambda e, ps=ps, hf=hf: e.matmul(ps[:], lhsT=onesf[0:1, :], rhs=rowst[0:1, hf * 512:(hf + 1) * 512], start=True, stop=True),
                   reads=[xsB[0], constB], writes=[psB])
            S.emit(V, lambda e, ps=ps, hf=hf: e.tensor_copy(out=fgB_t[:, hf * 512:(hf + 1) * 512], in_=ps[:]), reads=[psB], writes=[vecB])
        setup_toks = [vecB.lastw, constB.lastw, identbB.lastw]
        setup_toks.append(S.dma(P, bgrow[:], b_gate.rearrange("(o n) -> o n", o=1), writes=[vecB], track=Buf("bgrow")))
        setup_toks.append(S.dma(P, wgu[:], w_gate_up, writes=[vecB], track=Buf("wgu")))
        setup_toks.append(S.dma(SY, hp[:], hasprev, writes=[vecB], track=Buf("hp")))
        for eng in (A, V, T_, P):
            for tk in setup_toks:
                S._wait(eng, tk)
        vecB.lastw = None; vecB.readers = {}
        constB.lastw = None; constB.readers = {}

        tapon = [False]

        def dbg_tap(name, ap, bufs, force=False):
            if dbg and name in dbg and (tapon[0] or force):
                tb = Buf("dbg_" + name)
                S.dma(P, dbg_out[name], ap, reads=bufs, track=tb)
                S._wait(P, ("d", tb, tb.dcount * 16))

        for g in range(24):
            kc, part = divmod(g, 3)
            j0, nj = PARTS[part]
            bi = g % 3
            for jj in range(nj):
                col = (j0 + jj) * 8 + kc
                S.emit(V, lambda e, bi=bi, jj=jj, col=col: e.tensor_scalar(out=dg[bi][:, jj, :], in0=identb[:], scalar1=cwT[:, col:col + 1], scalar2=None, op0=ALU.mult),
                       reads=[identbB], writes=[dgB[bi]])
            tok = S.dma(SY, dgd[g][:, 0:nj * 128], dg[bi][:, 0:nj, :].rearrange("p j i -> p (j i)"), reads=[dgB[bi]], track=dgdB)
            dgdB.lastw = tok

        def load_w(slot_i, src, c0, ncols, r0=0, nk=8):
            S.dma(P, slots[slot_i][:, 0:nk, 0:ncols],
                  src[r0 * 128:(r0 + nk) * 128, c0:c0 + ncols].rearrange("(k p) n -> p k n", p=128),
                  writes=[slotB[slot_i]], track=slotB[slot_i])
        wbB = Buf("wb")

        def load_wb(slot_i, src, c0, ncols, r0=0, nk=8):
            S.dma(P, slots[slot_i][:, 0:nk, 0:ncols],
                  src[r0 * 128:(r0 + nk) * 128, c0:c0 + ncols].rearrange("(k p) n -> p k n", p=128),
                  reads=[wbB], writes=[slotB[slot_i]], track=slotB[slot_i])

        def convert_weights():
            bounce = [(actT, actTB), (silur, silurB), (m1, m1B)]
            cvB = [Buf(f"cv{i}") for i in range(3)]
            blocks = []
            for c0 in range(0, IN_DIM - 16, 512):
                blocks.append(("in", 0, 8, c0, 512))
            blocks.append(("in", 0, 8, IN_DIM - 16, 16))
            for nm in ("co", "go", "wo"):
                for hf in range(2):
                    blocks.append((nm, 0, 8, hf * 512, 512))
            for c0 in range(0, 2 * FFN, 512):
                blocks.append(("up", 0, 8, c0, 512))
            for r0, nk in ((0, 8), (8, 8), (16, 6)):
                for hf in range(2):
                    blocks.append(("dn", r0, nk, hf * 512, 512))
            for bi, (nm, r0, nk, c0, ncols) in enumerate(blocks):
                bt, bB = bounce[bi % 3]
                srcap = WSRC[nm][r0 * 128:(r0 + nk) * 128, c0:c0 + ncols].rearrange("(k p) n -> p k n", p=128)
                dstap = wb[nm][r0 * 128:(r0 + nk) * 128, c0:c0 + ncols].rearrange("(k p) n -> p k n", p=128)
                S.dma(P, bt[:, 0:nk, 0:ncols], srcap, writes=bB, track=cvB[bi % 3])
                tok = S.dma(P, dstap, bt[:, 0:nk, 0:ncols], reads=bB, track=wbB)
                wbB.lastw = tok

        WSRC = {"in": w_in, "co": w_conv_out, "go": w_gla_out, "wo": w_o, "up": w_ffn_up, "dn": w_ffn_down}

        def tile_wseq(is_halo):
            q = [("in", O_GL, 512, 0, 8), ("in", O_K, 512, 0, 8), ("in", O_V, 512, 0, 8), ("in", O_V + 512, 512, 0, 8), ("in", O_Q, 512, 0, 8),
                 ("in", O_R, 512, 0, 8), ("in", O_R + 512, 512, 0, 8)]
            for hf in range(2):
                q += [("in", O_A + hf * 512, 512, 0, 8), ("in", O_BG + hf * 512, 512, 0, 8)]
            for hf in range(2):
                q += [("go", hf * 512, 512, 0, 8), ("in", O_GB + hf * 512, 512, 0, 8)]
            for hf in range(2):
                q += [("in", O_GA + hf * 512, 512, 0, 8)]
            for hf in range(2):
                q += [("co", hf * 512, 512, 0, 8)]
            for hf in range(2):
                q += [("wo", hf * 512, 512, 0, 8)]
            for g in range(6):
                nblk = 4 if g < 5 else 2
                q += [("up", g * 512, nblk * 128, 0, 8), ("up", FFN + g * 512, nblk * 128, 0, 8)]
            if not is_halo:
                for hf in range(2):
                    for g3 in range(3):
                        q += [("dn", hf * 512, 512, g3 * 8, 8 if g3 < 2 else 6)]
            return q
        WSEQ = tile_wseq(True)
        for _ in range(n_main_tiles):
            WSEQ += tile_wseq(False)
        _wi = [0, 0]

        def next_w(*key):
            i = _wi[0]
            assert WSEQ[i] == key, (i, WSEQ[i], key)
            while _wi[1] < len(WSEQ) and _wi[1] <= i + NSLOT - 2:
                k = WSEQ[_wi[1]]
                load_wb(_wi[1] % NSLOT, wb[k[0]], k[1], k[2], r0=k[3], nk=k[4])
                _wi[1] += 1
            _wi[0] += 1
            return i % NSLOT

        def norm_a(srcs, ums):
            n = len(srcs)
            for i, (ap, bf) in enumerate(srcs):
                um, umB = ums[i]
                S.emit(A, lambda e, ap=ap, i=i, um=um: e.activation(out=um, in_=ap, func=AF.Square, accum_out=ss[:, i:i + 1]),
                       reads=[bf], writes=list(umB) + [ssB])
            S.emit(A, lambda e: e.activation(out=ss[:, 4:4 + n], in_=ss[:, 0:n], func=AF.Ln, scale=1.0 / D, bias=EPS), reads=[ssB], writes=[ssB])
            S.emit(A, lambda e: e.activation(out=ss[:, 4:4 + n], in_=ss[:, 4:4 + n], func=AF.Exp, scale=-0.5), reads=[ssB], writes=[ssB])
            for i, (ap, bf) in enumerate(srcs):
                um, umB = ums[i]
                if i % 2 == 0:
                    S.emit(V, lambda e, ap=ap, i=i, um=um: e.tensor_scalar(out=um, in0=ap, scalar1=ss[:, 4 + i:5 + i], scalar2=None, op0=ALU.mult),
                           reads=[bf, ssB], writes=list(umB))
                else:
                    S.emit(A, lambda e, ap=ap, i=i, um=um: e.activation(out=um, in_=ap, func=AF.Identity, scale=ss[:, 4 + i:5 + i]),
                           reads=[bf, ssB], writes=list(umB))

        def norm_b(ums, gT_, s0):
            for i, (um, umB) in enumerate(ums):
                pT, pTB = npT()
                for kc in range(8):
                    S.emit(T_, lambda e, kc=kc, um=um, pT=pT: e.transpose(out=pT[:, kc * 128:(kc + 1) * 128], in_=um[:, kc * 128:(kc + 1) * 128], identity=identb[:]),
                           reads=list(umB) + [identbB], writes=[pTB])
                s = s0 + i
                for kc in range(8):
                    S.emit(V, lambda e, kc=kc, s=s, pT=pT: e.tensor_scalar(out=uT[:, kc, s * 128:(s + 1) * 128], in0=pT[:, kc * 128:(kc + 1) * 128],
                                                                         scalar1=gT_[:, kc:kc + 1], scalar2=None, op0=ALU.mult),
                           reads=[pTB, vecB], writes=[uTB])

        def um_std(i):
            return (utms[i][:], [utmBs[i]])

        def norm_stage(srcs, gT_, s0):
            for c in range(0, len(srcs), 2):
                ums = [um_std(i) for i in range(min(2, len(srcs) - c))]
                norm_a(srcs[c:c + 2], ums)
                norm_b(ums, gT_, s0 + c)

        def load_x(row0, i):
            S.dma(SY, xs[i][:], x[row0:row0 + 128, :], writes=[xsB[i]], track=xsB[i])

        def proj_T(slot_i, s, ncols=512):
            ps, psB = nb()
            for kc in range(8):
                S.emit(T_, lambda e, kc=kc, ps=ps: e.matmul(ps[:, 0:ncols], lhsT=uT[:, kc, s * 128:(s + 1) * 128], rhs=slots[slot_i][:, kc, 0:ncols],
                                                           start=(kc == 0), stop=(kc == 7)),
                       reads=[uTB, slotB[slot_i]], writes=[psB])
            return ps, psB

        def proj_F(w_ap_fn, wB, rhs, rhsB, T, nk=8, M=128):
            ps, psB = nb()
            for kc in range(nk):
                S.emit(T_, lambda e, kc=kc, ps=ps: e.matmul(ps[0:M, 0:T], lhsT=w_ap_fn(kc), rhs=rhs[:, kc, 0:T],
                                                           start=(kc == 0), stop=(kc == nk - 1)),
                       reads=list(wB) + list(rhsB), writes=[psB])
            return ps, psB

        def gate_stage(T, NS, main, gsl, p=0):
            gate_g1(T, NS, gsl)
            gate_g2(NS, main, p)

        def gate_g1(T, NS, gsl):
            ps, psB = proj_F(lambda kc: slots[gsl][:, kc, 0:128], [slotB[gsl]], uT, [uTB], T)
            S.emit(A, lambda e, ps=ps: e.activation(out=glowT[:, 0:T], in_=ps[:, 0:T], func=AF.Copy), reads=[psB], writes=[glowTB])
            dbg_tap("glowT", glowT[0:16, :], [glowTB]); dbg_tap("wgu", wgu[:], []); dbg_tap("bgrow", bgrow[:], [])
            pre = []
            for s in range(NS):
                ps, psB = nb()
                S.emit(T_, lambda e, ps=ps, s=s: e.matmul(ps[:], lhsT=glowT[0:16, s * 128:(s + 1) * 128], rhs=wgu[:], start=True, stop=False),
                       reads=[glowTB, vecB], writes=[psB])
                S.emit(T_, lambda e, ps=ps: e.matmul(ps[:], lhsT=ones_row[:], rhs=bgrow[:], start=False, stop=True),
                       reads=[constB, vecB], writes=[psB])
                pre.append((ps, psB))
            for s in range(NS):
                ps, psB = pre[s]
                tm, tmB = ntmp()
                S.emit(A, lambda e, ps=ps, tm=tm: e.activation(out=tm[:, 0:512], in_=ps[:], func=AF.Exp, scale=-1.0), reads=[psB], writes=[tmB])
                if s == 0:
                    dbg_tap("expn", tm[:, 0:512], [tmB])
                S.emit(A, lambda e, tm=tm, s=s: e.activation(out=lbuf[:, s, :], in_=tm[:, 0:512], func=AF.Ln, bias=1.0), reads=[tmB], writes=[lB[s]])

        def gate_g2(NS, main, p=0):
            dec, decB = decs[p], decBs[p]
            for s in range(NS):
                ps, psB = nb()
                for h in range(4):
                    S.emit(T_, lambda e, ps=ps, s=s, h=h: e.matmul(ps[:, h * 2:h * 2 + 2], lhsT=lbuf[:, s, h * 128:(h + 1) * 128], rhs=chsel[:], start=True, stop=True),
                           reads=[lB[s], constB], writes=[psB])
                S.emit(A, lambda e, ps=ps, s=s: e.activation(out=dec[:, s, :], in_=ps[:, 0:8], func=AF.Exp), reads=[psB], writes=[decB[s]])
                if main:
                    ps, psB = nb()
                    for h in range(4):
                        S.emit(T_, lambda e, ps=ps, s=s, h=h: e.matmul(ps[:, h * 128:(h + 1) * 128], lhsT=lbuf[:, s, h * 128:(h + 1) * 128], rhs=tri_inc[:], start=True, stop=True),
                               reads=[lB[s], constB], writes=[psB])
                    psv = ps[:].rearrange("p (h t) -> p h t", h=4)
                    S.emit(A, lambda e, psv=psv, s=s: e.activation(out=Epl[:, :, s * 128:(s + 1) * 128], in_=psv, func=AF.Exp), reads=[psB], writes=[EB[s]])
                    S.emit(A, lambda e, psv=psv, s=s: e.activation(out=Emi[:, :, s * 128:(s + 1) * 128], in_=psv, func=AF.Exp, scale=-1.0), reads=[psB], writes=[EB[s]])

        def kv_stage(NS, main, kslot_fn, vslot_fns, after_k=None):
            kslot = kslot_fn()
            for s in range(NS):
                ps, psB = nb()
                S.emit(T_, lambda e, ps=ps, s=s: e.matmul(ps[:], lhsT=tri_end[:], rhs=lbuf[:, s, :], start=True, stop=True),
                       reads=[lB[s], constB], writes=[psB])
                tm, tmB = ntmp()
                S.emit(A, lambda e, ps=ps, tm=tm: e.activation(out=tm[:, 0:512], in_=ps[:], func=AF.Exp), reads=[psB], writes=[tmB])
                ps, psB = proj_T(kslot, s)
                S.emit(V, lambda e, ps=ps, tm=tm, s=s: e.tensor_tensor(out=kend[:, s, :], in0=ps[:], in1=tm[:, 0:512], op=ALU.mult),
                       reads=[psB, tmB], writes=[kendB[s]])
            if after_k is not None:
                after_k(kslot)
            for hf in range(2):
                vs = vslot_fns[hf]()
                for s in range(NS):
                    ps, psB = proj_T(vs, s)
                    S.emit(A, lambda e, ps=ps, s=s, hf=hf: e.activation(out=vtm[:, s, hf * 512:(hf + 1) * 512], in_=ps[:], func=AF.Copy),
                           reads=[psB], writes=[vtmB[s]])

        def state_chunk(s, c, want_bf=None, p=0):
            dec, decB = decs[p], decBs[p]
            pk = [nb(), nb()]
            for h in range(4):
                ps, psB = pk[h // 2]
                S.emit(T_, lambda e, ps=ps, h=h: e.matmul(ps[:, (h % 2) * 256:(h % 2) * 256 + 256], lhsT=kend[c * 64:(c + 1) * 64, s, h * 128:(h + 1) * 128],
                                                         rhs=vtm[c * 64:(c + 1) * 64, s, h * 256:(h + 1) * 256], start=True, stop=True),
                       reads=[kendB[s], vtmB[s]], writes=[psB])
            for h in range(4):
                ps, psB = pk[h // 2]
                S.emit(V, lambda e, ps=ps, h=h: e.scalar_tensor_tensor(out=Sst[:, h * 256:(h + 1) * 256], in0=Sst[:, h * 256:(h + 1) * 256],
                                                                      scalar=dec[:, s, h * 2 + c:h * 2 + c + 1], in1=ps[:, (h % 2) * 256:(h % 2) * 256 + 256],
                                                                      op0=ALU.mult, op1=ALU.add),
                       reads=[SstB, decB[s], psB], writes=[SstB])
            if want_bf is not None:
                S.emit(A, lambda e: e.activation(out=Sbf[want_bf][:], in_=Sst[:], func=AF.Copy), reads=[SstB], writes=[SbfB[want_bf]])

        hist_slots = None
        if n_hist_tiles > 0 or True:
            load_w(0, w_in, O_K, 512); load_w(1, w_in, O_V, 512); load_w(2, w_in, O_V + 512, 512); load_w(3, w_in, O_GL, 512)
            convert_weights()
        hums = [um_std(0), um_std(1), (on[:], [onB]), (Sbf[1][:], [SbfB[1]])]

        def hist_na(t):
            for pr in range(2):
                for i in range(2):
                    load_x(t * 512 + (pr * 2 + i) * 128, i)
                norm_a([(xs[i][:], xsB[i]) for i in range(2)], hums[pr * 2:pr * 2 + 2])

        def hist_state(t):
            for s4 in range(4):
                state_chunk(s4, 0, p=t % 2)
                state_chunk(s4, 1, p=t % 2)

        if n_hist_tiles > 0:
            hist_na(0)
            norm_b(hums, g1T, 0)
            gate_stage(512, 4, False, 3, p=0)
        for t in range(n_hist_tiles):
            if t + 1 < n_hist_tiles:
                hist_na(t + 1)
            kv_stage(4, False, lambda: 0, (lambda: 1, lambda: 2))
            if t + 1 < n_hist_tiles:
                norm_b(hums, g1T, 0)
                gate_g1(512, 4, 3)
                hist_state(t)
                gate_g2(4, False, (t + 1) % 2)
            else:
                hist_state(t)
        S.emit(A, lambda e: e.activation(out=Sbf[0][:], in_=Sst[:], func=AF.Copy), reads=[SstB], writes=[SbfB[0]])
        sbf_cur = [0]

        dbg_tap("S_hist", Sst[:], [SstB], force=True)

        def main_tile(row0, T, is_halo, out_row0, prefetched=False, next_row0=None):
            NS = T // 128
            pf_ums = [um_std(0), um_std(1),
                      (actT[:, 0:2, :].rearrange("p k t -> p (k t)"), actTB[0:2]), (actT[:, 2:4, :].rearrange("p k t -> p (k t)"), actTB[2:4])]
            for pr in range((NS + 1) // 2):
                nn = min(2, NS - pr * 2)
                if prefetched:
                    ums = pf_ums[pr * 2:pr * 2 + nn]
                else:
                    ums = [um_std(i) for i in range(nn)]
                    for i in range(nn):
                        load_x(row0 + (pr * 2 + i) * 128, i)
                    norm_a([(xs[i][:], xsB[i]) for i in range(nn)], ums)
                norm_b(ums, g1T, pr * 2)

            def prefetch_next():
                if next_row0 is not None:
                    for pr in range(2):
                        for i in range(2):
                            load_x(next_row0 + (pr * 2 + i) * 128, i)
                        norm_a([(xs[i][:], xsB[i]) for i in range(2)], pf_ums[pr * 2:pr * 2 + 2])
            dbg_tap("uT", uT[:], [uTB])
            gate_stage(T, NS, True, next_w("in", O_GL, 512, 0, 8))
            def k_feature_major(ks):
                for h in range(4):
                    ps, psB = proj_F(lambda kc, h=h: slots[ks][:, kc, h * 128:(h + 1) * 128], [slotB[ks]], uT, [uTB], T)
                    S.emit(V, lambda e, ps=ps, h=h: e.tensor_tensor(out=kT[:, h, 0:T], in0=ps[:, 0:T], in1=Emi[:, h, 0:T], op=ALU.mult),
                           reads=[psB] + EB[0:NS], writes=[kTB[h]])
            kv_stage(NS, True, lambda: next_w("in", O_K, 512, 0, 8),
                     (lambda: next_w("in", O_V, 512, 0, 8), lambda: next_w("in", O_V + 512, 512, 0, 8)), after_k=k_feature_major)
            qs = next_w("in", O_Q, 512, 0, 8)
            for h in range(4):
                ps, psB = proj_F(lambda kc, h=h: slots[qs][:, kc, h * 128:(h + 1) * 128], [slotB[qs]], uT, [uTB], T)
                S.emit(V, lambda e, ps=ps, h=h: e.scalar_tensor_tensor(out=qT[:, h, 0:T], in0=ps[:, 0:T], scalar=128.0 ** -0.5, in1=Epl[:, h, 0:T],
                                                                      op0=ALU.mult, op1=ALU.mult),
                       reads=[psB] + EB[0:NS], writes=[qTB[h]])
            for hf in range(2):
                rs = next_w("in", O_R + hf * 512, 512, 0, 8)
                for j in range(4):
                    kc = hf * 4 + j
                    ps, psB = proj_F(lambda kk, j=j, rs=rs: slots[rs][:, kk, j * 128:(j + 1) * 128], [slotB[rs]], uT, [uTB], T)
                    S.emit(A, lambda e, ps=ps, kc=kc: e.activation(out=silur[:, kc, 0:T], in_=ps[:, 0:T], func=AF.Silu), reads=[psB], writes=[silurB[kc]])
            dbg_tap("l", lbuf[:], lB); dbg_tap("kT", kT[:], kTB); dbg_tap("qT", qT[:], qTB); dbg_tap("vtm", vtm[:], vtmB)
            dbg_tap("kend", kend[:], kendB); dbg_tap("silur", silur[:], silurB); dbg_tap("dec", decs[0][:], decBs[0])
            for hf in range(2):
                sa = next_w("in", O_A + hf * 512, 512, 0, 8)
                sg = next_w("in", O_BG + hf * 512, 512, 0, 8)
                for j in range(4):
                    kc = hf * 4 + j
                    psa, psaB = proj_F(lambda kk, j=j, sa=sa: slots[sa][:, kk, j * 128:(j + 1) * 128], [slotB[sa]], uT, [uTB], T)
                    psg, psgB = proj_F(lambda kk, j=j, sg=sg: slots[sg][:, kk, j * 128:(j + 1) * 128], [slotB[sg]], uT, [uTB], T)
                    tm, tmB = ntmp()
                    S.emit(A, lambda e, psg=psg, tm=tm: e.activation(out=tm[:, 0:T], in_=psg[:, 0:T], func=AF.Sigmoid), reads=[psgB], writes=[tmB])
                    S.emit(V, lambda e, psa=psa, tm=tm, kc=kc: e.tensor_tensor(out=cin[:, kc, 30:30 + T], in0=psa[:, 0:T], in1=tm[:, 0:T], op=ALU.mult),
                           reads=[psaB, tmB], writes=[cinB[kc]])
            dgtrk = [Buf(f"dgl{i}") for i in range(3)] if not hasattr(main_tile, "_dgtrk") else main_tile._dgtrk
            main_tile._dgtrk = dgtrk

            def load_part(g):
                if g >= 24:
                    return
                j0, nj = PARTS[g % 3]
                bi = g % 3
                S.dma(SY, dg[bi][:, 0:nj, :], dgd[g][:, 0:nj * 128].rearrange("p (j i) -> p j i", i=128), reads=[dgdB], writes=[dgB[bi]], track=dgtrk[bi])

            def conv_block(kc):
                psc, pscB = nb()
                for part in range(3):
                    g = kc * 3 + part
                    j0, nj = PARTS[part]
                    bi = g % 3
                    load_part(g + 2)
                    for jj in range(nj):
                        j = j0 + jj
                        S.emit(T_, lambda e, bi=bi, jj=jj, j=j: e.matmul(psc[:, 0:T], lhsT=dg[bi][:, jj, :], rhs=cin[:, kc, j:j + T], start=(j == 0), stop=(j == 30)),
                               reads=[dgB[bi], cinB[kc]], writes=[pscB])
                S.emit(A, lambda e: e.activation(out=cacc[:, kc, 0:T], in_=psc[:, 0:T], func=AF.Identity, bias=cbT[:, kc:kc + 1]),
                       reads=[pscB], writes=[arB[kc]])
                S.emit(A, lambda e: e.activation(out=cin[:, kc, 0:30], in_=cin[:, kc, T:T + 30], func=AF.Copy), reads=[cinB[kc]], writes=[cinB[kc]])

            gla_po = {}

            def gla_a(s):
                tok = slice(s * 128, (s + 1) * 128)
                for h in range(4):
                    ps, psB = nb()
                    S.emit(T_, lambda e, ps=ps, h=h: e.matmul(ps[:, 0:128], lhsT=kT[:, h, tok], rhs=qT[:, h, tok], start=True, stop=True),
                           reads=[kTB[h], qTB[h]], writes=[psB])
                    S.emit(V, lambda e, ps=ps, h=h: e.tensor_tensor(out=attm[:, h, :], in0=ps[:, 0:128], in1=cmask[:], op=ALU.mult),
                           reads=[psB, constB], writes=[attmB[h]])
                c0 = sbf_cur[0]; c1 = 1 - c0
                state_chunk(s, 0, want_bf=c1)

            def gla_b(s):
                c0 = sbf_cur[0]; c1 = 1 - c0
                po = [nb(), nb()]
                gla_po[s] = po
                for h in range(4):
                    ps, psB = po[h // 2]
                    cols = slice((h % 2) * 256, (h % 2) * 256 + 256)
                    hc = slice(h * 256, (h + 1) * 256)
                    S.emit(T_, lambda e, ps=ps, h=h, cols=cols, hc=hc: e.matmul(ps[:, cols], lhsT=attm[:, h, :], rhs=vtm[:, s, hc], start=True, stop=False),
                           reads=[attmB[h], vtmB[s]], writes=[psB])
                    S.emit(T_, lambda e, ps=ps, h=h, cols=cols, hc=hc: e.matmul(ps[0:64, cols], lhsT=qT[:, h, s * 128:s * 128 + 64], rhs=Sbf[c0][:, hc], start=False, stop=False),
                           reads=[qTB[h], SbfB[c0]], writes=[psB])
                    S.emit(T_, lambda e, ps=ps, h=h, cols=cols, hc=hc: e.matmul(ps[64:128, cols], lhsT=qT[:, h, s * 128 + 64:s * 128 + 128], rhs=Sbf[c1][:, hc], start=False, stop=True,
                                                                             tile_position=(0, 64)),
                           reads=[qTB[h], SbfB[c1]], writes=[psB])
                state_chunk(s, 1, want_bf=c0)

            def gla_c(s):
                tok = slice(s * 128, (s + 1) * 128)
                po = gla_po[s]
                for h in range(4):
                    ps, psB = po[h // 2]
                    cols = slice((h % 2) * 256, (h % 2) * 256 + 256)
                    S.emit(A, lambda e, ps=ps, h=h, cols=cols: e.activation(out=on[:, h * 256:(h + 1) * 256], in_=ps[:, cols], func=AF.Square, accum_out=oss[:, h:h + 1]),
                           reads=[psB], writes=[onB, ossB])
                S.emit(A, lambda e: e.activation(out=oss[:, 4:8], in_=oss[:, 0:4], func=AF.Ln, scale=1.0 / 256, bias=EPS), reads=[ossB], writes=[ossB])
                S.emit(A, lambda e: e.activation(out=oss[:, 4:8], in_=oss[:, 4:8], func=AF.Exp, scale=-0.5), reads=[ossB], writes=[ossB])
                for h in range(4):
                    ps, psB = po[h // 2]
                    cols = slice((h % 2) * 256, (h % 2) * 256 + 256)
                    S.emit(A, lambda e, ps=ps, h=h, cols=cols: e.activation(out=on[:, h * 256:(h + 1) * 256], in_=ps[:, cols], func=AF.Identity, scale=oss[:, 4 + h:5 + h]),
                           reads=[psB, ossB], writes=[onB])
                pT, pTB = npT()
                for kc in range(8):
                    S.emit(T_, lambda e, kc=kc, pT=pT: e.transpose(out=pT[:, kc * 128:(kc + 1) * 128], in_=on[:, kc * 128:(kc + 1) * 128], identity=identb[:]),
                           reads=[onB, identbB], writes=[pTB])
                for kc in range(8):
                    S.emit(V, lambda e, kc=kc, pT=pT: e.scalar_tensor_tensor(out=actT[:, kc, tok], in0=pT[:, kc * 128:(kc + 1) * 128], scalar=gnT[:, kc % 2:kc % 2 + 1],
                                                                      in1=silur[:, kc, tok], op0=ALU.mult, op1=ALU.mult),
                           reads=[pTB, vecB, silurB[kc]], writes=[actTB[kc]])
            load_part(0)
            load_part(1)
            kc_next = 0
            for s in range(NS):
                gla_a(s)
                conv_block(kc_next); kc_next += 1
                gla_b(s)
                conv_block(kc_next); kc_next += 1
                gla_c(s)
            while kc_next < 8:
                conv_block(kc_next); kc_next += 1
            dbg_tap("cin", cin[:, :, 30:542], cinB); dbg_tap("cacc", cacc, arB); dbg_tap("oT", actT[:], actTB)
            psm = pTs[0].bitcast(F32); psmB = pTBs[0]
            psq = pTs[1].bitcast(F32); psqB = pTBs[1]
            for hf in range(2):
                sw = next_w("go", hf * 512, 512, 0, 8)
                sg = next_w("in", O_GB + hf * 512, 512, 0, 8)
                for j in range(4):
                    ob = hf * 4 + j
                    psy, psyB = proj_F(lambda kk, j=j, sw=sw: slots[sw][:, kk, j * 128:(j + 1) * 128], [slotB[sw]], actT, actTB, T)
                    psg, psgB = proj_F(lambda kk, j=j, sg=sg: slots[sg][:, kk, j * 128:(j + 1) * 128], [slotB[sg]], uT, [uTB], T)
                    tq, tqB = ntmp()
                    S.emit(A, lambda e, ob=ob, tq=tq: e.activation(out=tq[:, 0:T], in_=cacc[:, ob, 0:T], func=AF.Square), reads=[arB[ob]], writes=[tqB])
                    tm, tmB = ntmp()
                    S.emit(A, lambda e, psg=psg, tm=tm: e.activation(out=tm[:, 0:T], in_=psg[:, 0:T], func=AF.Sigmoid), reads=[psgB], writes=[tmB])
                    S.emit(V, lambda e, psy=psy, tm=tm, ob=ob: e.tensor_tensor(out=m1[:, ob, 0:T], in0=psy[:, 0:T], in1=tm[:, 0:T], op=ALU.mult),
                           reads=[psyB, tmB], writes=[m1B[ob]])
                    S.emit(T_, lambda e, ob=ob: e.matmul(psm[:, 0:T], lhsT=onesf[:], rhs=cacc[:, ob, 0:T], start=(ob == 0), stop=(ob == 7)),
                           reads=[constB, arB[ob]], writes=[psmB])
                    S.emit(T_, lambda e, ob=ob, tq=tq: e.matmul(psq[:, 0:T], lhsT=onesf[:], rhs=tq[:, 0:T], start=(ob == 0), stop=(ob == 7)),
                           reads=[constB, tqB], writes=[psqB])
            mean, rstd = lnst[0], lnst[1]
            meanB, rstdB = lnstB
            S.emit(A, lambda e: e.activation(out=mean[:, 0:T], in_=psm[:, 0:T], func=AF.Identity, scale=1.0 / D), reads=[psmB], writes=[meanB])
            S.emit(V, lambda e: e.tensor_tensor(out=rstd[:, 0:T], in0=mean[:, 0:T], in1=mean[:, 0:T], op=ALU.mult), reads=[meanB], writes=[rstdB])
            S.emit(V, lambda e: e.scalar_tensor_tensor(out=rstd[:, 0:T], in0=psq[:, 0:T], scalar=1.0 / D, in1=rstd[:, 0:T], op0=ALU.mult, op1=ALU.subtract),
                   reads=[psqB, rstdB], writes=[rstdB])
            S.emit(A, lambda e: e.activation(out=rstd[:, 0:T], in_=rstd[:, 0:T], func=AF.Ln, bias=EPS), reads=[rstdB], writes=[rstdB])
            S.emit(A, lambda e: e.activation(out=rstd[:, 0:T], in_=rstd[:, 0:T], func=AF.Exp, scale=-0.5), reads=[rstdB], writes=[rstdB])
            nmr, nmrB = mean, meanB
            S.emit(V, lambda e: e.scalar_tensor_tensor(out=mean[:, 0:T], in0=mean[:, 0:T], scalar=-1.0, in1=rstd[:, 0:T], op0=ALU.mult, op1=ALU.mult),
                   reads=[meanB, rstdB], writes=[meanB])
            for hf in range(2):
                sg = next_w("in", O_GA + hf * 512, 512, 0, 8)
                for j in range(4):
                    kc = hf * 4 + j
                    psg, psgB = proj_F(lambda kk, j=j, sg=sg: slots[sg][:, kk, j * 128:(j + 1) * 128], [slotB[sg]], uT, [uTB], T)
                    S.emit(A, lambda e, psg=psg, kc=kc: e.activation(out=cin[:, kc, 30:30 + T], in_=psg[:, 0:T], func=AF.Sigmoid), reads=[psgB], writes=[cinB[kc]])
            for kc in range(8):
                tm, tmB = ntmp()
                S.emit(V, lambda e, kc=kc, tm=tm: e.tensor_tensor(out=tm[:, 0:T], in0=cacc[:, kc, 0:T], in1=rstd[:, 0:T], op=ALU.mult),
                       reads=[arB[kc], rstdB], writes=[tmB])
                S.emit(V, lambda e, tm=tm: e.tensor_tensor(out=tm[:, 0:T], in0=tm[:, 0:T], in1=nmr[:, 0:T], op=ALU.add), reads=[tmB, nmrB], writes=[tmB])
                S.emit(A, lambda e, kc=kc, tm=tm: e.activation(out=actT[:, kc, 0:T], in_=tm[:, 0:T], func=AF.Silu, scale=lngT[:, kc:kc + 1], bias=lnbT[:, kc:kc + 1]),
                       reads=[tmB, vecB], writes=[actTB[kc]])
            dbg_tap("cact", actT[:], actTB)
            for hf in range(2):
                sw = next_w("co", hf * 512, 512, 0, 8)
                for j in range(4):
                    ob = hf * 4 + j
                    psy, psyB = proj_F(lambda kk, j=j, sw=sw: slots[sw][:, kk, j * 128:(j + 1) * 128], [slotB[sw]], actT, actTB, T)
                    tm, tmB = ntmp()
                    S.emit(V, lambda e, psy=psy, tm=tm, ob=ob: e.tensor_tensor(out=tm[:, 0:T], in0=psy[:, 0:T], in1=cin[:, ob, 30:30 + T], op=ALU.mult),
                           reads=[psyB, cinB[ob]], writes=[tmB])
                    S.emit(V, lambda e, tm=tm, ob=ob: e.tensor_tensor(out=silur[:, ob, 0:T], in0=tm[:, 0:T], in1=m1[:, ob, 0:T], op=ALU.add),
                           reads=[tmB, m1B[ob]], writes=[silurB[ob]])
            dbg_tap("merged", silur[:], silurB)
            def wo_half(hf):
                sw = next_w("wo", hf * 512, 512, 0, 8)
                for s in range(NS):
                    ps, psB = nb()
                    for kc in range(8):
                        S.emit(T_, lambda e, kc=kc, ps=ps, s=s: e.matmul(ps[:], lhsT=silur[:, kc, s * 128:(s + 1) * 128], rhs=slots[sw][:, kc, :], start=(kc == 0), stop=(kc == 7)),
                               reads=[silurB[kc], slotB[sw]], writes=[psB])
                    xi = (hf * NS + s) % 2
                    S.dma(SY, xs[xi][:, 0:512], x[row0 + s * 128:row0 + (s + 1) * 128, hf * 512:(hf + 1) * 512], writes=[xsB[xi]], track=xsB[xi])
                    S.emit(V, lambda e, ps=ps, s=s, hf=hf, xi=xi: e.tensor_tensor(out=hbuf[:, s, hf * 512:(hf + 1) * 512], in0=ps[:], in1=xs[xi][:, 0:512], op=ALU.add),
                           reads=[psB, xsB[xi]], writes=[hB[s]])
            for hf in range(2):
                wo_half(hf)
            dbg_tap("h1", hbuf[:], hB)
            norm_stage([(hbuf[:, s, :], hB[s]) for s in range(NS)], g2T, 0)
            ngrp = 6
            for g in range(ngrp):
                nblk = 4 if g < 5 else 2
                sa = next_w("up", g * 512, nblk * 128, 0, 8)
                sbb = next_w("up", FFN + g * 512, nblk * 128, 0, 8)
                for j in range(nblk):
                    i = g * 4 + j
                    accs = []
                    for which, sl in ((0, sa), (1, sbb)):
                        blk = which * 22 + i
                        ps, psB = proj_F(lambda kk, j=j, sl=sl: slots[sl][:, kk, j * 128:(j + 1) * 128], [slotB[sl]], uT, [uTB], T)
                        zb, zbB = ntmp()
                        S.emit(A, lambda e, zb=zb, blk=blk: e.activation(out=zb[:, 0:2], in_=zhalo[:, blk, :], func=AF.Copy), reads=[zhB], writes=[zbB])
                        S.emit(A, lambda e, zb=zb, ps=ps: e.activation(out=zb[:, 2:2 + T], in_=ps[:, 0:T], func=AF.Copy), reads=[psB], writes=[zbB])
                        S.emit(A, lambda e, zb=zb, blk=blk: e.activation(out=zhalo[:, blk, :], in_=zb[:, T:T + 2], func=AF.Copy), reads=[zbB], writes=[zhB])
                        ac, acB = ntmp()
                        S.emit(A, lambda e, zb=zb, ac=ac, blk=blk: e.activation(out=ac[:, 0:T], in_=zb[:, 0:T], func=AF.Identity,
                                                                               scale=fwT[:, blk:blk + 1], bias=fbT[:, blk:blk + 1]),
                               reads=[zbB, vecB], writes=[acB])
                        for jj in (1, 2):
                            S.emit(V, lambda e, zb=zb, ac=ac, blk=blk, jj=jj: e.scalar_tensor_tensor(out=ac[:, 0:T], in0=zb[:, jj:jj + T], scalar=fwT[:, jj * 44 + blk:jj * 44 + blk + 1],
                                                                                                in1=ac[:, 0:T], op0=ALU.mult, op1=ALU.add),
                                   reads=[zbB, vecB, acB], writes=[acB])
                        accs.append((ac, acB))
                    if not is_halo:
                        (aa, aaB), (ab, abB) = accs
                        S.emit(A, lambda e, aa=aa: e.activation(out=aa[:, 0:T], in_=aa[:, 0:T], func=AF.Silu), reads=[aaB], writes=[aaB])
                        S.emit(V, lambda e, aa=aa, ab=ab, i=i: e.tensor_tensor(out=gT[:, i, 0:T], in0=aa[:, 0:T], in1=ab[:, 0:T], op=ALU.mult),
                               reads=[aaB, abB], writes=[arB[i % 8]])
            prefetch_next()
            if is_halo:
                S.emit(V, lambda e: e.tensor_scalar(out=zhalo[:], in0=zhalo[:], scalar1=hp[:, 0:1], scalar2=None, op0=ALU.mult), reads=[zhB, vecB], writes=[zhB])
                return
            dbg_tap("gT", gT, arB)
            for hf in range(2):
                pss = [nb() for _ in range(NS)]
                for g3 in range(3):
                    nk = 8 if g3 < 2 else 6
                    sw = next_w("dn", hf * 512, 512, g3 * 8, nk)
                    for s in range(NS):
                        ps, psB = pss[s]
                        for kk in range(nk):
                            i = g3 * 8 + kk
                            S.emit(T_, lambda e, ps=ps, s=s, kk=kk, i=i, sw=sw: e.matmul(ps[:], lhsT=gT[:, i, s * 128:(s + 1) * 128], rhs=slots[sw][:, kk, :],
                                                                                   start=(i == 0), stop=(i == 21)),
                                   reads=[arB[i % 8], slotB[sw]], writes=[psB])
                for s in range(NS):
                    ps, psB = pss[s]
                    S.emit(V, lambda e, ps=ps, s=s, hf=hf: e.tensor_tensor(out=hbuf[:, s, hf * 512:(hf + 1) * 512], in0=ps[:], in1=hbuf[:, s, hf * 512:(hf + 1) * 512], op=ALU.add),
                           reads=[psB, hB[s]], writes=[hB[s]])
            for s in range(NS):
                S.emit(A, lambda e, s=s: e.activation(out=on[:], in_=hbuf[:, s, :], func=AF.Square, accum_out=ss[:, s:s + 1]), reads=[hB[s]], writes=[onB, ssB])
            S.emit(A, lambda e: e.activation(out=ss[:, 4:8], in_=ss[:, 0:4], func=AF.Ln, scale=1.0 / D, bias=EPS), reads=[ssB], writes=[ssB])
            S.emit(A, lambda e: e.activation(out=ss[:, 4:8], in_=ss[:, 4:8], func=AF.Exp, scale=-0.5), reads=[ssB], writes=[ssB])
            for s in range(NS):
                xi = s % 2
                S.emit(V, lambda e, s=s, xi=xi: e.scalar_tensor_tensor(out=xs[xi][:], in0=hbuf[:, s, :], scalar=ss[:, 4 + s:5 + s], in1=fgB_t[:], op0=ALU.mult, op1=ALU.mult),
                       reads=[hB[s], ssB, vecB], writes=[xsB[xi]])
                S.dma(SY, out[out_row0 + s * 128:out_row0 + (s + 1) * 128, :], xs[xi][:], reads=[xsB[xi]], track=xsB[xi])

        main_tile(HIST, HALO, True, None, prefetched=False, next_row0=(HIST + HALO if n_main_tiles > 0 else None))
        for t in range(n_main_tiles):
            tapon[0] = (t == 0)
            main_tile(HIST + HALO + t * 512, 512, False, t * 512, prefetched=True,
                      next_row0=(HIST + HALO + (t + 1) * 512 if t + 1 < n_main_tiles else None))
        for i in range(2):
            S._wait(SY, ("d", xsB[i], xsB[i].dcount * 16))
        S.replay(block)
    return nc


def make_in_maps(inputs):
    x = np.asarray(inputs["x"], dtype=np.float32)
    sq = lambda k: np.ascontiguousarray(np.asarray(inputs[k], dtype=np.float32)[0])
    shared = {k: sq(k) for k in ("norm1_g", "w_in", "conv_dw_w", "conv_dw_b", "conv_ln_g", "conv_ln_b", "w_conv_out",
                                 "w_gate_up", "b_gate", "gla_norm_g", "w_gla_out", "w_o", "norm2_g", "w_ffn_up",
                                 "ffn_dw_w", "ffn_dw_b", "w_ffn_down")}
    shared["final_g"] = np.ascontiguousarray(np.asarray(inputs["final_g"], dtype=np.float32))
    in_maps = []
    for c in range(NCORES):
        b, r = divmod(c, 4)
        start = r * SEG
        xc = np.zeros((ROWS, D), np.float32)
        lo = start - (HIST + HALO)
        src_lo = max(lo, 0)
        xc[src_lo - lo:] = x[b, src_lo:start + SEG]
        m = dict(shared)
        m["x"] = xc
        m["hasprev"] = np.full((128, 1), 1.0 if r > 0 else 0.0, np.float32)
        in_maps.append(m)
    return in_maps


_NC_CACHE = {}


def kernel(**inputs):
    if "nc" not in _NC_CACHE:
        _NC_CACHE["nc"] = build_program()
    nc = _NC_CACHE["nc"]
    in_maps = make_in_maps(inputs)
    res = run_bass_kernel_spmd(nc, in_maps, core_ids=list(range(NCORES)))
    outp = np.zeros((2, 4 * SEG, D), np.float32)
    for c in range(NCORES):
        b, r = divmod(c, 4)
        outp[b, r * SEG:(r + 1) * SEG] = res.results[c]["out"]
    return outp
```

```python
import sys
import numpy as np
import concourse.bass as bass
import concourse.mybir as mybir
from concourse.bass_utils import run_bass_kernel_spmd
from contextlib import ExitStack

F32 = mybir.dt.float32
BF16 = mybir.dt.bfloat16
AF = mybir.ActivationFunctionType
ALU = mybir.AluOpType

D = 1024
IN_DIM = 7184
FFN = 2816
NCORES = 8
SEG = 2048
HIST = 6144
HALO = 128
ROWS = HIST + HALO + SEG
EPS = 1e-6
EPOCH = 30000

O_A, O_BG, O_Q, O_K, O_V, O_R, O_GL, O_GA, O_GB = 0, 1024, 2048, 2560, 3072, 4096, 5120, 5136, 6160


class Buf:
    __slots__ = ("name", "lastw", "readers", "sem", "dcount")

    def __init__(self, name):
        self.name = name
        self.lastw = None
        self.readers = {}
        self.sem = None
        self.dcount = 0


class Sched:
    ENGS = ("sync", "scalar", "vector", "gpsimd", "tensor")

    def __init__(self, nc, stack):
        self.nc = nc
        self.stack = stack
        self.ops = {e: [] for e in self.ENGS}
        self.count = {e: 0 for e in self.ENGS}
        self.sems = {e: [] for e in self.ENGS}
        self.waited = {e: {} for e in self.ENGS}
        self.same_engine_sync = {"scalar": True, "vector": True, "gpsimd": True, "tensor": False, "sync": False}

    def new_sem(self, name):
        return self.stack.enter_context(self.nc.semaphore(name))

    def eng_sem(self, e, epoch):
        while len(self.sems[e]) <= epoch:
            self.sems[e].append(self.new_sem(f"p_{e}_{len(self.sems[e])}"))
        return self.sems[e][epoch]

    def _wait(self, eng, tok):
        if tok[0] == "e":
            _, src, idx = tok
            if src == eng and not self.same_engine_sync[eng]:
                return
            epoch, val = divmod(idx - 1, EPOCH)
            val += 1
            key = ("e", src, epoch)
            sem = self.eng_sem(src, epoch)
        else:
            _, buf, val = tok
            key = ("d", id(buf))
            sem = buf.sem
        if self.waited[eng].get(key, 0) >= val:
            return
        self.waited[eng][key] = val
        self.ops[eng].append(lambda e, sem=sem, val=val: e.wait_ge(sem, val))

    def _deps(self, eng, reads, writes):
        for b in reads:
            if b.lastw is not None:
                self._wait(eng, b.lastw)
        for b in writes:
            if b.lastw is not None and not (b.lastw[0] == "e" and b.lastw[1] == eng):
                self._wait(eng, b.lastw)
            for t in b.readers.values():
                if not (t[0] == "e" and t[1] == eng):
                    self._wait(eng, t)

    def emit(self, eng, fn, reads=(), writes=()):
        self._deps(eng, reads, writes)
        self.count[eng] += 1
        idx = self.count[eng]
        sem = self.eng_sem(eng, (idx - 1) // EPOCH)
        self.ops[eng].append(lambda e, fn=fn, sem=sem: fn(e).then_inc(sem, 1))
        tok = ("e", eng, idx)
        for b in writes:
            b.lastw = tok
            b.readers = {}
        for b in reads:
            b.readers[eng] = tok
        return tok

    def dma(self, eng, out, in_, reads=(), writes=(), track=None, **kw):
        self._deps(eng, reads, writes)
        if track.sem is None:
            track.sem = self.new_sem("d_" + track.name)
        track.dcount += 1
        val = track.dcount * 16
        sem = track.sem
        self.ops[eng].append(
            lambda e, sem=sem, out=out, in_=in_, kw=kw: e.dma_start(out=out, in_=in_, **kw).then_inc(sem, 16))
        tok = ("d", track, val)
        for b in writes:
            b.lastw = tok
            b.readers = {}
        for b in reads:
            b.readers["dma_" + track.name] = tok
        return tok

    def replay(self, block):
        for e in self.ENGS:
            ops = self.ops[e]

            def body(engine, ops=ops):
                for f in ops:
                    f(engine)
            getattr(block, e)(body)


def build_program(n_hist_tiles=HIST // 512, n_main_tiles=SEG // 512, dbg=None):
    nc = bass.Bass("TRN2", target_bir_lowering=False)
    dt_in = lambda name, shape: nc.dram_tensor(name, shape, F32, kind="ExternalInput").ap()
    x = dt_in("x", [ROWS, D])
    hasprev = dt_in("hasprev", [128, 1])
    norm1_g = dt_in("norm1_g", [D]); w_in = dt_in("w_in", [D, IN_DIM])
    conv_dw_w = dt_in("conv_dw_w", [31, D]); conv_dw_b = dt_in("conv_dw_b", [D])
    conv_ln_g = dt_in("conv_ln_g", [D]); conv_ln_b = dt_in("conv_ln_b", [D])
    w_conv_out = dt_in("w_conv_out", [D, D]); w_gate_up = dt_in("w_gate_up", [16, 512])
    b_gate = dt_in("b_gate", [512]); gla_norm_g = dt_in("gla_norm_g", [256])
    w_gla_out = dt_in("w_gla_out", [D, D]); w_o = dt_in("w_o", [D, D])
    norm2_g = dt_in("norm2_g", [D]); w_ffn_up = dt_in("w_ffn_up", [D, 2 * FFN])
    ffn_dw_w = dt_in("ffn_dw_w", [3, 2 * FFN]); ffn_dw_b = dt_in("ffn_dw_b", [2 * FFN])
    w_ffn_down = dt_in("w_ffn_down", [FFN, D]); final_g = dt_in("final_g", [D])
    out = nc.dram_tensor("out", [SEG, D], F32, kind="ExternalOutput").ap()
    dgd = nc.dram_tensor("dgd", [24, 128, 11 * 128], BF16).ap()
    wb = {"in": nc.dram_tensor("wb_in", [D, IN_DIM], BF16).ap(), "co": nc.dram_tensor("wb_co", [D, D], BF16).ap(),
          "go": nc.dram_tensor("wb_go", [D, D], BF16).ap(), "wo": nc.dram_tensor("wb_wo", [D, D], BF16).ap(),
          "up": nc.dram_tensor("wb_up", [D, 2 * FFN], BF16).ap(), "dn": nc.dram_tensor("wb_dn", [FFN, D], BF16).ap()}
    dbg_out = {}
    if dbg:
        for name, shape in dbg.items():
            dbg_out[name] = nc.dram_tensor("dbg_" + name, list(shape), F32, kind="ExternalOutput").ap()

    with ExitStack() as stack:
        S = Sched(nc, stack)
        _n = [0]

        def sb(shape, dt, name=None):
            _n[0] += 1
            return stack.enter_context(nc.sbuf_tensor(name or f"t{_n[0]}", list(shape), dt))

        identf = sb([128, 128], F32); identfB = Buf("identf")
        identb = sb([128, 128], BF16); identbB = Buf("identb")
        tri_inc = sb([128, 128], F32); tri_end = sb([128, 128], F32); cmask = sb([128, 128], F32)
        chsel = sb([128, 2], F32); onesf = sb([128, 128], F32); ones_row = sb([1, 128], BF16)
        constB = Buf("consts")
        stage = sb([128, 128], F32); stageB = Buf("stage")
        g1T = sb([128, 8], F32); g2T = sb([128, 8], F32); cbT = sb([128, 8], F32)
        lngT = sb([128, 8], F32); lnbT = sb([128, 8], F32); fbT = sb([128, 44], F32)
        gnT = sb([128, 2], F32); cwT = sb([128, 248], F32); fwT = sb([128, 132], F32)
        fgB_t = sb([128, D], F32)
        bgrow = sb([1, 512], BF16); wgu = sb([16, 512], BF16)
        hp = sb([128, 1], F32)
        vecB = Buf("vecs")
        banks = [stack.enter_context(nc.psum_tensor(f"ps{i}", [128, 512], F32)) for i in range(6)]
        bankB = [Buf(f"ps{i}") for i in range(6)]
        pTs = [stack.enter_context(nc.psum_tensor(f"pT{i}", [128, 1024], BF16)) for i in range(2)]; pTBs = [Buf(f"pT{i}") for i in range(2)]
        _pt = [0]

        def npT():
            i = _pt[0] % 2
            _pt[0] += 1
            return pTs[i], pTBs[i]
        _bk = [0]

        def nb():
            i = _bk[0] % 6
            _bk[0] += 1
            return banks[i], bankB[i]

        NSLOT = 4
        slots = [sb([128, 8, 512], BF16, f"slot{i}") for i in range(NSLOT)]
        slotB = [Buf(f"slot{i}") for i in range(NSLOT)]
        xs = [sb([128, D], F32, f"xs{i}") for i in range(2)]; xsB = [Buf(f"xs{i}") for i in range(2)]
        utms = [sb([128, D], BF16, f"utm{i}") for i in range(2)]; utmBs = [Buf(f"utm{i}") for i in range(2)]
        uT = sb([128, 8, 512], BF16); uTB = Buf("uT")
        glowT = sb([128, 512], BF16); glowTB = Buf("glowT")
        lbuf = sb([128, 4, 512], F32); lB = [Buf(f"l{i}") for i in range(4)]
        NTMP = 6
        tmps = [sb([128, 516], F32, f"tmp{i}") for i in range(NTMP)]; tmpB = [Buf(f"tmp{i}") for i in range(NTMP)]
        _tk = [0]

        def ntmp():
            i = _tk[0] % NTMP
            _tk[0] += 1
            return tmps[i], tmpB[i]
        Epl = sb([128, 4, 512], BF16); Emi = sb([128, 4, 512], BF16); EB = [Buf(f"E{i}") for i in range(4)]
        decs = [sb([128, 4, 8], F32, f"dec{p}") for p in range(2)]; decBs = [[Buf(f"dec{p}_{i}") for i in range(4)] for p in range(2)]
        dec = decs[0]; decB = decBs[0]
        kend = sb([128, 4, 512], BF16); kendB = [Buf(f"kend{i}") for i in range(4)]
        vtm = sb([128, 4, D], BF16); vtmB = [Buf(f"v{i}") for i in range(4)]
        kT = sb([128, 4, 512], BF16); kTB = [Buf(f"kT{i}") for i in range(4)]
        qT = sb([128, 4, 512], BF16); qTB = [Buf(f"qT{i}") for i in range(4)]
        silur = sb([128, 8, 512], BF16); silurB = [Buf(f"sr{i}") for i in range(8)]
        attm = sb([128, 4, 128], BF16); attmB = [Buf(f"att{i}") for i in range(4)]
        Sst = sb([128, D], F32); SstB = Buf("S")
        Sbf = [sb([128, D], BF16, f"Sbf{i}") for i in range(2)]; SbfB = [Buf(f"Sbf{i}") for i in range(2)]
        on = sb([128, D], BF16); onB = Buf("on")
        oss = sb([128, 8], F32); ossB = Buf("oss")
        actT = sb([128, 8, 512], BF16); actTB = [Buf(f"actT{i}") for i in range(8)]
        cin = sb([128, 8, 542], BF16); cinB = [Buf(f"cin{i}") for i in range(8)]
        arena = sb([128, 22 * 256], F32, "arena")
        cacc = arena[:, 0:4096].rearrange("p (k t) -> p k t", k=8)
        gT = arena.bitcast(BF16).rearrange("p (k t) -> p k t", k=22)
        arB = [Buf(f"ar{i}") for i in range(8)]
        m1 = sb([128, 8, 512], BF16); m1B = [Buf(f"m1{i}") for i in range(8)]
        hbuf = sb([128, 4, D], F32); hB = [Buf(f"h{i}") for i in range(4)]
        zhalo = sb([128, 44, 2], F32); zhB = Buf("zhalo")
        ss = sb([128, 8], F32); ssB = Buf("ss")
        dg = [sb([128, 11, 128], BF16, f"dg{i}") for i in range(3)]; dgB = [Buf(f"dg{i}") for i in range(3)]
        dgdB = Buf("dgd")
        PARTS = ((0, 11), (11, 10), (21, 10))
        lnst = [sb([128, 512], F32, f"lnst{i}") for i in range(2)]; lnstB = [Buf(f"lnst{i}") for i in range(2)]

        block = stack.enter_context(nc.Block())
        V, A, P, T_, SY = "vector", "scalar", "gpsimd", "tensor", "sync"

        def iota_sel(t, pattern, cmp, fill, base, cm, src=None):
            S.emit(P, lambda e: e.affine_select(out=t, in_=(src if src is not None else t), pattern=pattern,
                                                compare_op=cmp, fill=fill, base=base, channel_multiplier=cm),
                   reads=[constB], writes=[constB])
        S.emit(P, lambda e: e.memset(identf[:], 0.0), writes=[constB])
        iota_sel(identf[:], [[-1, 128]], ALU.not_equal, 1.0, 0, 1)
        S.emit(V, lambda e: e.tensor_copy(out=identb[:], in_=identf[:]), reads=[constB], writes=[identbB])
        S.emit(P, lambda e: e.memset(onesf[:], 1.0), writes=[constB])
        S.emit(P, lambda e: e.memset(ones_row[:], 1.0), writes=[constB])
        for t, val in ((tri_inc, -1.0 / 16), (tri_end, -1.0 / 16), (cmask, 1.0)):
            S.emit(P, lambda e, t=t, val=val: e.memset(t[:], val), writes=[constB])
        iota_sel(tri_inc[:], [[1, 128]], ALU.is_ge, 0.0, 0, -1)
        iota_sel(cmask[:], [[1, 128]], ALU.is_ge, 0.0, 0, -1)
        iota_sel(tri_end[:], [[-1, 128]], ALU.is_gt, 0.0, 0, 1)
        S.emit(P, lambda e: e.memset(tri_inc[0:64, 64:128], 0.0), writes=[constB])
        S.emit(P, lambda e: e.memset(cmask[0:64, 64:128], 0.0), writes=[constB])
        S.emit(P, lambda e: e.memset(tri_end[64:128, 0:64], 0.0), writes=[constB])
        S.emit(P, lambda e: e.memset(chsel[:], 0.0), writes=[constB])
        S.emit(P, lambda e: e.memset(chsel[0:64, 0:1], -1.0 / 16), writes=[constB])
        S.emit(P, lambda e: e.memset(chsel[64:128, 1:2], -1.0 / 16), writes=[constB])
        S.emit(P, lambda e: e.memset(zhalo[:], 0.0), writes=[zhB])
        S.emit(P, lambda e: e.memset(Sst[:], 0.0), writes=[SstB])
        S.emit(P, lambda e: e.memset(cin[:], 0.0), writes=cinB)

        def load_cols(dst, rows_ap, nrows):
            r0 = 0
            while r0 < nrows:
                n = min(128, nrows - r0)
                S.dma(SY, stage[0:n, :], rows_ap[r0:r0 + n, :], writes=[stageB], track=stageB)
                ps, psB = nb()
                S.emit(T_, lambda e, ps=ps, n=n: e.matmul(ps[:, 0:n], lhsT=stage[0:n, :], rhs=identf[0:n, 0:n], start=True, stop=True),
                       reads=[stageB, constB], writes=[psB])
                S.emit(V, lambda e, ps=ps, n=n, r0=r0: e.tensor_copy(out=dst[:, r0:r0 + n], in_=ps[:, 0:n]), reads=[psB], writes=[vecB])
                r0 += n
        load_cols(g1T, norm1_g.rearrange("(k p) -> k p", p=128), 8)
        load_cols(g2T, norm2_g.rearrange("(k p) -> k p", p=128), 8)
        load_cols(cbT, conv_dw_b.rearrange("(k p) -> k p", p=128), 8)
        load_cols(lngT, conv_ln_g.rearrange("(k p) -> k p", p=128), 8)
        load_cols(lnbT, conv_ln_b.rearrange("(k p) -> k p", p=128), 8)
        load_cols(fbT, ffn_dw_b.rearrange("(k p) -> k p", p=128), 44)
        load_cols(gnT, gla_norm_g.rearrange("(k p) -> k p", p=128), 2)
        load_cols(cwT, conv_dw_w.rearrange("j (k p) -> (j k) p", p=128), 248)
        load_cols(fwT, ffn_dw_w.rearrange("j (k p) -> (j k) p", p=128), 132)
        rowst = xs[0]
        S.dma(SY, rowst[0:1, :], final_g.rearrange("(o n) -> o n", o=1), writes=[xsB[0]], track=xsB[0])
        for hf in range(2):
            ps, psB = nb()
            S.emit(T_, lambda e, ps=ps, hf=hf: e.matmul(ps[:], lhsT=onesf[0:1, :], rhs=rowst[0:1, hf * 512:(hf + 1) * 512], start=True, stop=True),
                   reads=[xsB[0], constB], writes=[psB])
            S.emit(V, lambda e, ps=ps, hf=hf: e.tensor_copy(out=fgB_t[:, hf * 512:(hf + 1) * 512], in_=ps[:]), reads=[psB], writes=[vecB])
        setup_toks = [vecB.lastw, constB.lastw, identbB.lastw]
        setup_toks.append(S.dma(P, bgrow[:], b_gate.rearrange("(o n) -> o n", o=1), writes=[vecB], track=Buf("bgrow")))
        setup_toks.append(S.dma(P, wgu[:], w_gate_up, writes=[vecB], track=Buf("wgu")))
        setup_toks.append(S.dma(SY, hp[:], hasprev, writes=[vecB], track=Buf("hp")))
        for eng in (A, V, T_, P):
            for tk in setup_toks:
                S._wait(eng, tk)
        vecB.lastw = None; vecB.readers = {}
        constB.lastw = None; constB.readers = {}

        tapon = [False]

        def dbg_tap(name, ap, bufs, force=False):
            if dbg and name in dbg and (tapon[0] or force):
                tb = Buf("dbg_" + name)
                S.dma(P, dbg_out[name], ap, reads=bufs, track=tb)
                S._wait(P, ("d", tb, tb.dcount * 16))

        for g in range(24):
            kc, part = divmod(g, 3)
            j0, nj = PARTS[part]
            bi = g % 3
            for jj in range(nj):
                col = (j0 + jj) * 8 + kc
                S.emit(V, lambda e, bi=bi, jj=jj, col=col: e.tensor_scalar(out=dg[bi][:, jj, :], in0=identb[:], scalar1=cwT[:, col:col + 1], scalar2=None, op0=ALU.mult),
                       reads=[identbB], writes=[dgB[bi]])
            tok = S.dma(SY, dgd[g][:, 0:nj * 128], dg[bi][:, 0:nj, :].rearrange("p j i -> p (j i)"), reads=[dgB[bi]], track=dgdB)
            dgdB.lastw = tok

        def load_w(slot_i, src, c0, ncols, r0=0, nk=8):
            S.dma(P, slots[slot_i][:, 0:nk, 0:ncols],
                  src[r0 * 128:(r0 + nk) * 128, c0:c0 + ncols].rearrange("(k p) n -> p k n", p=128),
                  writes=[slotB[slot_i]], track=slotB[slot_i])
        wbB = Buf("wb")

        def load_wb(slot_i, src, c0, ncols, r0=0, nk=8):
            S.dma(P, slots[slot_i][:, 0:nk, 0:ncols],
                  src[r0 * 128:(r0 + nk) * 128, c0:c0 + ncols].rearrange("(k p) n -> p k n", p=128),
                  reads=[wbB], writes=[slotB[slot_i]], track=slotB[slot_i])

        def convert_weights():
            bounce = [(actT, actTB), (silur, silurB), (m1, m1B)]
            cvB = [Buf(f"cv{i}") for i in range(3)]
            blocks = []
            for c0 in range(0, IN_DIM - 16, 512):
                blocks.append(("in", 0, 8, c0, 512))
            blocks.append(("in", 0, 8, IN_DIM - 16, 16))
            for nm in ("co", "go", "wo"):
                for hf in range(2):
                    blocks.append((nm, 0, 8, hf * 512, 512))
            for c0 in range(0, 2 * FFN, 512):
                blocks.append(("up", 0, 8, c0, 512))
            for r0, nk in ((0, 8), (8, 8), (16, 6)):
                for hf in range(2):
                    blocks.append(("dn", r0, nk, hf * 512, 512))
            for bi, (nm, r0, nk, c0, ncols) in enumerate(blocks):
                bt, bB = bounce[bi % 3]
                srcap = WSRC[nm][r0 * 128:(r0 + nk) * 128, c0:c0 + ncols].rearrange("(k p) n -> p k n", p=128)
                dstap = wb[nm][r0 * 128:(r0 + nk) * 128, c0:c0 + ncols].rearrange("(k p) n -> p k n", p=128)
                S.dma(P, bt[:, 0:nk, 0:ncols], srcap, writes=bB, track=cvB[bi % 3])
                tok = S.dma(P, dstap, bt[:, 0:nk, 0:ncols], reads=bB, track=wbB)
                wbB.lastw = tok

        WSRC = {"in": w_in, "co": w_conv_out, "go": w_gla_out, "wo": w_o, "up": w_ffn_up, "dn": w_ffn_down}

        def tile_wseq(is_halo):
            q = [("in", O_GL, 512, 0, 8), ("in", O_K, 512, 0, 8), ("in", O_V, 512, 0, 8), ("in", O_V + 512, 512, 0, 8), ("in", O_Q, 512, 0, 8),
                 ("in", O_R, 512, 0, 8), ("in", O_R + 512, 512, 0, 8)]
            for hf in range(2):
                q += [("in", O_A + hf * 512, 512, 0, 8), ("in", O_BG + hf * 512, 512, 0, 8)]
            for hf in range(2):
                q += [("go", hf * 512, 512, 0, 8), ("in", O_GB + hf * 512, 512, 0, 8)]
            for hf in range(2):
                q += [("in", O_GA + hf * 512, 512, 0, 8)]
            for hf in range(2):
                q += [("co", hf * 512, 512, 0, 8)]
            for hf in range(2):
                q += [("wo", hf * 512, 512, 0, 8)]
            for g in range(6):
                nblk = 4 if g < 5 else 2
                q += [("up", g * 512, nblk * 128, 0, 8), ("up", FFN + g * 512, nblk * 128, 0, 8)]
            if not is_halo:
                for hf in range(2):
                    for g3 in range(3):
                        q += [("dn", hf * 512, 512, g3 * 8, 8 if g3 < 2 else 6)]
            return q
        WSEQ = tile_wseq(True)
        for _ in range(n_main_tiles):
            WSEQ += tile_wseq(False)
        _wi = [0, 0]

        def next_w(*key):
            i = _wi[0]
            assert WSEQ[i] == key, (i, WSEQ[i], key)
            while _wi[1] < len(WSEQ) and _wi[1] <= i + NSLOT - 2:
                k = WSEQ[_wi[1]]
                load_wb(_wi[1] % NSLOT, wb[k[0]], k[1], k[2], r0=k[3], nk=k[4])
                _wi[1] += 1
            _wi[0] += 1
            return i % NSLOT

        def norm_a(srcs, ums):
            n = len(srcs)
            for i, (ap, bf) in enumerate(srcs):
                um, umB = ums[i]
                S.emit(A, lambda e, ap=ap, i=i, um=um: e.activation(out=um, in_=ap, func=AF.Square, accum_out=ss[:, i:i + 1]),
                       reads=[bf], writes=list(umB) + [ssB])
            S.emit(A, lambda e: e.activation(out=ss[:, 4:4 + n], in_=ss[:, 0:n], func=AF.Ln, scale=1.0 / D, bias=EPS), reads=[ssB], writes=[ssB])
            S.emit(A, lambda e: e.activation(out=ss[:, 4:4 + n], in_=ss[:, 4:4 + n], func=AF.Exp, scale=-0.5), reads=[ssB], writes=[ssB])
            for i, (ap, bf) in enumerate(srcs):
                um, umB = ums[i]
                if i % 2 == 0:
                    S.emit(V, lambda e, ap=ap, i=i, um=um: e.tensor_scalar(out=um, in0=ap, scalar1=ss[:, 4 + i:5 + i], scalar2=None, op0=ALU.mult),
                           reads=[bf, ssB], writes=list(umB))
                else:
                    S.emit(A, lambda e, ap=ap, i=i, um=um: e.activation(out=um, in_=ap, func=AF.Identity, scale=ss[:, 4 + i:5 + i]),
                           reads=[bf, ssB], writes=list(umB))

        def norm_b(ums, gT_, s0):
            for i, (um, umB) in enumerate(ums):
                pT, pTB = npT()
                for kc in range(8):
                    S.emit(T_, lambda e, kc=kc, um=um, pT=pT: e.transpose(out=pT[:, kc * 128:(kc + 1) * 128], in_=um[:, kc * 128:(kc + 1) * 128], identity=identb[:]),
                           reads=list(umB) + [identbB], writes=[pTB])
                s = s0 + i
                for kc in range(8):
                    S.emit(V, lambda e, kc=kc, s=s, pT=pT: e.tensor_scalar(out=uT[:, kc, s * 128:(s + 1) * 128], in0=pT[:, kc * 128:(kc + 1) * 128],
                                                                         scalar1=gT_[:, kc:kc + 1], scalar2=None, op0=ALU.mult),
                           reads=[pTB, vecB], writes=[uTB])

        def um_std(i):
            return (utms[i][:], [utmBs[i]])

        def norm_stage(srcs, gT_, s0):
            for c in range(0, len(srcs), 2):
                ums = [um_std(i) for i in range(min(2, len(srcs) - c))]
                norm_a(srcs[c:c + 2], ums)
                norm_b(ums, gT_, s0 + c)

        def load_x(row0, i):
            S.dma(SY, xs[i][:], x[row0:row0 + 128, :], writes=[xsB[i]], track=xsB[i])

        def proj_T(slot_i, s, ncols=512):
            ps, psB = nb()
            for kc in range(8):
                S.emit(T_, lambda e, kc=kc, ps=ps: e.matmul(ps[:, 0:ncols], lhsT=uT[:, kc, s * 128:(s + 1) * 128], rhs=slots[slot_i][:, kc, 0:ncols],
                                                           start=(kc == 0), stop=(kc == 7)),
                       reads=[uTB, slotB[slot_i]], writes=[psB])
            return ps, psB

        def proj_F(w_ap_fn, wB, rhs, rhsB, T, nk=8, M=128):
            ps, psB = nb()
            for kc in range(nk):
                S.emit(T_, lambda e, kc=kc, ps=ps: e.matmul(ps[0:M, 0:T], lhsT=w_ap_fn(kc), rhs=rhs[:, kc, 0:T],
                                                           start=(kc == 0), stop=(kc == nk - 1)),
                       reads=list(wB) + list(rhsB), writes=[psB])
            return ps, psB

        def gate_stage(T, NS, main, gsl, p=0):
            gate_g1(T, NS, gsl)
            gate_g2(NS, main, p)

        def gate_g1(T, NS, gsl):
            ps, psB = proj_F(lambda kc: slots[gsl][:, kc, 0:128], [slotB[gsl]], uT, [uTB], T)
            S.emit(A, lambda e, ps=ps: e.activation(out=glowT[:, 0:T], in_=ps[:, 0:T], func=AF.Copy), reads=[psB], writes=[glowTB])
            dbg_tap("glowT", glowT[0:16, :], [glowTB]); dbg_tap("wgu", wgu[:], []); dbg_tap("bgrow", bgrow[:], [])
            pre = []
            for s in range(NS):
                ps, psB = nb()
                S.emit(T_, lambda e, ps=ps, s=s: e.matmul(ps[:], lhsT=glowT[0:16, s * 128:(s + 1) * 128], rhs=wgu[:], start=True, stop=False),
                       reads=[glowTB, vecB], writes=[psB])
                S.emit(T_, lambda e, ps=ps: e.matmul(ps[:], lhsT=ones_row[:], rhs=bgrow[:], start=False, stop=True),
                       reads=[constB, vecB], writes=[psB])
                pre.append((ps, psB))
            for s in range(NS):
                ps, psB = pre[s]
                tm, tmB = ntmp()
                S.emit(A, lambda e, ps=ps, tm=tm: e.activation(out=tm[:, 0:512], in_=ps[:], func=AF.Exp, scale=-1.0), reads=[psB], writes=[tmB])
                if s == 0:
                    dbg_tap("expn", tm[:, 0:512], [tmB])
                S.emit(A, lambda e, tm=tm, s=s: e.activation(out=lbuf[:, s, :], in_=tm[:, 0:512], func=AF.Ln, bias=1.0), reads=[tmB], writes=[lB[s]])

        def gate_g2(NS, main, p=0):
            dec, decB = decs[p], decBs[p]
            for s in range(NS):
                ps, psB = nb()
                for h in range(4):
                    S.emit(T_, lambda e, ps=ps, s=s, h=h: e.matmul(ps[:, h * 2:h * 2 + 2], lhsT=lbuf[:, s, h * 128:(h + 1) * 128], rhs=chsel[:], start=True, stop=True),
                           reads=[lB[s], constB], writes=[psB])
                S.emit(A, lambda e, ps=ps, s=s: e.activation(out=dec[:, s, :], in_=ps[:, 0:8], func=AF.Exp), reads=[psB], writes=[decB[s]])
                if main:
                    ps, psB = nb()
                    for h in range(4):
                        S.emit(T_, lambda e, ps=ps, s=s, h=h: e.matmul(ps[:, h * 128:(h + 1) * 128], lhsT=lbuf[:, s, h * 128:(h + 1) * 128], rhs=tri_inc[:], start=True, stop=True),
                               reads=[lB[s], constB], writes=[psB])
                    psv = ps[:].rearrange("p (h t) -> p h t", h=4)
                    S.emit(A, lambda e, psv=psv, s=s: e.activation(out=Epl[:, :, s * 128:(s + 1) * 128], in_=psv, func=AF.Exp), reads=[psB], writes=[EB[s]])
                    S.emit(A, lambda e, psv=psv, s=s: e.activation(out=Emi[:, :, s * 128:(s + 1) * 128], in_=psv, func=AF.Exp, scale=-1.0), reads=[psB], writes=[EB[s]])

        def kv_stage(NS, main, kslot_fn, vslot_fns, after_k=None):
            kslot = kslot_fn()
            for s in range(NS):
                ps, psB = nb()
                S.emit(T_, lambda e, ps=ps, s=s: e.matmul(ps[:], lhsT=tri_end[:], rhs=lbuf[:, s, :], start=True, stop=True),
                       reads=[lB[s], constB], writes=[psB])
                tm, tmB = ntmp()
                S.emit(A, lambda e, ps=ps, tm=tm: e.activation(out=tm[:, 0:512], in_=ps[:], func=AF.Exp), reads=[psB], writes=[tmB])
                ps, psB = proj_T(kslot, s)
                S.emit(V, lambda e, ps=ps, tm=tm, s=s: e.tensor_tensor(out=kend[:, s, :], in0=ps[:], in1=tm[:, 0:512], op=ALU.mult),
                       reads=[psB, tmB], writes=[kendB[s]])
            if after_k is not None:
                after_k(kslot)
            for hf in range(2):
                vs = vslot_fns[hf]()
                for s in range(NS):
                    ps, psB = proj_T(vs, s)
                    S.emit(A, lambda e, ps=ps, s=s, hf=hf: e.activation(out=vtm[:, s, hf * 512:(hf + 1) * 512], in_=ps[:], func=AF.Copy),
                           reads=[psB], writes=[vtmB[s]])

        def state_chunk(s, c, want_bf=None, p=0):
            dec, decB = decs[p], decBs[p]
            pk = [nb(), nb()]
            for h in range(4):
                ps, psB = pk[h // 2]
                S.emit(T_, lambda e, ps=ps, h=h: e.matmul(ps[:, (h % 2) * 256:(h % 2) * 256 + 256], lhsT=kend[c * 64:(c + 1) * 64, s, h * 128:(h + 1) * 128],
                                                         rhs=vtm[c * 64:(c + 1) * 64, s, h * 256:(h + 1) * 256], start=True, stop=True),
                       reads=[kendB[s], vtmB[s]], writes=[psB])
            for h in range(4):
                ps, psB = pk[h // 2]
                S.emit(V, lambda e, ps=ps, h=h: e.scalar_tensor_tensor(out=Sst[:, h * 256:(h + 1) * 256], in0=Sst[:, h * 256:(h + 1) * 256],
                                                                      scalar=dec[:, s, h * 2 + c:h * 2 + c + 1], in1=ps[:, (h % 2) * 256:(h % 2) * 256 + 256],
                                                                      op0=ALU.mult, op1=ALU.add),
                       reads=[SstB, decB[s], psB], writes=[SstB])
            if want_bf is not None:
                S.emit(A, lambda e: e.activation(out=Sbf[want_bf][:], in_=Sst[:], func=AF.Copy), reads=[SstB], writes=[SbfB[want_bf]])

        hist_slots = None
        if n_hist_tiles > 0 or True:
            load_w(0, w_in, O_K, 512); load_w(1, w_in, O_V, 512); load_w(2, w_in, O_V + 512, 512); load_w(3, w_in, O_GL, 512)
            convert_weights()
        hums = [um_std(0), um_std(1), (on[:], [onB]), (Sbf[1][:], [SbfB[1]])]

        def hist_na(t):
            for pr in range(2):
                for i in range(2):
                    load_x(t * 512 + (pr * 2 + i) * 128, i)
                norm_a([(xs[i][:], xsB[i]) for i in range(2)], hums[pr * 2:pr * 2 + 2])

        def hist_state(t):
            for s4 in range(4):
                state_chunk(s4, 0, p=t % 2)
                state_chunk(s4, 1, p=t % 2)

        if n_hist_tiles > 0:
            hist_na(0)
            norm_b(hums, g1T, 0)
            gate_stage(512, 4, False, 3, p=0)
        for t in range(n_hist_tiles):
            if t + 1 < n_hist_tiles:
                hist_na(t + 1)
            kv_stage(4, False, lambda: 0, (lambda: 1, lambda: 2))
            if t + 1 < n_hist_tiles:
                norm_b(hums, g1T, 0)
                gate_g1(512, 4, 3)
                hist_state(t)
                gate_g2(4, False, (t + 1) % 2)
            else:
                hist_state(t)
        S.emit(A, lambda e: e.activation(out=Sbf[0][:], in_=Sst[:], func=AF.Copy), reads=[SstB], writes=[SbfB[0]])
        sbf_cur = [0]

        dbg_tap("S_hist", Sst[:], [SstB], force=True)

        def main_tile(row0, T, is_halo, out_row0, prefetched=False, next_row0=None):
            NS = T // 128
            pf_ums = [um_std(0), um_std(1),
                      (actT[:, 0:2, :].rearrange("p k t -> p (k t)"), actTB[0:2]), (actT[:, 2:4, :].rearrange("p k t -> p (k t)"), actTB[2:4])]
            if not prefetched:
                for pr in range((NS + 1) // 2):
                    nn = min(2, NS - pr * 2)
                    ums = [um_std(i) for i in range(nn)]
                    for i in range(nn):
                        load_x(row0 + (pr * 2 + i) * 128, i)
                    norm_a([(xs[i][:], xsB[i]) for i in range(nn)], ums)
                    norm_b(ums, g1T, pr * 2)

            def transposes_next():
                if next_row0 is not None:
                    norm_b(pf_ums[0:2], g1T, 0)
                    norm_b(pf_ums[2:4], g1T, 2)

            def prefetch_next():
                if next_row0 is not None:
                    for pr in range(2):
                        for i in range(2):
                            load_x(next_row0 + (pr * 2 + i) * 128, i)
                        norm_a([(xs[i][:], xsB[i]) for i in range(2)], pf_ums[pr * 2:pr * 2 + 2])
            dbg_tap("uT", uT[:], [uTB])
            gate_stage(T, NS, True, next_w("in", O_GL, 512, 0, 8))
            def k_feature_major(ks):
                for h in range(4):
                    ps, psB = proj_F(lambda kc, h=h: slots[ks][:, kc, h * 128:(h + 1) * 128], [slotB[ks]], uT, [uTB], T)
                    S.emit(V, lambda e, ps=ps, h=h: e.tensor_tensor(out=kT[:, h, 0:T], in0=ps[:, 0:T], in1=Emi[:, h, 0:T], op=ALU.mult),
                           reads=[psB] + EB[0:NS], writes=[kTB[h]])
            kv_stage(NS, True, lambda: next_w("in", O_K, 512, 0, 8),
                     (lambda: next_w("in", O_V, 512, 0, 8), lambda: next_w("in", O_V + 512, 512, 0, 8)), after_k=k_feature_major)
            qs = next_w("in", O_Q, 512, 0, 8)
            for h in range(4):
                ps, psB = proj_F(lambda kc, h=h: slots[qs][:, kc, h * 128:(h + 1) * 128], [slotB[qs]], uT, [uTB], T)
                S.emit(V, lambda e, ps=ps, h=h: e.scalar_tensor_tensor(out=qT[:, h, 0:T], in0=ps[:, 0:T], scalar=128.0 ** -0.5, in1=Epl[:, h, 0:T],
                                                                      op0=ALU.mult, op1=ALU.mult),
                       reads=[psB] + EB[0:NS], writes=[qTB[h]])
            for hf in range(2):
                rs = next_w("in", O_R + hf * 512, 512, 0, 8)
                for j in range(4):
                    kc = hf * 4 + j
                    ps, psB = proj_F(lambda kk, j=j, rs=rs: slots[rs][:, kk, j * 128:(j + 1) * 128], [slotB[rs]], uT, [uTB], T)
                    S.emit(A, lambda e, ps=ps, kc=kc: e.activation(out=silur[:, kc, 0:T], in_=ps[:, 0:T], func=AF.Silu), reads=[psB], writes=[silurB[kc]])
            dbg_tap("l", lbuf[:], lB); dbg_tap("kT", kT[:], kTB); dbg_tap("qT", qT[:], qTB); dbg_tap("vtm", vtm[:], vtmB)
            dbg_tap("kend", kend[:], kendB); dbg_tap("silur", silur[:], silurB); dbg_tap("dec", decs[0][:], decBs[0])
            for hf in range(2):
                sa = next_w("in", O_A + hf * 512, 512, 0, 8)
                sg = next_w("in", O_BG + hf * 512, 512, 0, 8)
                for j in range(4):
                    kc = hf * 4 + j
                    psa, psaB = proj_F(lambda kk, j=j, sa=sa: slots[sa][:, kk, j * 128:(j + 1) * 128], [slotB[sa]], uT, [uTB], T)
                    psg, psgB = proj_F(lambda kk, j=j, sg=sg: slots[sg][:, kk, j * 128:(j + 1) * 128], [slotB[sg]], uT, [uTB], T)
                    tm, tmB = ntmp()
                    S.emit(A, lambda e, psg=psg, tm=tm: e.activation(out=tm[:, 0:T], in_=psg[:, 0:T], func=AF.Sigmoid), reads=[psgB], writes=[tmB])
                    S.emit(V, lambda e, psa=psa, tm=tm, kc=kc: e.tensor_tensor(out=cin[:, kc, 30:30 + T], in0=psa[:, 0:T], in1=tm[:, 0:T], op=ALU.mult),
                           reads=[psaB, tmB], writes=[cinB[kc]])
            dgtrk = [Buf(f"dgl{i}") for i in range(3)] if not hasattr(main_tile, "_dgtrk") else main_tile._dgtrk
            main_tile._dgtrk = dgtrk

            def load_part(g):
                if g >= 24:
                    return
                j0, nj = PARTS[g % 3]
                bi = g % 3
                S.dma(SY, dg[bi][:, 0:nj, :], dgd[g][:, 0:nj * 128].rearrange("p (j i) -> p j i", i=128), reads=[dgdB], writes=[dgB[bi]], track=dgtrk[bi])

            def conv_block(kc):
                psc, pscB = nb()
                for part in range(3):
                    g = kc * 3 + part
                    j0, nj = PARTS[part]
                    bi = g % 3
                    load_part(g + 2)
                    for jj in range(nj):
                        j = j0 + jj
                        S.emit(T_, lambda e, bi=bi, jj=jj, j=j: e.matmul(psc[:, 0:T], lhsT=dg[bi][:, jj, :], rhs=cin[:, kc, j:j + T], start=(j == 0), stop=(j == 30)),
                               reads=[dgB[bi], cinB[kc]], writes=[pscB])
                S.emit(A, lambda e: e.activation(out=cacc[:, kc, 0:T], in_=psc[:, 0:T], func=AF.Identity, bias=cbT[:, kc:kc + 1]),
                       reads=[pscB], writes=[arB[kc]])
                S.emit(A, lambda e: e.activation(out=cin[:, kc, 0:30], in_=cin[:, kc, T:T + 30], func=AF.Copy), reads=[cinB[kc]], writes=[cinB[kc]])

            gla_po = {}

            def gla_a(s):
                tok = slice(s * 128, (s + 1) * 128)
                for h in range(4):
                    ps, psB = nb()
                    S.emit(T_, lambda e, ps=ps, h=h: e.matmul(ps[:, 0:128], lhsT=kT[:, h, tok], rhs=qT[:, h, tok], start=True, stop=True),
                           reads=[kTB[h], qTB[h]], writes=[psB])
                    S.emit(V, lambda e, ps=ps, h=h: e.tensor_tensor(out=attm[:, h, :], in0=ps[:, 0:128], in1=cmask[:], op=ALU.mult),
                           reads=[psB, constB], writes=[attmB[h]])
                c0 = sbf_cur[0]; c1 = 1 - c0
                state_chunk(s, 0, want_bf=c1)

            def gla_b(s):
                c0 = sbf_cur[0]; c1 = 1 - c0
                po = [nb(), nb()]
                gla_po[s] = po
                for h in range(4):
                    ps, psB = po[h // 2]
                    cols = slice((h % 2) * 256, (h % 2) * 256 + 256)
                    hc = slice(h * 256, (h + 1) * 256)
                    S.emit(T_, lambda e, ps=ps, h=h, cols=cols, hc=hc: e.matmul(ps[:, cols], lhsT=attm[:, h, :], rhs=vtm[:, s, hc], start=True, stop=False),
                           reads=[attmB[h], vtmB[s]], writes=[psB])
                    S.emit(T_, lambda e, ps=ps, h=h, cols=cols, hc=hc: e.matmul(ps[0:64, cols], lhsT=qT[:, h, s * 128:s * 128 + 64], rhs=Sbf[c0][:, hc], start=False, stop=False),
                           reads=[qTB[h], SbfB[c0]], writes=[psB])
                    S.emit(T_, lambda e, ps=ps, h=h, cols=cols, hc=hc: e.matmul(ps[64:128, cols], lhsT=qT[:, h, s * 128 + 64:s * 128 + 128], rhs=Sbf[c1][:, hc], start=False, stop=True,
                                                                             tile_position=(0, 64)),
                           reads=[qTB[h], SbfB[c1]], writes=[psB])
                state_chunk(s, 1, want_bf=c0)

            def gla_c(s):
                tok = slice(s * 128, (s + 1) * 128)
                po = gla_po[s]
                for h in range(4):
                    ps, psB = po[h // 2]
                    cols = slice((h % 2) * 256, (h % 2) * 256 + 256)
                    S.emit(A, lambda e, ps=ps, h=h, cols=cols: e.activation(out=on[:, h * 256:(h + 1) * 256], in_=ps[:, cols], func=AF.Square, accum_out=oss[:, h:h + 1]),
                           reads=[psB], writes=[onB, ossB])
                S.emit(A, lambda e: e.activation(out=oss[:, 4:8], in_=oss[:, 0:4], func=AF.Ln, scale=1.0 / 256, bias=EPS), reads=[ossB], writes=[ossB])
                S.emit(A, lambda e: e.activation(out=oss[:, 4:8], in_=oss[:, 4:8], func=AF.Exp, scale=-0.5), reads=[ossB], writes=[ossB])
                for h in range(4):
                    ps, psB = po[h // 2]
                    cols = slice((h % 2) * 256, (h % 2) * 256 + 256)
                    S.emit(A, lambda e, ps=ps, h=h, cols=cols: e.activation(out=on[:, h * 256:(h + 1) * 256], in_=ps[:, cols], func=AF.Identity, scale=oss[:, 4 + h:5 + h]),
                           reads=[psB, ossB], writes=[onB])

            def gla_c2(s):
                tok = slice(s * 128, (s + 1) * 128)
                pT, pTB = npT()
                for kc in range(8):
                    S.emit(T_, lambda e, kc=kc, pT=pT: e.transpose(out=pT[:, kc * 128:(kc + 1) * 128], in_=on[:, kc * 128:(kc + 1) * 128], identity=identb[:]),
                           reads=[onB, identbB], writes=[pTB])
                for kc in range(8):
                    S.emit(V, lambda e, kc=kc, pT=pT: e.scalar_tensor_tensor(out=actT[:, kc, tok], in0=pT[:, kc * 128:(kc + 1) * 128], scalar=gnT[:, kc % 2:kc % 2 + 1],
                                                                      in1=silur[:, kc, tok], op0=ALU.mult, op1=ALU.mult),
                           reads=[pTB, vecB, silurB[kc]], writes=[actTB[kc]])
            load_part(0)
            load_part(1)
            kc_next = 0
            for s in range(NS):
                gla_a(s)
                conv_block(kc_next); kc_next += 1
                gla_b(s)
                gla_c(s)
                conv_block(kc_next); kc_next += 1
                gla_c2(s)
            while kc_next < 8:
                conv_block(kc_next); kc_next += 1
            dbg_tap("cin", cin[:, :, 30:542], cinB); dbg_tap("cacc", cacc, arB); dbg_tap("oT", actT[:], actTB)
            psm = pTs[0].bitcast(F32); psmB = pTBs[0]
            psq = pTs[1].bitcast(F32); psqB = pTBs[1]
            for hf in range(2):
                sw = next_w("go", hf * 512, 512, 0, 8)
                sg = next_w("in", O_GB + hf * 512, 512, 0, 8)
                for j in range(4):
                    ob = hf * 4 + j
                    psy, psyB = proj_F(lambda kk, j=j, sw=sw: slots[sw][:, kk, j * 128:(j + 1) * 128], [slotB[sw]], actT, actTB, T)
                    psg, psgB = proj_F(lambda kk, j=j, sg=sg: slots[sg][:, kk, j * 128:(j + 1) * 128], [slotB[sg]], uT, [uTB], T)
                    tq, tqB = ntmp()
                    S.emit(A, lambda e, ob=ob, tq=tq: e.activation(out=tq[:, 0:T], in_=cacc[:, ob, 0:T], func=AF.Square), reads=[arB[ob]], writes=[tqB])
                    tm, tmB = ntmp()
                    S.emit(A, lambda e, psg=psg, tm=tm: e.activation(out=tm[:, 0:T], in_=psg[:, 0:T], func=AF.Sigmoid), reads=[psgB], writes=[tmB])
                    S.emit(V, lambda e, psy=psy, tm=tm, ob=ob: e.tensor_tensor(out=m1[:, ob, 0:T], in0=psy[:, 0:T], in1=tm[:, 0:T], op=ALU.mult),
                           reads=[psyB, tmB], writes=[m1B[ob]])
                    S.emit(T_, lambda e, ob=ob: e.matmul(psm[:, 0:T], lhsT=onesf[:], rhs=cacc[:, ob, 0:T], start=(ob == 0), stop=(ob == 7)),
                           reads=[constB, arB[ob]], writes=[psmB])
                    S.emit(T_, lambda e, ob=ob, tq=tq: e.matmul(psq[:, 0:T], lhsT=onesf[:], rhs=tq[:, 0:T], start=(ob == 0), stop=(ob == 7)),
                           reads=[constB, tqB], writes=[psqB])
            mean, rstd = lnst[0], lnst[1]
            meanB, rstdB = lnstB
            S.emit(A, lambda e: e.activation(out=mean[:, 0:T], in_=psm[:, 0:T], func=AF.Identity, scale=1.0 / D), reads=[psmB], writes=[meanB])
            S.emit(V, lambda e: e.tensor_tensor(out=rstd[:, 0:T], in0=mean[:, 0:T], in1=mean[:, 0:T], op=ALU.mult), reads=[meanB], writes=[rstdB])
            S.emit(V, lambda e: e.scalar_tensor_tensor(out=rstd[:, 0:T], in0=psq[:, 0:T], scalar=1.0 / D, in1=rstd[:, 0:T], op0=ALU.mult, op1=ALU.subtract),
                   reads=[psqB, rstdB], writes=[rstdB])
            S.emit(A, lambda e: e.activation(out=rstd[:, 0:T], in_=rstd[:, 0:T], func=AF.Ln, bias=EPS), reads=[rstdB], writes=[rstdB])
            S.emit(A, lambda e: e.activation(out=rstd[:, 0:T], in_=rstd[:, 0:T], func=AF.Exp, scale=-0.5), reads=[rstdB], writes=[rstdB])
            nmr, nmrB = mean, meanB
            S.emit(V, lambda e: e.scalar_tensor_tensor(out=mean[:, 0:T], in0=mean[:, 0:T], scalar=-1.0, in1=rstd[:, 0:T], op0=ALU.mult, op1=ALU.mult),
                   reads=[meanB, rstdB], writes=[meanB])
            for hf in range(2):
                sg = next_w("in", O_GA + hf * 512, 512, 0, 8)
                for j in range(4):
                    kc = hf * 4 + j
                    psg, psgB = proj_F(lambda kk, j=j, sg=sg: slots[sg][:, kk, j * 128:(j + 1) * 128], [slotB[sg]], uT, [uTB], T)
                    S.emit(A, lambda e, psg=psg, kc=kc: e.activation(out=cin[:, kc, 30:30 + T], in_=psg[:, 0:T], func=AF.Sigmoid), reads=[psgB], writes=[cinB[kc]])
            for kc in range(8):
                tm, tmB = ntmp()
                S.emit(V, lambda e, kc=kc, tm=tm: e.tensor_tensor(out=tm[:, 0:T], in0=cacc[:, kc, 0:T], in1=rstd[:, 0:T], op=ALU.mult),
                       reads=[arB[kc], rstdB], writes=[tmB])
                S.emit(V, lambda e, tm=tm: e.tensor_tensor(out=tm[:, 0:T], in0=tm[:, 0:T], in1=nmr[:, 0:T], op=ALU.add), reads=[tmB, nmrB], writes=[tmB])
                S.emit(A, lambda e, kc=kc, tm=tm: e.activation(out=actT[:, kc, 0:T], in_=tm[:, 0:T], func=AF.Silu, scale=lngT[:, kc:kc + 1], bias=lnbT[:, kc:kc + 1]),
                       reads=[tmB, vecB], writes=[actTB[kc]])
            dbg_tap("cact", actT[:], actTB)
            for hf in range(2):
                sw = next_w("co", hf * 512, 512, 0, 8)
                for j in range(4):
                    ob = hf * 4 + j
                    psy, psyB = proj_F(lambda kk, j=j, sw=sw: slots[sw][:, kk, j * 128:(j + 1) * 128], [slotB[sw]], actT, actTB, T)
                    tm, tmB = ntmp()
                    S.emit(V, lambda e, psy=psy, tm=tm, ob=ob: e.tensor_tensor(out=tm[:, 0:T], in0=psy[:, 0:T], in1=cin[:, ob, 30:30 + T], op=ALU.mult),
                           reads=[psyB, cinB[ob]], writes=[tmB])
                    S.emit(V, lambda e, tm=tm, ob=ob: e.tensor_tensor(out=silur[:, ob, 0:T], in0=tm[:, 0:T], in1=m1[:, ob, 0:T], op=ALU.add),
                           reads=[tmB, m1B[ob]], writes=[silurB[ob]])
            dbg_tap("merged", silur[:], silurB)
            sws = [next_w("wo", 0, 512, 0, 8), next_w("wo", 512, 512, 0, 8)]

            def wo_sub(s):
                for hf in range(2):
                    sw = sws[hf]
                    ps, psB = nb()
                    for kc in range(8):
                        S.emit(T_, lambda e, kc=kc, ps=ps, sw=sw: e.matmul(ps[:], lhsT=silur[:, kc, s * 128:(s + 1) * 128], rhs=slots[sw][:, kc, :], start=(kc == 0), stop=(kc == 7)),
                               reads=[silurB[kc], slotB[sw]], writes=[psB])
                    xi = hf
                    S.dma(SY, xs[xi][:, 0:512], x[row0 + s * 128:row0 + (s + 1) * 128, hf * 512:(hf + 1) * 512], writes=[xsB[xi]], track=xsB[xi])
                    S.emit(V, lambda e, ps=ps, hf=hf, xi=xi: e.tensor_tensor(out=hbuf[:, s, hf * 512:(hf + 1) * 512], in0=ps[:], in1=xs[xi][:, 0:512], op=ALU.add),
                           reads=[psB, xsB[xi]], writes=[hB[s]])
            for s in range(NS):
                wo_sub(s)
                if s % 2 == 1 or s == NS - 1:
                    p0 = (s // 2) * 2
                    norm_a([(hbuf[:, q, :], hB[q]) for q in range(p0, s + 1)], pf_ums[p0:s + 1])
            dbg_tap("h1", hbuf[:], hB)
            for p0 in range(0, NS, 2):
                norm_b(pf_ums[p0:min(p0 + 2, NS)], g2T, p0)
            ngrp = 6
            for g in range(ngrp):
                nblk = 4 if g < 5 else 2
                sa = next_w("up", g * 512, nblk * 128, 0, 8)
                sbb = next_w("up", FFN + g * 512, nblk * 128, 0, 8)
                for j in range(nblk):
                    i = g * 4 + j
                    accs = []
                    for which, sl in ((0, sa), (1, sbb)):
                        blk = which * 22 + i
                        ps, psB = proj_F(lambda kk, j=j, sl=sl: slots[sl][:, kk, j * 128:(j + 1) * 128], [slotB[sl]], uT, [uTB], T)
                        zb, zbB = ntmp()
                        S.emit(A, lambda e, zb=zb, blk=blk: e.activation(out=zb[:, 0:2], in_=zhalo[:, blk, :], func=AF.Copy), reads=[zhB], writes=[zbB])
                        S.emit(A, lambda e, zb=zb, ps=ps: e.activation(out=zb[:, 2:2 + T], in_=ps[:, 0:T], func=AF.Copy), reads=[psB], writes=[zbB])
                        S.emit(A, lambda e, zb=zb, blk=blk: e.activation(out=zhalo[:, blk, :], in_=zb[:, T:T + 2], func=AF.Copy), reads=[zbB], writes=[zhB])
                        ac, acB = ntmp()
                        S.emit(A, lambda e, zb=zb, ac=ac, blk=blk: e.activation(out=ac[:, 0:T], in_=zb[:, 0:T], func=AF.Identity,
                                                                               scale=fwT[:, blk:blk + 1], bias=fbT[:, blk:blk + 1]),
                               reads=[zbB, vecB], writes=[acB])
                        for jj in (1, 2):
                            S.emit(V, lambda e, zb=zb, ac=ac, blk=blk, jj=jj: e.scalar_tensor_tensor(out=ac[:, 0:T], in0=zb[:, jj:jj + T], scalar=fwT[:, jj * 44 + blk:jj * 44 + blk + 1],
                                                                                                in1=ac[:, 0:T], op0=ALU.mult, op1=ALU.add),
                                   reads=[zbB, vecB, acB], writes=[acB])
                        accs.append((ac, acB))
                    if not is_halo:
                        (aa, aaB), (ab, abB) = accs
                        S.emit(A, lambda e, aa=aa: e.activation(out=aa[:, 0:T], in_=aa[:, 0:T], func=AF.Silu), reads=[aaB], writes=[aaB])
                        S.emit(V, lambda e, aa=aa, ab=ab, i=i: e.tensor_tensor(out=gT[:, i, 0:T], in0=aa[:, 0:T], in1=ab[:, 0:T], op=ALU.mult),
                               reads=[aaB, abB], writes=[arB[i % 8]])
            prefetch_next()
            if is_halo:
                S.emit(V, lambda e: e.tensor_scalar(out=zhalo[:], in0=zhalo[:], scalar1=hp[:, 0:1], scalar2=None, op0=ALU.mult), reads=[zhB, vecB], writes=[zhB])
                transposes_next()
                return
            dbg_tap("gT", gT, arB)
            for hf in range(2):
                pss = [nb() for _ in range(NS)]
                for g3 in range(3):
                    nk = 8 if g3 < 2 else 6
                    sw = next_w("dn", hf * 512, 512, g3 * 8, nk)
                    for s in range(NS):
                        ps, psB = pss[s]
                        for kk in range(nk):
                            i = g3 * 8 + kk
                            S.emit(T_, lambda e, ps=ps, s=s, kk=kk, i=i, sw=sw: e.matmul(ps[:], lhsT=gT[:, i, s * 128:(s + 1) * 128], rhs=slots[sw][:, kk, :],
                                                                                   start=(i == 0), stop=(i == 21)),
                                   reads=[arB[i % 8], slotB[sw]], writes=[psB])
                for s in range(NS):
                    ps, psB = pss[s]
                    S.emit(V, lambda e, ps=ps, s=s, hf=hf: e.tensor_tensor(out=hbuf[:, s, hf * 512:(hf + 1) * 512], in0=ps[:], in1=hbuf[:, s, hf * 512:(hf + 1) * 512], op=ALU.add),
                           reads=[psB, hB[s]], writes=[hB[s]])
            transposes_next()
            for s in range(NS):
                S.emit(A, lambda e, s=s: e.activation(out=on[:], in_=hbuf[:, s, :], func=AF.Square, accum_out=ss[:, s:s + 1]), reads=[hB[s]], writes=[onB, ssB])
            S.emit(A, lambda e: e.activation(out=ss[:, 4:8], in_=ss[:, 0:4], func=AF.Ln, scale=1.0 / D, bias=EPS), reads=[ssB], writes=[ssB])
            S.emit(A, lambda e: e.activation(out=ss[:, 4:8], in_=ss[:, 4:8], func=AF.Exp, scale=-0.5), reads=[ssB], writes=[ssB])
            for s in range(NS):
                xi = s % 2
                S.emit(V, lambda e, s=s, xi=xi: e.scalar_tensor_tensor(out=xs[xi][:], in0=hbuf[:, s, :], scalar=ss[:, 4 + s:5 + s], in1=fgB_t[:], op0=ALU.mult, op1=ALU.mult),
                       reads=[hB[s], ssB, vecB], writes=[xsB[xi]])
                S.dma(SY, out[out_row0 + s * 128:out_row0 + (s + 1) * 128, :], xs[xi][:], reads=[xsB[xi]], track=xsB[xi])

        main_tile(HIST, HALO, True, None, prefetched=False, next_row0=(HIST + HALO if n_main_tiles > 0 else None))
        for t in range(n_main_tiles):
            tapon[0] = (t == 0)
            main_tile(HIST + HALO + t * 512, 512, False, t * 512, prefetched=True,
                      next_row0=(HIST + HALO + (t + 1) * 512 if t + 1 < n_main_tiles else None))
        for i in range(2):
            S._wait(SY, ("d", xsB[i], xsB[i].dcount * 16))
        S.replay(block)
    return nc


def make_in_maps(inputs):
    x = np.asarray(inputs["x"], dtype=np.float32)
    sq = lambda k: np.ascontiguousarray(np.asarray(inputs[k], dtype=np.float32)[0])
    shared = {k: sq(k) for k in ("norm1_g", "w_in", "conv_dw_w", "conv_dw_b", "conv_ln_g", "conv_ln_b", "w_conv_out",
                                 "w_gate_up", "b_gate", "gla_norm_g", "w_gla_out", "w_o", "norm2_g", "w_ffn_up",
                                 "ffn_dw_w", "ffn_dw_b", "w_ffn_down")}
    shared["final_g"] = np.ascontiguousarray(np.asarray(inputs["final_g"], dtype=np.float32))
    in_maps = []
    for c in range(NCORES):
        b, r = divmod(c, 4)
        start = r * SEG
        xc = np.zeros((ROWS, D), np.float32)
        lo = start - (HIST + HALO)
        src_lo = max(lo, 0)
        xc[src_lo - lo:] = x[b, src_lo:start + SEG]
        m = dict(shared)
        m["x"] = xc
        m["hasprev"] = np.full((128, 1), 1.0 if r > 0 else 0.0, np.float32)
        in_maps.append(m)
    return in_maps


_NC_CACHE = {}


def kernel(**inputs):
    if "nc" not in _NC_CACHE:
        _NC_CACHE["nc"] = build_program()
    nc = _NC_CACHE["nc"]
    in_maps = make_in_maps(inputs)
    res = run_bass_kernel_spmd(nc, in_maps, core_ids=list(range(NCORES)))
    outp = np.zeros((2, 4 * SEG, D), np.float32)
    for c in range(NCORES):
        b, r = divmod(c, 4)
        outp[b, r * SEG:(r + 1) * SEG] = res.results[c]["out"]
    return outp
```

```python
import sys
import numpy as np
import concourse.bass as bass
import concourse.mybir as mybir
from concourse.bass_utils import run_bass_kernel_spmd
from contextlib import ExitStack

F32 = mybir.dt.float32
BF16 = mybir.dt.bfloat16
AF = mybir.ActivationFunctionType
ALU = mybir.AluOpType

D = 1024
IN_DIM = 7184
FFN = 2816
NCORES = 8
SEG = 2048
HIST = 6144
HALO = 128
ROWS = HIST + HALO + SEG
EPS = 1e-6
EPOCH = 30000

O_A, O_BG, O_Q, O_K, O_V, O_R, O_GL, O_GA, O_GB = 0, 1024, 2048, 2560, 3072, 4096, 5120, 5136, 6160


class Buf:
    __slots__ = ("name", "lastw", "readers", "sem", "dcount")

    def __init__(self, name):
        self.name = name
        self.lastw = None
        self.readers = {}
        self.sem = None
        self.dcount = 0


class Sched:
    ENGS = ("sync", "scalar", "vector", "gpsimd", "tensor")

    def __init__(self, nc, stack):
        self.nc = nc
        self.stack = stack
        self.ops = {e: [] for e in self.ENGS}
        self.count = {e: 0 for e in self.ENGS}
        self.sems = {e: [] for e in self.ENGS}
        self.waited = {e: {} for e in self.ENGS}
        self.same_engine_sync = {"scalar": True, "vector": True, "gpsimd": True, "tensor": False, "sync": False}

    def new_sem(self, name):
        return self.stack.enter_context(self.nc.semaphore(name))

    def eng_sem(self, e, epoch):
        while len(self.sems[e]) <= epoch:
            self.sems[e].append(self.new_sem(f"p_{e}_{len(self.sems[e])}"))
        return self.sems[e][epoch]

    def _wait(self, eng, tok):
        if tok[0] == "e":
            _, src, idx = tok
            if src == eng and not self.same_engine_sync[eng]:
                return
            epoch, val = divmod(idx - 1, EPOCH)
            val += 1
            key = ("e", src, epoch)
            sem = self.eng_sem(src, epoch)
        else:
            _, buf, val = tok
            key = ("d", id(buf))
            sem = buf.sem
        if self.waited[eng].get(key, 0) >= val:
            return
        self.waited[eng][key] = val
        self.ops[eng].append(lambda e, sem=sem, val=val: e.wait_ge(sem, val))

    def _deps(self, eng, reads, writes):
        for b in reads:
            if b.lastw is not None:
                self._wait(eng, b.lastw)
        for b in writes:
            if b.lastw is not None and not (b.lastw[0] == "e" and b.lastw[1] == eng):
                self._wait(eng, b.lastw)
            for t in b.readers.values():
                if not (t[0] == "e" and t[1] == eng):
                    self._wait(eng, t)

    def emit(self, eng, fn, reads=(), writes=()):
        self._deps(eng, reads, writes)
        self.count[eng] += 1
        idx = self.count[eng]
        sem = self.eng_sem(eng, (idx - 1) // EPOCH)
        self.ops[eng].append(lambda e, fn=fn, sem=sem: fn(e).then_inc(sem, 1))
        tok = ("e", eng, idx)
        for b in writes:
            b.lastw = tok
            b.readers = {}
        for b in reads:
            b.readers[eng] = tok
        return tok

    def dma(self, eng, out, in_, reads=(), writes=(), track=None, **kw):
        self._deps(eng, reads, writes)
        if track.sem is None:
            track.sem = self.new_sem("d_" + track.name)
        track.dcount += 1
        val = track.dcount * 16
        sem = track.sem
        self.ops[eng].append(
            lambda e, sem=sem, out=out, in_=in_, kw=kw: e.dma_start(out=out, in_=in_, **kw).then_inc(sem, 16))
        tok = ("d", track, val)
        for b in writes:
            b.lastw = tok
            b.readers = {}
        for b in reads:
            b.readers["dma_" + track.name] = tok
        return tok

    def replay(self, block):
        for e in self.ENGS:
            ops = self.ops[e]

            def body(engine, ops=ops):
                for f in ops:
                    f(engine)
            getattr(block, e)(body)


def build_program(n_hist_tiles=HIST // 512, n_main_tiles=SEG // 512, dbg=None):
    nc = bass.Bass("TRN2", target_bir_lowering=False)
    dt_in = lambda name, shape: nc.dram_tensor(name, shape, F32, kind="ExternalInput").ap()
    x = dt_in("x", [ROWS, D])
    hasprev = dt_in("hasprev", [128, 1])
    norm1_g = dt_in("norm1_g", [D]); w_in = dt_in("w_in", [D, IN_DIM])
    conv_dw_w = dt_in("conv_dw_w", [31, D]); conv_dw_b = dt_in("conv_dw_b", [D])
    conv_ln_g = dt_in("conv_ln_g", [D]); conv_ln_b = dt_in("conv_ln_b", [D])
    w_conv_out = dt_in("w_conv_out", [D, D]); w_gate_up = dt_in("w_gate_up", [16, 512])
    b_gate = dt_in("b_gate", [512]); gla_norm_g = dt_in("gla_norm_g", [256])
    w_gla_out = dt_in("w_gla_out", [D, D]); w_o = dt_in("w_o", [D, D])
    norm2_g = dt_in("norm2_g", [D]); w_ffn_up = dt_in("w_ffn_up", [D, 2 * FFN])
    ffn_dw_w = dt_in("ffn_dw_w", [3, 2 * FFN]); ffn_dw_b = dt_in("ffn_dw_b", [2 * FFN])
    w_ffn_down = dt_in("w_ffn_down", [FFN, D]); final_g = dt_in("final_g", [D])
    out = nc.dram_tensor("out", [SEG, D], F32, kind="ExternalOutput").ap()
    dgd = nc.dram_tensor("dgd", [24, 128, 11 * 128], BF16).ap()
    wb = {"in": nc.dram_tensor("wb_in", [D, IN_DIM], BF16).ap(), "co": nc.dram_tensor("wb_co", [D, D], BF16).ap(),
          "go": nc.dram_tensor("wb_go", [D, D], BF16).ap(), "wo": nc.dram_tensor("wb_wo", [D, D], BF16).ap(),
          "up": nc.dram_tensor("wb_up", [D, 2 * FFN], BF16).ap(), "dn": nc.dram_tensor("wb_dn", [FFN, D], BF16).ap()}
    dbg_out = {}
    if dbg:
        for name, shape in dbg.items():
            dbg_out[name] = nc.dram_tensor("dbg_" + name, list(shape), F32, kind="ExternalOutput").ap()

    with ExitStack() as stack:
        S = Sched(nc, stack)
        _n = [0]

        def sb(shape, dt, name=None):
            _n[0] += 1
            return stack.enter_context(nc.sbuf_tensor(name or f"t{_n[0]}", list(shape), dt))

        identf = sb([128, 128], F32); identfB = Buf("identf")
        identb = sb([128, 128], BF16); identbB = Buf("identb")
        tri_inc = sb([128, 128], F32); tri_end = sb([128, 128], F32); cmask = sb([128, 128], F32)
        chsel = sb([128, 2], F32); onesf = sb([128, 128], F32); ones_row = sb([1, 128], BF16)
        constB = Buf("consts")
        stage = sb([128, 128], F32); stageB = Buf("stage")
        g1T = sb([128, 8], F32); g2T = sb([128, 8], F32); cbT = sb([128, 8], F32)
        lngT = sb([128, 8], F32); lnbT = sb([128, 8], F32); fbT = sb([128, 44], F32)
        gnT = sb([128, 2], F32); cwT = sb([128, 248], F32); fwT = sb([128, 132], F32)
        fgB_t = sb([128, D], F32)
        bgrow = sb([1, 512], BF16); wgu = sb([16, 512], BF16)
        hp = sb([128, 1], F32)
        vecB = Buf("vecs")
        banks = [stack.enter_context(nc.psum_tensor(f"ps{i}", [128, 512], F32)) for i in range(6)]
        bankB = [Buf(f"ps{i}") for i in range(6)]
        pTs = [stack.enter_context(nc.psum_tensor(f"pT{i}", [128, 1024], BF16)) for i in range(2)]; pTBs = [Buf(f"pT{i}") for i in range(2)]
        _pt = [0]

        def npT():
            i = _pt[0] % 2
            _pt[0] += 1
            return pTs[i], pTBs[i]
        _bk = [0]

        def nb():
            i = _bk[0] % 6
            _bk[0] += 1
            return banks[i], bankB[i]

        NSLOT = 4
        slots = [sb([128, 8, 512], BF16, f"slot{i}") for i in range(NSLOT)]
        slotB = [Buf(f"slot{i}") for i in range(NSLOT)]
        xs = [sb([128, D], F32, f"xs{i}") for i in range(2)]; xsB = [Buf(f"xs{i}") for i in range(2)]
        utms = [sb([128, D], BF16, f"utm{i}") for i in range(2)]; utmBs = [Buf(f"utm{i}") for i in range(2)]
        uT = sb([128, 8, 512], BF16); uTB = Buf("uT")
        glowT = sb([128, 512], BF16); glowTB = Buf("glowT")
        lbuf = sb([128, 4, 512], F32); lB = [Buf(f"l{i}") for i in range(4)]
        NTMP = 6
        tmps = [sb([128, 516], F32, f"tmp{i}") for i in range(NTMP)]; tmpB = [Buf(f"tmp{i}") for i in range(NTMP)]
        _tk = [0]

        def ntmp():
            i = _tk[0] % NTMP
            _tk[0] += 1
            return tmps[i], tmpB[i]
        Epl = sb([128, 4, 512], BF16); Emi = sb([128, 4, 512], BF16); EB = [Buf(f"E{i}") for i in range(4)]
        decs = [sb([128, 4, 8], F32, f"dec{p}") for p in range(2)]; decBs = [[Buf(f"dec{p}_{i}") for i in range(4)] for p in range(2)]
        dec = decs[0]; decB = decBs[0]
        kend = sb([128, 4, 512], BF16); kendB = [Buf(f"kend{i}") for i in range(4)]
        vtm = sb([128, 4, D], BF16); vtmB = [Buf(f"v{i}") for i in range(4)]
        kT = sb([128, 4, 512], BF16); kTB = [Buf(f"kT{i}") for i in range(4)]
        qT = sb([128, 4, 512], BF16); qTB = [Buf(f"qT{i}") for i in range(4)]
        silur = sb([128, 8, 512], BF16); silurB = [Buf(f"sr{i}") for i in range(8)]
        attm = sb([128, 4, 128], BF16); attmB = [Buf(f"att{i}") for i in range(4)]
        Sst = sb([128, D], F32); SstB = Buf("S")
        Sbf = [sb([128, D], BF16, f"Sbf{i}") for i in range(2)]; SbfB = [Buf(f"Sbf{i}") for i in range(2)]
        on = sb([128, D], BF16); onB = Buf("on")
        oss = sb([128, 8], F32); ossB = Buf("oss")
        actT = sb([128, 8, 512], BF16); actTB = [Buf(f"actT{i}") for i in range(8)]
        cin = sb([128, 8, 542], BF16); cinB = [Buf(f"cin{i}") for i in range(8)]
        arena = sb([128, 22 * 256], F32, "arena")
        cacc = arena[:, 0:4096].rearrange("p (k t) -> p k t", k=8)
        gT = arena.bitcast(BF16).rearrange("p (k t) -> p k t", k=22)
        arB = [Buf(f"ar{i}") for i in range(8)]
        m1 = sb([128, 8, 512], BF16); m1B = [Buf(f"m1{i}") for i in range(8)]
        hbuf = sb([128, 4, D], F32); hB = [Buf(f"h{i}") for i in range(4)]
        zhalo = sb([128, 44, 2], F32); zhB = Buf("zhalo")
        ss = sb([128, 8], F32); ssB = Buf("ss")
        dg = [sb([128, 11, 128], BF16, f"dg{i}") for i in range(3)]; dgB = [Buf(f"dg{i}") for i in range(3)]
        dgdB = Buf("dgd")
        PARTS = ((0, 11), (11, 10), (21, 10))
        lnst = [sb([128, 512], F32, f"lnst{i}") for i in range(2)]; lnstB = [Buf(f"lnst{i}") for i in range(2)]

        block = stack.enter_context(nc.Block())
        V, A, P, T_, SY = "vector", "scalar", "gpsimd", "tensor", "sync"

        def iota_sel(t, pattern, cmp, fill, base, cm, src=None):
            S.emit(P, lambda e: e.affine_select(out=t, in_=(src if src is not None else t), pattern=pattern,
                                                compare_op=cmp, fill=fill, base=base, channel_multiplier=cm),
                   reads=[constB], writes=[constB])
        S.emit(P, lambda e: e.memset(identf[:], 0.0), writes=[constB])
        iota_sel(identf[:], [[-1, 128]], ALU.not_equal, 1.0, 0, 1)
        S.emit(V, lambda e: e.tensor_copy(out=identb[:], in_=identf[:]), reads=[constB], writes=[identbB])
        S.emit(P, lambda e: e.memset(onesf[:], 1.0), writes=[constB])
        S.emit(P, lambda e: e.memset(ones_row[:], 1.0), writes=[constB])
        for t, val in ((tri_inc, -1.0 / 16), (tri_end, -1.0 / 16), (cmask, 1.0)):
            S.emit(P, lambda e, t=t, val=val: e.memset(t[:], val), writes=[constB])
        iota_sel(tri_inc[:], [[1, 128]], ALU.is_ge, 0.0, 0, -1)
        iota_sel(cmask[:], [[1, 128]], ALU.is_ge, 0.0, 0, -1)
        iota_sel(tri_end[:], [[-1, 128]], ALU.is_gt, 0.0, 0, 1)
        S.emit(P, lambda e: e.memset(tri_inc[0:64, 64:128], 0.0), writes=[constB])
        S.emit(P, lambda e: e.memset(cmask[0:64, 64:128], 0.0), writes=[constB])
        S.emit(P, lambda e: e.memset(tri_end[64:128, 0:64], 0.0), writes=[constB])
        S.emit(P, lambda e: e.memset(chsel[:], 0.0), writes=[constB])
        S.emit(P, lambda e: e.memset(chsel[0:64, 0:1], -1.0 / 16), writes=[constB])
        S.emit(P, lambda e: e.memset(chsel[64:128, 1:2], -1.0 / 16), writes=[constB])
        S.emit(P, lambda e: e.memset(zhalo[:], 0.0), writes=[zhB])
        S.emit(P, lambda e: e.memset(Sst[:], 0.0), writes=[SstB])
        S.emit(P, lambda e: e.memset(cin[:], 0.0), writes=cinB)

        def load_cols(dst, rows_ap, nrows):
            r0 = 0
            while r0 < nrows:
                n = min(128, nrows - r0)
                S.dma(SY, stage[0:n, :], rows_ap[r0:r0 + n, :], writes=[stageB], track=stageB)
                ps, psB = nb()
                S.emit(T_, lambda e, ps=ps, n=n: e.matmul(ps[:, 0:n], lhsT=stage[0:n, :], rhs=identf[0:n, 0:n], start=True, stop=True),
                       reads=[stageB, constB], writes=[psB])
                S.emit(V, lambda e, ps=ps, n=n, r0=r0: e.tensor_copy(out=dst[:, r0:r0 + n], in_=ps[:, 0:n]), reads=[psB], writes=[vecB])
                r0 += n
        load_cols(g1T, norm1_g.rearrange("(k p) -> k p", p=128), 8)
        load_cols(g2T, norm2_g.rearrange("(k p) -> k p", p=128), 8)
        load_cols(cbT, conv_dw_b.rearrange("(k p) -> k p", p=128), 8)
        load_cols(lngT, conv_ln_g.rearrange("(k p) -> k p", p=128), 8)
        load_cols(lnbT, conv_ln_b.rearrange("(k p) -> k p", p=128), 8)
        load_cols(fbT, ffn_dw_b.rearrange("(k p) -> k p", p=128), 44)
        load_cols(gnT, gla_norm_g.rearrange("(k p) -> k p", p=128), 2)
        load_cols(cwT, conv_dw_w.rearrange("j (k p) -> (j k) p", p=128), 248)
        load_cols(fwT, ffn_dw_w.rearrange("j (k p) -> (j k) p", p=128), 132)
        rowst = xs[0]
        S.dma(SY, rowst[0:1, :], final_g.rearrange("(o n) -> o n", o=1), writes=[xsB[0]], track=xsB[0])
        for hf in range(2):
            ps, psB = nb()
            S.emit(T_, lambda e, ps=ps, hf=hf: e.matmul(ps[:], lhsT=onesf[0:1, :], rhs=rowst[0:1, hf * 512:(hf + 1) * 512], start=True, stop=True),
                   reads=[xsB[0], constB], writes=[psB])
            S.emit(V, lambda e, ps=ps, hf=hf: e.tensor_copy(out=fgB_t[:, hf * 512:(hf + 1) * 512], in_=ps[:]), reads=[psB], writes=[vecB])
        setup_toks = [vecB.lastw, constB.lastw, identbB.lastw]
        setup_toks.append(S.dma(P, bgrow[:], b_gate.rearrange("(o n) -> o n", o=1), writes=[vecB], track=Buf("bgrow")))
        setup_toks.append(S.dma(P, wgu[:], w_gate_up, writes=[vecB], track=Buf("wgu")))
        setup_toks.append(S.dma(SY, hp[:], hasprev, writes=[vecB], track=Buf("hp")))
        for eng in (A, V, T_, P):
            for tk in setup_toks:
                S._wait(eng, tk)
        vecB.lastw = None; vecB.readers = {}
        constB.lastw = None; constB.readers = {}

        tapon = [False]

        def dbg_tap(name, ap, bufs, force=False):
            if dbg and name in dbg and (tapon[0] or force):
                tb = Buf("dbg_" + name)
                S.dma(P, dbg_out[name], ap, reads=bufs, track=tb)
                S._wait(P, ("d", tb, tb.dcount * 16))

        def build_diag(g):
            kc, part = divmod(g, 3)
            j0, nj = PARTS[part]
            bi = g % 3
            for jj in range(nj):
                col = (j0 + jj) * 8 + kc
                S.emit(V, lambda e, bi=bi, jj=jj, col=col: e.tensor_scalar(out=dg[bi][:, jj, :], in0=identb[:], scalar1=cwT[:, col:col + 1], scalar2=None, op0=ALU.mult),
                       reads=[identbB], writes=[dgB[bi]])
            tok = S.dma(SY, dgd[g][:, 0:nj * 128], dg[bi][:, 0:nj, :].rearrange("p j i -> p (j i)"), reads=[dgB[bi]], track=dgdB)
            dgdB.lastw = tok
        _dgn = [0]

        def build_diags(n):
            for _ in range(n):
                if _dgn[0] < 24:
                    build_diag(_dgn[0])
                    _dgn[0] += 1

        def load_w(slot_i, src, c0, ncols, r0=0, nk=8):
            S.dma(P, slots[slot_i][:, 0:nk, 0:ncols],
                  src[r0 * 128:(r0 + nk) * 128, c0:c0 + ncols].rearrange("(k p) n -> p k n", p=128),
                  writes=[slotB[slot_i]], track=slotB[slot_i])
        wbB = Buf("wb")

        def load_wb(slot_i, src, c0, ncols, r0=0, nk=8):
            S.dma(P, slots[slot_i][:, 0:nk, 0:ncols],
                  src[r0 * 128:(r0 + nk) * 128, c0:c0 + ncols].rearrange("(k p) n -> p k n", p=128),
                  reads=[wbB], writes=[slotB[slot_i]], track=slotB[slot_i])

        def convert_weights():
            bounce = [(actT, actTB), (silur, silurB), (m1, m1B)]
            cvB = [Buf(f"cv{i}") for i in range(3)]
            blocks = []
            for c0 in range(0, IN_DIM - 16, 512):
                blocks.append(("in", 0, 8, c0, 512))
            blocks.append(("in", 0, 8, IN_DIM - 16, 16))
            for nm in ("co", "go", "wo"):
                for hf in range(2):
                    blocks.append((nm, 0, 8, hf * 512, 512))
            for c0 in range(0, 2 * FFN, 512):
                blocks.append(("up", 0, 8, c0, 512))
            for r0, nk in ((0, 8), (8, 8), (16, 6)):
                for hf in range(2):
                    blocks.append(("dn", r0, nk, hf * 512, 512))
            for bi, (nm, r0, nk, c0, ncols) in enumerate(blocks):
                bt, bB = bounce[bi % 3]
                srcap = WSRC[nm][r0 * 128:(r0 + nk) * 128, c0:c0 + ncols].rearrange("(k p) n -> p k n", p=128)
                dstap = wb[nm][r0 * 128:(r0 + nk) * 128, c0:c0 + ncols].rearrange("(k p) n -> p k n", p=128)
                S.dma(P, bt[:, 0:nk, 0:ncols], srcap, writes=bB, track=cvB[bi % 3])
                tok = S.dma(P, dstap, bt[:, 0:nk, 0:ncols], reads=bB, track=wbB)
                wbB.lastw = tok

        WSRC = {"in": w_in, "co": w_conv_out, "go": w_gla_out, "wo": w_o, "up": w_ffn_up, "dn": w_ffn_down}

        def tile_wseq(is_halo):
            q = [("in", O_GL, 512, 0, 8), ("in", O_K, 512, 0, 8), ("in", O_V, 512, 0, 8), ("in", O_V + 512, 512, 0, 8), ("in", O_Q, 512, 0, 8),
                 ("in", O_R, 512, 0, 8), ("in", O_R + 512, 512, 0, 8)]
            for hf in range(2):
                q += [("in", O_A + hf * 512, 512, 0, 8), ("in", O_BG + hf * 512, 512, 0, 8)]
            for hf in range(2):
                q += [("go", hf * 512, 512, 0, 8), ("in", O_GB + hf * 512, 512, 0, 8)]
            for hf in range(2):
                q += [("in", O_GA + hf * 512, 512, 0, 8)]
            for hf in range(2):
                q += [("co", hf * 512, 512, 0, 8)]
            for hf in range(2):
                q += [("wo", hf * 512, 512, 0, 8)]
            for g in range(6):
                nblk = 4 if g < 5 else 2
                q += [("up", g * 512, nblk * 128, 0, 8), ("up", FFN + g * 512, nblk * 128, 0, 8)]
            if not is_halo:
                for hf in range(2):
                    for g3 in range(3):
                        q += [("dn", hf * 512, 512, g3 * 8, 8 if g3 < 2 else 6)]
            return q
        WSEQ = tile_wseq(True)
        for _ in range(n_main_tiles):
            WSEQ += tile_wseq(False)
        _wi = [0, 0]

        def next_w(*key):
            i = _wi[0]
            assert WSEQ[i] == key, (i, WSEQ[i], key)
            while _wi[1] < len(WSEQ) and _wi[1] <= i + NSLOT - 2:
                k = WSEQ[_wi[1]]
                load_wb(_wi[1] % NSLOT, wb[k[0]], k[1], k[2], r0=k[3], nk=k[4])
                _wi[1] += 1
            _wi[0] += 1
            return i % NSLOT

        def norm_a(srcs, ums):
            n = len(srcs)
            for i, (ap, bf) in enumerate(srcs):
                um, umB = ums[i]
                S.emit(A, lambda e, ap=ap, i=i, um=um: e.activation(out=um, in_=ap, func=AF.Square, accum_out=ss[:, i:i + 1]),
                       reads=[bf], writes=list(umB) + [ssB])
            S.emit(A, lambda e: e.activation(out=ss[:, 4:4 + n], in_=ss[:, 0:n], func=AF.Ln, scale=1.0 / D, bias=EPS), reads=[ssB], writes=[ssB])
            S.emit(A, lambda e: e.activation(out=ss[:, 4:4 + n], in_=ss[:, 4:4 + n], func=AF.Exp, scale=-0.5), reads=[ssB], writes=[ssB])
            for i, (ap, bf) in enumerate(srcs):
                um, umB = ums[i]
                if i % 2 == 0:
                    S.emit(V, lambda e, ap=ap, i=i, um=um: e.tensor_scalar(out=um, in0=ap, scalar1=ss[:, 4 + i:5 + i], scalar2=None, op0=ALU.mult),
                           reads=[bf, ssB], writes=list(umB))
                else:
                    S.emit(A, lambda e, ap=ap, i=i, um=um: e.activation(out=um, in_=ap, func=AF.Identity, scale=ss[:, 4 + i:5 + i]),
                           reads=[bf, ssB], writes=list(umB))

        def norm_b(ums, gT_, s0):
            for i, (um, umB) in enumerate(ums):
                pT, pTB = npT()
                for kc in range(8):
                    S.emit(T_, lambda e, kc=kc, um=um, pT=pT: e.transpose(out=pT[:, kc * 128:(kc + 1) * 128], in_=um[:, kc * 128:(kc + 1) * 128], identity=identb[:]),
                           reads=list(umB) + [identbB], writes=[pTB])
                s = s0 + i
                for kc in range(8):
                    S.emit(V, lambda e, kc=kc, s=s, pT=pT: e.tensor_scalar(out=uT[:, kc, s * 128:(s + 1) * 128], in0=pT[:, kc * 128:(kc + 1) * 128],
                                                                         scalar1=gT_[:, kc:kc + 1], scalar2=None, op0=ALU.mult),
                           reads=[pTB, vecB], writes=[uTB])

        def um_std(i):
            return (utms[i][:], [utmBs[i]])

        def norm_stage(srcs, gT_, s0):
            for c in range(0, len(srcs), 2):
                ums = [um_std(i) for i in range(min(2, len(srcs) - c))]
                norm_a(srcs[c:c + 2], ums)
                norm_b(ums, gT_, s0 + c)

        def load_x(row0, i):
            S.dma(SY, xs[i][:], x[row0:row0 + 128, :], writes=[xsB[i]], track=xsB[i])

        def proj_T(slot_i, s, ncols=512):
            ps, psB = nb()
            for kc in range(8):
                S.emit(T_, lambda e, kc=kc, ps=ps: e.matmul(ps[:, 0:ncols], lhsT=uT[:, kc, s * 128:(s + 1) * 128], rhs=slots[slot_i][:, kc, 0:ncols],
                                                           start=(kc == 0), stop=(kc == 7)),
                       reads=[uTB, slotB[slot_i]], writes=[psB])
            return ps, psB

        def proj_F(w_ap_fn, wB, rhs, rhsB, T, nk=8, M=128):
            ps, psB = nb()
            for kc in range(nk):
                S.emit(T_, lambda e, kc=kc, ps=ps: e.matmul(ps[0:M, 0:T], lhsT=w_ap_fn(kc), rhs=rhs[:, kc, 0:T],
                                                           start=(kc == 0), stop=(kc == nk - 1)),
                       reads=list(wB) + list(rhsB), writes=[psB])
            return ps, psB

        def gate_stage(T, NS, main, gsl, p=0):
            gate_g1(T, NS, gsl)
            gate_g2(NS, main, p)

        def gate_g1(T, NS, gsl):
            ps, psB = proj_F(lambda kc: slots[gsl][:, kc, 0:128], [slotB[gsl]], uT, [uTB], T)
            S.emit(A, lambda e, ps=ps: e.activation(out=glowT[:, 0:T], in_=ps[:, 0:T], func=AF.Copy), reads=[psB], writes=[glowTB])
            dbg_tap("glowT", glowT[0:16, :], [glowTB]); dbg_tap("wgu", wgu[:], []); dbg_tap("bgrow", bgrow[:], [])
            pre = []
            for s in range(NS):
                ps, psB = nb()
                S.emit(T_, lambda e, ps=ps, s=s: e.matmul(ps[:], lhsT=glowT[0:16, s * 128:(s + 1) * 128], rhs=wgu[:], start=True, stop=False),
                       reads=[glowTB, vecB], writes=[psB])
                S.emit(T_, lambda e, ps=ps: e.matmul(ps[:], lhsT=ones_row[:], rhs=bgrow[:], start=False, stop=True),
                       reads=[constB, vecB], writes=[psB])
                pre.append((ps, psB))
            for s in range(NS):
                ps, psB = pre[s]
                tm, tmB = ntmp()
                S.emit(A, lambda e, ps=ps, tm=tm: e.activation(out=tm[:, 0:512], in_=ps[:], func=AF.Exp, scale=-1.0), reads=[psB], writes=[tmB])
                if s == 0:
                    dbg_tap("expn", tm[:, 0:512], [tmB])
                S.emit(A, lambda e, tm=tm, s=s: e.activation(out=lbuf[:, s, :], in_=tm[:, 0:512], func=AF.Ln, bias=1.0), reads=[tmB], writes=[lB[s]])

        def gate_g2(NS, main, p=0):
            dec, decB = decs[p], decBs[p]
            for s in range(NS):
                ps, psB = nb()
                for h in range(4):
                    S.emit(T_, lambda e, ps=ps, s=s, h=h: e.matmul(ps[:, h * 2:h * 2 + 2], lhsT=lbuf[:, s, h * 128:(h + 1) * 128], rhs=chsel[:], start=True, stop=True),
                           reads=[lB[s], constB], writes=[psB])
                S.emit(A, lambda e, ps=ps, s=s: e.activation(out=dec[:, s, :], in_=ps[:, 0:8], func=AF.Exp), reads=[psB], writes=[decB[s]])
                if main:
                    ps, psB = nb()
                    for h in range(4):
                        S.emit(T_, lambda e, ps=ps, s=s, h=h: e.matmul(ps[:, h * 128:(h + 1) * 128], lhsT=lbuf[:, s, h * 128:(h + 1) * 128], rhs=tri_inc[:], start=True, stop=True),
                               reads=[lB[s], constB], writes=[psB])
                    psv = ps[:].rearrange("p (h t) -> p h t", h=4)
                    S.emit(A, lambda e, psv=psv, s=s: e.activation(out=Epl[:, :, s * 128:(s + 1) * 128], in_=psv, func=AF.Exp), reads=[psB], writes=[EB[s]])
                    S.emit(A, lambda e, psv=psv, s=s: e.activation(out=Emi[:, :, s * 128:(s + 1) * 128], in_=psv, func=AF.Exp, scale=-1.0), reads=[psB], writes=[EB[s]])

        def kv_stage(NS, main, kslot_fn, vslot_fns, after_k=None):
            kslot = kslot_fn()
            for s in range(NS):
                ps, psB = nb()
                S.emit(T_, lambda e, ps=ps, s=s: e.matmul(ps[:], lhsT=tri_end[:], rhs=lbuf[:, s, :], start=True, stop=True),
                       reads=[lB[s], constB], writes=[psB])
                tm, tmB = ntmp()
                S.emit(A, lambda e, ps=ps, tm=tm: e.activation(out=tm[:, 0:512], in_=ps[:], func=AF.Exp), reads=[psB], writes=[tmB])
                ps, psB = proj_T(kslot, s)
                S.emit(V, lambda e, ps=ps, tm=tm, s=s: e.tensor_tensor(out=kend[:, s, :], in0=ps[:], in1=tm[:, 0:512], op=ALU.mult),
                       reads=[psB, tmB], writes=[kendB[s]])
            if after_k is not None:
                after_k(kslot)
            for hf in range(2):
                vs = vslot_fns[hf]()
                for s in range(NS):
                    ps, psB = proj_T(vs, s)
                    S.emit(A, lambda e, ps=ps, s=s, hf=hf: e.activation(out=vtm[:, s, hf * 512:(hf + 1) * 512], in_=ps[:], func=AF.Copy),
                           reads=[psB], writes=[vtmB[s]])

        def state_chunk(s, c, want_bf=None, p=0):
            dec, decB = decs[p], decBs[p]
            pk = [nb(), nb()]
            for h in range(4):
                ps, psB = pk[h // 2]
                S.emit(T_, lambda e, ps=ps, h=h: e.matmul(ps[:, (h % 2) * 256:(h % 2) * 256 + 256], lhsT=kend[c * 64:(c + 1) * 64, s, h * 128:(h + 1) * 128],
                                                         rhs=vtm[c * 64:(c + 1) * 64, s, h * 256:(h + 1) * 256], start=True, stop=True),
                       reads=[kendB[s], vtmB[s]], writes=[psB])
            for h in range(4):
                ps, psB = pk[h // 2]
                S.emit(V, lambda e, ps=ps, h=h: e.scalar_tensor_tensor(out=Sst[:, h * 256:(h + 1) * 256], in0=Sst[:, h * 256:(h + 1) * 256],
                                                                      scalar=dec[:, s, h * 2 + c:h * 2 + c + 1], in1=ps[:, (h % 2) * 256:(h % 2) * 256 + 256],
                                                                      op0=ALU.mult, op1=ALU.add),
                       reads=[SstB, decB[s], psB], writes=[SstB])
            if want_bf is not None:
                S.emit(A, lambda e: e.activation(out=Sbf[want_bf][:], in_=Sst[:], func=AF.Copy), reads=[SstB], writes=[SbfB[want_bf]])

        hist_slots = None
        if n_hist_tiles > 0 or True:
            load_w(0, w_in, O_K, 512); load_w(1, w_in, O_V, 512); load_w(2, w_in, O_V + 512, 512); load_w(3, w_in, O_GL, 512)
            convert_weights()
        hums = [um_std(0), um_std(1), (on[:], [onB]), (Sbf[1][:], [SbfB[1]])]

        def hist_na(t):
            for pr in range(2):
                for i in range(2):
                    load_x(t * 512 + (pr * 2 + i) * 128, i)
                norm_a([(xs[i][:], xsB[i]) for i in range(2)], hums[pr * 2:pr * 2 + 2])

        def hist_state(t):
            for s4 in range(4):
                state_chunk(s4, 0, p=t % 2)
                state_chunk(s4, 1, p=t % 2)

        if n_hist_tiles > 0:
            hist_na(0)
            norm_b(hums, g1T, 0)
            gate_stage(512, 4, False, 3, p=0)
        for t in range(n_hist_tiles):
            if t + 1 < n_hist_tiles:
                hist_na(t + 1)
            build_diags(2)
            kv_stage(4, False, lambda: 0, (lambda: 1, lambda: 2))
            if t + 1 < n_hist_tiles:
                norm_b(hums, g1T, 0)
                gate_g1(512, 4, 3)
                hist_state(t)
                gate_g2(4, False, (t + 1) % 2)
            else:
                hist_state(t)
        build_diags(24)
        S.emit(A, lambda e: e.activation(out=Sbf[0][:], in_=Sst[:], func=AF.Copy), reads=[SstB], writes=[SbfB[0]])
        sbf_cur = [0]

        dbg_tap("S_hist", Sst[:], [SstB], force=True)

        def main_tile(row0, T, is_halo, out_row0, prefetched=False, next_row0=None):
            NS = T // 128
            pf_ums = [um_std(0), um_std(1),
                      (actT[:, 0:2, :].rearrange("p k t -> p (k t)"), actTB[0:2]), (actT[:, 2:4, :].rearrange("p k t -> p (k t)"), actTB[2:4])]
            if not prefetched:
                for pr in range((NS + 1) // 2):
                    nn = min(2, NS - pr * 2)
                    ums = [um_std(i) for i in range(nn)]
                    for i in range(nn):
                        load_x(row0 + (pr * 2 + i) * 128, i)
                    norm_a([(xs[i][:], xsB[i]) for i in range(nn)], ums)
                    norm_b(ums, g1T, pr * 2)

            def transposes_next():
                if next_row0 is not None:
                    norm_b(pf_ums[0:2], g1T, 0)
                    norm_b(pf_ums[2:4], g1T, 2)

            def prefetch_next():
                if next_row0 is not None:
                    for pr in range(2):
                        for i in range(2):
                            load_x(next_row0 + (pr * 2 + i) * 128, i)
                        norm_a([(xs[i][:], xsB[i]) for i in range(2)], pf_ums[pr * 2:pr * 2 + 2])
            dbg_tap("uT", uT[:], [uTB])
            gate_stage(T, NS, True, next_w("in", O_GL, 512, 0, 8))
            def k_feature_major(ks):
                for h in range(4):
                    ps, psB = proj_F(lambda kc, h=h: slots[ks][:, kc, h * 128:(h + 1) * 128], [slotB[ks]], uT, [uTB], T)
                    S.emit(V, lambda e, ps=ps, h=h: e.tensor_tensor(out=kT[:, h, 0:T], in0=ps[:, 0:T], in1=Emi[:, h, 0:T], op=ALU.mult),
                           reads=[psB] + EB[0:NS], writes=[kTB[h]])
            kv_stage(NS, True, lambda: next_w("in", O_K, 512, 0, 8),
                     (lambda: next_w("in", O_V, 512, 0, 8), lambda: next_w("in", O_V + 512, 512, 0, 8)), after_k=k_feature_major)
            qs = next_w("in", O_Q, 512, 0, 8)
            for h in range(4):
                ps, psB = proj_F(lambda kc, h=h: slots[qs][:, kc, h * 128:(h + 1) * 128], [slotB[qs]], uT, [uTB], T)
                S.emit(V, lambda e, ps=ps, h=h: e.scalar_tensor_tensor(out=qT[:, h, 0:T], in0=ps[:, 0:T], scalar=128.0 ** -0.5, in1=Epl[:, h, 0:T],
                                                                      op0=ALU.mult, op1=ALU.mult),
                       reads=[psB] + EB[0:NS], writes=[qTB[h]])
            for hf in range(2):
                rs = next_w("in", O_R + hf * 512, 512, 0, 8)
                for j in range(4):
                    kc = hf * 4 + j
                    ps, psB = proj_F(lambda kk, j=j, rs=rs: slots[rs][:, kk, j * 128:(j + 1) * 128], [slotB[rs]], uT, [uTB], T)
                    S.emit(A, lambda e, ps=ps, kc=kc: e.activation(out=silur[:, kc, 0:T], in_=ps[:, 0:T], func=AF.Silu), reads=[psB], writes=[silurB[kc]])
            dbg_tap("l", lbuf[:], lB); dbg_tap("kT", kT[:], kTB); dbg_tap("qT", qT[:], qTB); dbg_tap("vtm", vtm[:], vtmB)
            dbg_tap("kend", kend[:], kendB); dbg_tap("silur", silur[:], silurB); dbg_tap("dec", decs[0][:], decBs[0])
            for hf in range(2):
                sa = next_w("in", O_A + hf * 512, 512, 0, 8)
                sg = next_w("in", O_BG + hf * 512, 512, 0, 8)
                for j in range(4):
                    kc = hf * 4 + j
                    psa, psaB = proj_F(lambda kk, j=j, sa=sa: slots[sa][:, kk, j * 128:(j + 1) * 128], [slotB[sa]], uT, [uTB], T)
                    psg, psgB = proj_F(lambda kk, j=j, sg=sg: slots[sg][:, kk, j * 128:(j + 1) * 128], [slotB[sg]], uT, [uTB], T)
                    tm, tmB = ntmp()
                    S.emit(A, lambda e, psg=psg, tm=tm: e.activation(out=tm[:, 0:T], in_=psg[:, 0:T], func=AF.Sigmoid), reads=[psgB], writes=[tmB])
                    S.emit(V, lambda e, psa=psa, tm=tm, kc=kc: e.tensor_tensor(out=cin[:, kc, 30:30 + T], in0=psa[:, 0:T], in1=tm[:, 0:T], op=ALU.mult),
                           reads=[psaB, tmB], writes=[cinB[kc]])
            dgtrk = [Buf(f"dgl{i}") for i in range(3)] if not hasattr(main_tile, "_dgtrk") else main_tile._dgtrk
            main_tile._dgtrk = dgtrk

            def load_part(g):
                if g >= 24:
                    return
                j0, nj = PARTS[g % 3]
                bi = g % 3
                S.dma(SY, dg[bi][:, 0:nj, :], dgd[g][:, 0:nj * 128].rearrange("p (j i) -> p j i", i=128), reads=[dgdB], writes=[dgB[bi]], track=dgtrk[bi])

            def conv_block(kc):
                psc, pscB = nb()
                for part in range(3):
                    g = kc * 3 + part
                    j0, nj = PARTS[part]
                    bi = g % 3
                    load_part(g + 2)
                    for jj in range(nj):
                        j = j0 + jj
                        S.emit(T_, lambda e, bi=bi, jj=jj, j=j: e.matmul(psc[:, 0:T], lhsT=dg[bi][:, jj, :], rhs=cin[:, kc, j:j + T], start=(j == 0), stop=(j == 30)),
                               reads=[dgB[bi], cinB[kc]], writes=[pscB])
                S.emit(A, lambda e: e.activation(out=cacc[:, kc, 0:T], in_=psc[:, 0:T], func=AF.Identity, bias=cbT[:, kc:kc + 1]),
                       reads=[pscB], writes=[arB[kc]])
                S.emit(A, lambda e: e.activation(out=cin[:, kc, 0:30], in_=cin[:, kc, T:T + 30], func=AF.Copy), reads=[cinB[kc]], writes=[cinB[kc]])

            gla_po = {}

            def gla_a(s):
                tok = slice(s * 128, (s + 1) * 128)
                for h in range(4):
                    ps, psB = nb()
                    S.emit(T_, lambda e, ps=ps, h=h: e.matmul(ps[:, 0:128], lhsT=kT[:, h, tok], rhs=qT[:, h, tok], start=True, stop=True),
                           reads=[kTB[h], qTB[h]], writes=[psB])
                    S.emit(V, lambda e, ps=ps, h=h: e.tensor_tensor(out=attm[:, h, :], in0=ps[:, 0:128], in1=cmask[:], op=ALU.mult),
                           reads=[psB, constB], writes=[attmB[h]])
                c0 = sbf_cur[0]; c1 = 1 - c0
                state_chunk(s, 0, want_bf=c1)

            def gla_b(s):
                c0 = sbf_cur[0]; c1 = 1 - c0
                po = [nb(), nb()]
                gla_po[s] = po
                for h in range(4):
                    ps, psB = po[h // 2]
                    cols = slice((h % 2) * 256, (h % 2) * 256 + 256)
                    hc = slice(h * 256, (h + 1) * 256)
                    S.emit(T_, lambda e, ps=ps, h=h, cols=cols, hc=hc: e.matmul(ps[:, cols], lhsT=attm[:, h, :], rhs=vtm[:, s, hc], start=True, stop=False),
                           reads=[attmB[h], vtmB[s]], writes=[psB])
                    S.emit(T_, lambda e, ps=ps, h=h, cols=cols, hc=hc: e.matmul(ps[0:64, cols], lhsT=qT[:, h, s * 128:s * 128 + 64], rhs=Sbf[c0][:, hc], start=False, stop=False),
                           reads=[qTB[h], SbfB[c0]], writes=[psB])
                    S.emit(T_, lambda e, ps=ps, h=h, cols=cols, hc=hc: e.matmul(ps[64:128, cols], lhsT=qT[:, h, s * 128 + 64:s * 128 + 128], rhs=Sbf[c1][:, hc], start=False, stop=True,
                                                                             tile_position=(0, 64)),
                           reads=[qTB[h], SbfB[c1]], writes=[psB])
                state_chunk(s, 1, want_bf=c0)

            def gla_c(s):
                tok = slice(s * 128, (s + 1) * 128)
                po = gla_po[s]
                for h in range(4):
                    ps, psB = po[h // 2]
                    cols = slice((h % 2) * 256, (h % 2) * 256 + 256)
                    S.emit(A, lambda e, ps=ps, h=h, cols=cols: e.activation(out=on[:, h * 256:(h + 1) * 256], in_=ps[:, cols], func=AF.Square, accum_out=oss[:, h:h + 1]),
                           reads=[psB], writes=[onB, ossB])
                S.emit(A, lambda e: e.activation(out=oss[:, 4:8], in_=oss[:, 0:4], func=AF.Ln, scale=1.0 / 256, bias=EPS), reads=[ossB], writes=[ossB])
                S.emit(A, lambda e: e.activation(out=oss[:, 4:8], in_=oss[:, 4:8], func=AF.Exp, scale=-0.5), reads=[ossB], writes=[ossB])
                for h in range(4):
                    ps, psB = po[h // 2]
                    cols = slice((h % 2) * 256, (h % 2) * 256 + 256)
                    S.emit(A, lambda e, ps=ps, h=h, cols=cols: e.activation(out=on[:, h * 256:(h + 1) * 256], in_=ps[:, cols], func=AF.Identity, scale=oss[:, 4 + h:5 + h]),
                           reads=[psB, ossB], writes=[onB])

            def gla_c2(s):
                tok = slice(s * 128, (s + 1) * 128)
                pT, pTB = npT()
                for kc in range(8):
                    S.emit(T_, lambda e, kc=kc, pT=pT: e.transpose(out=pT[:, kc * 128:(kc + 1) * 128], in_=on[:, kc * 128:(kc + 1) * 128], identity=identb[:]),
                           reads=[onB, identbB], writes=[pTB])
                for kc in range(8):
                    S.emit(V, lambda e, kc=kc, pT=pT: e.scalar_tensor_tensor(out=actT[:, kc, tok], in0=pT[:, kc * 128:(kc + 1) * 128], scalar=gnT[:, kc % 2:kc % 2 + 1],
                                                                      in1=silur[:, kc, tok], op0=ALU.mult, op1=ALU.mult),
                           reads=[pTB, vecB, silurB[kc]], writes=[actTB[kc]])
            load_part(0)
            load_part(1)
            kc_next = 0
            for s in range(NS):
                gla_a(s)
                conv_block(kc_next); kc_next += 1
                gla_b(s)
                gla_c(s)
                conv_block(kc_next); kc_next += 1
                gla_c2(s)
            while kc_next < 8:
                conv_block(kc_next); kc_next += 1
            dbg_tap("cin", cin[:, :, 30:542], cinB); dbg_tap("cacc", cacc, arB); dbg_tap("oT", actT[:], actTB)
            psm = pTs[0].bitcast(F32); psmB = pTBs[0]
            psq = pTs[1].bitcast(F32); psqB = pTBs[1]
            for hf in range(2):
                sw = next_w("go", hf * 512, 512, 0, 8)
                sg = next_w("in", O_GB + hf * 512, 512, 0, 8)
                for j in range(4):
                    ob = hf * 4 + j
                    psy, psyB = proj_F(lambda kk, j=j, sw=sw: slots[sw][:, kk, j * 128:(j + 1) * 128], [slotB[sw]], actT, actTB, T)
                    psg, psgB = proj_F(lambda kk, j=j, sg=sg: slots[sg][:, kk, j * 128:(j + 1) * 128], [slotB[sg]], uT, [uTB], T)
                    tq, tqB = ntmp()
                    S.emit(A, lambda e, ob=ob, tq=tq: e.activation(out=tq[:, 0:T], in_=cacc[:, ob, 0:T], func=AF.Square), reads=[arB[ob]], writes=[tqB])
                    tm, tmB = ntmp()
                    S.emit(A, lambda e, psg=psg, tm=tm: e.activation(out=tm[:, 0:T], in_=psg[:, 0:T], func=AF.Sigmoid), reads=[psgB], writes=[tmB])
                    S.emit(V, lambda e, psy=psy, tm=tm, ob=ob: e.tensor_tensor(out=m1[:, ob, 0:T], in0=psy[:, 0:T], in1=tm[:, 0:T], op=ALU.mult),
                           reads=[psyB, tmB], writes=[m1B[ob]])
                    S.emit(T_, lambda e, ob=ob: e.matmul(psm[:, 0:T], lhsT=onesf[:], rhs=cacc[:, ob, 0:T], start=(ob == 0), stop=(ob == 7)),
                           reads=[constB, arB[ob]], writes=[psmB])
                    S.emit(T_, lambda e, ob=ob, tq=tq: e.matmul(psq[:, 0:T], lhsT=onesf[:], rhs=tq[:, 0:T], start=(ob == 0), stop=(ob == 7)),
                           reads=[constB, tqB], writes=[psqB])
            mean, rstd = lnst[0], lnst[1]
            meanB, rstdB = lnstB
            S.emit(A, lambda e: e.activation(out=mean[:, 0:T], in_=psm[:, 0:T], func=AF.Identity, scale=1.0 / D), reads=[psmB], writes=[meanB])
            S.emit(V, lambda e: e.tensor_tensor(out=rstd[:, 0:T], in0=mean[:, 0:T], in1=mean[:, 0:T], op=ALU.mult), reads=[meanB], writes=[rstdB])
            S.emit(V, lambda e: e.scalar_tensor_tensor(out=rstd[:, 0:T], in0=psq[:, 0:T], scalar=1.0 / D, in1=rstd[:, 0:T], op0=ALU.mult, op1=ALU.subtract),
                   reads=[psqB, rstdB], writes=[rstdB])
            S.emit(A, lambda e: e.activation(out=rstd[:, 0:T], in_=rstd[:, 0:T], func=AF.Ln, bias=EPS), reads=[rstdB], writes=[rstdB])
            S.emit(A, lambda e: e.activation(out=rstd[:, 0:T], in_=rstd[:, 0:T], func=AF.Exp, scale=-0.5), reads=[rstdB], writes=[rstdB])
            nmr, nmrB = mean, meanB
            S.emit(V, lambda e: e.scalar_tensor_tensor(out=mean[:, 0:T], in0=mean[:, 0:T], scalar=-1.0, in1=rstd[:, 0:T], op0=ALU.mult, op1=ALU.mult),
                   reads=[meanB, rstdB], writes=[meanB])
            for hf in range(2):
                sg = next_w("in", O_GA + hf * 512, 512, 0, 8)
                for j in range(4):
                    kc = hf * 4 + j
                    psg, psgB = proj_F(lambda kk, j=j, sg=sg: slots[sg][:, kk, j * 128:(j + 1) * 128], [slotB[sg]], uT, [uTB], T)
                    S.emit(A, lambda e, psg=psg, kc=kc: e.activation(out=cin[:, kc, 30:30 + T], in_=psg[:, 0:T], func=AF.Sigmoid), reads=[psgB], writes=[cinB[kc]])
            for kc in range(8):
                tm, tmB = ntmp()
                S.emit(V, lambda e, kc=kc, tm=tm: e.tensor_tensor(out=tm[:, 0:T], in0=cacc[:, kc, 0:T], in1=rstd[:, 0:T], op=ALU.mult),
                       reads=[arB[kc], rstdB], writes=[tmB])
                S.emit(V, lambda e, tm=tm: e.tensor_tensor(out=tm[:, 0:T], in0=tm[:, 0:T], in1=nmr[:, 0:T], op=ALU.add), reads=[tmB, nmrB], writes=[tmB])
                S.emit(A, lambda e, kc=kc, tm=tm: e.activation(out=actT[:, kc, 0:T], in_=tm[:, 0:T], func=AF.Silu, scale=lngT[:, kc:kc + 1], bias=lnbT[:, kc:kc + 1]),
                       reads=[tmB, vecB], writes=[actTB[kc]])
            dbg_tap("cact", actT[:], actTB)
            for hf in range(2):
                sw = next_w("co", hf * 512, 512, 0, 8)
                for j in range(4):
                    ob = hf * 4 + j
                    psy, psyB = proj_F(lambda kk, j=j, sw=sw: slots[sw][:, kk, j * 128:(j + 1) * 128], [slotB[sw]], actT, actTB, T)
                    tm, tmB = ntmp()
                    S.emit(V, lambda e, psy=psy, tm=tm, ob=ob: e.tensor_tensor(out=tm[:, 0:T], in0=psy[:, 0:T], in1=cin[:, ob, 30:30 + T], op=ALU.mult),
                           reads=[psyB, cinB[ob]], writes=[tmB])
                    S.emit(V, lambda e, tm=tm, ob=ob: e.tensor_tensor(out=silur[:, ob, 0:T], in0=tm[:, 0:T], in1=m1[:, ob, 0:T], op=ALU.add),
                           reads=[tmB, m1B[ob]], writes=[silurB[ob]])
            dbg_tap("merged", silur[:], silurB)
            sws = [next_w("wo", 0, 512, 0, 8), next_w("wo", 512, 512, 0, 8)]

            def wo_sub(s):
                for hf in range(2):
                    sw = sws[hf]
                    ps, psB = nb()
                    for kc in range(8):
                        S.emit(T_, lambda e, kc=kc, ps=ps, sw=sw: e.matmul(ps[:], lhsT=silur[:, kc, s * 128:(s + 1) * 128], rhs=slots[sw][:, kc, :], start=(kc == 0), stop=(kc == 7)),
                               reads=[silurB[kc], slotB[sw]], writes=[psB])
                    xi = hf
                    S.dma(SY, xs[xi][:, 0:512], x[row0 + s * 128:row0 + (s + 1) * 128, hf * 512:(hf + 1) * 512], writes=[xsB[xi]], track=xsB[xi])
                    S.emit(V, lambda e, ps=ps, hf=hf, xi=xi: e.tensor_tensor(out=hbuf[:, s, hf * 512:(hf + 1) * 512], in0=ps[:], in1=xs[xi][:, 0:512], op=ALU.add),
                           reads=[psB, xsB[xi]], writes=[hB[s]])
            for s in range(NS):
                wo_sub(s)
                if s % 2 == 1 or s == NS - 1:
                    p0 = (s // 2) * 2
                    norm_a([(hbuf[:, q, :], hB[q]) for q in range(p0, s + 1)], pf_ums[p0:s + 1])
            dbg_tap("h1", hbuf[:], hB)
            for p0 in range(0, NS, 2):
                norm_b(pf_ums[p0:min(p0 + 2, NS)], g2T, p0)
            ngrp = 6
            for g in range(ngrp):
                nblk = 4 if g < 5 else 2
                sa = next_w("up", g * 512, nblk * 128, 0, 8)
                sbb = next_w("up", FFN + g * 512, nblk * 128, 0, 8)
                for j in range(nblk):
                    i = g * 4 + j
                    accs = []
                    for which, sl in ((0, sa), (1, sbb)):
                        blk = which * 22 + i
                        ps, psB = proj_F(lambda kk, j=j, sl=sl: slots[sl][:, kk, j * 128:(j + 1) * 128], [slotB[sl]], uT, [uTB], T)
                        zb, zbB = ntmp()
                        S.emit(A, lambda e, zb=zb, blk=blk: e.activation(out=zb[:, 0:2], in_=zhalo[:, blk, :], func=AF.Copy), reads=[zhB], writes=[zbB])
                        S.emit(A, lambda e, zb=zb, ps=ps: e.activation(out=zb[:, 2:2 + T], in_=ps[:, 0:T], func=AF.Copy), reads=[psB], writes=[zbB])
                        S.emit(A, lambda e, zb=zb, blk=blk: e.activation(out=zhalo[:, blk, :], in_=zb[:, T:T + 2], func=AF.Copy), reads=[zbB], writes=[zhB])
                        ac, acB = ntmp()
                        S.emit(A, lambda e, zb=zb, ac=ac, blk=blk: e.activation(out=ac[:, 0:T], in_=zb[:, 0:T], func=AF.Identity,
                                                                               scale=fwT[:, blk:blk + 1], bias=fbT[:, blk:blk + 1]),
                               reads=[zbB, vecB], writes=[acB])
                        for jj in (1, 2):
                            S.emit(V, lambda e, zb=zb, ac=ac, blk=blk, jj=jj: e.scalar_tensor_tensor(out=ac[:, 0:T], in0=zb[:, jj:jj + T], scalar=fwT[:, jj * 44 + blk:jj * 44 + blk + 1],
                                                                                                in1=ac[:, 0:T], op0=ALU.mult, op1=ALU.add),
                                   reads=[zbB, vecB, acB], writes=[acB])
                        accs.append((ac, acB))
                    if not is_halo:
                        (aa, aaB), (ab, abB) = accs
                        S.emit(A, lambda e, aa=aa: e.activation(out=aa[:, 0:T], in_=aa[:, 0:T], func=AF.Silu), reads=[aaB], writes=[aaB])
                        S.emit(V, lambda e, aa=aa, ab=ab, i=i: e.tensor_tensor(out=gT[:, i, 0:T], in0=aa[:, 0:T], in1=ab[:, 0:T], op=ALU.mult),
                               reads=[aaB, abB], writes=[arB[i % 8]])
            prefetch_next()
            if is_halo:
                S.emit(V, lambda e: e.tensor_scalar(out=zhalo[:], in0=zhalo[:], scalar1=hp[:, 0:1], scalar2=None, op0=ALU.mult), reads=[zhB, vecB], writes=[zhB])
                transposes_next()
                return
            dbg_tap("gT", gT, arB)
            for hf in range(2):
                pss = [nb() for _ in range(NS)]
                for g3 in range(3):
                    nk = 8 if g3 < 2 else 6
                    sw = next_w("dn", hf * 512, 512, g3 * 8, nk)
                    for s in range(NS):
                        ps, psB = pss[s]
                        for kk in range(nk):
                            i = g3 * 8 + kk
                            S.emit(T_, lambda e, ps=ps, s=s, kk=kk, i=i, sw=sw: e.matmul(ps[:], lhsT=gT[:, i, s * 128:(s + 1) * 128], rhs=slots[sw][:, kk, :],
                                                                                   start=(i == 0), stop=(i == 21)),
                                   reads=[arB[i % 8], slotB[sw]], writes=[psB])
                for s in range(NS):
                    ps, psB = pss[s]
                    S.emit(V, lambda e, ps=ps, s=s, hf=hf: e.tensor_tensor(out=hbuf[:, s, hf * 512:(hf + 1) * 512], in0=ps[:], in1=hbuf[:, s, hf * 512:(hf + 1) * 512], op=ALU.add),
                           reads=[psB, hB[s]], writes=[hB[s]])
            transposes_next()
            for s in range(NS):
                S.emit(A, lambda e, s=s: e.activation(out=on[:], in_=hbuf[:, s, :], func=AF.Square, accum_out=ss[:, s:s + 1]), reads=[hB[s]], writes=[onB, ssB])
            S.emit(A, lambda e: e.activation(out=ss[:, 4:8], in_=ss[:, 0:4], func=AF.Ln, scale=1.0 / D, bias=EPS), reads=[ssB], writes=[ssB])
            S.emit(A, lambda e: e.activation(out=ss[:, 4:8], in_=ss[:, 4:8], func=AF.Exp, scale=-0.5), reads=[ssB], writes=[ssB])
            for s in range(NS):
                xi = s % 2
                S.emit(V, lambda e, s=s, xi=xi: e.scalar_tensor_tensor(out=xs[xi][:], in0=hbuf[:, s, :], scalar=ss[:, 4 + s:5 + s], in1=fgB_t[:], op0=ALU.mult, op1=ALU.mult),
                       reads=[hB[s], ssB, vecB], writes=[xsB[xi]])
                S.dma(SY, out[out_row0 + s * 128:out_row0 + (s + 1) * 128, :], xs[xi][:], reads=[xsB[xi]], track=xsB[xi])

        main_tile(HIST, HALO, True, None, prefetched=False, next_row0=(HIST + HALO if n_main_tiles > 0 else None))
        for t in range(n_main_tiles):
            tapon[0] = (t == 0)
            main_tile(HIST + HALO + t * 512, 512, False, t * 512, prefetched=True,
                      next_row0=(HIST + HALO + (t + 1) * 512 if t + 1 < n_main_tiles else None))
        for i in range(2):
            S._wait(SY, ("d", xsB[i], xsB[i].dcount * 16))
        S.replay(block)
    return nc


def make_in_maps(inputs):
    x = np.asarray(inputs["x"], dtype=np.float32)
    sq = lambda k: np.ascontiguousarray(np.asarray(inputs[k], dtype=np.float32)[0])
    shared = {k: sq(k) for k in ("norm1_g", "w_in", "conv_dw_w", "conv_dw_b", "conv_ln_g", "conv_ln_b", "w_conv_out",
                                 "w_gate_up", "b_gate", "gla_norm_g", "w_gla_out", "w_o", "norm2_g", "w_ffn_up",
                                 "ffn_dw_w", "ffn_dw_b", "w_ffn_down")}
    shared["final_g"] = np.ascontiguousarray(np.asarray(inputs["final_g"], dtype=np.float32))
    in_maps = []
    for c in range(NCORES):
        b, r = divmod(c, 4)
        start = r * SEG
        xc = np.zeros((ROWS, D), np.float32)
        lo = start - (HIST + HALO)
        src_lo = max(lo, 0)
        xc[src_lo - lo:] = x[b, src_lo:start + SEG]
        m = dict(shared)
        m["x"] = xc
        m["hasprev"] = np.full((128, 1), 1.0 if r > 0 else 0.0, np.float32)
        in_maps.append(m)
    return in_maps


_NC_CACHE = {}


def kernel(**inputs):
    if "nc" not in _NC_CACHE:
        _NC_CACHE["nc"] = build_program()
    nc = _NC_CACHE["nc"]
    in_maps = make_in_maps(inputs)
    res = run_bass_kernel_spmd(nc, in_maps, core_ids=list(range(NCORES)))
    outp = np.zeros((2, 4 * SEG, D), np.float32)
    for c in range(NCORES):
        b, r = divmod(c, 4)
        outp[b, r * SEG:(r + 1) * SEG] = res.results[c]["out"]
    return outp
```

```python
import sys
import numpy as np
import concourse.bass as bass
import concourse.mybir as mybir
from concourse.bass_utils import run_bass_kernel_spmd
from contextlib import ExitStack

F32 = mybir.dt.float32
BF16 = mybir.dt.bfloat16
AF = mybir.ActivationFunctionType
ALU = mybir.AluOpType

D = 1024
IN_DIM = 7184
FFN = 2816
NCORES = 8
SEG = 2048
HIST = 6144
HALO = 128
ROWS = HIST + HALO + SEG
EPS = 1e-6
EPOCH = 30000

O_A, O_BG, O_Q, O_K, O_V, O_R, O_GL, O_GA, O_GB = 0, 1024, 2048, 2560, 3072, 4096, 5120, 5136, 6160


class Buf:
    __slots__ = ("name", "lastw", "readers", "sem", "dcount")

    def __init__(self, name):
        self.name = name
        self.lastw = None
        self.readers = {}
        self.sem = None
        self.dcount = 0


class Sched:
    ENGS = ("sync", "scalar", "vector", "gpsimd", "tensor")

    def __init__(self, nc, stack):
        self.nc = nc
        self.stack = stack
        self.ops = {e: [] for e in self.ENGS}
        self.count = {e: 0 for e in self.ENGS}
        self.sems = {e: [] for e in self.ENGS}
        self.waited = {e: {} for e in self.ENGS}
        self.same_engine_sync = {"scalar": True, "vector": True, "gpsimd": True, "tensor": False, "sync": False}

    def new_sem(self, name):
        return self.stack.enter_context(self.nc.semaphore(name))

    def eng_sem(self, e, epoch):
        while len(self.sems[e]) <= epoch:
            self.sems[e].append(self.new_sem(f"p_{e}_{len(self.sems[e])}"))
        return self.sems[e][epoch]

    def _wait(self, eng, tok):
        if tok[0] == "e":
            _, src, idx = tok
            if src == eng and not self.same_engine_sync[eng]:
                return
            epoch, val = divmod(idx - 1, EPOCH)
            val += 1
            key = ("e", src, epoch)
            sem = self.eng_sem(src, epoch)
        else:
            _, buf, val = tok
            key = ("d", id(buf))
            sem = buf.sem
        if self.waited[eng].get(key, 0) >= val:
            return
        self.waited[eng][key] = val
        self.ops[eng].append(lambda e, sem=sem, val=val: e.wait_ge(sem, val))

    def _deps(self, eng, reads, writes):
        for b in reads:
            if b.lastw is not None:
                self._wait(eng, b.lastw)
        for b in writes:
            if b.lastw is not None and not (b.lastw[0] == "e" and b.lastw[1] == eng):
                self._wait(eng, b.lastw)
            for t in b.readers.values():
                if not (t[0] == "e" and t[1] == eng):
                    self._wait(eng, t)

    def emit(self, eng, fn, reads=(), writes=()):
        self._deps(eng, reads, writes)
        self.count[eng] += 1
        idx = self.count[eng]
        sem = self.eng_sem(eng, (idx - 1) // EPOCH)
        self.ops[eng].append(lambda e, fn=fn, sem=sem: fn(e).then_inc(sem, 1))
        tok = ("e", eng, idx)
        for b in writes:
            b.lastw = tok
            b.readers = {}
        for b in reads:
            b.readers[eng] = tok
        return tok

    def dma(self, eng, out, in_, reads=(), writes=(), track=None, **kw):
        self._deps(eng, reads, writes)
        if track.sem is None:
            track.sem = self.new_sem("d_" + track.name)
        track.dcount += 1
        val = track.dcount * 16
        sem = track.sem
        self.ops[eng].append(
            lambda e, sem=sem, out=out, in_=in_, kw=kw: e.dma_start(out=out, in_=in_, **kw).then_inc(sem, 16))
        tok = ("d", track, val)
        for b in writes:
            b.lastw = tok
            b.readers = {}
        for b in reads:
            b.readers["dma_" + track.name] = tok
        return tok

    def replay(self, block):
        for e in self.ENGS:
            ops = self.ops[e]

            def body(engine, ops=ops):
                for f in ops:
                    f(engine)
            getattr(block, e)(body)


def build_program(n_hist_tiles=HIST // 512, n_main_tiles=SEG // 512, dbg=None):
    nc = bass.Bass("TRN2", target_bir_lowering=False)
    dt_in = lambda name, shape: nc.dram_tensor(name, shape, F32, kind="ExternalInput").ap()
    x = dt_in("x", [ROWS, D])
    hasprev = dt_in("hasprev", [128, 1])
    norm1_g = dt_in("norm1_g", [D]); w_in = dt_in("w_in", [D, IN_DIM])
    conv_dw_w = dt_in("conv_dw_w", [31, D]); conv_dw_b = dt_in("conv_dw_b", [D])
    conv_ln_g = dt_in("conv_ln_g", [D]); conv_ln_b = dt_in("conv_ln_b", [D])
    w_conv_out = dt_in("w_conv_out", [D, D]); w_gate_up = dt_in("w_gate_up", [16, 512])
    b_gate = dt_in("b_gate", [512]); gla_norm_g = dt_in("gla_norm_g", [256])
    w_gla_out = dt_in("w_gla_out", [D, D]); w_o = dt_in("w_o", [D, D])
    norm2_g = dt_in("norm2_g", [D]); w_ffn_up = dt_in("w_ffn_up", [D, 2 * FFN])
    ffn_dw_w = dt_in("ffn_dw_w", [3, 2 * FFN]); ffn_dw_b = dt_in("ffn_dw_b", [2 * FFN])
    w_ffn_down = dt_in("w_ffn_down", [FFN, D]); final_g = dt_in("final_g", [D])
    out = nc.dram_tensor("out", [SEG, D], F32, kind="ExternalOutput").ap()
    dgd = nc.dram_tensor("dgd", [24, 128, 11 * 128], BF16).ap()
    wb = {"in": nc.dram_tensor("wb_in", [D, IN_DIM], BF16).ap(), "co": nc.dram_tensor("wb_co", [D, D], BF16).ap(),
          "go": nc.dram_tensor("wb_go", [D, D], BF16).ap(), "wo": nc.dram_tensor("wb_wo", [D, D], BF16).ap(),
          "up": nc.dram_tensor("wb_up", [D, 2 * FFN], BF16).ap(), "dn": nc.dram_tensor("wb_dn", [FFN, D], BF16).ap()}
    dbg_out = {}
    if dbg:
        for name, shape in dbg.items():
            dbg_out[name] = nc.dram_tensor("dbg_" + name, list(shape), F32, kind="ExternalOutput").ap()

    with ExitStack() as stack:
        S = Sched(nc, stack)
        _n = [0]

        def sb(shape, dt, name=None):
            _n[0] += 1
            return stack.enter_context(nc.sbuf_tensor(name or f"t{_n[0]}", list(shape), dt))

        identf = sb([128, 128], F32); identfB = Buf("identf")
        identb = sb([128, 128], BF16); identbB = Buf("identb")
        tri_inc = sb([128, 128], F32); tri_end = sb([128, 128], F32); cmask = sb([128, 128], F32)
        chsel = sb([128, 2], F32); onesf = sb([128, 128], F32); ones_row = sb([1, 128], BF16)
        constB = Buf("consts")
        tri_end_h = sb([128, 128], F32); negc = sb([128, 1], F32)
        stage = sb([128, 128], F32); stageB = Buf("stage")
        g1T = sb([128, 8], F32); g2T = sb([128, 8], F32); cbT = sb([128, 8], F32)
        lngT = sb([128, 8], F32); lnbT = sb([128, 8], F32); fbT = sb([128, 44], F32)
        gnT = sb([128, 2], F32); cwT = sb([128, 248], F32); fwT = sb([128, 132], F32)
        fgB_t = sb([128, D], F32)
        bgrow = sb([1, 512], BF16); wgu = sb([16, 512], BF16)
        hp = sb([128, 1], F32)
        vecB = Buf("vecs")
        banks = [stack.enter_context(nc.psum_tensor(f"ps{i}", [128, 512], F32)) for i in range(6)]
        bankB = [Buf(f"ps{i}") for i in range(6)]
        pTs = [stack.enter_context(nc.psum_tensor(f"pT{i}", [128, 1024], BF16)) for i in range(2)]; pTBs = [Buf(f"pT{i}") for i in range(2)]
        _pt = [0]

        def npT():
            i = _pt[0] % 2
            _pt[0] += 1
            return pTs[i], pTBs[i]
        _bk = [0]

        def nb():
            i = _bk[0] % 6
            _bk[0] += 1
            return banks[i], bankB[i]

        NSLOT = 4
        slots = [sb([128, 8, 512], BF16, f"slot{i}") for i in range(NSLOT)]
        slotB = [Buf(f"slot{i}") for i in range(NSLOT)]
        xs = [sb([128, D], F32, f"xs{i}") for i in range(2)]; xsB = [Buf(f"xs{i}") for i in range(2)]
        utms = [sb([128, D], BF16, f"utm{i}") for i in range(2)]; utmBs = [Buf(f"utm{i}") for i in range(2)]
        uT = sb([128, 8, 512], BF16); uTB = Buf("uT")
        glowT = sb([128, 512], BF16); glowTB = Buf("glowT")
        lbuf = sb([128, 4, 512], F32); lB = [Buf(f"l{i}") for i in range(4)]
        NTMP = 6
        tmps = [sb([128, 516], F32, f"tmp{i}") for i in range(NTMP)]; tmpB = [Buf(f"tmp{i}") for i in range(NTMP)]
        _tk = [0]

        def ntmp():
            i = _tk[0] % NTMP
            _tk[0] += 1
            return tmps[i], tmpB[i]
        Epl = sb([128, 4, 512], BF16); Emi = sb([128, 4, 512], BF16); EB = [Buf(f"E{i}") for i in range(4)]
        decs = [sb([128, 4, 8], F32, f"dec{p}") for p in range(2)]; decBs = [[Buf(f"dec{p}_{i}") for i in range(4)] for p in range(2)]
        dec = decs[0]; decB = decBs[0]
        kend = sb([128, 4, 512], BF16); kendB = [Buf(f"kend{i}") for i in range(4)]
        vtm = sb([128, 4, D], BF16); vtmB = [Buf(f"v{i}") for i in range(4)]
        kT = sb([128, 4, 512], BF16); kTB = [Buf(f"kT{i}") for i in range(4)]
        qT = sb([128, 4, 512], BF16); qTB = [Buf(f"qT{i}") for i in range(4)]
        silur = sb([128, 8, 512], BF16); silurB = [Buf(f"sr{i}") for i in range(8)]
        attm = sb([128, 4, 128], BF16); attmB = [Buf(f"att{i}") for i in range(4)]
        Sst = sb([128, D], F32); SstB = Buf("S")
        Sbf = [sb([128, D], BF16, f"Sbf{i}") for i in range(2)]; SbfB = [Buf(f"Sbf{i}") for i in range(2)]
        on = sb([128, D], BF16); onB = Buf("on")
        oss = sb([128, 8], F32); ossB = Buf("oss")
        actT = sb([128, 8, 512], BF16); actTB = [Buf(f"actT{i}") for i in range(8)]
        cin = sb([128, 8, 542], BF16); cinB = [Buf(f"cin{i}") for i in range(8)]
        arena = sb([128, 22 * 256], F32, "arena")
        cacc = arena[:, 0:4096].rearrange("p (k t) -> p k t", k=8)
        gT = arena.bitcast(BF16).rearrange("p (k t) -> p k t", k=22)
        arB = [Buf(f"ar{i}") for i in range(8)]
        m1 = sb([128, 8, 512], BF16); m1B = [Buf(f"m1{i}") for i in range(8)]
        hbuf = sb([128, 4, D], F32); hB = [Buf(f"h{i}") for i in range(4)]
        zhalo = sb([128, 44, 2], F32); zhB = Buf("zhalo")
        ss = sb([128, 8], F32); ssB = Buf("ss")
        dg = [sb([128, 11, 128], BF16, f"dg{i}") for i in range(3)]; dgB = [Buf(f"dg{i}") for i in range(3)]
        dgdB = Buf("dgd")
        PARTS = ((0, 11), (11, 10), (21, 10))
        lnst = [sb([128, 512], F32, f"lnst{i}") for i in range(2)]; lnstB = [Buf(f"lnst{i}") for i in range(2)]

        block = stack.enter_context(nc.Block())
        V, A, P, T_, SY = "vector", "scalar", "gpsimd", "tensor", "sync"

        def iota_sel(t, pattern, cmp, fill, base, cm, src=None):
            S.emit(P, lambda e: e.affine_select(out=t, in_=(src if src is not None else t), pattern=pattern,
                                                compare_op=cmp, fill=fill, base=base, channel_multiplier=cm),
                   reads=[constB], writes=[constB])
        S.emit(P, lambda e: e.memset(identf[:], 0.0), writes=[constB])
        iota_sel(identf[:], [[-1, 128]], ALU.not_equal, 1.0, 0, 1)
        S.emit(V, lambda e: e.tensor_copy(out=identb[:], in_=identf[:]), reads=[constB], writes=[identbB])
        S.emit(P, lambda e: e.memset(onesf[:], 1.0), writes=[constB])
        S.emit(P, lambda e: e.memset(ones_row[:], 1.0), writes=[constB])
        for t, val in ((tri_inc, -1.0 / 16), (tri_end, -1.0 / 16), (cmask, 1.0)):
            S.emit(P, lambda e, t=t, val=val: e.memset(t[:], val), writes=[constB])
        iota_sel(tri_inc[:], [[1, 128]], ALU.is_ge, 0.0, 0, -1)
        iota_sel(cmask[:], [[1, 128]], ALU.is_ge, 0.0, 0, -1)
        iota_sel(tri_end[:], [[-1, 128]], ALU.is_gt, 0.0, 0, 1)
        S.emit(P, lambda e: e.memset(tri_inc[0:64, 64:128], 0.0), writes=[constB])
        S.emit(P, lambda e: e.memset(cmask[0:64, 64:128], 0.0), writes=[constB])
        S.emit(P, lambda e: e.memset(tri_end[64:128, 0:64], 0.0), writes=[constB])
        S.emit(P, lambda e: e.memset(tri_end_h[:], -1.0 / 16), writes=[constB])
        iota_sel(tri_end_h[:], [[-1, 128]], ALU.is_gt, 0.0, 0, 1)
        S.emit(P, lambda e: e.memset(negc[:], -1.0 / 16), writes=[constB])
        S.emit(P, lambda e: e.memset(chsel[:], 0.0), writes=[constB])
        S.emit(P, lambda e: e.memset(chsel[0:64, 0:1], -1.0 / 16), writes=[constB])
        S.emit(P, lambda e: e.memset(chsel[64:128, 1:2], -1.0 / 16), writes=[constB])
        S.emit(P, lambda e: e.memset(zhalo[:], 0.0), writes=[zhB])
        S.emit(P, lambda e: e.memset(Sst[:], 0.0), writes=[SstB])
        S.emit(P, lambda e: e.memset(cin[:], 0.0), writes=cinB)

        def load_cols(dst, rows_ap, nrows):
            r0 = 0
            while r0 < nrows:
                n = min(128, nrows - r0)
                S.dma(SY, stage[0:n, :], rows_ap[r0:r0 + n, :], writes=[stageB], track=stageB)
                ps, psB = nb()
                S.emit(T_, lambda e, ps=ps, n=n: e.matmul(ps[:, 0:n], lhsT=stage[0:n, :], rhs=identf[0:n, 0:n], start=True, stop=True),
                       reads=[stageB, constB], writes=[psB])
                S.emit(V, lambda e, ps=ps, n=n, r0=r0: e.tensor_copy(out=dst[:, r0:r0 + n], in_=ps[:, 0:n]), reads=[psB], writes=[vecB])
                r0 += n
        load_cols(g1T, norm1_g.rearrange("(k p) -> k p", p=128), 8)
        load_cols(g2T, norm2_g.rearrange("(k p) -> k p", p=128), 8)
        load_cols(cbT, conv_dw_b.rearrange("(k p) -> k p", p=128), 8)
        load_cols(lngT, conv_ln_g.rearrange("(k p) -> k p", p=128), 8)
        load_cols(lnbT, conv_ln_b.rearrange("(k p) -> k p", p=128), 8)
        load_cols(fbT, ffn_dw_b.rearrange("(k p) -> k p", p=128), 44)
        load_cols(gnT, gla_norm_g.rearrange("(k p) -> k p", p=128), 2)
        load_cols(cwT, conv_dw_w.rearrange("j (k p) -> (j k) p", p=128), 248)
        load_cols(fwT, ffn_dw_w.rearrange("j (k p) -> (j k) p", p=128), 132)
        rowst = xs[0]
        S.dma(SY, rowst[0:1, :], final_g.rearrange("(o n) -> o n", o=1), writes=[xsB[0]], track=xsB[0])
        for hf in range(2):
            ps, psB = nb()
            S.emit(T_, lambda e, ps=ps, hf=hf: e.matmul(ps[:], lhsT=onesf[0:1, :], rhs=rowst[0:1, hf * 512:(hf + 1) * 512], start=True, stop=True),
                   reads=[xsB[0], constB], writes=[psB])
            S.emit(V, lambda e, ps=ps, hf=hf: e.tensor_copy(out=fgB_t[:, hf * 512:(hf + 1) * 512], in_=ps[:]), reads=[psB], writes=[vecB])
        setup_toks = [vecB.lastw, constB.lastw, identbB.lastw]
        setup_toks.append(S.dma(P, bgrow[:], b_gate.rearrange("(o n) -> o n", o=1), writes=[vecB], track=Buf("bgrow")))
        setup_toks.append(S.dma(P, wgu[:], w_gate_up, writes=[vecB], track=Buf("wgu")))
        setup_toks.append(S.dma(SY, hp[:], hasprev, writes=[vecB], track=Buf("hp")))
        for eng in (A, V, T_, P):
            for tk in setup_toks:
                S._wait(eng, tk)
        vecB.lastw = None; vecB.readers = {}
        constB.lastw = None; constB.readers = {}

        tapon = [False]

        def dbg_tap(name, ap, bufs, force=False):
            if dbg and name in dbg and (tapon[0] or force):
                tb = Buf("dbg_" + name)
                S.dma(P, dbg_out[name], ap, reads=bufs, track=tb)
                S._wait(P, ("d", tb, tb.dcount * 16))

        def build_diag(g):
            kc, part = divmod(g, 3)
            j0, nj = PARTS[part]
            bi = g % 3
            for jj in range(nj):
                col = (j0 + jj) * 8 + kc
                S.emit(V, lambda e, bi=bi, jj=jj, col=col: e.tensor_scalar(out=dg[bi][:, jj, :], in0=identb[:], scalar1=cwT[:, col:col + 1], scalar2=None, op0=ALU.mult),
                       reads=[identbB], writes=[dgB[bi]])
            tok = S.dma(SY, dgd[g][:, 0:nj * 128], dg[bi][:, 0:nj, :].rearrange("p j i -> p (j i)"), reads=[dgB[bi]], track=dgdB)
            dgdB.lastw = tok
        _dgn = [0]

        def build_diags(n):
            for _ in range(n):
                if _dgn[0] < 24:
                    build_diag(_dgn[0])
                    _dgn[0] += 1

        def load_w(slot_i, src, c0, ncols, r0=0, nk=8):
            S.dma(P, slots[slot_i][:, 0:nk, 0:ncols],
                  src[r0 * 128:(r0 + nk) * 128, c0:c0 + ncols].rearrange("(k p) n -> p k n", p=128),
                  writes=[slotB[slot_i]], track=slotB[slot_i])
        wbB = Buf("wb")

        def load_wb(slot_i, src, c0, ncols, r0=0, nk=8):
            S.dma(P, slots[slot_i][:, 0:nk, 0:ncols],
                  src[r0 * 128:(r0 + nk) * 128, c0:c0 + ncols].rearrange("(k p) n -> p k n", p=128),
                  reads=[wbB], writes=[slotB[slot_i]], track=slotB[slot_i])

        def convert_weights():
            bounce = [(actT, actTB), (silur, silurB), (m1, m1B)]
            cvB = [Buf(f"cv{i}") for i in range(3)]
            blocks = []
            for c0 in range(0, IN_DIM - 16, 512):
                blocks.append(("in", 0, 8, c0, 512))
            blocks.append(("in", 0, 8, IN_DIM - 16, 16))
            for nm in ("co", "go", "wo"):
                for hf in range(2):
                    blocks.append((nm, 0, 8, hf * 512, 512))
            for c0 in range(0, 2 * FFN, 512):
                blocks.append(("up", 0, 8, c0, 512))
            for r0, nk in ((0, 8), (8, 8), (16, 6)):
                for hf in range(2):
                    blocks.append(("dn", r0, nk, hf * 512, 512))
            for bi, (nm, r0, nk, c0, ncols) in enumerate(blocks):
                bt, bB = bounce[bi % 3]
                srcap = WSRC[nm][r0 * 128:(r0 + nk) * 128, c0:c0 + ncols].rearrange("(k p) n -> p k n", p=128)
                dstap = wb[nm][r0 * 128:(r0 + nk) * 128, c0:c0 + ncols].rearrange("(k p) n -> p k n", p=128)
                S.dma(P, bt[:, 0:nk, 0:ncols], srcap, writes=bB, track=cvB[bi % 3])
                tok = S.dma(P, dstap, bt[:, 0:nk, 0:ncols], reads=bB, track=wbB)
                wbB.lastw = tok

        WSRC = {"in": w_in, "co": w_conv_out, "go": w_gla_out, "wo": w_o, "up": w_ffn_up, "dn": w_ffn_down}

        def tile_wseq(is_halo):
            q = [("in", O_GL, 512, 0, 8), ("in", O_K, 512, 0, 8), ("in", O_V, 512, 0, 8), ("in", O_V + 512, 512, 0, 8), ("in", O_Q, 512, 0, 8),
                 ("in", O_R, 512, 0, 8), ("in", O_R + 512, 512, 0, 8)]
            for hf in range(2):
                q += [("in", O_A + hf * 512, 512, 0, 8), ("in", O_BG + hf * 512, 512, 0, 8)]
            for hf in range(2):
                q += [("go", hf * 512, 512, 0, 8), ("in", O_GB + hf * 512, 512, 0, 8)]
            for hf in range(2):
                q += [("in", O_GA + hf * 512, 512, 0, 8)]
            for hf in range(2):
                q += [("co", hf * 512, 512, 0, 8)]
            for hf in range(2):
                q += [("wo", hf * 512, 512, 0, 8)]
            for g in range(6):
                nblk = 4 if g < 5 else 2
                q += [("up", g * 512, nblk * 128, 0, 8), ("up", FFN + g * 512, nblk * 128, 0, 8)]
            if not is_halo:
                for hf in range(2):
                    for g3 in range(3):
                        q += [("dn", hf * 512, 512, g3 * 8, 8 if g3 < 2 else 6)]
            return q
        WSEQ = tile_wseq(True)
        for _ in range(n_main_tiles):
            WSEQ += tile_wseq(False)
        _wi = [0, 0]

        def next_w(*key):
            i = _wi[0]
            assert WSEQ[i] == key, (i, WSEQ[i], key)
            while _wi[1] < len(WSEQ) and _wi[1] <= i + NSLOT - 2:
                k = WSEQ[_wi[1]]
                load_wb(_wi[1] % NSLOT, wb[k[0]], k[1], k[2], r0=k[3], nk=k[4])
                _wi[1] += 1
            _wi[0] += 1
            return i % NSLOT

        def norm_a(srcs, ums):
            n = len(srcs)
            for i, (ap, bf) in enumerate(srcs):
                um, umB = ums[i]
                S.emit(A, lambda e, ap=ap, i=i, um=um: e.activation(out=um, in_=ap, func=AF.Square, accum_out=ss[:, i:i + 1]),
                       reads=[bf], writes=list(umB) + [ssB])
            S.emit(A, lambda e: e.activation(out=ss[:, 4:4 + n], in_=ss[:, 0:n], func=AF.Ln, scale=1.0 / D, bias=EPS), reads=[ssB], writes=[ssB])
            S.emit(A, lambda e: e.activation(out=ss[:, 4:4 + n], in_=ss[:, 4:4 + n], func=AF.Exp, scale=-0.5), reads=[ssB], writes=[ssB])
            for i, (ap, bf) in enumerate(srcs):
                um, umB = ums[i]
                if i % 2 == 0:
                    S.emit(V, lambda e, ap=ap, i=i, um=um: e.tensor_scalar(out=um, in0=ap, scalar1=ss[:, 4 + i:5 + i], scalar2=None, op0=ALU.mult),
                           reads=[bf, ssB], writes=list(umB))
                else:
                    S.emit(A, lambda e, ap=ap, i=i, um=um: e.activation(out=um, in_=ap, func=AF.Identity, scale=ss[:, 4 + i:5 + i]),
                           reads=[bf, ssB], writes=list(umB))

        def norm_b(ums, gT_, s0):
            for i, (um, umB) in enumerate(ums):
                pT, pTB = npT()
                for kc in range(8):
                    S.emit(T_, lambda e, kc=kc, um=um, pT=pT: e.transpose(out=pT[:, kc * 128:(kc + 1) * 128], in_=um[:, kc * 128:(kc + 1) * 128], identity=identb[:]),
                           reads=list(umB) + [identbB], writes=[pTB])
                s = s0 + i
                for kc in range(8):
                    S.emit(V, lambda e, kc=kc, s=s, pT=pT: e.tensor_scalar(out=uT[:, kc, s * 128:(s + 1) * 128], in0=pT[:, kc * 128:(kc + 1) * 128],
                                                                         scalar1=gT_[:, kc:kc + 1], scalar2=None, op0=ALU.mult),
                           reads=[pTB, vecB], writes=[uTB])

        def um_std(i):
            return (utms[i][:], [utmBs[i]])

        def norm_stage(srcs, gT_, s0):
            for c in range(0, len(srcs), 2):
                ums = [um_std(i) for i in range(min(2, len(srcs) - c))]
                norm_a(srcs[c:c + 2], ums)
                norm_b(ums, gT_, s0 + c)

        def load_x(row0, i):
            S.dma(SY, xs[i][:], x[row0:row0 + 128, :], writes=[xsB[i]], track=xsB[i])

        def proj_T(slot_i, s, ncols=512):
            ps, psB = nb()
            for kc in range(8):
                S.emit(T_, lambda e, kc=kc, ps=ps: e.matmul(ps[:, 0:ncols], lhsT=uT[:, kc, s * 128:(s + 1) * 128], rhs=slots[slot_i][:, kc, 0:ncols],
                                                           start=(kc == 0), stop=(kc == 7)),
                       reads=[uTB, slotB[slot_i]], writes=[psB])
            return ps, psB

        def proj_F(w_ap_fn, wB, rhs, rhsB, T, nk=8, M=128):
            ps, psB = nb()
            for kc in range(nk):
                S.emit(T_, lambda e, kc=kc, ps=ps: e.matmul(ps[0:M, 0:T], lhsT=w_ap_fn(kc), rhs=rhs[:, kc, 0:T],
                                                           start=(kc == 0), stop=(kc == nk - 1)),
                       reads=list(wB) + list(rhsB), writes=[psB])
            return ps, psB

        def gate_stage(T, NS, main, gsl, p=0):
            gate_g1(T, NS, gsl)
            gate_g2(NS, main, p)

        def gate_g1(T, NS, gsl):
            ps, psB = proj_F(lambda kc: slots[gsl][:, kc, 0:128], [slotB[gsl]], uT, [uTB], T)
            S.emit(A, lambda e, ps=ps: e.activation(out=glowT[:, 0:T], in_=ps[:, 0:T], func=AF.Copy), reads=[psB], writes=[glowTB])
            dbg_tap("glowT", glowT[0:16, :], [glowTB]); dbg_tap("wgu", wgu[:], []); dbg_tap("bgrow", bgrow[:], [])
            pre = []
            for s in range(NS):
                ps, psB = nb()
                S.emit(T_, lambda e, ps=ps, s=s: e.matmul(ps[:], lhsT=glowT[0:16, s * 128:(s + 1) * 128], rhs=wgu[:], start=True, stop=False),
                       reads=[glowTB, vecB], writes=[psB])
                S.emit(T_, lambda e, ps=ps: e.matmul(ps[:], lhsT=ones_row[:], rhs=bgrow[:], start=False, stop=True),
                       reads=[constB, vecB], writes=[psB])
                pre.append((ps, psB))
            for s in range(NS):
                ps, psB = pre[s]
                tm, tmB = ntmp()
                S.emit(A, lambda e, ps=ps, tm=tm: e.activation(out=tm[:, 0:512], in_=ps[:], func=AF.Exp, scale=-1.0), reads=[psB], writes=[tmB])
                if s == 0:
                    dbg_tap("expn", tm[:, 0:512], [tmB])
                S.emit(A, lambda e, tm=tm, s=s: e.activation(out=lbuf[:, s, :], in_=tm[:, 0:512], func=AF.Ln, bias=1.0), reads=[tmB], writes=[lB[s]])

        def gate_g2(NS, main, p=0):
            dec, decB = decs[p], decBs[p]
            if not main:
                for s in range(NS):
                    ps, psB = nb()
                    for h in range(4):
                        S.emit(T_, lambda e, ps=ps, s=s, h=h: e.matmul(ps[:, h:h + 1], lhsT=lbuf[:, s, h * 128:(h + 1) * 128], rhs=negc[:], start=True, stop=True),
                               reads=[lB[s], constB], writes=[psB])
                    S.emit(A, lambda e, ps=ps, s=s: e.activation(out=dec[:, s, 0:4], in_=ps[:, 0:4], func=AF.Exp), reads=[psB], writes=[decB[s]])
                return
            for s in range(NS):
                ps, psB = nb()
                for h in range(4):
                    S.emit(T_, lambda e, ps=ps, s=s, h=h: e.matmul(ps[:, h * 2:h * 2 + 2], lhsT=lbuf[:, s, h * 128:(h + 1) * 128], rhs=chsel[:], start=True, stop=True),
                           reads=[lB[s], constB], writes=[psB])
                S.emit(A, lambda e, ps=ps, s=s: e.activation(out=dec[:, s, :], in_=ps[:, 0:8], func=AF.Exp), reads=[psB], writes=[decB[s]])
                if main:
                    ps, psB = nb()
                    for h in range(4):
                        S.emit(T_, lambda e, ps=ps, s=s, h=h: e.matmul(ps[:, h * 128:(h + 1) * 128], lhsT=lbuf[:, s, h * 128:(h + 1) * 128], rhs=tri_inc[:], start=True, stop=True),
                               reads=[lB[s], constB], writes=[psB])
                    psv = ps[:].rearrange("p (h t) -> p h t", h=4)
                    S.emit(A, lambda e, psv=psv, s=s: e.activation(out=Epl[:, :, s * 128:(s + 1) * 128], in_=psv, func=AF.Exp), reads=[psB], writes=[EB[s]])
                    S.emit(A, lambda e, psv=psv, s=s: e.activation(out=Emi[:, :, s * 128:(s + 1) * 128], in_=psv, func=AF.Exp, scale=-1.0), reads=[psB], writes=[EB[s]])

        def kv_stage(NS, main, kslot_fn, vslot_fns, after_k=None):
            kslot = kslot_fn()
            for s in range(NS):
                ps, psB = nb()
                tri = tri_end if main else tri_end_h
                S.emit(T_, lambda e, ps=ps, s=s, tri=tri: e.matmul(ps[:], lhsT=tri[:], rhs=lbuf[:, s, :], start=True, stop=True),
                       reads=[lB[s], constB], writes=[psB])
                tm, tmB = ntmp()
                S.emit(A, lambda e, ps=ps, tm=tm: e.activation(out=tm[:, 0:512], in_=ps[:], func=AF.Exp), reads=[psB], writes=[tmB])
                ps, psB = proj_T(kslot, s)
                S.emit(V, lambda e, ps=ps, tm=tm, s=s: e.tensor_tensor(out=kend[:, s, :], in0=ps[:], in1=tm[:, 0:512], op=ALU.mult),
                       reads=[psB, tmB], writes=[kendB[s]])
            if after_k is not None:
                after_k(kslot)
            for hf in range(2):
                vs = vslot_fns[hf]()
                for s in range(NS):
                    ps, psB = proj_T(vs, s)
                    S.emit(A, lambda e, ps=ps, s=s, hf=hf: e.activation(out=vtm[:, s, hf * 512:(hf + 1) * 512], in_=ps[:], func=AF.Copy),
                           reads=[psB], writes=[vtmB[s]])

        def state_chunk(s, c, want_bf=None, p=0):
            dec, decB = decs[p], decBs[p]
            pk = [nb(), nb()]
            for h in range(4):
                ps, psB = pk[h // 2]
                S.emit(T_, lambda e, ps=ps, h=h: e.matmul(ps[:, (h % 2) * 256:(h % 2) * 256 + 256], lhsT=kend[c * 64:(c + 1) * 64, s, h * 128:(h + 1) * 128],
                                                         rhs=vtm[c * 64:(c + 1) * 64, s, h * 256:(h + 1) * 256], start=True, stop=True),
                       reads=[kendB[s], vtmB[s]], writes=[psB])
            for h in range(4):
                ps, psB = pk[h // 2]
                S.emit(V, lambda e, ps=ps, h=h: e.scalar_tensor_tensor(out=Sst[:, h * 256:(h + 1) * 256], in0=Sst[:, h * 256:(h + 1) * 256],
                                                                      scalar=dec[:, s, h * 2 + c:h * 2 + c + 1], in1=ps[:, (h % 2) * 256:(h % 2) * 256 + 256],
                                                                      op0=ALU.mult, op1=ALU.add),
                       reads=[SstB, decB[s], psB], writes=[SstB])
            if want_bf is not None:
                S.emit(A, lambda e: e.activation(out=Sbf[want_bf][:], in_=Sst[:], func=AF.Copy), reads=[SstB], writes=[SbfB[want_bf]])

        hist_slots = None
        if n_hist_tiles > 0 or True:
            load_w(0, w_in, O_K, 512); load_w(1, w_in, O_V, 512); load_w(2, w_in, O_V + 512, 512); load_w(3, w_in, O_GL, 512)
            convert_weights()
        hums = [um_std(0), um_std(1), (on[:], [onB]), (Sbf[1][:], [SbfB[1]])]

        def hist_na(t):
            for pr in range(2):
                for i in range(2):
                    load_x(t * 512 + (pr * 2 + i) * 128, i)
                norm_a([(xs[i][:], xsB[i]) for i in range(2)], hums[pr * 2:pr * 2 + 2])

        def hist_state(t):
            dec, decB = decs[t % 2], decBs[t % 2]
            for s4 in range(4):
                pk = [nb(), nb()]
                for h in range(4):
                    ps, psB = pk[h // 2]
                    S.emit(T_, lambda e, ps=ps, h=h, s4=s4: e.matmul(ps[:, (h % 2) * 256:(h % 2) * 256 + 256], lhsT=kend[:, s4, h * 128:(h + 1) * 128],
                                                                    rhs=vtm[:, s4, h * 256:(h + 1) * 256], start=True, stop=True),
                           reads=[kendB[s4], vtmB[s4]], writes=[psB])
                for h in range(4):
                    ps, psB = pk[h // 2]
                    S.emit(V, lambda e, ps=ps, h=h, s4=s4, dec=dec: e.scalar_tensor_tensor(out=Sst[:, h * 256:(h + 1) * 256], in0=Sst[:, h * 256:(h + 1) * 256],
                                                                                     scalar=dec[:, s4, h:h + 1], in1=ps[:, (h % 2) * 256:(h % 2) * 256 + 256],
                                                                                     op0=ALU.mult, op1=ALU.add),
                           reads=[SstB, decB[s4], psB], writes=[SstB])

        if n_hist_tiles > 0:
            hist_na(0)
            norm_b(hums, g1T, 0)
            gate_stage(512, 4, False, 3, p=0)
        for t in range(n_hist_tiles):
            if t + 1 < n_hist_tiles:
                hist_na(t + 1)
            build_diags(2)
            kv_stage(4, False, lambda: 0, (lambda: 1, lambda: 2))
            if t + 1 < n_hist_tiles:
                norm_b(hums, g1T, 0)
                gate_g1(512, 4, 3)
                hist_state(t)
                gate_g2(4, False, (t + 1) % 2)
            else:
                hist_state(t)
        build_diags(24)
        S.emit(A, lambda e: e.activation(out=Sbf[0][:], in_=Sst[:], func=AF.Copy), reads=[SstB], writes=[SbfB[0]])
        sbf_cur = [0]

        dbg_tap("S_hist", Sst[:], [SstB], force=True)

        def main_tile(row0, T, is_halo, out_row0, prefetched=False, next_row0=None):
            NS = T // 128
            pf_ums = [um_std(0), um_std(1),
                      (actT[:, 0:2, :].rearrange("p k t -> p (k t)"), actTB[0:2]), (actT[:, 2:4, :].rearrange("p k t -> p (k t)"), actTB[2:4])]
            if not prefetched:
                for pr in range((NS + 1) // 2):
                    nn = min(2, NS - pr * 2)
                    ums = [um_std(i) for i in range(nn)]
                    for i in range(nn):
                        load_x(row0 + (pr * 2 + i) * 128, i)
                    norm_a([(xs[i][:], xsB[i]) for i in range(nn)], ums)
                    norm_b(ums, g1T, pr * 2)

            def transposes_next():
                if next_row0 is not None:
                    norm_b(pf_ums[0:2], g1T, 0)
                    norm_b(pf_ums[2:4], g1T, 2)

            def prefetch_next():
                if next_row0 is not None:
                    for pr in range(2):
                        for i in range(2):
                            load_x(next_row0 + (pr * 2 + i) * 128, i)
                        norm_a([(xs[i][:], xsB[i]) for i in range(2)], pf_ums[pr * 2:pr * 2 + 2])
            dbg_tap("uT", uT[:], [uTB])
            gate_stage(T, NS, True, next_w("in", O_GL, 512, 0, 8))
            def k_feature_major(ks):
                for h in range(4):
                    ps, psB = proj_F(lambda kc, h=h: slots[ks][:, kc, h * 128:(h + 1) * 128], [slotB[ks]], uT, [uTB], T)
                    S.emit(V, lambda e, ps=ps, h=h: e.tensor_tensor(out=kT[:, h, 0:T], in0=ps[:, 0:T], in1=Emi[:, h, 0:T], op=ALU.mult),
                           reads=[psB] + EB[0:NS], writes=[kTB[h]])
            kv_stage(NS, True, lambda: next_w("in", O_K, 512, 0, 8),
                     (lambda: next_w("in", O_V, 512, 0, 8), lambda: next_w("in", O_V + 512, 512, 0, 8)), after_k=k_feature_major)
            qs = next_w("in", O_Q, 512, 0, 8)
            for h in range(4):
                ps, psB = proj_F(lambda kc, h=h: slots[qs][:, kc, h * 128:(h + 1) * 128], [slotB[qs]], uT, [uTB], T)
                S.emit(V, lambda e, ps=ps, h=h: e.scalar_tensor_tensor(out=qT[:, h, 0:T], in0=ps[:, 0:T], scalar=128.0 ** -0.5, in1=Epl[:, h, 0:T],
                                                                      op0=ALU.mult, op1=ALU.mult),
                       reads=[psB] + EB[0:NS], writes=[qTB[h]])
            for hf in range(2):
                rs = next_w("in", O_R + hf * 512, 512, 0, 8)
                for j in range(4):
                    kc = hf * 4 + j
                    ps, psB = proj_F(lambda kk, j=j, rs=rs: slots[rs][:, kk, j * 128:(j + 1) * 128], [slotB[rs]], uT, [uTB], T)
                    S.emit(A, lambda e, ps=ps, kc=kc: e.activation(out=silur[:, kc, 0:T], in_=ps[:, 0:T], func=AF.Silu), reads=[psB], writes=[silurB[kc]])
            dbg_tap("l", lbuf[:], lB); dbg_tap("kT", kT[:], kTB); dbg_tap("qT", qT[:], qTB); dbg_tap("vtm", vtm[:], vtmB)
            dbg_tap("kend", kend[:], kendB); dbg_tap("silur", silur[:], silurB); dbg_tap("dec", decs[0][:], decBs[0])
            for hf in range(2):
                sa = next_w("in", O_A + hf * 512, 512, 0, 8)
                sg = next_w("in", O_BG + hf * 512, 512, 0, 8)
                for j in range(4):
                    kc = hf * 4 + j
                    psa, psaB = proj_F(lambda kk, j=j, sa=sa: slots[sa][:, kk, j * 128:(j + 1) * 128], [slotB[sa]], uT, [uTB], T)
                    psg, psgB = proj_F(lambda kk, j=j, sg=sg: slots[sg][:, kk, j * 128:(j + 1) * 128], [slotB[sg]], uT, [uTB], T)
                    tm, tmB = ntmp()
                    S.emit(A, lambda e, psg=psg, tm=tm: e.activation(out=tm[:, 0:T], in_=psg[:, 0:T], func=AF.Sigmoid), reads=[psgB], writes=[tmB])
                    S.emit(V, lambda e, psa=psa, tm=tm, kc=kc: e.tensor_tensor(out=cin[:, kc, 30:30 + T], in0=psa[:, 0:T], in1=tm[:, 0:T], op=ALU.mult),
                           reads=[psaB, tmB], writes=[cinB[kc]])
            dgtrk = [Buf(f"dgl{i}") for i in range(3)] if not hasattr(main_tile, "_dgtrk") else main_tile._dgtrk
            main_tile._dgtrk = dgtrk

            def load_part(g):
                if g >= 24:
                    return
                j0, nj = PARTS[g % 3]
                bi = g % 3
                S.dma(SY, dg[bi][:, 0:nj, :], dgd[g][:, 0:nj * 128].rearrange("p (j i) -> p j i", i=128), reads=[dgdB], writes=[dgB[bi]], track=dgtrk[bi])

            def conv_block(kc):
                psc, pscB = nb()
                for part in range(3):
                    g = kc * 3 + part
                    j0, nj = PARTS[part]
                    bi = g % 3
                    load_part(g + 2)
                    for jj in range(nj):
                        j = j0 + jj
                        S.emit(T_, lambda e, bi=bi, jj=jj, j=j: e.matmul(psc[:, 0:T], lhsT=dg[bi][:, jj, :], rhs=cin[:, kc, j:j + T], start=(j == 0), stop=(j == 30)),
                               reads=[dgB[bi], cinB[kc]], writes=[pscB])
                S.emit(A, lambda e: e.activation(out=cacc[:, kc, 0:T], in_=psc[:, 0:T], func=AF.Identity, bias=cbT[:, kc:kc + 1]),
                       reads=[pscB], writes=[arB[kc]])
                S.emit(A, lambda e: e.activation(out=cin[:, kc, 0:30], in_=cin[:, kc, T:T + 30], func=AF.Copy), reads=[cinB[kc]], writes=[cinB[kc]])

            gla_po = {}

            def gla_a(s):
                tok = slice(s * 128, (s + 1) * 128)
                for h in range(4):
                    ps, psB = nb()
                    S.emit(T_, lambda e, ps=ps, h=h: e.matmul(ps[:, 0:128], lhsT=kT[:, h, tok], rhs=qT[:, h, tok], start=True, stop=True),
                           reads=[kTB[h], qTB[h]], writes=[psB])
                    S.emit(V, lambda e, ps=ps, h=h: e.tensor_tensor(out=attm[:, h, :], in0=ps[:, 0:128], in1=cmask[:], op=ALU.mult),
                           reads=[psB, constB], writes=[attmB[h]])
                c0 = sbf_cur[0]; c1 = 1 - c0
                state_chunk(s, 0, want_bf=c1)

            def gla_b(s):
                c0 = sbf_cur[0]; c1 = 1 - c0
                po = [nb(), nb()]
                gla_po[s] = po
                for h in range(4):
                    ps, psB = po[h // 2]
                    cols = slice((h % 2) * 256, (h % 2) * 256 + 256)
                    hc = slice(h * 256, (h + 1) * 256)
                    S.emit(T_, lambda e, ps=ps, h=h, cols=cols, hc=hc: e.matmul(ps[:, cols], lhsT=attm[:, h, :], rhs=vtm[:, s, hc], start=True, stop=False),
                           reads=[attmB[h], vtmB[s]], writes=[psB])
                    S.emit(T_, lambda e, ps=ps, h=h, cols=cols, hc=hc: e.matmul(ps[0:64, cols], lhsT=qT[:, h, s * 128:s * 128 + 64], rhs=Sbf[c0][:, hc], start=False, stop=False),
                           reads=[qTB[h], SbfB[c0]], writes=[psB])
                    S.emit(T_, lambda e, ps=ps, h=h, cols=cols, hc=hc: e.matmul(ps[64:128, cols], lhsT=qT[:, h, s * 128 + 64:s * 128 + 128], rhs=Sbf[c1][:, hc], start=False, stop=True,
                                                                             tile_position=(0, 64)),
                           reads=[qTB[h], SbfB[c1]], writes=[psB])
                state_chunk(s, 1, want_bf=c0)

            def gla_c(s):
                tok = slice(s * 128, (s + 1) * 128)
                po = gla_po[s]
                for h in range(4):
                    ps, psB = po[h // 2]
                    cols = slice((h % 2) * 256, (h % 2) * 256 + 256)
                    S.emit(A, lambda e, ps=ps, h=h, cols=cols: e.activation(out=on[:, h * 256:(h + 1) * 256], in_=ps[:, cols], func=AF.Square, accum_out=oss[:, h:h + 1]),
                           reads=[psB], writes=[onB, ossB])
                S.emit(A, lambda e: e.activation(out=oss[:, 4:8], in_=oss[:, 0:4], func=AF.Ln, scale=1.0 / 256, bias=EPS), reads=[ossB], writes=[ossB])
                S.emit(A, lambda e: e.activation(out=oss[:, 4:8], in_=oss[:, 4:8], func=AF.Exp, scale=-0.5), reads=[ossB], writes=[ossB])
                for h in range(4):
                    ps, psB = po[h // 2]
                    cols = slice((h % 2) * 256, (h % 2) * 256 + 256)
                    S.emit(A, lambda e, ps=ps, h=h, cols=cols: e.activation(out=on[:, h * 256:(h + 1) * 256], in_=ps[:, cols], func=AF.Identity, scale=oss[:, 4 + h:5 + h]),
                           reads=[psB, ossB], writes=[onB])

            def gla_c2(s):
                tok = slice(s * 128, (s + 1) * 128)
                pT, pTB = npT()
                for kc in range(8):
                    S.emit(T_, lambda e, kc=kc, pT=pT: e.transpose(out=pT[:, kc * 128:(kc + 1) * 128], in_=on[:, kc * 128:(kc + 1) * 128], identity=identb[:]),
                           reads=[onB, identbB], writes=[pTB])
                for kc in range(8):
                    S.emit(V, lambda e, kc=kc, pT=pT: e.scalar_tensor_tensor(out=actT[:, kc, tok], in0=pT[:, kc * 128:(kc + 1) * 128], scalar=gnT[:, kc % 2:kc % 2 + 1],
                                                                      in1=silur[:, kc, tok], op0=ALU.mult, op1=ALU.mult),
                           reads=[pTB, vecB, silurB[kc]], writes=[actTB[kc]])
            load_part(0)
            load_part(1)
            kc_next = 0
            for s in range(NS):
                gla_a(s)
                conv_block(kc_next); kc_next += 1
                gla_b(s)
                gla_c(s)
                conv_block(kc_next); kc_next += 1
                gla_c2(s)
            while kc_next < 8:
                conv_block(kc_next); kc_next += 1
            dbg_tap("cin", cin[:, :, 30:542], cinB); dbg_tap("cacc", cacc, arB); dbg_tap("oT", actT[:], actTB)
            psm = pTs[0].bitcast(F32); psmB = pTBs[0]
            psq = pTs[1].bitcast(F32); psqB = pTBs[1]
            for hf in range(2):
                sw = next_w("go", hf * 512, 512, 0, 8)
                sg = next_w("in", O_GB + hf * 512, 512, 0, 8)
                for j in range(4):
                    ob = hf * 4 + j
                    psy, psyB = proj_F(lambda kk, j=j, sw=sw: slots[sw][:, kk, j * 128:(j + 1) * 128], [slotB[sw]], actT, actTB, T)
                    psg, psgB = proj_F(lambda kk, j=j, sg=sg: slots[sg][:, kk, j * 128:(j + 1) * 128], [slotB[sg]], uT, [uTB], T)
                    tq, tqB = ntmp()
                    S.emit(A, lambda e, ob=ob, tq=tq: e.activation(out=tq[:, 0:T], in_=cacc[:, ob, 0:T], func=AF.Square), reads=[arB[ob]], writes=[tqB])
                    tm, tmB = ntmp()
                    S.emit(A, lambda e, psg=psg, tm=tm: e.activation(out=tm[:, 0:T], in_=psg[:, 0:T], func=AF.Sigmoid), reads=[psgB], writes=[tmB])
                    S.emit(V, lambda e, psy=psy, tm=tm, ob=ob: e.tensor_tensor(out=m1[:, ob, 0:T], in0=psy[:, 0:T], in1=tm[:, 0:T], op=ALU.mult),
                           reads=[psyB, tmB], writes=[m1B[ob]])
                    S.emit(T_, lambda e, ob=ob: e.matmul(psm[:, 0:T], lhsT=onesf[:], rhs=cacc[:, ob, 0:T], start=(ob == 0), stop=(ob == 7)),
                           reads=[constB, arB[ob]], writes=[psmB])
                    S.emit(T_, lambda e, ob=ob, tq=tq: e.matmul(psq[:, 0:T], lhsT=onesf[:], rhs=tq[:, 0:T], start=(ob == 0), stop=(ob == 7)),
                           reads=[constB, tqB], writes=[psqB])
            mean, rstd = lnst[0], lnst[1]
            meanB, rstdB = lnstB
            S.emit(A, lambda e: e.activation(out=mean[:, 0:T], in_=psm[:, 0:T], func=AF.Identity, scale=1.0 / D), reads=[psmB], writes=[meanB])
            S.emit(V, lambda e: e.tensor_tensor(out=rstd[:, 0:T], in0=mean[:, 0:T], in1=mean[:, 0:T], op=ALU.mult), reads=[meanB], writes=[rstdB])
            S.emit(V, lambda e: e.scalar_tensor_tensor(out=rstd[:, 0:T], in0=psq[:, 0:T], scalar=1.0 / D, in1=rstd[:, 0:T], op0=ALU.mult, op1=ALU.subtract),
                   reads=[psqB, rstdB], writes=[rstdB])
            S.emit(A, lambda e: e.activation(out=rstd[:, 0:T], in_=rstd[:, 0:T], func=AF.Ln, bias=EPS), reads=[rstdB], writes=[rstdB])
            S.emit(A, lambda e: e.activation(out=rstd[:, 0:T], in_=rstd[:, 0:T], func=AF.Exp, scale=-0.5), reads=[rstdB], writes=[rstdB])
            nmr, nmrB = mean, meanB
            S.emit(V, lambda e: e.scalar_tensor_tensor(out=mean[:, 0:T], in0=mean[:, 0:T], scalar=-1.0, in1=rstd[:, 0:T], op0=ALU.mult, op1=ALU.mult),
                   reads=[meanB, rstdB], writes=[meanB])
            for hf in range(2):
                sg = next_w("in", O_GA + hf * 512, 512, 0, 8)
                for j in range(4):
                    kc = hf * 4 + j
                    psg, psgB = proj_F(lambda kk, j=j, sg=sg: slots[sg][:, kk, j * 128:(j + 1) * 128], [slotB[sg]], uT, [uTB], T)
                    S.emit(A, lambda e, psg=psg, kc=kc: e.activation(out=cin[:, kc, 30:30 + T], in_=psg[:, 0:T], func=AF.Sigmoid), reads=[psgB], writes=[cinB[kc]])
            for kc in range(8):
                tm, tmB = ntmp()
                S.emit(V, lambda e, kc=kc, tm=tm: e.tensor_tensor(out=tm[:, 0:T], in0=cacc[:, kc, 0:T], in1=rstd[:, 0:T], op=ALU.mult),
                       reads=[arB[kc], rstdB], writes=[tmB])
                S.emit(V, lambda e, tm=tm: e.tensor_tensor(out=tm[:, 0:T], in0=tm[:, 0:T], in1=nmr[:, 0:T], op=ALU.add), reads=[tmB, nmrB], writes=[tmB])
                S.emit(A, lambda e, kc=kc, tm=tm: e.activation(out=actT[:, kc, 0:T], in_=tm[:, 0:T], func=AF.Silu, scale=lngT[:, kc:kc + 1], bias=lnbT[:, kc:kc + 1]),
                       reads=[tmB, vecB], writes=[actTB[kc]])
            dbg_tap("cact", actT[:], actTB)
            for hf in range(2):
                sw = next_w("co", hf * 512, 512, 0, 8)
                for j in range(4):
                    ob = hf * 4 + j
                    psy, psyB = proj_F(lambda kk, j=j, sw=sw: slots[sw][:, kk, j * 128:(j + 1) * 128], [slotB[sw]], actT, actTB, T)
                    tm, tmB = ntmp()
                    S.emit(V, lambda e, psy=psy, tm=tm, ob=ob: e.tensor_tensor(out=tm[:, 0:T], in0=psy[:, 0:T], in1=cin[:, ob, 30:30 + T], op=ALU.mult),
                           reads=[psyB, cinB[ob]], writes=[tmB])
                    S.emit(V, lambda e, tm=tm, ob=ob: e.tensor_tensor(out=silur[:, ob, 0:T], in0=tm[:, 0:T], in1=m1[:, ob, 0:T], op=ALU.add),
                           reads=[tmB, m1B[ob]], writes=[silurB[ob]])
            dbg_tap("merged", silur[:], silurB)
            sws = [next_w("wo", 0, 512, 0, 8), next_w("wo", 512, 512, 0, 8)]

            def wo_sub(s):
                for hf in range(2):
                    sw = sws[hf]
                    ps, psB = nb()
                    for kc in range(8):
                        S.emit(T_, lambda e, kc=kc, ps=ps, sw=sw: e.matmul(ps[:], lhsT=silur[:, kc, s * 128:(s + 1) * 128], rhs=slots[sw][:, kc, :], start=(kc == 0), stop=(kc == 7)),
                               reads=[silurB[kc], slotB[sw]], writes=[psB])
                    xi = hf
                    S.dma(SY, xs[xi][:, 0:512], x[row0 + s * 128:row0 + (s + 1) * 128, hf * 512:(hf + 1) * 512], writes=[xsB[xi]], track=xsB[xi])
                    S.emit(V, lambda e, ps=ps, hf=hf, xi=xi: e.tensor_tensor(out=hbuf[:, s, hf * 512:(hf + 1) * 512], in0=ps[:], in1=xs[xi][:, 0:512], op=ALU.add),
                           reads=[psB, xsB[xi]], writes=[hB[s]])
            for s in range(NS):
                wo_sub(s)
                if s % 2 == 1 or s == NS - 1:
                    p0 = (s // 2) * 2
                    norm_a([(hbuf[:, q, :], hB[q]) for q in range(p0, s + 1)], pf_ums[p0:s + 1])
            dbg_tap("h1", hbuf[:], hB)
            for p0 in range(0, NS, 2):
                norm_b(pf_ums[p0:min(p0 + 2, NS)], g2T, p0)
            ngrp = 6
            for g in range(ngrp):
                nblk = 4 if g < 5 else 2
                sa = next_w("up", g * 512, nblk * 128, 0, 8)
                sbb = next_w("up", FFN + g * 512, nblk * 128, 0, 8)
                for j in range(nblk):
                    i = g * 4 + j
                    accs = []
                    for which, sl in ((0, sa), (1, sbb)):
                        blk = which * 22 + i
                        ps, psB = proj_F(lambda kk, j=j, sl=sl: slots[sl][:, kk, j * 128:(j + 1) * 128], [slotB[sl]], uT, [uTB], T)
                        zb, zbB = ntmp()
                        S.emit(A, lambda e, zb=zb, blk=blk: e.activation(out=zb[:, 0:2], in_=zhalo[:, blk, :], func=AF.Copy), reads=[zhB], writes=[zbB])
                        S.emit(A, lambda e, zb=zb, ps=ps: e.activation(out=zb[:, 2:2 + T], in_=ps[:, 0:T], func=AF.Copy), reads=[psB], writes=[zbB])
                        S.emit(A, lambda e, zb=zb, blk=blk: e.activation(out=zhalo[:, blk, :], in_=zb[:, T:T + 2], func=AF.Copy), reads=[zbB], writes=[zhB])
                        ac, acB = ntmp()
                        S.emit(A, lambda e, zb=zb, ac=ac, blk=blk: e.activation(out=ac[:, 0:T], in_=zb[:, 0:T], func=AF.Identity,
                                                                               scale=fwT[:, blk:blk + 1], bias=fbT[:, blk:blk + 1]),
                               reads=[zbB, vecB], writes=[acB])
                        for jj in (1, 2):
                            S.emit(V, lambda e, zb=zb, ac=ac, blk=blk, jj=jj: e.scalar_tensor_tensor(out=ac[:, 0:T], in0=zb[:, jj:jj + T], scalar=fwT[:, jj * 44 + blk:jj * 44 + blk + 1],
                                                                                                in1=ac[:, 0:T], op0=ALU.mult, op1=ALU.add),
                                   reads=[zbB, vecB, acB], writes=[acB])
                        accs.append((ac, acB))
                    if not is_halo:
                        (aa, aaB), (ab, abB) = accs
                        S.emit(A, lambda e, aa=aa: e.activation(out=aa[:, 0:T], in_=aa[:, 0:T], func=AF.Silu), reads=[aaB], writes=[aaB])
                        S.emit(V, lambda e, aa=aa, ab=ab, i=i: e.tensor_tensor(out=gT[:, i, 0:T], in0=aa[:, 0:T], in1=ab[:, 0:T], op=ALU.mult),
                               reads=[aaB, abB], writes=[arB[i % 8]])
            prefetch_next()
            if is_halo:
                S.emit(V, lambda e: e.tensor_scalar(out=zhalo[:], in0=zhalo[:], scalar1=hp[:, 0:1], scalar2=None, op0=ALU.mult), reads=[zhB, vecB], writes=[zhB])
                transposes_next()
                return
            dbg_tap("gT", gT, arB)
            for hf in range(2):
                pss = [nb() for _ in range(NS)]
                for g3 in range(3):
                    nk = 8 if g3 < 2 else 6
                    sw = next_w("dn", hf * 512, 512, g3 * 8, nk)
                    for s in range(NS):
                        ps, psB = pss[s]
                        for kk in range(nk):
                            i = g3 * 8 + kk
                            S.emit(T_, lambda e, ps=ps, s=s, kk=kk, i=i, sw=sw: e.matmul(ps[:], lhsT=gT[:, i, s * 128:(s + 1) * 128], rhs=slots[sw][:, kk, :],
                                                                                   start=(i == 0), stop=(i == 21)),
                                   reads=[arB[i % 8], slotB[sw]], writes=[psB])
                for s in range(NS):
                    ps, psB = pss[s]
                    S.emit(V, lambda e, ps=ps, s=s, hf=hf: e.tensor_tensor(out=hbuf[:, s, hf * 512:(hf + 1) * 512], in0=ps[:], in1=hbuf[:, s, hf * 512:(hf + 1) * 512], op=ALU.add),
                           reads=[psB, hB[s]], writes=[hB[s]])
            transposes_next()
            for s in range(NS):
                S.emit(A, lambda e, s=s: e.activation(out=on[:], in_=hbuf[:, s, :], func=AF.Square, accum_out=ss[:, s:s + 1]), reads=[hB[s]], writes=[onB, ssB])
            S.emit(A, lambda e: e.activation(out=ss[:, 4:8], in_=ss[:, 0:4], func=AF.Ln, scale=1.0 / D, bias=EPS), reads=[ssB], writes=[ssB])
            S.emit(A, lambda e: e.activation(out=ss[:, 4:8], in_=ss[:, 4:8], func=AF.Exp, scale=-0.5), reads=[ssB], writes=[ssB])
            for s in range(NS):
                xi = s % 2
                S.emit(V, lambda e, s=s, xi=xi: e.scalar_tensor_tensor(out=xs[xi][:], in0=hbuf[:, s, :], scalar=ss[:, 4 + s:5 + s], in1=fgB_t[:], op0=ALU.mult, op1=ALU.mult),
                       reads=[hB[s], ssB, vecB], writes=[xsB[xi]])
                S.dma(SY, out[out_row0 + s * 128:out_row0 + (s + 1) * 128, :], xs[xi][:], reads=[xsB[xi]], track=xsB[xi])

        main_tile(HIST, HALO, True, None, prefetched=False, next_row0=(HIST + HALO if n_main_tiles > 0 else None))
        for t in range(n_main_tiles):
            tapon[0] = (t == 0)
            main_tile(HIST + HALO + t * 512, 512, False, t * 512, prefetched=True,
                      next_row0=(HIST + HALO + (t + 1) * 512 if t + 1 < n_main_tiles else None))
        for i in range(2):
            S._wait(SY, ("d", xsB[i], xsB[i].dcount * 16))
        S.replay(block)
    return nc


def make_in_maps(inputs):
    x = np.asarray(inputs["x"], dtype=np.float32)
    sq = lambda k: np.ascontiguousarray(np.asarray(inputs[k], dtype=np.float32)[0])
    shared = {k: sq(k) for k in ("norm1_g", "w_in", "conv_dw_w", "conv_dw_b", "conv_ln_g", "conv_ln_b", "w_conv_out",
                                 "w_gate_up", "b_gate", "gla_norm_g", "w_gla_out", "w_o", "norm2_g", "w_ffn_up",
                                 "ffn_dw_w", "ffn_dw_b", "w_ffn_down")}
    shared["final_g"] = np.ascontiguousarray(np.asarray(inputs["final_g"], dtype=np.float32))
    in_maps = []
    for c in range(NCORES):
        b, r = divmod(c, 4)
        start = r * SEG
        xc = np.zeros((ROWS, D), np.float32)
        lo = start - (HIST + HALO)
        src_lo = max(lo, 0)
        xc[src_lo - lo:] = x[b, src_lo:start + SEG]
        m = dict(shared)
        m["x"] = xc
        m["hasprev"] = np.full((128, 1), 1.0 if r > 0 else 0.0, np.float32)
        in_maps.append(m)
    return in_maps


_NC_CACHE = {}


def kernel(**inputs):
    if "nc" not in _NC_CACHE:
        _NC_CACHE["nc"] = build_program()
    nc = _NC_CACHE["nc"]
    in_maps = make_in_maps(inputs)
    res = run_bass_kernel_spmd(nc, in_maps, core_ids=list(range(NCORES)))
    outp = np.zeros((2, 4 * SEG, D), np.float32)
    for c in range(NCORES):
        b, r = divmod(c, 4)
        outp[b, r * SEG:(r + 1) * SEG] = res.results[c]["out"]
    return outp
```

```python
import sys
import numpy as np
import concourse.bass as bass
import concourse.mybir as mybir
from concourse.bass_utils import run_bass_kernel_spmd
from contextlib import ExitStack

F32 = mybir.dt.float32
BF16 = mybir.dt.bfloat16
AF = mybir.ActivationFunctionType
ALU = mybir.AluOpType

D = 1024
IN_DIM = 7184
FFN = 2816
NCORES = 8
SEG = 2048
HIST = 6144
HALO = 128
ROWS = HIST + HALO + SEG
EPS = 1e-6
EPOCH = 30000

O_A, O_BG, O_Q, O_K, O_V, O_R, O_GL, O_GA, O_GB = 0, 1024, 2048, 2560, 3072, 4096, 5120, 5136, 6160


class Buf:
    __slots__ = ("name", "lastw", "readers", "sem", "dcount")

    def __init__(self, name):
        self.name = name
        self.lastw = None
        self.readers = {}
        self.sem = None
        self.dcount = 0


class Sched:
    ENGS = ("sync", "scalar", "vector", "gpsimd", "tensor")

    def __init__(self, nc, stack):
        self.nc = nc
        self.stack = stack
        self.ops = {e: [] for e in self.ENGS}
        self.count = {e: 0 for e in self.ENGS}
        self.sems = {e: [] for e in self.ENGS}
        self.waited = {e: {} for e in self.ENGS}
        self.same_engine_sync = {"scalar": True, "vector": True, "gpsimd": True, "tensor": False, "sync": False}

    def new_sem(self, name):
        return self.stack.enter_context(self.nc.semaphore(name))

    def eng_sem(self, e, epoch):
        while len(self.sems[e]) <= epoch:
            self.sems[e].append(self.new_sem(f"p_{e}_{len(self.sems[e])}"))
        return self.sems[e][epoch]

    def _wait(self, eng, tok):
        if tok[0] == "e":
            _, src, idx = tok
            if src == eng and not self.same_engine_sync[eng]:
                return
            epoch, val = divmod(idx - 1, EPOCH)
            val += 1
            key = ("e", src, epoch)
            sem = self.eng_sem(src, epoch)
        else:
            _, buf, val = tok
            key = ("d", id(buf))
            sem = buf.sem
        if self.waited[eng].get(key, 0) >= val:
            return
        self.waited[eng][key] = val
        self.ops[eng].append(lambda e, sem=sem, val=val: e.wait_ge(sem, val))

    def _deps(self, eng, reads, writes):
        for b in reads:
            if b.lastw is not None:
                self._wait(eng, b.lastw)
        for b in writes:
            if b.lastw is not None and not (b.lastw[0] == "e" and b.lastw[1] == eng):
                self._wait(eng, b.lastw)
            for t in b.readers.values():
                if not (t[0] == "e" and t[1] == eng):
                    self._wait(eng, t)

    def emit(self, eng, fn, reads=(), writes=()):
        self._deps(eng, reads, writes)
        self.count[eng] += 1
        idx = self.count[eng]
        sem = self.eng_sem(eng, (idx - 1) // EPOCH)
        self.ops[eng].append(lambda e, fn=fn, sem=sem: fn(e).then_inc(sem, 1))
        tok = ("e", eng, idx)
        for b in writes:
            b.lastw = tok
            b.readers = {}
        for b in reads:
            b.readers[eng] = tok
        return tok

    def dma(self, eng, out, in_, reads=(), writes=(), track=None, **kw):
        self._deps(eng, reads, writes)
        if track.sem is None:
            track.sem = self.new_sem("d_" + track.name)
        track.dcount += 1
        val = track.dcount * 16
        sem = track.sem
        self.ops[eng].append(
            lambda e, sem=sem, out=out, in_=in_, kw=kw: e.dma_start(out=out, in_=in_, **kw).then_inc(sem, 16))
        tok = ("d", track, val)
        for b in writes:
            b.lastw = tok
            b.readers = {}
        for b in reads:
            b.readers["dma_" + track.name] = tok
        return tok

    def replay(self, block):
        for e in self.ENGS:
            ops = self.ops[e]

            def body(engine, ops=ops):
                for f in ops:
                    f(engine)
            getattr(block, e)(body)


def build_program(n_hist_tiles=HIST // 512, n_main_tiles=SEG // 512, dbg=None):
    nc = bass.Bass("TRN2", target_bir_lowering=False)
    dt_in = lambda name, shape: nc.dram_tensor(name, shape, F32, kind="ExternalInput").ap()
    x = dt_in("x", [ROWS, D])
    hasprev = dt_in("hasprev", [128, 1])
    norm1_g = dt_in("norm1_g", [D]); w_in = dt_in("w_in", [D, IN_DIM])
    conv_dw_w = dt_in("conv_dw_w", [31, D]); conv_dw_b = dt_in("conv_dw_b", [D])
    conv_ln_g = dt_in("conv_ln_g", [D]); conv_ln_b = dt_in("conv_ln_b", [D])
    w_conv_out = dt_in("w_conv_out", [D, D]); w_gate_up = dt_in("w_gate_up", [16, 512])
    b_gate = dt_in("b_gate", [512]); gla_norm_g = dt_in("gla_norm_g", [256])
    w_gla_out = dt_in("w_gla_out", [D, D]); w_o = dt_in("w_o", [D, D])
    norm2_g = dt_in("norm2_g", [D]); w_ffn_up = dt_in("w_ffn_up", [D, 2 * FFN])
    ffn_dw_w = dt_in("ffn_dw_w", [3, 2 * FFN]); ffn_dw_b = dt_in("ffn_dw_b", [2 * FFN])
    w_ffn_down = dt_in("w_ffn_down", [FFN, D]); final_g = dt_in("final_g", [D])
    out = nc.dram_tensor("out", [SEG, D], F32, kind="ExternalOutput").ap()
    dgd = nc.dram_tensor("dgd", [24, 128, 11 * 128], BF16).ap()
    wb = {"in": nc.dram_tensor("wb_in", [D, IN_DIM], BF16).ap(), "co": nc.dram_tensor("wb_co", [D, D], BF16).ap(),
          "go": nc.dram_tensor("wb_go", [D, D], BF16).ap(), "wo": nc.dram_tensor("wb_wo", [D, D], BF16).ap(),
          "up": nc.dram_tensor("wb_up", [D, 2 * FFN], BF16).ap(), "dn": nc.dram_tensor("wb_dn", [FFN, D], BF16).ap()}
    dbg_out = {}
    if dbg:
        for name, shape in dbg.items():
            dbg_out[name] = nc.dram_tensor("dbg_" + name, list(shape), F32, kind="ExternalOutput").ap()

    with ExitStack() as stack:
        S = Sched(nc, stack)
        _n = [0]

        def sb(shape, dt, name=None):
            _n[0] += 1
            return stack.enter_context(nc.sbuf_tensor(name or f"t{_n[0]}", list(shape), dt))

        identf = sb([128, 128], F32); identfB = Buf("identf")
        identb = sb([128, 128], BF16); identbB = Buf("identb")
        tri_inc = sb([128, 128], F32); tri_end = sb([128, 128], F32); cmask = sb([128, 128], F32)
        onesf = sb([128, 128], F32); ones_row = sb([1, 128], BF16)
        constB = Buf("consts")
        negc = sb([128, 1], F32)
        stage = sb([128, 128], F32); stageB = Buf("stage")
        g1T = sb([128, 8], F32); g2T = sb([128, 8], F32); cbT = sb([128, 8], F32)
        lngT = sb([128, 8], F32); lnbT = sb([128, 8], F32); fbT = sb([128, 44], F32)
        gnT = sb([128, 2], F32); cwT = sb([128, 248], F32); fwT = sb([128, 132], F32)
        fgB_t = sb([128, D], F32)
        bgrow = sb([1, 512], BF16); wgu = sb([16, 512], BF16)
        hp = sb([128, 1], F32)
        vecB = Buf("vecs")
        banks = [stack.enter_context(nc.psum_tensor(f"ps{i}", [128, 512], F32)) for i in range(6)]
        bankB = [Buf(f"ps{i}") for i in range(6)]
        pTs = [stack.enter_context(nc.psum_tensor(f"pT{i}", [128, 1024], BF16)) for i in range(2)]; pTBs = [Buf(f"pT{i}") for i in range(2)]
        _pt = [0]

        def npT():
            i = _pt[0] % 2
            _pt[0] += 1
            return pTs[i], pTBs[i]
        _bk = [0]

        def nb():
            i = _bk[0] % 6
            _bk[0] += 1
            return banks[i], bankB[i]

        NSLOT = 4
        slots = [sb([128, 8, 512], BF16, f"slot{i}") for i in range(NSLOT)]
        slotB = [Buf(f"slot{i}") for i in range(NSLOT)]
        xs = [sb([128, D], F32, f"xs{i}") for i in range(2)]; xsB = [Buf(f"xs{i}") for i in range(2)]
        utms = [sb([128, D], BF16, f"utm{i}") for i in range(2)]; utmBs = [Buf(f"utm{i}") for i in range(2)]
        uT = sb([128, 8, 512], BF16); uTB = Buf("uT")
        glowT = sb([128, 512], BF16); glowTB = Buf("glowT")
        lbuf = sb([128, 4, 512], F32); lB = [Buf(f"l{i}") for i in range(4)]
        NTMP = 6
        tmps = [sb([128, 516], F32, f"tmp{i}") for i in range(NTMP)]; tmpB = [Buf(f"tmp{i}") for i in range(NTMP)]
        _tk = [0]

        def ntmp():
            i = _tk[0] % NTMP
            _tk[0] += 1
            return tmps[i], tmpB[i]
        Epl = sb([128, 4, 512], BF16); Emi = sb([128, 4, 512], BF16); EB = [Buf(f"E{i}") for i in range(4)]
        decs = [sb([128, 4, 8], F32, f"dec{p}") for p in range(2)]; decBs = [[Buf(f"dec{p}_{i}") for i in range(4)] for p in range(2)]
        dec = decs[0]; decB = decBs[0]
        kend = sb([128, 4, 512], BF16); kendB = [Buf(f"kend{i}") for i in range(4)]
        vtm = sb([128, 4, D], BF16); vtmB = [Buf(f"v{i}") for i in range(4)]
        kT = sb([128, 4, 512], BF16); kTB = [Buf(f"kT{i}") for i in range(4)]
        qT = sb([128, 4, 512], BF16); qTB = [Buf(f"qT{i}") for i in range(4)]
        silur = sb([128, 8, 512], BF16); silurB = [Buf(f"sr{i}") for i in range(8)]
        attm = sb([128, 4, 128], BF16); attmB = [Buf(f"att{i}") for i in range(4)]
        Sst = sb([128, D], F32); SstB = Buf("S")
        Sbf = [sb([128, D], BF16, f"Sbf{i}") for i in range(2)]; SbfB = [Buf(f"Sbf{i}") for i in range(2)]
        on = sb([128, D], BF16); onB = Buf("on")
        oss = sb([128, 8], F32); ossB = Buf("oss")
        actT = sb([128, 8, 512], BF16); actTB = [Buf(f"actT{i}") for i in range(8)]
        cin = sb([128, 8, 542], BF16); cinB = [Buf(f"cin{i}") for i in range(8)]
        arena = sb([128, 22 * 256], F32, "arena")
        cacc = arena[:, 0:4096].rearrange("p (k t) -> p k t", k=8)
        gT = arena.bitcast(BF16).rearrange("p (k t) -> p k t", k=22)
        arB = [Buf(f"ar{i}") for i in range(8)]
        m1 = sb([128, 8, 512], BF16); m1B = [Buf(f"m1{i}") for i in range(8)]
        hbuf = sb([128, 4, D], F32); hB = [Buf(f"h{i}") for i in range(4)]
        zhalo = sb([128, 44, 2], F32); zhB = Buf("zhalo")
        ss = sb([128, 8], F32); ssB = Buf("ss")
        dg = [sb([128, 11, 128], BF16, f"dg{i}") for i in range(3)]; dgB = [Buf(f"dg{i}") for i in range(3)]
        dgdB = Buf("dgd")
        PARTS = ((0, 11), (11, 10), (21, 10))
        lnst = [sb([128, 512], F32, f"lnst{i}") for i in range(2)]; lnstB = [Buf(f"lnst{i}") for i in range(2)]

        block = stack.enter_context(nc.Block())
        V, A, P, T_, SY = "vector", "scalar", "gpsimd", "tensor", "sync"

        def iota_sel(t, pattern, cmp, fill, base, cm, src=None):
            S.emit(P, lambda e: e.affine_select(out=t, in_=(src if src is not None else t), pattern=pattern,
                                                compare_op=cmp, fill=fill, base=base, channel_multiplier=cm),
                   reads=[constB], writes=[constB])
        S.emit(P, lambda e: e.memset(identf[:], 0.0), writes=[constB])
        iota_sel(identf[:], [[-1, 128]], ALU.not_equal, 1.0, 0, 1)
        S.emit(V, lambda e: e.tensor_copy(out=identb[:], in_=identf[:]), reads=[constB], writes=[identbB])
        S.emit(P, lambda e: e.memset(onesf[:], 1.0), writes=[constB])
        S.emit(P, lambda e: e.memset(ones_row[:], 1.0), writes=[constB])
        for t, val in ((tri_inc, -1.0 / 16), (tri_end, -1.0 / 16), (cmask, 1.0)):
            S.emit(P, lambda e, t=t, val=val: e.memset(t[:], val), writes=[constB])
        iota_sel(tri_inc[:], [[1, 128]], ALU.is_ge, 0.0, 0, -1)
        iota_sel(cmask[:], [[1, 128]], ALU.is_ge, 0.0, 0, -1)
        iota_sel(tri_end[:], [[-1, 128]], ALU.is_gt, 0.0, 0, 1)
        S.emit(P, lambda e: e.memset(negc[:], -1.0 / 16), writes=[constB])
        S.emit(P, lambda e: e.memset(zhalo[:], 0.0), writes=[zhB])
        S.emit(P, lambda e: e.memset(Sst[:], 0.0), writes=[SstB])
        S.emit(P, lambda e: e.memset(cin[:], 0.0), writes=cinB)

        def load_cols(dst, rows_ap, nrows):
            r0 = 0
            while r0 < nrows:
                n = min(128, nrows - r0)
                S.dma(SY, stage[0:n, :], rows_ap[r0:r0 + n, :], writes=[stageB], track=stageB)
                ps, psB = nb()
                S.emit(T_, lambda e, ps=ps, n=n: e.matmul(ps[:, 0:n], lhsT=stage[0:n, :], rhs=identf[0:n, 0:n], start=True, stop=True),
                       reads=[stageB, constB], writes=[psB])
                S.emit(V, lambda e, ps=ps, n=n, r0=r0: e.tensor_copy(out=dst[:, r0:r0 + n], in_=ps[:, 0:n]), reads=[psB], writes=[vecB])
                r0 += n
        load_cols(g1T, norm1_g.rearrange("(k p) -> k p", p=128), 8)
        load_cols(g2T, norm2_g.rearrange("(k p) -> k p", p=128), 8)
        load_cols(cbT, conv_dw_b.rearrange("(k p) -> k p", p=128), 8)
        load_cols(lngT, conv_ln_g.rearrange("(k p) -> k p", p=128), 8)
        load_cols(lnbT, conv_ln_b.rearrange("(k p) -> k p", p=128), 8)
        load_cols(fbT, ffn_dw_b.rearrange("(k p) -> k p", p=128), 44)
        load_cols(gnT, gla_norm_g.rearrange("(k p) -> k p", p=128), 2)
        load_cols(cwT, conv_dw_w.rearrange("j (k p) -> (j k) p", p=128), 248)
        load_cols(fwT, ffn_dw_w.rearrange("j (k p) -> (j k) p", p=128), 132)
        rowst = xs[0]
        S.dma(SY, rowst[0:1, :], final_g.rearrange("(o n) -> o n", o=1), writes=[xsB[0]], track=xsB[0])
        for hf in range(2):
            ps, psB = nb()
            S.emit(T_, lambda e, ps=ps, hf=hf: e.matmul(ps[:], lhsT=onesf[0:1, :], rhs=rowst[0:1, hf * 512:(hf + 1) * 512], start=True, stop=True),
                   reads=[xsB[0], constB], writes=[psB])
            S.emit(V, lambda e, ps=ps, hf=hf: e.tensor_copy(out=fgB_t[:, hf * 512:(hf + 1) * 512], in_=ps[:]), reads=[psB], writes=[vecB])
        setup_toks = [vecB.lastw, constB.lastw, identbB.lastw]
        setup_toks.append(S.dma(P, bgrow[:], b_gate.rearrange("(o n) -> o n", o=1), writes=[vecB], track=Buf("bgrow")))
        setup_toks.append(S.dma(P, wgu[:], w_gate_up, writes=[vecB], track=Buf("wgu")))
        setup_toks.append(S.dma(SY, hp[:], hasprev, writes=[vecB], track=Buf("hp")))
        for eng in (A, V, T_, P):
            for tk in setup_toks:
                S._wait(eng, tk)
        vecB.lastw = None; vecB.readers = {}
        constB.lastw = None; constB.readers = {}

        tapon = [False]

        def dbg_tap(name, ap, bufs, force=False):
            if dbg and name in dbg and (tapon[0] or force):
                tb = Buf("dbg_" + name)
                S.dma(P, dbg_out[name], ap, reads=bufs, track=tb)
                S._wait(P, ("d", tb, tb.dcount * 16))

        def build_diag(g):
            kc, part = divmod(g, 3)
            j0, nj = PARTS[part]
            bi = g % 3
            for jj in range(nj):
                col = (j0 + jj) * 8 + kc
                S.emit(V, lambda e, bi=bi, jj=jj, col=col: e.tensor_scalar(out=dg[bi][:, jj, :], in0=identb[:], scalar1=cwT[:, col:col + 1], scalar2=None, op0=ALU.mult),
                       reads=[identbB], writes=[dgB[bi]])
            tok = S.dma(SY, dgd[g][:, 0:nj * 128], dg[bi][:, 0:nj, :].rearrange("p j i -> p (j i)"), reads=[dgB[bi]], track=dgdB)
            dgdB.lastw = tok
        _dgn = [0]

        def build_diags(n):
            for _ in range(n):
                if _dgn[0] < 24:
                    build_diag(_dgn[0])
                    _dgn[0] += 1

        def load_w(slot_i, src, c0, ncols, r0=0, nk=8):
            S.dma(P, slots[slot_i][:, 0:nk, 0:ncols],
                  src[r0 * 128:(r0 + nk) * 128, c0:c0 + ncols].rearrange("(k p) n -> p k n", p=128),
                  writes=[slotB[slot_i]], track=slotB[slot_i])
        wbB = Buf("wb")

        def load_wb(slot_i, src, c0, ncols, r0=0, nk=8):
            S.dma(P, slots[slot_i][:, 0:nk, 0:ncols],
                  src[r0 * 128:(r0 + nk) * 128, c0:c0 + ncols].rearrange("(k p) n -> p k n", p=128),
                  reads=[wbB], writes=[slotB[slot_i]], track=slotB[slot_i])

        def convert_weights():
            bounce = [(actT, actTB), (silur, silurB), (m1, m1B)]
            cvB = [Buf(f"cv{i}") for i in range(3)]
            blocks = []
            for c0 in range(0, IN_DIM - 16, 512):
                blocks.append(("in", 0, 8, c0, 512))
            blocks.append(("in", 0, 8, IN_DIM - 16, 16))
            for nm in ("co", "go", "wo"):
                for hf in range(2):
                    blocks.append((nm, 0, 8, hf * 512, 512))
            for c0 in range(0, 2 * FFN, 512):
                blocks.append(("up", 0, 8, c0, 512))
            for r0, nk in ((0, 8), (8, 8), (16, 6)):
                for hf in range(2):
                    blocks.append(("dn", r0, nk, hf * 512, 512))
            for bi, (nm, r0, nk, c0, ncols) in enumerate(blocks):
                bt, bB = bounce[bi % 3]
                srcap = WSRC[nm][r0 * 128:(r0 + nk) * 128, c0:c0 + ncols].rearrange("(k p) n -> p k n", p=128)
                dstap = wb[nm][r0 * 128:(r0 + nk) * 128, c0:c0 + ncols].rearrange("(k p) n -> p k n", p=128)
                S.dma(P, bt[:, 0:nk, 0:ncols], srcap, writes=bB, track=cvB[bi % 3])
                tok = S.dma(P, dstap, bt[:, 0:nk, 0:ncols], reads=bB, track=wbB)
                wbB.lastw = tok

        WSRC = {"in": w_in, "co": w_conv_out, "go": w_gla_out, "wo": w_o, "up": w_ffn_up, "dn": w_ffn_down}

        def tile_wseq(is_halo):
            q = [("in", O_GL, 512, 0, 8), ("in", O_K, 512, 0, 8), ("in", O_V, 512, 0, 8), ("in", O_V + 512, 512, 0, 8), ("in", O_Q, 512, 0, 8),
                 ("in", O_R, 512, 0, 8), ("in", O_R + 512, 512, 0, 8)]
            for hf in range(2):
                q += [("in", O_A + hf * 512, 512, 0, 8), ("in", O_BG + hf * 512, 512, 0, 8)]
            for hf in range(2):
                q += [("go", hf * 512, 512, 0, 8), ("in", O_GB + hf * 512, 512, 0, 8)]
            for hf in range(2):
                q += [("in", O_GA + hf * 512, 512, 0, 8)]
            for hf in range(2):
                q += [("co", hf * 512, 512, 0, 8)]
            for hf in range(2):
                q += [("wo", hf * 512, 512, 0, 8)]
            for g in range(6):
                nblk = 4 if g < 5 else 2
                q += [("up", g * 512, nblk * 128, 0, 8), ("up", FFN + g * 512, nblk * 128, 0, 8)]
            if not is_halo:
                for hf in range(2):
                    for g3 in range(3):
                        q += [("dn", hf * 512, 512, g3 * 8, 8 if g3 < 2 else 6)]
            return q
        WSEQ = tile_wseq(True)
        for _ in range(n_main_tiles):
            WSEQ += tile_wseq(False)
        _wi = [0, 0]

        def next_w(*key):
            i = _wi[0]
            assert WSEQ[i] == key, (i, WSEQ[i], key)
            while _wi[1] < len(WSEQ) and _wi[1] <= i + NSLOT - 2:
                k = WSEQ[_wi[1]]
                load_wb(_wi[1] % NSLOT, wb[k[0]], k[1], k[2], r0=k[3], nk=k[4])
                _wi[1] += 1
            _wi[0] += 1
            return i % NSLOT

        def norm_a(srcs, ums):
            n = len(srcs)
            for i, (ap, bf) in enumerate(srcs):
                um, umB = ums[i]
                S.emit(A, lambda e, ap=ap, i=i, um=um: e.activation(out=um, in_=ap, func=AF.Square, accum_out=ss[:, i:i + 1]),
                       reads=[bf], writes=list(umB) + [ssB])
            S.emit(A, lambda e: e.activation(out=ss[:, 4:4 + n], in_=ss[:, 0:n], func=AF.Ln, scale=1.0 / D, bias=EPS), reads=[ssB], writes=[ssB])
            S.emit(A, lambda e: e.activation(out=ss[:, 4:4 + n], in_=ss[:, 4:4 + n], func=AF.Exp, scale=-0.5), reads=[ssB], writes=[ssB])
            for i, (ap, bf) in enumerate(srcs):
                um, umB = ums[i]
                if i % 2 == 0:
                    S.emit(V, lambda e, ap=ap, i=i, um=um: e.tensor_scalar(out=um, in0=ap, scalar1=ss[:, 4 + i:5 + i], scalar2=None, op0=ALU.mult),
                           reads=[bf, ssB], writes=list(umB))
                else:
                    S.emit(A, lambda e, ap=ap, i=i, um=um: e.activation(out=um, in_=ap, func=AF.Identity, scale=ss[:, 4 + i:5 + i]),
                           reads=[bf, ssB], writes=list(umB))

        def norm_b(ums, gT_, s0):
            for i, (um, umB) in enumerate(ums):
                pT, pTB = npT()
                for kc in range(8):
                    S.emit(T_, lambda e, kc=kc, um=um, pT=pT: e.transpose(out=pT[:, kc * 128:(kc + 1) * 128], in_=um[:, kc * 128:(kc + 1) * 128], identity=identb[:]),
                           reads=list(umB) + [identbB], writes=[pTB])
                s = s0 + i
                for kc in range(8):
                    S.emit(V, lambda e, kc=kc, s=s, pT=pT: e.tensor_scalar(out=uT[:, kc, s * 128:(s + 1) * 128], in0=pT[:, kc * 128:(kc + 1) * 128],
                                                                         scalar1=gT_[:, kc:kc + 1], scalar2=None, op0=ALU.mult),
                           reads=[pTB, vecB], writes=[uTB])

        def um_std(i):
            return (utms[i][:], [utmBs[i]])

        def norm_stage(srcs, gT_, s0):
            for c in range(0, len(srcs), 2):
                ums = [um_std(i) for i in range(min(2, len(srcs) - c))]
                norm_a(srcs[c:c + 2], ums)
                norm_b(ums, gT_, s0 + c)

        def load_x(row0, i):
            S.dma(SY, xs[i][:], x[row0:row0 + 128, :], writes=[xsB[i]], track=xsB[i])

        def proj_T(slot_i, s, ncols=512):
            ps, psB = nb()
            for kc in range(8):
                S.emit(T_, lambda e, kc=kc, ps=ps: e.matmul(ps[:, 0:ncols], lhsT=uT[:, kc, s * 128:(s + 1) * 128], rhs=slots[slot_i][:, kc, 0:ncols],
                                                           start=(kc == 0), stop=(kc == 7)),
                       reads=[uTB, slotB[slot_i]], writes=[psB])
            return ps, psB

        def proj_F(w_ap_fn, wB, rhs, rhsB, T, nk=8, M=128):
            ps, psB = nb()
            for kc in range(nk):
                S.emit(T_, lambda e, kc=kc, ps=ps: e.matmul(ps[0:M, 0:T], lhsT=w_ap_fn(kc), rhs=rhs[:, kc, 0:T],
                                                           start=(kc == 0), stop=(kc == nk - 1)),
                       reads=list(wB) + list(rhsB), writes=[psB])
            return ps, psB

        def gate_stage(T, NS, main, gsl, p=0):
            gate_g1(T, NS, gsl)
            gate_g2(NS, main, p)

        def gate_g1(T, NS, gsl):
            ps, psB = proj_F(lambda kc: slots[gsl][:, kc, 0:128], [slotB[gsl]], uT, [uTB], T)
            S.emit(A, lambda e, ps=ps: e.activation(out=glowT[:, 0:T], in_=ps[:, 0:T], func=AF.Copy), reads=[psB], writes=[glowTB])
            dbg_tap("glowT", glowT[0:16, :], [glowTB]); dbg_tap("wgu", wgu[:], []); dbg_tap("bgrow", bgrow[:], [])
            pre = []
            for s in range(NS):
                ps, psB = nb()
                S.emit(T_, lambda e, ps=ps, s=s: e.matmul(ps[:], lhsT=glowT[0:16, s * 128:(s + 1) * 128], rhs=wgu[:], start=True, stop=False),
                       reads=[glowTB, vecB], writes=[psB])
                S.emit(T_, lambda e, ps=ps: e.matmul(ps[:], lhsT=ones_row[:], rhs=bgrow[:], start=False, stop=True),
                       reads=[constB, vecB], writes=[psB])
                pre.append((ps, psB))
            for s in range(NS):
                ps, psB = pre[s]
                tm, tmB = ntmp()
                S.emit(A, lambda e, ps=ps, tm=tm: e.activation(out=tm[:, 0:512], in_=ps[:], func=AF.Exp, scale=-1.0), reads=[psB], writes=[tmB])
                if s == 0:
                    dbg_tap("expn", tm[:, 0:512], [tmB])
                S.emit(A, lambda e, tm=tm, s=s: e.activation(out=lbuf[:, s, :], in_=tm[:, 0:512], func=AF.Ln, bias=1.0), reads=[tmB], writes=[lB[s]])

        def gate_g2(NS, main, p=0):
            dec, decB = decs[p], decBs[p]
            for s in range(NS):
                ps, psB = nb()
                for h in range(4):
                    S.emit(T_, lambda e, ps=ps, s=s, h=h: e.matmul(ps[:, h:h + 1], lhsT=lbuf[:, s, h * 128:(h + 1) * 128], rhs=negc[:], start=True, stop=True),
                           reads=[lB[s], constB], writes=[psB])
                S.emit(A, lambda e, ps=ps, s=s: e.activation(out=dec[:, s, 0:4], in_=ps[:, 0:4], func=AF.Exp), reads=[psB], writes=[decB[s]])
                if main:
                    ps, psB = nb()
                    for h in range(4):
                        S.emit(T_, lambda e, ps=ps, s=s, h=h: e.matmul(ps[:, h * 128:(h + 1) * 128], lhsT=lbuf[:, s, h * 128:(h + 1) * 128], rhs=tri_inc[:], start=True, stop=True),
                               reads=[lB[s], constB], writes=[psB])
                    psv = ps[:].rearrange("p (h t) -> p h t", h=4)
                    S.emit(A, lambda e, psv=psv, s=s: e.activation(out=Epl[:, :, s * 128:(s + 1) * 128], in_=psv, func=AF.Exp), reads=[psB], writes=[EB[s]])
                    S.emit(A, lambda e, psv=psv, s=s: e.activation(out=Emi[:, :, s * 128:(s + 1) * 128], in_=psv, func=AF.Exp, scale=-1.0), reads=[psB], writes=[EB[s]])

        def kv_stage(NS, main, kslot_fn, vslot_fns, after_k=None):
            kslot = kslot_fn()
            for s in range(NS):
                ps, psB = nb()
                tri = tri_end
                S.emit(T_, lambda e, ps=ps, s=s, tri=tri: e.matmul(ps[:], lhsT=tri[:], rhs=lbuf[:, s, :], start=True, stop=True),
                       reads=[lB[s], constB], writes=[psB])
                tm, tmB = ntmp()
                S.emit(A, lambda e, ps=ps, tm=tm: e.activation(out=tm[:, 0:512], in_=ps[:], func=AF.Exp), reads=[psB], writes=[tmB])
                ps, psB = proj_T(kslot, s)
                S.emit(V, lambda e, ps=ps, tm=tm, s=s: e.tensor_tensor(out=kend[:, s, :], in0=ps[:], in1=tm[:, 0:512], op=ALU.mult),
                       reads=[psB, tmB], writes=[kendB[s]])
            if after_k is not None:
                after_k(kslot)
            for hf in range(2):
                vs = vslot_fns[hf]()
                for s in range(NS):
                    ps, psB = proj_T(vs, s)
                    S.emit(A, lambda e, ps=ps, s=s, hf=hf: e.activation(out=vtm[:, s, hf * 512:(hf + 1) * 512], in_=ps[:], func=AF.Copy),
                           reads=[psB], writes=[vtmB[s]])

        def state_sub(s, want_bf=None, p=0):
            dec, decB = decs[p], decBs[p]
            pk = [nb(), nb()]
            for h in range(4):
                ps, psB = pk[h // 2]
                S.emit(T_, lambda e, ps=ps, h=h: e.matmul(ps[:, (h % 2) * 256:(h % 2) * 256 + 256], lhsT=kend[:, s, h * 128:(h + 1) * 128],
                                                         rhs=vtm[:, s, h * 256:(h + 1) * 256], start=True, stop=True),
                       reads=[kendB[s], vtmB[s]], writes=[psB])
            for h in range(4):
                ps, psB = pk[h // 2]
                S.emit(V, lambda e, ps=ps, h=h: e.scalar_tensor_tensor(out=Sst[:, h * 256:(h + 1) * 256], in0=Sst[:, h * 256:(h + 1) * 256],
                                                                      scalar=dec[:, s, h:h + 1], in1=ps[:, (h % 2) * 256:(h % 2) * 256 + 256],
                                                                      op0=ALU.mult, op1=ALU.add),
                       reads=[SstB, decB[s], psB], writes=[SstB])
            if want_bf is not None:
                S.emit(A, lambda e: e.activation(out=Sbf[want_bf][:], in_=Sst[:], func=AF.Copy), reads=[SstB], writes=[SbfB[want_bf]])

        hist_slots = None
        if n_hist_tiles > 0 or True:
            load_w(0, w_in, O_K, 512); load_w(1, w_in, O_V, 512); load_w(2, w_in, O_V + 512, 512); load_w(3, w_in, O_GL, 512)
            convert_weights()
        hums = [um_std(0), um_std(1), (on[:], [onB]), (Sbf[1][:], [SbfB[1]])]

        def hist_na(t):
            for pr in range(2):
                for i in range(2):
                    load_x(t * 512 + (pr * 2 + i) * 128, i)
                norm_a([(xs[i][:], xsB[i]) for i in range(2)], hums[pr * 2:pr * 2 + 2])

        def hist_state(t):
            for s4 in range(4):
                state_sub(s4, p=t % 2)

        if n_hist_tiles > 0:
            hist_na(0)
            norm_b(hums, g1T, 0)
            gate_stage(512, 4, False, 3, p=0)
        for t in range(n_hist_tiles):
            if t + 1 < n_hist_tiles:
                hist_na(t + 1)
            build_diags(2)
            kv_stage(4, False, lambda: 0, (lambda: 1, lambda: 2))
            if t + 1 < n_hist_tiles:
                norm_b(hums, g1T, 0)
                gate_g1(512, 4, 3)
                hist_state(t)
                gate_g2(4, False, (t + 1) % 2)
            else:
                hist_state(t)
        build_diags(24)
        S.emit(A, lambda e: e.activation(out=Sbf[0][:], in_=Sst[:], func=AF.Copy), reads=[SstB], writes=[SbfB[0]])
        sbf_cur = [0]

        dbg_tap("S_hist", Sst[:], [SstB], force=True)

        def main_tile(row0, T, is_halo, out_row0, prefetched=False, next_row0=None):
            NS = T // 128
            pf_ums = [um_std(0), um_std(1),
                      (actT[:, 0:2, :].rearrange("p k t -> p (k t)"), actTB[0:2]), (actT[:, 2:4, :].rearrange("p k t -> p (k t)"), actTB[2:4])]
            if not prefetched:
                for pr in range((NS + 1) // 2):
                    nn = min(2, NS - pr * 2)
                    ums = [um_std(i) for i in range(nn)]
                    for i in range(nn):
                        load_x(row0 + (pr * 2 + i) * 128, i)
                    norm_a([(xs[i][:], xsB[i]) for i in range(nn)], ums)
                    norm_b(ums, g1T, pr * 2)

            def transposes_next():
                if next_row0 is not None:
                    norm_b(pf_ums[0:2], g1T, 0)
                    norm_b(pf_ums[2:4], g1T, 2)

            def prefetch_next():
                if next_row0 is not None:
                    for pr in range(2):
                        for i in range(2):
                            load_x(next_row0 + (pr * 2 + i) * 128, i)
                        norm_a([(xs[i][:], xsB[i]) for i in range(2)], pf_ums[pr * 2:pr * 2 + 2])
            dbg_tap("uT", uT[:], [uTB])
            gate_stage(T, NS, True, next_w("in", O_GL, 512, 0, 8))
            def k_feature_major(ks):
                for h in range(4):
                    ps, psB = proj_F(lambda kc, h=h: slots[ks][:, kc, h * 128:(h + 1) * 128], [slotB[ks]], uT, [uTB], T)
                    S.emit(V, lambda e, ps=ps, h=h: e.tensor_tensor(out=kT[:, h, 0:T], in0=ps[:, 0:T], in1=Emi[:, h, 0:T], op=ALU.mult),
                           reads=[psB] + EB[0:NS], writes=[kTB[h]])
            kv_stage(NS, True, lambda: next_w("in", O_K, 512, 0, 8),
                     (lambda: next_w("in", O_V, 512, 0, 8), lambda: next_w("in", O_V + 512, 512, 0, 8)), after_k=k_feature_major)
            qs = next_w("in", O_Q, 512, 0, 8)
            for h in range(4):
                ps, psB = proj_F(lambda kc, h=h: slots[qs][:, kc, h * 128:(h + 1) * 128], [slotB[qs]], uT, [uTB], T)
                S.emit(V, lambda e, ps=ps, h=h: e.scalar_tensor_tensor(out=qT[:, h, 0:T], in0=ps[:, 0:T], scalar=128.0 ** -0.5, in1=Epl[:, h, 0:T],
                                                                      op0=ALU.mult, op1=ALU.mult),
                       reads=[psB] + EB[0:NS], writes=[qTB[h]])
            for hf in range(2):
                rs = next_w("in", O_R + hf * 512, 512, 0, 8)
                for j in range(4):
                    kc = hf * 4 + j
                    ps, psB = proj_F(lambda kk, j=j, rs=rs: slots[rs][:, kk, j * 128:(j + 1) * 128], [slotB[rs]], uT, [uTB], T)
                    S.emit(A, lambda e, ps=ps, kc=kc: e.activation(out=silur[:, kc, 0:T], in_=ps[:, 0:T], func=AF.Silu), reads=[psB], writes=[silurB[kc]])
            dbg_tap("l", lbuf[:], lB); dbg_tap("kT", kT[:], kTB); dbg_tap("qT", qT[:], qTB); dbg_tap("vtm", vtm[:], vtmB)
            dbg_tap("kend", kend[:], kendB); dbg_tap("silur", silur[:], silurB); pass
            for hf in range(2):
                sa = next_w("in", O_A + hf * 512, 512, 0, 8)
                sg = next_w("in", O_BG + hf * 512, 512, 0, 8)
                for j in range(4):
                    kc = hf * 4 + j
                    psa, psaB = proj_F(lambda kk, j=j, sa=sa: slots[sa][:, kk, j * 128:(j + 1) * 128], [slotB[sa]], uT, [uTB], T)
                    psg, psgB = proj_F(lambda kk, j=j, sg=sg: slots[sg][:, kk, j * 128:(j + 1) * 128], [slotB[sg]], uT, [uTB], T)
                    tm, tmB = ntmp()
                    S.emit(A, lambda e, psg=psg, tm=tm: e.activation(out=tm[:, 0:T], in_=psg[:, 0:T], func=AF.Sigmoid), reads=[psgB], writes=[tmB])
                    S.emit(V, lambda e, psa=psa, tm=tm, kc=kc: e.tensor_tensor(out=cin[:, kc, 30:30 + T], in0=psa[:, 0:T], in1=tm[:, 0:T], op=ALU.mult),
                           reads=[psaB, tmB], writes=[cinB[kc]])
            dgtrk = [Buf(f"dgl{i}") for i in range(3)] if not hasattr(main_tile, "_dgtrk") else main_tile._dgtrk
            main_tile._dgtrk = dgtrk

            def load_part(g):
                if g >= 24:
                    return
                j0, nj = PARTS[g % 3]
                bi = g % 3
                S.dma(SY, dg[bi][:, 0:nj, :], dgd[g][:, 0:nj * 128].rearrange("p (j i) -> p j i", i=128), reads=[dgdB], writes=[dgB[bi]], track=dgtrk[bi])

            def conv_block(kc):
                psc, pscB = nb()
                for part in range(3):
                    g = kc * 3 + part
                    j0, nj = PARTS[part]
                    bi = g % 3
                    load_part(g + 2)
                    for jj in range(nj):
                        j = j0 + jj
                        S.emit(T_, lambda e, bi=bi, jj=jj, j=j: e.matmul(psc[:, 0:T], lhsT=dg[bi][:, jj, :], rhs=cin[:, kc, j:j + T], start=(j == 0), stop=(j == 30)),
                               reads=[dgB[bi], cinB[kc]], writes=[pscB])
                S.emit(A, lambda e: e.activation(out=cacc[:, kc, 0:T], in_=psc[:, 0:T], func=AF.Identity, bias=cbT[:, kc:kc + 1]),
                       reads=[pscB], writes=[arB[kc]])
                S.emit(A, lambda e: e.activation(out=cin[:, kc, 0:30], in_=cin[:, kc, T:T + 30], func=AF.Copy), reads=[cinB[kc]], writes=[cinB[kc]])

            gla_po = {}

            def gla_a(s):
                tok = slice(s * 128, (s + 1) * 128)
                for h in range(4):
                    ps, psB = nb()
                    S.emit(T_, lambda e, ps=ps, h=h: e.matmul(ps[:, 0:128], lhsT=kT[:, h, tok], rhs=qT[:, h, tok], start=True, stop=True),
                           reads=[kTB[h], qTB[h]], writes=[psB])
                    S.emit(V, lambda e, ps=ps, h=h: e.tensor_tensor(out=attm[:, h, :], in0=ps[:, 0:128], in1=cmask[:], op=ALU.mult),
                           reads=[psB, constB], writes=[attmB[h]])

            def gla_b(s):
                c0 = sbf_cur[0]; c1 = 1 - c0
                po = [nb(), nb()]
                gla_po[s] = po
                for h in range(4):
                    ps, psB = po[h // 2]
                    cols = slice((h % 2) * 256, (h % 2) * 256 + 256)
                    hc = slice(h * 256, (h + 1) * 256)
                    S.emit(T_, lambda e, ps=ps, h=h, cols=cols, hc=hc: e.matmul(ps[:, cols], lhsT=attm[:, h, :], rhs=vtm[:, s, hc], start=True, stop=False),
                           reads=[attmB[h], vtmB[s]], writes=[psB])
                    S.emit(T_, lambda e, ps=ps, h=h, cols=cols, hc=hc: e.matmul(ps[:, cols], lhsT=qT[:, h, s * 128:(s + 1) * 128], rhs=Sbf[c0][:, hc], start=False, stop=True),
                           reads=[qTB[h], SbfB[c0]], writes=[psB])
                state_sub(s, want_bf=c1)
                sbf_cur[0] = c1

            def gla_c(s):
                tok = slice(s * 128, (s + 1) * 128)
                po = gla_po[s]
                for h in range(4):
                    ps, psB = po[h // 2]
                    cols = slice((h % 2) * 256, (h % 2) * 256 + 256)
                    S.emit(A, lambda e, ps=ps, h=h, cols=cols: e.activation(out=on[:, h * 256:(h + 1) * 256], in_=ps[:, cols], func=AF.Square, accum_out=oss[:, h:h + 1]),
                           reads=[psB], writes=[onB, ossB])
                S.emit(A, lambda e: e.activation(out=oss[:, 4:8], in_=oss[:, 0:4], func=AF.Ln, scale=1.0 / 256, bias=EPS), reads=[ossB], writes=[ossB])
                S.emit(A, lambda e: e.activation(out=oss[:, 4:8], in_=oss[:, 4:8], func=AF.Exp, scale=-0.5), reads=[ossB], writes=[ossB])
                for h in range(4):
                    ps, psB = po[h // 2]
                    cols = slice((h % 2) * 256, (h % 2) * 256 + 256)
                    S.emit(A, lambda e, ps=ps, h=h, cols=cols: e.activation(out=on[:, h * 256:(h + 1) * 256], in_=ps[:, cols], func=AF.Identity, scale=oss[:, 4 + h:5 + h]),
                           reads=[psB, ossB], writes=[onB])

            def gla_c2(s):
                tok = slice(s * 128, (s + 1) * 128)
                pT, pTB = npT()
                for kc in range(8):
                    S.emit(T_, lambda e, kc=kc, pT=pT: e.transpose(out=pT[:, kc * 128:(kc + 1) * 128], in_=on[:, kc * 128:(kc + 1) * 128], identity=identb[:]),
                           reads=[onB, identbB], writes=[pTB])
                for kc in range(8):
                    S.emit(V, lambda e, kc=kc, pT=pT: e.scalar_tensor_tensor(out=actT[:, kc, tok], in0=pT[:, kc * 128:(kc + 1) * 128], scalar=gnT[:, kc % 2:kc % 2 + 1],
                                                                      in1=silur[:, kc, tok], op0=ALU.mult, op1=ALU.mult),
                           reads=[pTB, vecB, silurB[kc]], writes=[actTB[kc]])
            load_part(0)
            load_part(1)
            kc_next = 0
            for s in range(NS):
                gla_a(s)
                conv_block(kc_next); kc_next += 1
                gla_b(s)
                gla_c(s)
                conv_block(kc_next); kc_next += 1
                gla_c2(s)
            while kc_next < 8:
                conv_block(kc_next); kc_next += 1
            dbg_tap("cin", cin[:, :, 30:542], cinB); dbg_tap("cacc", cacc, arB); dbg_tap("oT", actT[:], actTB)
            psm = pTs[0].bitcast(F32); psmB = pTBs[0]
            psq = pTs[1].bitcast(F32); psqB = pTBs[1]
            for hf in range(2):
                sw = next_w("go", hf * 512, 512, 0, 8)
                sg = next_w("in", O_GB + hf * 512, 512, 0, 8)
                for j in range(4):
                    ob = hf * 4 + j
                    psy, psyB = proj_F(lambda kk, j=j, sw=sw: slots[sw][:, kk, j * 128:(j + 1) * 128], [slotB[sw]], actT, actTB, T)
                    psg, psgB = proj_F(lambda kk, j=j, sg=sg: slots[sg][:, kk, j * 128:(j + 1) * 128], [slotB[sg]], uT, [uTB], T)
                    tq, tqB = ntmp()
                    S.emit(A, lambda e, ob=ob, tq=tq: e.activation(out=tq[:, 0:T], in_=cacc[:, ob, 0:T], func=AF.Square), reads=[arB[ob]], writes=[tqB])
                    tm, tmB = ntmp()
                    S.emit(A, lambda e, psg=psg, tm=tm: e.activation(out=tm[:, 0:T], in_=psg[:, 0:T], func=AF.Sigmoid), reads=[psgB], writes=[tmB])
                    S.emit(V, lambda e, psy=psy, tm=tm, ob=ob: e.tensor_tensor(out=m1[:, ob, 0:T], in0=psy[:, 0:T], in1=tm[:, 0:T], op=ALU.mult),
                           reads=[psyB, tmB], writes=[m1B[ob]])
                    S.emit(T_, lambda e, ob=ob: e.matmul(psm[:, 0:T], lhsT=onesf[:], rhs=cacc[:, ob, 0:T], start=(ob == 0), stop=(ob == 7)),
                           reads=[constB, arB[ob]], writes=[psmB])
                    S.emit(T_, lambda e, ob=ob, tq=tq: e.matmul(psq[:, 0:T], lhsT=onesf[:], rhs=tq[:, 0:T], start=(ob == 0), stop=(ob == 7)),
                           reads=[constB, tqB], writes=[psqB])
            mean, rstd = lnst[0], lnst[1]
            meanB, rstdB = lnstB
            S.emit(A, lambda e: e.activation(out=mean[:, 0:T], in_=psm[:, 0:T], func=AF.Identity, scale=1.0 / D), reads=[psmB], writes=[meanB])
            S.emit(V, lambda e: e.tensor_tensor(out=rstd[:, 0:T], in0=mean[:, 0:T], in1=mean[:, 0:T], op=ALU.mult), reads=[meanB], writes=[rstdB])
            S.emit(V, lambda e: e.scalar_tensor_tensor(out=rstd[:, 0:T], in0=psq[:, 0:T], scalar=1.0 / D, in1=rstd[:, 0:T], op0=ALU.mult, op1=ALU.subtract),
                   reads=[psqB, rstdB], writes=[rstdB])
            S.emit(A, lambda e: e.activation(out=rstd[:, 0:T], in_=rstd[:, 0:T], func=AF.Ln, bias=EPS), reads=[rstdB], writes=[rstdB])
            S.emit(A, lambda e: e.activation(out=rstd[:, 0:T], in_=rstd[:, 0:T], func=AF.Exp, scale=-0.5), reads=[rstdB], writes=[rstdB])
            nmr, nmrB = mean, meanB
            S.emit(V, lambda e: e.scalar_tensor_tensor(out=mean[:, 0:T], in0=mean[:, 0:T], scalar=-1.0, in1=rstd[:, 0:T], op0=ALU.mult, op1=ALU.mult),
                   reads=[meanB, rstdB], writes=[meanB])
            for hf in range(2):
                sg = next_w("in", O_GA + hf * 512, 512, 0, 8)
                for j in range(4):
                    kc = hf * 4 + j
                    psg, psgB = proj_F(lambda kk, j=j, sg=sg: slots[sg][:, kk, j * 128:(j + 1) * 128], [slotB[sg]], uT, [uTB], T)
                    S.emit(A, lambda e, psg=psg, kc=kc: e.activation(out=cin[:, kc, 30:30 + T], in_=psg[:, 0:T], func=AF.Sigmoid), reads=[psgB], writes=[cinB[kc]])
            for kc in range(8):
                tm, tmB = ntmp()
                S.emit(V, lambda e, kc=kc, tm=tm: e.tensor_tensor(out=tm[:, 0:T], in0=cacc[:, kc, 0:T], in1=rstd[:, 0:T], op=ALU.mult),
                       reads=[arB[kc], rstdB], writes=[tmB])
                S.emit(V, lambda e, tm=tm: e.tensor_tensor(out=tm[:, 0:T], in0=tm[:, 0:T], in1=nmr[:, 0:T], op=ALU.add), reads=[tmB, nmrB], writes=[tmB])
                S.emit(A, lambda e, kc=kc, tm=tm: e.activation(out=actT[:, kc, 0:T], in_=tm[:, 0:T], func=AF.Silu, scale=lngT[:, kc:kc + 1], bias=lnbT[:, kc:kc + 1]),
                       reads=[tmB, vecB], writes=[actTB[kc]])
            dbg_tap("cact", actT[:], actTB)
            for hf in range(2):
                sw = next_w("co", hf * 512, 512, 0, 8)
                for j in range(4):
                    ob = hf * 4 + j
                    psy, psyB = proj_F(lambda kk, j=j, sw=sw: slots[sw][:, kk, j * 128:(j + 1) * 128], [slotB[sw]], actT, actTB, T)
                    tm, tmB = ntmp()
                    S.emit(V, lambda e, psy=psy, tm=tm, ob=ob: e.tensor_tensor(out=tm[:, 0:T], in0=psy[:, 0:T], in1=cin[:, ob, 30:30 + T], op=ALU.mult),
                           reads=[psyB, cinB[ob]], writes=[tmB])
                    S.emit(V, lambda e, tm=tm, ob=ob: e.tensor_tensor(out=silur[:, ob, 0:T], in0=tm[:, 0:T], in1=m1[:, ob, 0:T], op=ALU.add),
                           reads=[tmB, m1B[ob]], writes=[silurB[ob]])
            dbg_tap("merged", silur[:], silurB)
            sws = [next_w("wo", 0, 512, 0, 8), next_w("wo", 512, 512, 0, 8)]

            def wo_sub(s):
                for hf in range(2):
                    sw = sws[hf]
                    ps, psB = nb()
                    for kc in range(8):
                        S.emit(T_, lambda e, kc=kc, ps=ps, sw=sw: e.matmul(ps[:], lhsT=silur[:, kc, s * 128:(s + 1) * 128], rhs=slots[sw][:, kc, :], start=(kc == 0), stop=(kc == 7)),
                               reads=[silurB[kc], slotB[sw]], writes=[psB])
                    xi = hf
                    S.dma(SY, xs[xi][:, 0:512], x[row0 + s * 128:row0 + (s + 1) * 128, hf * 512:(hf + 1) * 512], writes=[xsB[xi]], track=xsB[xi])
                    S.emit(V, lambda e, ps=ps, hf=hf, xi=xi: e.tensor_tensor(out=hbuf[:, s, hf * 512:(hf + 1) * 512], in0=ps[:], in1=xs[xi][:, 0:512], op=ALU.add),
                           reads=[psB, xsB[xi]], writes=[hB[s]])
            for s in range(NS):
                wo_sub(s)
                if s % 2 == 1 or s == NS - 1:
                    p0 = (s // 2) * 2
                    norm_a([(hbuf[:, q, :], hB[q]) for q in range(p0, s + 1)], pf_ums[p0:s + 1])
            dbg_tap("h1", hbuf[:], hB)
            for p0 in range(0, NS, 2):
                norm_b(pf_ums[p0:min(p0 + 2, NS)], g2T, p0)
            ngrp = 6
            for g in range(ngrp):
                nblk = 4 if g < 5 else 2
                sa = next_w("up", g * 512, nblk * 128, 0, 8)
                sbb = next_w("up", FFN + g * 512, nblk * 128, 0, 8)
                for j in range(nblk):
                    i = g * 4 + j
                    accs = []
                    for which, sl in ((0, sa), (1, sbb)):
                        blk = which * 22 + i
                        ps, psB = proj_F(lambda kk, j=j, sl=sl: slots[sl][:, kk, j * 128:(j + 1) * 128], [slotB[sl]], uT, [uTB], T)
                        zb, zbB = ntmp()
                        S.emit(A, lambda e, zb=zb, blk=blk: e.activation(out=zb[:, 0:2], in_=zhalo[:, blk, :], func=AF.Copy), reads=[zhB], writes=[zbB])
                        S.emit(A, lambda e, zb=zb, ps=ps: e.activation(out=zb[:, 2:2 + T], in_=ps[:, 0:T], func=AF.Copy), reads=[psB], writes=[zbB])
                        S.emit(A, lambda e, zb=zb, blk=blk: e.activation(out=zhalo[:, blk, :], in_=zb[:, T:T + 2], func=AF.Copy), reads=[zbB], writes=[zhB])
                        ac, acB = ntmp()
                        S.emit(A, lambda e, zb=zb, ac=ac, blk=blk: e.activation(out=ac[:, 0:T], in_=zb[:, 0:T], func=AF.Identity,
                                                                               scale=fwT[:, blk:blk + 1], bias=fbT[:, blk:blk + 1]),
                               reads=[zbB, vecB], writes=[acB])
                        for jj in (1, 2):
                            S.emit(V, lambda e, zb=zb, ac=ac, blk=blk, jj=jj: e.scalar_tensor_tensor(out=ac[:, 0:T], in0=zb[:, jj:jj + T], scalar=fwT[:, jj * 44 + blk:jj * 44 + blk + 1],
                                                                                                in1=ac[:, 0:T], op0=ALU.mult, op1=ALU.add),
                                   reads=[zbB, vecB, acB], writes=[acB])
                        accs.append((ac, acB))
                    if not is_halo:
                        (aa, aaB), (ab, abB) = accs
                        S.emit(A, lambda e, aa=aa: e.activation(out=aa[:, 0:T], in_=aa[:, 0:T], func=AF.Silu), reads=[aaB], writes=[aaB])
                        S.emit(V, lambda e, aa=aa, ab=ab, i=i: e.tensor_tensor(out=gT[:, i, 0:T], in0=aa[:, 0:T], in1=ab[:, 0:T], op=ALU.mult),
                               reads=[aaB, abB], writes=[arB[i % 8]])
            prefetch_next()
            if is_halo:
                S.emit(V, lambda e: e.tensor_scalar(out=zhalo[:], in0=zhalo[:], scalar1=hp[:, 0:1], scalar2=None, op0=ALU.mult), reads=[zhB, vecB], writes=[zhB])
                transposes_next()
                return
            dbg_tap("gT", gT, arB)
            for hf in range(2):
                pss = [nb() for _ in range(NS)]
                for g3 in range(3):
                    nk = 8 if g3 < 2 else 6
                    sw = next_w("dn", hf * 512, 512, g3 * 8, nk)
                    for s in range(NS):
                        ps, psB = pss[s]
                        for kk in range(nk):
                            i = g3 * 8 + kk
                            S.emit(T_, lambda e, ps=ps, s=s, kk=kk, i=i, sw=sw: e.matmul(ps[:], lhsT=gT[:, i, s * 128:(s + 1) * 128], rhs=slots[sw][:, kk, :],
                                                                                   start=(i == 0), stop=(i == 21)),
                                   reads=[arB[i % 8], slotB[sw]], writes=[psB])
                for s in range(NS):
                    ps, psB = pss[s]
                    S.emit(V, lambda e, ps=ps, s=s, hf=hf: e.tensor_tensor(out=hbuf[:, s, hf * 512:(hf + 1) * 512], in0=ps[:], in1=hbuf[:, s, hf * 512:(hf + 1) * 512], op=ALU.add),
                           reads=[psB, hB[s]], writes=[hB[s]])
            transposes_next()
            for s in range(NS):
                S.emit(A, lambda e, s=s: e.activation(out=on[:], in_=hbuf[:, s, :], func=AF.Square, accum_out=ss[:, s:s + 1]), reads=[hB[s]], writes=[onB, ssB])
            S.emit(A, lambda e: e.activation(out=ss[:, 4:8], in_=ss[:, 0:4], func=AF.Ln, scale=1.0 / D, bias=EPS), reads=[ssB], writes=[ssB])
            S.emit(A, lambda e: e.activation(out=ss[:, 4:8], in_=ss[:, 4:8], func=AF.Exp, scale=-0.5), reads=[ssB], writes=[ssB])
            for s in range(NS):
                xi = s % 2
                S.emit(V, lambda e, s=s, xi=xi: e.scalar_tensor_tensor(out=xs[xi][:], in0=hbuf[:, s, :], scalar=ss[:, 4 + s:5 + s], in1=fgB_t[:], op0=ALU.mult, op1=ALU.mult),
                       reads=[hB[s], ssB, vecB], writes=[xsB[xi]])
                S.dma(SY, out[out_row0 + s * 128:out_row0 + (s + 1) * 128, :], xs[xi][:], reads=[xsB[xi]], track=xsB[xi])

        main_tile(HIST, HALO, True, None, prefetched=False, next_row0=(HIST + HALO if n_main_tiles > 0 else None))
        for t in range(n_main_tiles):
            tapon[0] = (t == 0)
            main_tile(HIST + HALO + t * 512, 512, False, t * 512, prefetched=True,
                      next_row0=(HIST + HALO + (t + 1) * 512 if t + 1 < n_main_tiles else None))
        for i in range(2):
            S._wait(SY, ("d", xsB[i], xsB[i].dcount * 16))
        S.replay(block)
    return nc


def make_in_maps(inputs):
    x = np.asarray(inputs["x"], dtype=np.float32)
    sq = lambda k: np.ascontiguousarray(np.asarray(inputs[k], dtype=np.float32)[0])
    shared = {k: sq(k) for k in ("norm1_g", "w_in", "conv_dw_w", "conv_dw_b", "conv_ln_g", "conv_ln_b", "w_conv_out",
                                 "w_gate_up", "b_gate", "gla_norm_g", "w_gla_out", "w_o", "norm2_g", "w_ffn_up",
                                 "ffn_dw_w", "ffn_dw_b", "w_ffn_down")}
    shared["final_g"] = np.ascontiguousarray(np.asarray(inputs["final_g"], dtype=np.float32))
    in_maps = []
    for c in range(NCORES):
        b, r = divmod(c, 4)
        start = r * SEG
        xc = np.zeros((ROWS, D), np.float32)
        lo = start - (HIST + HALO)
        src_lo = max(lo, 0)
        xc[src_lo - lo:] = x[b, src_lo:start + SEG]
        m = dict(shared)
        m["x"] = xc
        m["hasprev"] = np.full((128, 1), 1.0 if r > 0 else 0.0, np.float32)
        in_maps.append(m)
    return in_maps


_NC_CACHE = {}


def kernel(**inputs):
    if "nc" not in _NC_CACHE:
        _NC_CACHE["nc"] = build_program()
    nc = _NC_CACHE["nc"]
    in_maps = make_in_maps(inputs)
    res = run_bass_kernel_spmd(nc, in_maps, core_ids=list(range(NCORES)))
    outp = np.zeros((2, 4 * SEG, D), np.float32)
    for c in range(NCORES):
        b, r = divmod(c, 4)
        outp[b, r * SEG:(r + 1) * SEG] = res.results[c]["out"]
    return outp
```

```python
import sys
import numpy as np
import concourse.bass as bass
import concourse.mybir as mybir
from concourse.bass_utils import run_bass_kernel_spmd
from contextlib import ExitStack

F32 = mybir.dt.float32
BF16 = mybir.dt.bfloat16
AF = mybir.ActivationFunctionType
ALU = mybir.AluOpType

D = 1024
IN_DIM = 7184
FFN = 2816
NCORES = 8
SEG = 2048
HIST = 6144
HALO = 128
ROWS = HIST + HALO + SEG
EPS = 1e-6
EPOCH = 30000

O_A, O_BG, O_Q, O_K, O_V, O_R, O_GL, O_GA, O_GB = 0, 1024, 2048, 2560, 3072, 4096, 5120, 5136, 6160


class Buf:
    __slots__ = ("name", "lastw", "readers", "sem", "dcount")

    def __init__(self, name):
        self.name = name
        self.lastw = None
        self.readers = {}
        self.sem = None
        self.dcount = 0


class Sched:
    ENGS = ("sync", "scalar", "vector", "gpsimd", "tensor")

    def __init__(self, nc, stack):
        self.nc = nc
        self.stack = stack
        self.ops = {e: [] for e in self.ENGS}
        self.count = {e: 0 for e in self.ENGS}
        self.sems = {e: [] for e in self.ENGS}
        self.waited = {e: {} for e in self.ENGS}
        self.same_engine_sync = {"scalar": True, "vector": True, "gpsimd": True, "tensor": False, "sync": False}

    def new_sem(self, name):
        return self.stack.enter_context(self.nc.semaphore(name))

    def eng_sem(self, e, epoch):
        while len(self.sems[e]) <= epoch:
            self.sems[e].append(self.new_sem(f"p_{e}_{len(self.sems[e])}"))
        return self.sems[e][epoch]

    def _wait(self, eng, tok):
        if tok[0] == "e":
            _, src, idx = tok
            if src == eng and not self.same_engine_sync[eng]:
                return
            epoch, val = divmod(idx - 1, EPOCH)
            val += 1
            key = ("e", src, epoch)
            sem = self.eng_sem(src, epoch)
        else:
            _, buf, val = tok
            key = ("d", id(buf))
            sem = buf.sem
        if self.waited[eng].get(key, 0) >= val:
            return
        self.waited[eng][key] = val
        self.ops[eng].append(lambda e, sem=sem, val=val: e.wait_ge(sem, val))

    def _deps(self, eng, reads, writes):
        for b in reads:
            if b.lastw is not None:
                self._wait(eng, b.lastw)
        for b in writes:
            if b.lastw is not None and not (b.lastw[0] == "e" and b.lastw[1] == eng):
                self._wait(eng, b.lastw)
            for t in b.readers.values():
                if not (t[0] == "e" and t[1] == eng):
                    self._wait(eng, t)

    def emit(self, eng, fn, reads=(), writes=()):
        self._deps(eng, reads, writes)
        self.count[eng] += 1
        idx = self.count[eng]
        sem = self.eng_sem(eng, (idx - 1) // EPOCH)
        self.ops[eng].append(lambda e, fn=fn, sem=sem: fn(e).then_inc(sem, 1))
        tok = ("e", eng, idx)
        for b in writes:
            b.lastw = tok
            b.readers = {}
        for b in reads:
            b.readers[eng] = tok
        return tok

    def dma(self, eng, out, in_, reads=(), writes=(), track=None, **kw):
        self._deps(eng, reads, writes)
        if track.sem is None:
            track.sem = self.new_sem("d_" + track.name)
        track.dcount += 1
        val = track.dcount * 16
        sem = track.sem
        self.ops[eng].append(
            lambda e, sem=sem, out=out, in_=in_, kw=kw: e.dma_start(out=out, in_=in_, **kw).then_inc(sem, 16))
        tok = ("d", track, val)
        for b in writes:
            b.lastw = tok
            b.readers = {}
        for b in reads:
            b.readers["dma_" + track.name] = tok
        return tok

    def replay(self, block):
        for e in self.ENGS:
            ops = self.ops[e]

            def body(engine, ops=ops):
                for f in ops:
                    f(engine)
            getattr(block, e)(body)


def build_program(n_hist_tiles=HIST // 512, n_main_tiles=SEG // 512, dbg=None):
    nc = bass.Bass("TRN2", target_bir_lowering=False)
    dt_in = lambda name, shape: nc.dram_tensor(name, shape, F32, kind="ExternalInput").ap()
    x = dt_in("x", [ROWS, D])
    hasprev = dt_in("hasprev", [128, 1])
    norm1_g = dt_in("norm1_g", [D]); w_in = dt_in("w_in", [D, IN_DIM])
    conv_dw_w = dt_in("conv_dw_w", [31, D]); conv_dw_b = dt_in("conv_dw_b", [D])
    conv_ln_g = dt_in("conv_ln_g", [D]); conv_ln_b = dt_in("conv_ln_b", [D])
    w_conv_out = dt_in("w_conv_out", [D, D]); w_gate_up = dt_in("w_gate_up", [16, 512])
    b_gate = dt_in("b_gate", [512]); gla_norm_g = dt_in("gla_norm_g", [256])
    w_gla_out = dt_in("w_gla_out", [D, D]); w_o = dt_in("w_o", [D, D])
    norm2_g = dt_in("norm2_g", [D]); w_ffn_up = dt_in("w_ffn_up", [D, 2 * FFN])
    ffn_dw_w = dt_in("ffn_dw_w", [3, 2 * FFN]); ffn_dw_b = dt_in("ffn_dw_b", [2 * FFN])
    w_ffn_down = dt_in("w_ffn_down", [FFN, D]); final_g = dt_in("final_g", [D])
    out = nc.dram_tensor("out", [SEG, D], F32, kind="ExternalOutput").ap()
    dgd = nc.dram_tensor("dgd", [24, 128, 11 * 128], BF16).ap()
    wb = {"in": nc.dram_tensor("wb_in", [D, IN_DIM], BF16).ap(), "co": nc.dram_tensor("wb_co", [D, D], BF16).ap(),
          "go": nc.dram_tensor("wb_go", [D, D], BF16).ap(), "wo": nc.dram_tensor("wb_wo", [D, D], BF16).ap(),
          "up": nc.dram_tensor("wb_up", [D, 2 * FFN], BF16).ap(), "dn": nc.dram_tensor("wb_dn", [FFN, D], BF16).ap()}
    dbg_out = {}
    if dbg:
        for name, shape in dbg.items():
            dbg_out[name] = nc.dram_tensor("dbg_" + name, list(shape), F32, kind="ExternalOutput").ap()

    with ExitStack() as stack:
        S = Sched(nc, stack)
        _n = [0]

        def sb(shape, dt, name=None):
            _n[0] += 1
            return stack.enter_context(nc.sbuf_tensor(name or f"t{_n[0]}", list(shape), dt))

        identf = sb([128, 128], F32); identfB = Buf("identf")
        identb = sb([128, 128], BF16); identbB = Buf("identb")
        tri_inc = sb([128, 128], F32); tri_end = sb([128, 128], F32); cmask = sb([128, 128], F32)
        onesf = sb([128, 128], F32); ones_row = sb([1, 128], BF16)
        constB = Buf("consts")
        negc = sb([128, 1], F32)
        stage = sb([128, 128], F32); stageB = Buf("stage")
        g1T = sb([128, 8], F32); g2T = sb([128, 8], F32); cbT = sb([128, 8], F32)
        lngT = sb([128, 8], F32); lnbT = sb([128, 8], F32); fbT = sb([128, 44], F32)
        gnT = sb([128, 2], F32); cwT = sb([128, 248], F32); fwT = sb([128, 132], F32)
        fgB_t = sb([128, D], F32)
        bgrow = sb([1, 512], BF16); wgu = sb([16, 512], BF16)
        hp = sb([128, 1], F32)
        vecB = Buf("vecs")
        banks = [stack.enter_context(nc.psum_tensor(f"ps{i}", [128, 512], F32)) for i in range(6)]
        bankB = [Buf(f"ps{i}") for i in range(6)]
        pTs = [stack.enter_context(nc.psum_tensor(f"pT{i}", [128, 1024], BF16)) for i in range(2)]; pTBs = [Buf(f"pT{i}") for i in range(2)]
        _pt = [0]

        def npT():
            i = _pt[0] % 2
            _pt[0] += 1
            return pTs[i], pTBs[i]
        _bk = [0]

        def nb():
            i = _bk[0] % 6
            _bk[0] += 1
            return banks[i], bankB[i]

        NSLOT = 4
        slots = [sb([128, 8, 512], BF16, f"slot{i}") for i in range(NSLOT)]
        slotB = [Buf(f"slot{i}") for i in range(NSLOT)]
        xs = [sb([128, D], F32, f"xs{i}") for i in range(2)]; xsB = [Buf(f"xs{i}") for i in range(2)]
        utms = [sb([128, D], BF16, f"utm{i}") for i in range(2)]; utmBs = [Buf(f"utm{i}") for i in range(2)]
        uT = sb([128, 8, 512], BF16); uTB = Buf("uT")
        glowT = sb([128, 512], BF16); glowTB = Buf("glowT")
        lbuf = sb([128, 4, 512], F32); lB = [Buf(f"l{i}") for i in range(4)]
        NTMP = 6
        tmps = [sb([128, 516], F32, f"tmp{i}") for i in range(NTMP)]; tmpB = [Buf(f"tmp{i}") for i in range(NTMP)]
        _tk = [0]

        def ntmp():
            i = _tk[0] % NTMP
            _tk[0] += 1
            return tmps[i], tmpB[i]
        Epl = sb([128, 4, 512], BF16); Emi = sb([128, 4, 512], BF16); EB = [Buf(f"E{i}") for i in range(4)]
        decs = [sb([128, 4, 8], F32, f"dec{p}") for p in range(2)]; decBs = [[Buf(f"dec{p}_{i}") for i in range(4)] for p in range(2)]
        dec = decs[0]; decB = decBs[0]
        kend = sb([128, 4, 512], BF16); kendB = [Buf(f"kend{i}") for i in range(4)]
        vtm = sb([128, 4, D], BF16); vtmB = [Buf(f"v{i}") for i in range(4)]
        kT = sb([128, 4, 512], BF16); kTB = [Buf(f"kT{i}") for i in range(4)]
        qT = sb([128, 4, 512], BF16); qTB = [Buf(f"qT{i}") for i in range(4)]
        silur = sb([128, 8, 512], BF16); silurB = [Buf(f"sr{i}") for i in range(8)]
        attm = sb([128, 4, 128], BF16); attmB = [Buf(f"att{i}") for i in range(4)]
        Sst = sb([128, D], F32); SstB = Buf("S")
        Sbf = [sb([128, D], BF16, f"Sbf{i}") for i in range(2)]; SbfB = [Buf(f"Sbf{i}") for i in range(2)]
        on = sb([128, D], BF16); onB = Buf("on")
        oss = sb([128, 8], F32); ossB = Buf("oss")
        actT = sb([128, 8, 512], BF16); actTB = [Buf(f"actT{i}") for i in range(8)]
        cin = sb([128, 8, 542], BF16); cinB = [Buf(f"cin{i}") for i in range(8)]
        arena = sb([128, 22 * 256], F32, "arena")
        cacc = arena[:, 0:4096].rearrange("p (k t) -> p k t", k=8)
        gT = arena.bitcast(BF16).rearrange("p (k t) -> p k t", k=22)
        arB = [Buf(f"ar{i}") for i in range(8)]
        m1 = sb([128, 8, 512], BF16); m1B = [Buf(f"m1{i}") for i in range(8)]
        hbuf = sb([128, 4, D], F32); hB = [Buf(f"h{i}") for i in range(4)]
        zhalo = sb([128, 44, 2], F32); zhB = Buf("zhalo")
        ss = sb([128, 8], F32); ssB = Buf("ss")
        dg = [sb([128, 11, 128], BF16, f"dg{i}") for i in range(3)]; dgB = [Buf(f"dg{i}") for i in range(3)]
        dgdBs = [Buf(f"dgd{i}") for i in range(3)]
        PARTS = ((0, 11), (11, 10), (21, 10))
        lnst = [sb([128, 512], F32, f"lnst{i}") for i in range(2)]; lnstB = [Buf(f"lnst{i}") for i in range(2)]

        block = stack.enter_context(nc.Block())
        V, A, P, T_, SY = "vector", "scalar", "gpsimd", "tensor", "sync"

        def iota_sel(t, pattern, cmp, fill, base, cm, src=None):
            S.emit(P, lambda e: e.affine_select(out=t, in_=(src if src is not None else t), pattern=pattern,
                                                compare_op=cmp, fill=fill, base=base, channel_multiplier=cm),
                   reads=[constB], writes=[constB])
        S.emit(P, lambda e: e.memset(identf[:], 0.0), writes=[constB])
        iota_sel(identf[:], [[-1, 128]], ALU.not_equal, 1.0, 0, 1)
        S.emit(V, lambda e: e.tensor_copy(out=identb[:], in_=identf[:]), reads=[constB], writes=[identbB])
        S.emit(P, lambda e: e.memset(onesf[:], 1.0), writes=[constB])
        S.emit(P, lambda e: e.memset(ones_row[:], 1.0), writes=[constB])
        for t, val in ((tri_inc, -1.0 / 16), (tri_end, -1.0 / 16), (cmask, 1.0)):
            S.emit(P, lambda e, t=t, val=val: e.memset(t[:], val), writes=[constB])
        iota_sel(tri_inc[:], [[1, 128]], ALU.is_ge, 0.0, 0, -1)
        iota_sel(cmask[:], [[1, 128]], ALU.is_ge, 0.0, 0, -1)
        iota_sel(tri_end[:], [[-1, 128]], ALU.is_gt, 0.0, 0, 1)
        S.emit(P, lambda e: e.memset(negc[:], -1.0 / 16), writes=[constB])
        S.emit(P, lambda e: e.memset(zhalo[:], 0.0), writes=[zhB])
        S.emit(P, lambda e: e.memset(Sst[:], 0.0), writes=[SstB])
        S.emit(P, lambda e: e.memset(cin[:], 0.0), writes=cinB)

        def load_cols(dst, rows_ap, nrows):
            r0 = 0
            while r0 < nrows:
                n = min(128, nrows - r0)
                S.dma(SY, stage[0:n, :], rows_ap[r0:r0 + n, :], writes=[stageB], track=stageB)
                ps, psB = nb()
                S.emit(T_, lambda e, ps=ps, n=n: e.matmul(ps[:, 0:n], lhsT=stage[0:n, :], rhs=identf[0:n, 0:n], start=True, stop=True),
                       reads=[stageB, constB], writes=[psB])
                S.emit(V, lambda e, ps=ps, n=n, r0=r0: e.tensor_copy(out=dst[:, r0:r0 + n], in_=ps[:, 0:n]), reads=[psB], writes=[vecB])
                r0 += n
        load_cols(g1T, norm1_g.rearrange("(k p) -> k p", p=128), 8)
        load_cols(g2T, norm2_g.rearrange("(k p) -> k p", p=128), 8)
        load_cols(cbT, conv_dw_b.rearrange("(k p) -> k p", p=128), 8)
        load_cols(lngT, conv_ln_g.rearrange("(k p) -> k p", p=128), 8)
        load_cols(lnbT, conv_ln_b.rearrange("(k p) -> k p", p=128), 8)
        load_cols(fbT, ffn_dw_b.rearrange("(k p) -> k p", p=128), 44)
        load_cols(gnT, gla_norm_g.rearrange("(k p) -> k p", p=128), 2)
        load_cols(cwT, conv_dw_w.rearrange("j (k p) -> (j k) p", p=128), 248)
        load_cols(fwT, ffn_dw_w.rearrange("j (k p) -> (j k) p", p=128), 132)
        rowst = xs[0]
        S.dma(SY, rowst[0:1, :], final_g.rearrange("(o n) -> o n", o=1), writes=[xsB[0]], track=xsB[0])
        for hf in range(2):
            ps, psB = nb()
            S.emit(T_, lambda e, ps=ps, hf=hf: e.matmul(ps[:], lhsT=onesf[0:1, :], rhs=rowst[0:1, hf * 512:(hf + 1) * 512], start=True, stop=True),
                   reads=[xsB[0], constB], writes=[psB])
            S.emit(V, lambda e, ps=ps, hf=hf: e.tensor_copy(out=fgB_t[:, hf * 512:(hf + 1) * 512], in_=ps[:]), reads=[psB], writes=[vecB])
        setup_toks = [vecB.lastw, constB.lastw, identbB.lastw]
        setup_toks.append(S.dma(P, bgrow[:], b_gate.rearrange("(o n) -> o n", o=1), writes=[vecB], track=Buf("bgrow")))
        setup_toks.append(S.dma(P, wgu[:], w_gate_up, writes=[vecB], track=Buf("wgu")))
        setup_toks.append(S.dma(SY, hp[:], hasprev, writes=[vecB], track=Buf("hp")))
        for eng in (A, V, T_, P):
            for tk in setup_toks:
                S._wait(eng, tk)
        vecB.lastw = None; vecB.readers = {}
        constB.lastw = None; constB.readers = {}

        tapon = [False]

        def dbg_tap(name, ap, bufs, force=False):
            if dbg and name in dbg and (tapon[0] or force):
                tb = Buf("dbg_" + name)
                S.dma(P, dbg_out[name], ap, reads=bufs, track=tb)
                S._wait(P, ("d", tb, tb.dcount * 16))

        def build_diag(g):
            kc, part = divmod(g, 3)
            j0, nj = PARTS[part]
            bi = g % 3
            for jj in range(nj):
                col = (j0 + jj) * 8 + kc
                S.emit(V, lambda e, bi=bi, jj=jj, col=col: e.tensor_scalar(out=dg[bi][:, jj, :], in0=identb[:], scalar1=cwT[:, col:col + 1], scalar2=None, op0=ALU.mult),
                       reads=[identbB], writes=[dgB[bi]])
            tok = S.dma(SY, dgd[g][:, 0:nj * 128], dg[bi][:, 0:nj, :].rearrange("p j i -> p (j i)"), reads=[dgB[bi]], track=dgdBs[bi])
            dgdBs[bi].lastw = tok
        _dgn = [0]

        def build_diags(n):
            for _ in range(n):
                if _dgn[0] < 24:
                    build_diag(_dgn[0])
                    _dgn[0] += 1

        def load_w(slot_i, src, c0, ncols, r0=0, nk=8):
            S.dma(P, slots[slot_i][:, 0:nk, 0:ncols],
                  src[r0 * 128:(r0 + nk) * 128, c0:c0 + ncols].rearrange("(k p) n -> p k n", p=128),
                  writes=[slotB[slot_i]], track=slotB[slot_i])
        wbBs = [Buf(f"wb{i}") for i in range(3)]

        def load_wb(slot_i, src, c0, ncols, r0=0, nk=8):
            S.dma(P, slots[slot_i][:, 0:nk, 0:ncols],
                  src[r0 * 128:(r0 + nk) * 128, c0:c0 + ncols].rearrange("(k p) n -> p k n", p=128),
                  reads=wbBs, writes=[slotB[slot_i]], track=slotB[slot_i])

        def convert_weights():
            bounce = [(actT, actTB), (silur, silurB), (m1, m1B)]
            cvB = [Buf(f"cv{i}") for i in range(3)]
            blocks = []
            for c0 in range(0, IN_DIM - 16, 512):
                blocks.append(("in", 0, 8, c0, 512))
            blocks.append(("in", 0, 8, IN_DIM - 16, 16))
            for nm in ("co", "go", "wo"):
                for hf in range(2):
                    blocks.append((nm, 0, 8, hf * 512, 512))
            for c0 in range(0, 2 * FFN, 512):
                blocks.append(("up", 0, 8, c0, 512))
            for r0, nk in ((0, 8), (8, 8), (16, 6)):
                for hf in range(2):
                    blocks.append(("dn", r0, nk, hf * 512, 512))
            for bi, (nm, r0, nk, c0, ncols) in enumerate(blocks):
                bt, bB = bounce[bi % 3]
                srcap = WSRC[nm][r0 * 128:(r0 + nk) * 128, c0:c0 + ncols].rearrange("(k p) n -> p k n", p=128)
                dstap = wb[nm][r0 * 128:(r0 + nk) * 128, c0:c0 + ncols].rearrange("(k p) n -> p k n", p=128)
                S.dma(P, bt[:, 0:nk, 0:ncols], srcap, writes=bB, track=cvB[bi % 3])
                tok = S.dma(P, dstap, bt[:, 0:nk, 0:ncols], reads=bB, track=wbBs[bi % 3])
                wbBs[bi % 3].lastw = tok

        WSRC = {"in": w_in, "co": w_conv_out, "go": w_gla_out, "wo": w_o, "up": w_ffn_up, "dn": w_ffn_down}

        def tile_wseq(is_halo):
            q = [("in", O_GL, 512, 0, 8), ("in", O_K, 512, 0, 8), ("in", O_V, 512, 0, 8), ("in", O_V + 512, 512, 0, 8), ("in", O_Q, 512, 0, 8),
                 ("in", O_R, 512, 0, 8), ("in", O_R + 512, 512, 0, 8)]
            for hf in range(2):
                q += [("in", O_A + hf * 512, 512, 0, 8), ("in", O_BG + hf * 512, 512, 0, 8)]
            for hf in range(2):
                q += [("go", hf * 512, 512, 0, 8), ("in", O_GB + hf * 512, 512, 0, 8)]
            for hf in range(2):
                q += [("in", O_GA + hf * 512, 512, 0, 8)]
            for hf in range(2):
                q += [("co", hf * 512, 512, 0, 8)]
            for hf in range(2):
                q += [("wo", hf * 512, 512, 0, 8)]
            for g in range(6):
                nblk = 4 if g < 5 else 2
                q += [("up", g * 512, nblk * 128, 0, 8), ("up", FFN + g * 512, nblk * 128, 0, 8)]
            if not is_halo:
                for hf in range(2):
                    for g3 in range(3):
                        q += [("dn", hf * 512, 512, g3 * 8, 8 if g3 < 2 else 6)]
            return q
        WSEQ = tile_wseq(True)
        for _ in range(n_main_tiles):
            WSEQ += tile_wseq(False)
        _wi = [0, 0]

        def next_w(*key):
            i = _wi[0]
            assert WSEQ[i] == key, (i, WSEQ[i], key)
            while _wi[1] < len(WSEQ) and _wi[1] <= i + NSLOT - 2:
                k = WSEQ[_wi[1]]
                load_wb(_wi[1] % NSLOT, wb[k[0]], k[1], k[2], r0=k[3], nk=k[4])
                _wi[1] += 1
            _wi[0] += 1
            return i % NSLOT

        def norm_a(srcs, ums):
            n = len(srcs)
            for i, (ap, bf) in enumerate(srcs):
                um, umB = ums[i]
                S.emit(A, lambda e, ap=ap, i=i, um=um: e.activation(out=um, in_=ap, func=AF.Square, accum_out=ss[:, i:i + 1]),
                       reads=[bf], writes=list(umB) + [ssB])
            S.emit(A, lambda e: e.activation(out=ss[:, 4:4 + n], in_=ss[:, 0:n], func=AF.Ln, scale=1.0 / D, bias=EPS), reads=[ssB], writes=[ssB])
            S.emit(A, lambda e: e.activation(out=ss[:, 4:4 + n], in_=ss[:, 4:4 + n], func=AF.Exp, scale=-0.5), reads=[ssB], writes=[ssB])
            for i, (ap, bf) in enumerate(srcs):
                um, umB = ums[i]
                if i % 2 == 0:
                    S.emit(V, lambda e, ap=ap, i=i, um=um: e.tensor_scalar(out=um, in0=ap, scalar1=ss[:, 4 + i:5 + i], scalar2=None, op0=ALU.mult),
                           reads=[bf, ssB], writes=list(umB))
                else:
                    S.emit(A, lambda e, ap=ap, i=i, um=um: e.activation(out=um, in_=ap, func=AF.Identity, scale=ss[:, 4 + i:5 + i]),
                           reads=[bf, ssB], writes=list(umB))

        def norm_b(ums, gT_, s0):
            for i, (um, umB) in enumerate(ums):
                pT, pTB = npT()
                for kc in range(8):
                    S.emit(T_, lambda e, kc=kc, um=um, pT=pT: e.transpose(out=pT[:, kc * 128:(kc + 1) * 128], in_=um[:, kc * 128:(kc + 1) * 128], identity=identb[:]),
                           reads=list(umB) + [identbB], writes=[pTB])
                s = s0 + i
                for kc in range(8):
                    S.emit(V, lambda e, kc=kc, s=s, pT=pT: e.tensor_scalar(out=uT[:, kc, s * 128:(s + 1) * 128], in0=pT[:, kc * 128:(kc + 1) * 128],
                                                                         scalar1=gT_[:, kc:kc + 1], scalar2=None, op0=ALU.mult),
                           reads=[pTB, vecB], writes=[uTB])

        def um_std(i):
            return (utms[i][:], [utmBs[i]])

        def norm_stage(srcs, gT_, s0):
            for c in range(0, len(srcs), 2):
                ums = [um_std(i) for i in range(min(2, len(srcs) - c))]
                norm_a(srcs[c:c + 2], ums)
                norm_b(ums, gT_, s0 + c)

        def load_x(row0, i):
            S.dma(SY, xs[i][:], x[row0:row0 + 128, :], writes=[xsB[i]], track=xsB[i])

        def proj_T(slot_i, s, ncols=512):
            ps, psB = nb()
            for kc in range(8):
                S.emit(T_, lambda e, kc=kc, ps=ps: e.matmul(ps[:, 0:ncols], lhsT=uT[:, kc, s * 128:(s + 1) * 128], rhs=slots[slot_i][:, kc, 0:ncols],
                                                           start=(kc == 0), stop=(kc == 7)),
                       reads=[uTB, slotB[slot_i]], writes=[psB])
            return ps, psB

        def proj_F(w_ap_fn, wB, rhs, rhsB, T, nk=8, M=128):
            ps, psB = nb()
            for kc in range(nk):
                S.emit(T_, lambda e, kc=kc, ps=ps: e.matmul(ps[0:M, 0:T], lhsT=w_ap_fn(kc), rhs=rhs[:, kc, 0:T],
                                                           start=(kc == 0), stop=(kc == nk - 1)),
                       reads=list(wB) + list(rhsB), writes=[psB])
            return ps, psB

        def gate_stage(T, NS, main, gsl, p=0):
            gate_g1(T, NS, gsl)
            gate_g2(NS, main, p)

        def gate_g1(T, NS, gsl):
            ps, psB = proj_F(lambda kc: slots[gsl][:, kc, 0:128], [slotB[gsl]], uT, [uTB], T)
            S.emit(A, lambda e, ps=ps: e.activation(out=glowT[:, 0:T], in_=ps[:, 0:T], func=AF.Copy), reads=[psB], writes=[glowTB])
            dbg_tap("glowT", glowT[0:16, :], [glowTB]); dbg_tap("wgu", wgu[:], []); dbg_tap("bgrow", bgrow[:], [])
            pre = []
            for s in range(NS):
                ps, psB = nb()
                S.emit(T_, lambda e, ps=ps, s=s: e.matmul(ps[:], lhsT=glowT[0:16, s * 128:(s + 1) * 128], rhs=wgu[:], start=True, stop=False),
                       reads=[glowTB, vecB], writes=[psB])
                S.emit(T_, lambda e, ps=ps: e.matmul(ps[:], lhsT=ones_row[:], rhs=bgrow[:], start=False, stop=True),
                       reads=[constB, vecB], writes=[psB])
                pre.append((ps, psB))
            for s in range(NS):
                ps, psB = pre[s]
                tm, tmB = ntmp()
                S.emit(A, lambda e, ps=ps, tm=tm: e.activation(out=tm[:, 0:512], in_=ps[:], func=AF.Exp, scale=-1.0), reads=[psB], writes=[tmB])
                if s == 0:
                    dbg_tap("expn", tm[:, 0:512], [tmB])
                S.emit(A, lambda e, tm=tm, s=s: e.activation(out=lbuf[:, s, :], in_=tm[:, 0:512], func=AF.Ln, bias=1.0), reads=[tmB], writes=[lB[s]])

        def gate_g2(NS, main, p=0):
            dec, decB = decs[p], decBs[p]
            for s in range(NS):
                ps, psB = nb()
                for h in range(4):
                    S.emit(T_, lambda e, ps=ps, s=s, h=h: e.matmul(ps[:, h:h + 1], lhsT=lbuf[:, s, h * 128:(h + 1) * 128], rhs=negc[:], start=True, stop=True),
                           reads=[lB[s], constB], writes=[psB])
                S.emit(A, lambda e, ps=ps, s=s: e.activation(out=dec[:, s, 0:4], in_=ps[:, 0:4], func=AF.Exp), reads=[psB], writes=[decB[s]])
                if main:
                    ps, psB = nb()
                    for h in range(4):
                        S.emit(T_, lambda e, ps=ps, s=s, h=h: e.matmul(ps[:, h * 128:(h + 1) * 128], lhsT=lbuf[:, s, h * 128:(h + 1) * 128], rhs=tri_inc[:], start=True, stop=True),
                               reads=[lB[s], constB], writes=[psB])
                    psv = ps[:].rearrange("p (h t) -> p h t", h=4)
                    S.emit(A, lambda e, psv=psv, s=s: e.activation(out=Epl[:, :, s * 128:(s + 1) * 128], in_=psv, func=AF.Exp), reads=[psB], writes=[EB[s]])
                    S.emit(A, lambda e, psv=psv, s=s: e.activation(out=Emi[:, :, s * 128:(s + 1) * 128], in_=psv, func=AF.Exp, scale=-1.0), reads=[psB], writes=[EB[s]])

        def kv_stage(NS, main, kslot_fn, vslot_fns, after_k=None):
            kslot = kslot_fn()
            for s in range(NS):
                ps, psB = nb()
                tri = tri_end
                S.emit(T_, lambda e, ps=ps, s=s, tri=tri: e.matmul(ps[:], lhsT=tri[:], rhs=lbuf[:, s, :], start=True, stop=True),
                       reads=[lB[s], constB], writes=[psB])
                tm, tmB = ntmp()
                S.emit(A, lambda e, ps=ps, tm=tm: e.activation(out=tm[:, 0:512], in_=ps[:], func=AF.Exp), reads=[psB], writes=[tmB])
                ps, psB = proj_T(kslot, s)
                S.emit(V, lambda e, ps=ps, tm=tm, s=s: e.tensor_tensor(out=kend[:, s, :], in0=ps[:], in1=tm[:, 0:512], op=ALU.mult),
                       reads=[psB, tmB], writes=[kendB[s]])
            if after_k is not None:
                after_k(kslot)
            for hf in range(2):
                vs = vslot_fns[hf]()
                for s in range(NS):
                    ps, psB = proj_T(vs, s)
                    S.emit(A, lambda e, ps=ps, s=s, hf=hf: e.activation(out=vtm[:, s, hf * 512:(hf + 1) * 512], in_=ps[:], func=AF.Copy),
                           reads=[psB], writes=[vtmB[s]])

        def state_sub(s, want_bf=None, p=0):
            dec, decB = decs[p], decBs[p]
            pk = [nb(), nb()]
            for h in range(4):
                ps, psB = pk[h // 2]
                S.emit(T_, lambda e, ps=ps, h=h: e.matmul(ps[:, (h % 2) * 256:(h % 2) * 256 + 256], lhsT=kend[:, s, h * 128:(h + 1) * 128],
                                                         rhs=vtm[:, s, h * 256:(h + 1) * 256], start=True, stop=True),
                       reads=[kendB[s], vtmB[s]], writes=[psB])
            for h in range(4):
                ps, psB = pk[h // 2]
                S.emit(V, lambda e, ps=ps, h=h: e.scalar_tensor_tensor(out=Sst[:, h * 256:(h + 1) * 256], in0=Sst[:, h * 256:(h + 1) * 256],
                                                                      scalar=dec[:, s, h:h + 1], in1=ps[:, (h % 2) * 256:(h % 2) * 256 + 256],
                                                                      op0=ALU.mult, op1=ALU.add),
                       reads=[SstB, decB[s], psB], writes=[SstB])
            if want_bf is not None:
                S.emit(A, lambda e: e.activation(out=Sbf[want_bf][:], in_=Sst[:], func=AF.Copy), reads=[SstB], writes=[SbfB[want_bf]])

        hist_slots = None
        if n_hist_tiles > 0 or True:
            load_w(0, w_in, O_K, 512); load_w(1, w_in, O_V, 512); load_w(2, w_in, O_V + 512, 512); load_w(3, w_in, O_GL, 512)
            convert_weights()
        hums = [um_std(0), um_std(1), (on[:], [onB]), (Sbf[1][:], [SbfB[1]])]

        def hist_na(t):
            for pr in range(2):
                for i in range(2):
                    load_x(t * 512 + (pr * 2 + i) * 128, i)
                norm_a([(xs[i][:], xsB[i]) for i in range(2)], hums[pr * 2:pr * 2 + 2])

        def hist_state(t):
            for s4 in range(4):
                state_sub(s4, p=t % 2)

        if n_hist_tiles > 0:
            hist_na(0)
            norm_b(hums, g1T, 0)
            gate_stage(512, 4, False, 3, p=0)
        for t in range(n_hist_tiles):
            if t + 1 < n_hist_tiles:
                hist_na(t + 1)
            build_diags(2)
            kv_stage(4, False, lambda: 0, (lambda: 1, lambda: 2))
            if t + 1 < n_hist_tiles:
                norm_b(hums, g1T, 0)
                gate_g1(512, 4, 3)
                hist_state(t)
                gate_g2(4, False, (t + 1) % 2)
            else:
                hist_state(t)
        build_diags(24)
        S.emit(A, lambda e: e.activation(out=Sbf[0][:], in_=Sst[:], func=AF.Copy), reads=[SstB], writes=[SbfB[0]])
        sbf_cur = [0]

        dbg_tap("S_hist", Sst[:], [SstB], force=True)

        def main_tile(row0, T, is_halo, out_row0, prefetched=False, next_row0=None):
            NS = T // 128
            pf_ums = [um_std(0), um_std(1),
                      (actT[:, 0:2, :].rearrange("p k t -> p (k t)"), actTB[0:2]), (actT[:, 2:4, :].rearrange("p k t -> p (k t)"), actTB[2:4])]
            if not prefetched:
                for pr in range((NS + 1) // 2):
                    nn = min(2, NS - pr * 2)
                    ums = [um_std(i) for i in range(nn)]
                    for i in range(nn):
                        load_x(row0 + (pr * 2 + i) * 128, i)
                    norm_a([(xs[i][:], xsB[i]) for i in range(nn)], ums)
                    norm_b(ums, g1T, pr * 2)

            def transposes_next():
                if next_row0 is not None:
                    norm_b(pf_ums[0:2], g1T, 0)
                    norm_b(pf_ums[2:4], g1T, 2)

            def prefetch_next():
                if next_row0 is not None:
                    for pr in range(2):
                        for i in range(2):
                            load_x(next_row0 + (pr * 2 + i) * 128, i)
                        norm_a([(xs[i][:], xsB[i]) for i in range(2)], pf_ums[pr * 2:pr * 2 + 2])
            dbg_tap("uT", uT[:], [uTB])
            gate_stage(T, NS, True, next_w("in", O_GL, 512, 0, 8))
            def k_feature_major(ks):
                for h in range(4):
                    ps, psB = proj_F(lambda kc, h=h: slots[ks][:, kc, h * 128:(h + 1) * 128], [slotB[ks]], uT, [uTB], T)
                    S.emit(V, lambda e, ps=ps, h=h: e.tensor_tensor(out=kT[:, h, 0:T], in0=ps[:, 0:T], in1=Emi[:, h, 0:T], op=ALU.mult),
                           reads=[psB] + EB[0:NS], writes=[kTB[h]])
            kv_stage(NS, True, lambda: next_w("in", O_K, 512, 0, 8),
                     (lambda: next_w("in", O_V, 512, 0, 8), lambda: next_w("in", O_V + 512, 512, 0, 8)), after_k=k_feature_major)
            qs = next_w("in", O_Q, 512, 0, 8)
            for h in range(4):
                ps, psB = proj_F(lambda kc, h=h: slots[qs][:, kc, h * 128:(h + 1) * 128], [slotB[qs]], uT, [uTB], T)
                S.emit(V, lambda e, ps=ps, h=h: e.scalar_tensor_tensor(out=qT[:, h, 0:T], in0=ps[:, 0:T], scalar=128.0 ** -0.5, in1=Epl[:, h, 0:T],
                                                                      op0=ALU.mult, op1=ALU.mult),
                       reads=[psB] + EB[0:NS], writes=[qTB[h]])
            for hf in range(2):
                rs = next_w("in", O_R + hf * 512, 512, 0, 8)
                for j in range(4):
                    kc = hf * 4 + j
                    ps, psB = proj_F(lambda kk, j=j, rs=rs: slots[rs][:, kk, j * 128:(j + 1) * 128], [slotB[rs]], uT, [uTB], T)
                    S.emit(A, lambda e, ps=ps, kc=kc: e.activation(out=silur[:, kc, 0:T], in_=ps[:, 0:T], func=AF.Silu), reads=[psB], writes=[silurB[kc]])
            dbg_tap("l", lbuf[:], lB); dbg_tap("kT", kT[:], kTB); dbg_tap("qT", qT[:], qTB); dbg_tap("vtm", vtm[:], vtmB)
            dbg_tap("kend", kend[:], kendB); dbg_tap("silur", silur[:], silurB); pass
            for hf in range(2):
                sa = next_w("in", O_A + hf * 512, 512, 0, 8)
                sg = next_w("in", O_BG + hf * 512, 512, 0, 8)
                for j in range(4):
                    kc = hf * 4 + j
                    psa, psaB = proj_F(lambda kk, j=j, sa=sa: slots[sa][:, kk, j * 128:(j + 1) * 128], [slotB[sa]], uT, [uTB], T)
                    psg, psgB = proj_F(lambda kk, j=j, sg=sg: slots[sg][:, kk, j * 128:(j + 1) * 128], [slotB[sg]], uT, [uTB], T)
                    tm, tmB = ntmp()
                    S.emit(A, lambda e, psg=psg, tm=tm: e.activation(out=tm[:, 0:T], in_=psg[:, 0:T], func=AF.Sigmoid), reads=[psgB], writes=[tmB])
                    S.emit(V, lambda e, psa=psa, tm=tm, kc=kc: e.tensor_tensor(out=cin[:, kc, 30:30 + T], in0=psa[:, 0:T], in1=tm[:, 0:T], op=ALU.mult),
                           reads=[psaB, tmB], writes=[cinB[kc]])
            dgtrk = [Buf(f"dgl{i}") for i in range(3)] if not hasattr(main_tile, "_dgtrk") else main_tile._dgtrk
            main_tile._dgtrk = dgtrk

            def load_part(g):
                if g >= 24:
                    return
                j0, nj = PARTS[g % 3]
                bi = g % 3
                S.dma(SY, dg[bi][:, 0:nj, :], dgd[g][:, 0:nj * 128].rearrange("p (j i) -> p j i", i=128), reads=dgdBs, writes=[dgB[bi]], track=dgtrk[bi])

            def conv_block(kc):
                psc, pscB = nb()
                for part in range(3):
                    g = kc * 3 + part
                    j0, nj = PARTS[part]
                    bi = g % 3
                    load_part(g + 2)
                    for jj in range(nj):
                        j = j0 + jj
                        S.emit(T_, lambda e, bi=bi, jj=jj, j=j: e.matmul(psc[:, 0:T], lhsT=dg[bi][:, jj, :], rhs=cin[:, kc, j:j + T], start=(j == 0), stop=(j == 30)),
                               reads=[dgB[bi], cinB[kc]], writes=[pscB])
                S.emit(A, lambda e: e.activation(out=cacc[:, kc, 0:T], in_=psc[:, 0:T], func=AF.Identity, bias=cbT[:, kc:kc + 1]),
                       reads=[pscB], writes=[arB[kc]])
                S.emit(A, lambda e: e.activation(out=cin[:, kc, 0:30], in_=cin[:, kc, T:T + 30], func=AF.Copy), reads=[cinB[kc]], writes=[cinB[kc]])

            gla_po = {}

            def gla_a(s):
                tok = slice(s * 128, (s + 1) * 128)
                for h in range(4):
                    ps, psB = nb()
                    S.emit(T_, lambda e, ps=ps, h=h: e.matmul(ps[:, 0:128], lhsT=kT[:, h, tok], rhs=qT[:, h, tok], start=True, stop=True),
                           reads=[kTB[h], qTB[h]], writes=[psB])
                    S.emit(V, lambda e, ps=ps, h=h: e.tensor_tensor(out=attm[:, h, :], in0=ps[:, 0:128], in1=cmask[:], op=ALU.mult),
                           reads=[psB, constB], writes=[attmB[h]])

            def gla_b(s):
                c0 = sbf_cur[0]; c1 = 1 - c0
                po = [nb(), nb()]
                gla_po[s] = po
                for h in range(4):
                    ps, psB = po[h // 2]
                    cols = slice((h % 2) * 256, (h % 2) * 256 + 256)
                    hc = slice(h * 256, (h + 1) * 256)
                    S.emit(T_, lambda e, ps=ps, h=h, cols=cols, hc=hc: e.matmul(ps[:, cols], lhsT=attm[:, h, :], rhs=vtm[:, s, hc], start=True, stop=False),
                           reads=[attmB[h], vtmB[s]], writes=[psB])
                    S.emit(T_, lambda e, ps=ps, h=h, cols=cols, hc=hc: e.matmul(ps[:, cols], lhsT=qT[:, h, s * 128:(s + 1) * 128], rhs=Sbf[c0][:, hc], start=False, stop=True),
                           reads=[qTB[h], SbfB[c0]], writes=[psB])
                state_sub(s, want_bf=c1)
                sbf_cur[0] = c1

            def gla_c(s):
                tok = slice(s * 128, (s + 1) * 128)
                po = gla_po[s]
                for h in range(4):
                    ps, psB = po[h // 2]
                    cols = slice((h % 2) * 256, (h % 2) * 256 + 256)
                    S.emit(A, lambda e, ps=ps, h=h, cols=cols: e.activation(out=on[:, h * 256:(h + 1) * 256], in_=ps[:, cols], func=AF.Square, accum_out=oss[:, h:h + 1]),
                           reads=[psB], writes=[onB, ossB])
                S.emit(A, lambda e: e.activation(out=oss[:, 4:8], in_=oss[:, 0:4], func=AF.Ln, scale=1.0 / 256, bias=EPS), reads=[ossB], writes=[ossB])
                S.emit(A, lambda e: e.activation(out=oss[:, 4:8], in_=oss[:, 4:8], func=AF.Exp, scale=-0.5), reads=[ossB], writes=[ossB])
                for h in range(4):
                    ps, psB = po[h // 2]
                    cols = slice((h % 2) * 256, (h % 2) * 256 + 256)
                    S.emit(A, lambda e, ps=ps, h=h, cols=cols: e.activation(out=on[:, h * 256:(h + 1) * 256], in_=ps[:, cols], func=AF.Identity, scale=oss[:, 4 + h:5 + h]),
                           reads=[psB, ossB], writes=[onB])

            def gla_c2(s):
                tok = slice(s * 128, (s + 1) * 128)
                pT, pTB = npT()
                for kc in range(8):
                    S.emit(T_, lambda e, kc=kc, pT=pT: e.transpose(out=pT[:, kc * 128:(kc + 1) * 128], in_=on[:, kc * 128:(kc + 1) * 128], identity=identb[:]),
                           reads=[onB, identbB], writes=[pTB])
                for kc in range(8):
                    S.emit(V, lambda e, kc=kc, pT=pT: e.scalar_tensor_tensor(out=actT[:, kc, tok], in0=pT[:, kc * 128:(kc + 1) * 128], scalar=gnT[:, kc % 2:kc % 2 + 1],
                                                                      in1=silur[:, kc, tok], op0=ALU.mult, op1=ALU.mult),
                           reads=[pTB, vecB, silurB[kc]], writes=[actTB[kc]])
            load_part(0)
            load_part(1)
            kc_next = 0
            for s in range(NS):
                gla_a(s)
                conv_block(kc_next); kc_next += 1
                gla_b(s)
                gla_c(s)
                conv_block(kc_next); kc_next += 1
                gla_c2(s)
            while kc_next < 8:
                conv_block(kc_next); kc_next += 1
            dbg_tap("cin", cin[:, :, 30:542], cinB); dbg_tap("cacc", cacc, arB); dbg_tap("oT", actT[:], actTB)
            psm = pTs[0].bitcast(F32); psmB = pTBs[0]
            psq = pTs[1].bitcast(F32); psqB = pTBs[1]
            for hf in range(2):
                sw = next_w("go", hf * 512, 512, 0, 8)
                sg = next_w("in", O_GB + hf * 512, 512, 0, 8)
                for j in range(4):
                    ob = hf * 4 + j
                    psy, psyB = proj_F(lambda kk, j=j, sw=sw: slots[sw][:, kk, j * 128:(j + 1) * 128], [slotB[sw]], actT, actTB, T)
                    psg, psgB = proj_F(lambda kk, j=j, sg=sg: slots[sg][:, kk, j * 128:(j + 1) * 128], [slotB[sg]], uT, [uTB], T)
                    tq, tqB = ntmp()
                    S.emit(A, lambda e, ob=ob, tq=tq: e.activation(out=tq[:, 0:T], in_=cacc[:, ob, 0:T], func=AF.Square), reads=[arB[ob]], writes=[tqB])
                    tm, tmB = ntmp()
                    S.emit(A, lambda e, psg=psg, tm=tm: e.activation(out=tm[:, 0:T], in_=psg[:, 0:T], func=AF.Sigmoid), reads=[psgB], writes=[tmB])
                    S.emit(V, lambda e, psy=psy, tm=tm, ob=ob: e.tensor_tensor(out=m1[:, ob, 0:T], in0=psy[:, 0:T], in1=tm[:, 0:T], op=ALU.mult),
                           reads=[psyB, tmB], writes=[m1B[ob]])
                    S.emit(T_, lambda e, ob=ob: e.matmul(psm[:, 0:T], lhsT=onesf[:], rhs=cacc[:, ob, 0:T], start=(ob == 0), stop=(ob == 7)),
                           reads=[constB, arB[ob]], writes=[psmB])
                    S.emit(T_, lambda e, ob=ob, tq=tq: e.matmul(psq[:, 0:T], lhsT=onesf[:], rhs=tq[:, 0:T], start=(ob == 0), stop=(ob == 7)),
                           reads=[constB, tqB], writes=[psqB])
            mean, rstd = lnst[0], lnst[1]
            meanB, rstdB = lnstB
            S.emit(A, lambda e: e.activation(out=mean[:, 0:T], in_=psm[:, 0:T], func=AF.Identity, scale=1.0 / D), reads=[psmB], writes=[meanB])
            S.emit(V, lambda e: e.tensor_tensor(out=rstd[:, 0:T], in0=mean[:, 0:T], in1=mean[:, 0:T], op=ALU.mult), reads=[meanB], writes=[rstdB])
            S.emit(V, lambda e: e.scalar_tensor_tensor(out=rstd[:, 0:T], in0=psq[:, 0:T], scalar=1.0 / D, in1=rstd[:, 0:T], op0=ALU.mult, op1=ALU.subtract),
                   reads=[psqB, rstdB], writes=[rstdB])
            S.emit(A, lambda e: e.activation(out=rstd[:, 0:T], in_=rstd[:, 0:T], func=AF.Ln, bias=EPS), reads=[rstdB], writes=[rstdB])
            S.emit(A, lambda e: e.activation(out=rstd[:, 0:T], in_=rstd[:, 0:T], func=AF.Exp, scale=-0.5), reads=[rstdB], writes=[rstdB])
            nmr, nmrB = mean, meanB
            S.emit(V, lambda e: e.scalar_tensor_tensor(out=mean[:, 0:T], in0=mean[:, 0:T], scalar=-1.0, in1=rstd[:, 0:T], op0=ALU.mult, op1=ALU.mult),
                   reads=[meanB, rstdB], writes=[meanB])
            for hf in range(2):
                sg = next_w("in", O_GA + hf * 512, 512, 0, 8)
                for j in range(4):
                    kc = hf * 4 + j
                    psg, psgB = proj_F(lambda kk, j=j, sg=sg: slots[sg][:, kk, j * 128:(j + 1) * 128], [slotB[sg]], uT, [uTB], T)
                    S.emit(A, lambda e, psg=psg, kc=kc: e.activation(out=cin[:, kc, 30:30 + T], in_=psg[:, 0:T], func=AF.Sigmoid), reads=[psgB], writes=[cinB[kc]])
            for kc in range(8):
                tm, tmB = ntmp()
                S.emit(V, lambda e, kc=kc, tm=tm: e.tensor_tensor(out=tm[:, 0:T], in0=cacc[:, kc, 0:T], in1=rstd[:, 0:T], op=ALU.mult),
                       reads=[arB[kc], rstdB], writes=[tmB])
                S.emit(V, lambda e, tm=tm: e.tensor_tensor(out=tm[:, 0:T], in0=tm[:, 0:T], in1=nmr[:, 0:T], op=ALU.add), reads=[tmB, nmrB], writes=[tmB])
                S.emit(A, lambda e, kc=kc, tm=tm: e.activation(out=actT[:, kc, 0:T], in_=tm[:, 0:T], func=AF.Silu, scale=lngT[:, kc:kc + 1], bias=lnbT[:, kc:kc + 1]),
                       reads=[tmB, vecB], writes=[actTB[kc]])
            dbg_tap("cact", actT[:], actTB)
            for hf in range(2):
                sw = next_w("co", hf * 512, 512, 0, 8)
                for j in range(4):
                    ob = hf * 4 + j
                    psy, psyB = proj_F(lambda kk, j=j, sw=sw: slots[sw][:, kk, j * 128:(j + 1) * 128], [slotB[sw]], actT, actTB, T)
                    tm, tmB = ntmp()
                    S.emit(V, lambda e, psy=psy, tm=tm, ob=ob: e.tensor_tensor(out=tm[:, 0:T], in0=psy[:, 0:T], in1=cin[:, ob, 30:30 + T], op=ALU.mult),
                           reads=[psyB, cinB[ob]], writes=[tmB])
                    S.emit(V, lambda e, tm=tm, ob=ob: e.tensor_tensor(out=silur[:, ob, 0:T], in0=tm[:, 0:T], in1=m1[:, ob, 0:T], op=ALU.add),
                           reads=[tmB, m1B[ob]], writes=[silurB[ob]])
            dbg_tap("merged", silur[:], silurB)
            sws = [next_w("wo", 0, 512, 0, 8), next_w("wo", 512, 512, 0, 8)]

            def wo_sub(s):
                for hf in range(2):
                    sw = sws[hf]
                    ps, psB = nb()
                    for kc in range(8):
                        S.emit(T_, lambda e, kc=kc, ps=ps, sw=sw: e.matmul(ps[:], lhsT=silur[:, kc, s * 128:(s + 1) * 128], rhs=slots[sw][:, kc, :], start=(kc == 0), stop=(kc == 7)),
                               reads=[silurB[kc], slotB[sw]], writes=[psB])
                    xi = hf
                    S.dma(SY, xs[xi][:, 0:512], x[row0 + s * 128:row0 + (s + 1) * 128, hf * 512:(hf + 1) * 512], writes=[xsB[xi]], track=xsB[xi])
                    S.emit(V, lambda e, ps=ps, hf=hf, xi=xi: e.tensor_tensor(out=hbuf[:, s, hf * 512:(hf + 1) * 512], in0=ps[:], in1=xs[xi][:, 0:512], op=ALU.add),
                           reads=[psB, xsB[xi]], writes=[hB[s]])
            for s in range(NS):
                wo_sub(s)
                if s % 2 == 1 or s == NS - 1:
                    p0 = (s // 2) * 2
                    norm_a([(hbuf[:, q, :], hB[q]) for q in range(p0, s + 1)], pf_ums[p0:s + 1])
            dbg_tap("h1", hbuf[:], hB)
            for p0 in range(0, NS, 2):
                norm_b(pf_ums[p0:min(p0 + 2, NS)], g2T, p0)
            ngrp = 6
            for g in range(ngrp):
                nblk = 4 if g < 5 else 2
                sa = next_w("up", g * 512, nblk * 128, 0, 8)
                sbb = next_w("up", FFN + g * 512, nblk * 128, 0, 8)
                for j in range(nblk):
                    i = g * 4 + j
                    accs = []
                    for which, sl in ((0, sa), (1, sbb)):
                        blk = which * 22 + i
                        ps, psB = proj_F(lambda kk, j=j, sl=sl: slots[sl][:, kk, j * 128:(j + 1) * 128], [slotB[sl]], uT, [uTB], T)
                        zb, zbB = ntmp()
                        S.emit(A, lambda e, zb=zb, blk=blk: e.activation(out=zb[:, 0:2], in_=zhalo[:, blk, :], func=AF.Copy), reads=[zhB], writes=[zbB])
                        S.emit(A, lambda e, zb=zb, ps=ps: e.activation(out=zb[:, 2:2 + T], in_=ps[:, 0:T], func=AF.Copy), reads=[psB], writes=[zbB])
                        S.emit(A, lambda e, zb=zb, blk=blk: e.activation(out=zhalo[:, blk, :], in_=zb[:, T:T + 2], func=AF.Copy), reads=[zbB], writes=[zhB])
                        ac, acB = ntmp()
                        S.emit(A, lambda e, zb=zb, ac=ac, blk=blk: e.activation(out=ac[:, 0:T], in_=zb[:, 0:T], func=AF.Identity,
                                                                               scale=fwT[:, blk:blk + 1], bias=fbT[:, blk:blk + 1]),
                               reads=[zbB, vecB], writes=[acB])
                        for jj in (1, 2):
                            S.emit(V, lambda e, zb=zb, ac=ac, blk=blk, jj=jj: e.scalar_tensor_tensor(out=ac[:, 0:T], in0=zb[:, jj:jj + T], scalar=fwT[:, jj * 44 + blk:jj * 44 + blk + 1],
                                                                                                in1=ac[:, 0:T], op0=ALU.mult, op1=ALU.add),
                                   reads=[zbB, vecB, acB], writes=[acB])
                        accs.append((ac, acB))
                    if not is_halo:
                        (aa, aaB), (ab, abB) = accs
                        S.emit(A, lambda e, aa=aa: e.activation(out=aa[:, 0:T], in_=aa[:, 0:T], func=AF.Silu), reads=[aaB], writes=[aaB])
                        S.emit(V, lambda e, aa=aa, ab=ab, i=i: e.tensor_tensor(out=gT[:, i, 0:T], in0=aa[:, 0:T], in1=ab[:, 0:T], op=ALU.mult),
                               reads=[aaB, abB], writes=[arB[i % 8]])
            prefetch_next()
            if is_halo:
                S.emit(V, lambda e: e.tensor_scalar(out=zhalo[:], in0=zhalo[:], scalar1=hp[:, 0:1], scalar2=None, op0=ALU.mult), reads=[zhB, vecB], writes=[zhB])
                transposes_next()
                return
            dbg_tap("gT", gT, arB)
            for hf in range(2):
                pss = [nb() for _ in range(NS)]
                for g3 in range(3):
                    nk = 8 if g3 < 2 else 6
                    sw = next_w("dn", hf * 512, 512, g3 * 8, nk)
                    for s in range(NS):
                        ps, psB = pss[s]
                        for kk in range(nk):
                            i = g3 * 8 + kk
                            S.emit(T_, lambda e, ps=ps, s=s, kk=kk, i=i, sw=sw: e.matmul(ps[:], lhsT=gT[:, i, s * 128:(s + 1) * 128], rhs=slots[sw][:, kk, :],
                                                                                   start=(i == 0), stop=(i == 21)),
                                   reads=[arB[i % 8], slotB[sw]], writes=[psB])
                for s in range(NS):
                    ps, psB = pss[s]
                    S.emit(V, lambda e, ps=ps, s=s, hf=hf: e.tensor_tensor(out=hbuf[:, s, hf * 512:(hf + 1) * 512], in0=ps[:], in1=hbuf[:, s, hf * 512:(hf + 1) * 512], op=ALU.add),
                           reads=[psB, hB[s]], writes=[hB[s]])
            transposes_next()
            for s in range(NS):
                S.emit(A, lambda e, s=s: e.activation(out=on[:], in_=hbuf[:, s, :], func=AF.Square, accum_out=ss[:, s:s + 1]), reads=[hB[s]], writes=[onB, ssB])
            S.emit(A, lambda e: e.activation(out=ss[:, 4:8], in_=ss[:, 0:4], func=AF.Ln, scale=1.0 / D, bias=EPS), reads=[ssB], writes=[ssB])
            S.emit(A, lambda e: e.activation(out=ss[:, 4:8], in_=ss[:, 4:8], func=AF.Exp, scale=-0.5), reads=[ssB], writes=[ssB])
            for s in range(NS):
                xi = s % 2
                S.emit(V, lambda e, s=s, xi=xi: e.scalar_tensor_tensor(out=xs[xi][:], in0=hbuf[:, s, :], scalar=ss[:, 4 + s:5 + s], in1=fgB_t[:], op0=ALU.mult, op1=ALU.mult),
                       reads=[hB[s], ssB, vecB], writes=[xsB[xi]])
                S.dma(SY, out[out_row0 + s * 128:out_row0 + (s + 1) * 128, :], xs[xi][:], reads=[xsB[xi]], track=xsB[xi])

        main_tile(HIST, HALO, True, None, prefetched=False, next_row0=(HIST + HALO if n_main_tiles > 0 else None))
        for t in range(n_main_tiles):
            tapon[0] = (t == 0)
            main_tile(HIST + HALO + t * 512, 512, False, t * 512, prefetched=True,
                      next_row0=(HIST + HALO + (t + 1) * 512 if t + 1 < n_main_tiles else None))
        for i in range(2):
            S._wait(SY, ("d", xsB[i], xsB[i].dcount * 16))
        S.replay(block)
    return nc


def make_in_maps(inputs):
    x = np.asarray(inputs["x"], dtype=np.float32)
    sq = lambda k: np.ascontiguousarray(np.asarray(inputs[k], dtype=np.float32)[0])
    shared = {k: sq(k) for k in ("norm1_g", "w_in", "conv_dw_w", "conv_dw_b", "conv_ln_g", "conv_ln_b", "w_conv_out",
                                 "w_gate_up", "b_gate", "gla_norm_g", "w_gla_out", "w_o", "norm2_g", "w_ffn_up",
                                 "ffn_dw_w", "ffn_dw_b", "w_ffn_down")}
    shared["final_g"] = np.ascontiguousarray(np.asarray(inputs["final_g"], dtype=np.float32))
    in_maps = []
    for c in range(NCORES):
        b, r = divmod(c, 4)
        start = r * SEG
        xc = np.zeros((ROWS, D), np.float32)
        lo = start - (HIST + HALO)
        src_lo = max(lo, 0)
        xc[src_lo - lo:] = x[b, src_lo:start + SEG]
        m = dict(shared)
        m["x"] = xc
        m["hasprev"] = np.full((128, 1), 1.0 if r > 0 else 0.0, np.float32)
        in_maps.append(m)
    return in_maps


_NC_CACHE = {}


def kernel(**inputs):
    if "nc" not in _NC_CACHE:
        _NC_CACHE["nc"] = build_program()
    nc = _NC_CACHE["nc"]
    in_maps = make_in_maps(inputs)
    res = run_bass_kernel_spmd(nc, in_maps, core_ids=list(range(NCORES)))
    outp = np.zeros((2, 4 * SEG, D), np.float32)
    for c in range(NCORES):
        b, r = divmod(c, 4)
        outp[b, r * SEG:(r + 1) * SEG] = res.results[c]["out"]
    return outp
```
